# Optimizing a Trainium2 kernel written in Bass

```python
import math
import jax, jax.numpy as jnp
from jax import lax
import numpy as np

D_MODEL = 4096
BATCH = 2
SEQ = 4096
DEPTH = 2

GRID_W = 64
CTX_LEN = 256
EPS = 1e-6

SSM_GROUPS = 64
SSM_GROUP_CH = 16
SSM_STATE = 64
SSM_WIDTH = SSM_GROUPS * SSM_GROUP_CH
DT_MIN = 1e-3
DT_MAX = 1e-1
GLA_HEADS = 8
GLA_DK = 64
GLA_DV = 128
GLA_KW = GLA_HEADS * GLA_DK
GLA_VW = GLA_HEADS * GLA_DV
GLA_RANK = 16
GLA_TAU = 16.0
GLA_CHUNK = 16
ATTN_HEADS = 16
ATTN_KV_HEADS = 4
HEAD_DIM = 128
ATTN_QW = ATTN_HEADS * HEAD_DIM
ATTN_KVW = ATTN_KV_HEADS * HEAD_DIM
Q_BLOCK = 128
ROPE_THETA = 10000.0
N_BRANCH = 3

IN_SPLITS = (SSM_WIDTH, SSM_WIDTH,
             GLA_KW, GLA_KW, GLA_VW, GLA_VW, 2 * GLA_RANK,
             ATTN_QW, ATTN_KVW, ATTN_KVW, ATTN_QW,
             N_BRANCH * D_MODEL)
IN_WIDTH = 22560

kernel_name = "hybrid_s5_gla_gqa_prefix_dit"


def rms_norm(x, g):
    xf = x.astype(jnp.float32)
    y = xf * lax.rsqrt(jnp.mean(xf * xf, axis=-1, keepdims=True) + EPS)
    return (y * g.astype(jnp.float32)).astype(x.dtype)


def split_columns(p):
    offsets = np.cumsum(IN_SPLITS)[:-1].tolist()
    return jnp.split(p, offsets, axis=-1)


def s5_discretize(lam_re, lam_im, log_dt, b_re, b_im):
    f32 = jnp.float32
    lam = lax.complex(jnp.minimum(lam_re.astype(f32), -1e-4), lam_im.astype(f32))
    dt = jnp.exp(log_dt.astype(f32))[:, None]
    lam_bar = jnp.exp(lam * dt)
    b = lax.complex(b_re.astype(f32), b_im.astype(f32))
    b_bar = ((lam_bar - 1.0) / lam)[..., None] * b
    return lam_bar, b_bar


def s5_scan(lam_bar, bu, h0, reverse):
    if h0 is not None:
        edge = bu.shape[0] - 1 if reverse else 0
        bu = bu.at[edge].add(lam_bar * h0)
    a = jnp.broadcast_to(lam_bar, (bu.shape[0], 1) + lam_bar.shape)

    def combine(e1, e2):
        a1, b1 = e1
        a2, b2 = e2
        return a1 * a2, a2 * b1 + b2

    _, h = lax.associative_scan(combine, (a, bu), reverse=reverse)
    return h


def s5_branch(u_lat, u_ctx, lam_re, lam_im, log_dt, b_re, b_im, c_re, c_im, d_skip, ctx_out):
    f32 = jnp.float32
    B, T, _ = u_lat.shape
    Bc, Tc, _ = u_ctx.shape
    ul = u_lat.astype(f32).reshape(B, T, SSM_GROUPS, SSM_GROUP_CH)
    uc = u_ctx.astype(f32).reshape(Bc, Tc, SSM_GROUPS, SSM_GROUP_CH)
    d = d_skip.astype(f32).reshape(SSM_GROUPS, SSM_GROUP_CH)
    y_lat = d * ul
    y_ctx = d * uc if ctx_out else None
    for direction in range(2):
        reverse = direction == 1
        lam_bar, b_bar = s5_discretize(lam_re[direction], lam_im[direction], log_dt[direction],
                                       b_re[direction], b_im[direction])
        c_mat = lax.complex(c_re[direction].astype(f32), c_im[direction].astype(f32))
        h_ctx = s5_scan(lam_bar, jnp.einsum('btgp,gnp->tbgn', uc, b_bar), None, reverse)
        h0 = h_ctx[0] if reverse else h_ctx[-1]
        h_lat = s5_scan(lam_bar, jnp.einsum('btgp,gnp->tbgn', ul, b_bar), h0, reverse)
        y_lat = y_lat + jnp.einsum('tbgn,gpn->btgp', h_lat, c_mat).real
        if ctx_out:
            y_ctx = y_ctx + jnp.einsum('tbgn,gpn->btgp', h_ctx, c_mat).real
    y_lat = y_lat.reshape(B, T, SSM_WIDTH)
    if ctx_out:
        y_ctx = y_ctx.reshape(Bc, Tc, SSM_WIDTH)
    return y_lat, y_ctx


def gla_chunked(q, k, v, log_a, s0, need_out):
    B, T, H, DK = q.shape
    DV = v.shape[-1]
    C = GLA_CHUNK
    n = T // C
    q, k, log_a = (t.reshape(B, n, C, H, DK) for t in (q, k, log_a))
    v = v.reshape(B, n, C, H, DV)
    b = jnp.cumsum(log_a, axis=2)
    b_end = b[:, :, -1]
    kv_chunk = jnp.einsum('bnchk,bnchv->bnhkv', k * jnp.exp(b_end[:, :, None] - b), v)
    if s0 is None:
        s0 = jnp.zeros((B, H, DK, DV), jnp.float32)

    def step(s, inp):
        dec, kv = inp
        return dec[..., None] * s + kv, s

    s_final, s_start = lax.scan(step, s0, (jnp.moveaxis(jnp.exp(b_end), 1, 0), jnp.moveaxis(kv_chunk, 1, 0)))
    if not need_out:
        return None, s_final
    s_start = jnp.moveaxis(s_start, 0, 1)
    o_inter = jnp.einsum('bnchk,bnhkv->bnchv', q * jnp.exp(b), s_start)
    causal = jnp.tril(jnp.ones((C, C), dtype=bool))
    rel = jnp.where(causal[:, :, None, None], b[:, :, :, None] - b[:, :, None, :], -jnp.inf)
    scores = jnp.einsum('bnthk,bnshk,bntshk->bnths', q, k, jnp.exp(rel))
    o_intra = jnp.einsum('bnths,bnshv->bnthv', scores, v)
    return (o_inter + o_intra).reshape(B, T, H, DV), s_final


def gla_branch(q_l, k_l, v_l, lr_l, q_c, k_c, v_c, lr_c, w_a, b_a, g_o, ctx_out):
    f32 = jnp.float32

    def heads(t, d):
        return t.astype(f32).reshape(t.shape[0], t.shape[1], GLA_HEADS, d)

    def log_gate(lr, direction):
        z = (lr[..., direction * GLA_RANK:(direction + 1) * GLA_RANK].astype(f32) @ w_a[direction].astype(f32)
             + b_a[direction].astype(f32))
        return heads(jax.nn.log_sigmoid(z) / GLA_TAU, GLA_DK)

    scale = GLA_DK ** -0.5
    lat = [heads(q_l, GLA_DK) * scale, heads(k_l, GLA_DK), heads(v_l, GLA_DV)]
    con = [heads(q_c, GLA_DK) * scale, heads(k_c, GLA_DK), heads(v_c, GLA_DV)]
    y_l, y_c = 0.0, 0.0
    for direction in range(2):
        if direction == 1:
            orient = lambda t: jnp.flip(t, axis=1)
        else:
            orient = lambda t: t
        qc, kc, vc = [orient(t) for t in con]
        ql, kl, vl = [orient(t) for t in lat]
        o_c, s_c = gla_chunked(qc, kc, vc, orient(log_gate(lr_c, direction)), None, ctx_out)
        o_l, _ = gla_chunked(ql, kl, vl, orient(log_gate(lr_l, direction)), s_c, True)
        y_l = y_l + orient(o_l)
        if ctx_out:
            y_c = y_c + orient(o_c)
    y_l = rms_norm(y_l, g_o).reshape(q_l.shape[0], q_l.shape[1], GLA_VW)
    y_c = rms_norm(y_c, g_o).reshape(q_c.shape[0], q_c.shape[1], GLA_VW) if ctx_out else None
    return y_l, y_c


def axial_rotary(rows):
    f32 = jnp.float32
    n_freq = HEAD_DIM // 4
    row = jnp.repeat(jnp.arange(rows, dtype=f32), GRID_W)
    col = jnp.tile(jnp.arange(GRID_W, dtype=f32), rows)
    inv_freq = ROPE_THETA ** (-jnp.arange(n_freq, dtype=f32) / n_freq)
    ang = jnp.stack([row[:, None] * inv_freq, col[:, None] * inv_freq], axis=1)
    ang = jnp.concatenate([ang, ang], axis=-1).reshape(rows * GRID_W, HEAD_DIM)
    return jnp.cos(ang)[:, None, :], jnp.sin(ang)[:, None, :]


def apply_axial_rotary(x, cos, sin):
    xr = x.reshape(x.shape[:-1] + (2, 2, HEAD_DIM // 4))
    rot = jnp.concatenate([-xr[..., 1:, :], xr[..., :1, :]], axis=-2).reshape(x.shape)
    return x * cos + rot * sin


def attention_branch(q_l, k_l, v_l, q_c, k_c, v_c, g_q, g_k, cos, sin, ctx_out):
    f32 = jnp.float32
    B, T, _ = q_l.shape
    Bc, Tc, _ = k_c.shape
    n_rep = ATTN_HEADS // ATTN_KV_HEADS
    scale = HEAD_DIM ** -0.5
    ql = apply_axial_rotary(rms_norm(q_l.reshape(B, T, ATTN_HEADS, HEAD_DIM), g_q), cos, sin)
    kl = apply_axial_rotary(rms_norm(k_l.reshape(B, T, ATTN_KV_HEADS, HEAD_DIM), g_k), cos, sin)
    vl = v_l.reshape(B, T, ATTN_KV_HEADS, HEAD_DIM)
    kc = rms_norm(k_c.reshape(Bc, Tc, ATTN_KV_HEADS, HEAD_DIM), g_k)
    vc = v_c.reshape(Bc, Tc, ATTN_KV_HEADS, HEAD_DIM)
    keys = jnp.concatenate([kc, kl], axis=1).astype(f32)
    vals = jnp.concatenate([vc, vl], axis=1).astype(f32)

    def attend(qb, kk, vv):
        s = jnp.einsum('bqkgd,bskd->bkgqs', qb.astype(f32), kk) * scale
        p = jax.nn.softmax(s, axis=-1)
        return jnp.einsum('bkgqs,bskd->bqkgd', p, vv)

    nb = T // Q_BLOCK
    qb = ql.reshape(B, nb, Q_BLOCK, ATTN_KV_HEADS, n_rep, HEAD_DIM).swapaxes(0, 1)
    o = lax.map(lambda blk: attend(blk, keys, vals), qb)
    y_l = o.swapaxes(0, 1).reshape(B, T, ATTN_QW)
    y_c = None
    if ctx_out:
        qc = rms_norm(q_c.reshape(Bc, Tc, ATTN_KV_HEADS, n_rep, HEAD_DIM), g_q)
        y_c = attend(qc, kc.astype(f32), vc.astype(f32)).reshape(Bc, Tc, ATTN_QW)
    return y_l, y_c


def merge_branches(y_ssm, z_ssm, y_gla, z_gla, y_attn, z_attn, gate_logits, w_glu, w_ps, w_pg, w_pa, w_o):
    g = jax.nn.gelu(y_ssm)
    s = g * jax.nn.sigmoid(g @ w_glu) * jax.nn.silu(z_ssm)
    gl = y_gla * jax.nn.silu(z_gla)
    at = y_attn * jax.nn.silu(z_attn)
    gates = jax.nn.sigmoid(gate_logits.astype(jnp.float32))
    g_s, g_g, g_a = jnp.split(gates, N_BRANCH, axis=-1)
    merged = g_s * (s @ w_ps) + g_g * (gl @ w_pg) + g_a * (at @ w_pa)
    return merged @ w_o


def setup_inputs(seed: int = 0) -> dict:
    key = jax.random.key(seed)
    keys = iter(jax.random.split(key, 40))

    def normal(shape, std):
        return std * jax.random.normal(next(keys), shape, jnp.float32)

    L, D = DEPTH, D_MODEL
    G, N, P = SSM_GROUPS, SSM_STATE, SSM_GROUP_CH
    x = normal((BATCH, SEQ, D), 1.0)
    c = normal((BATCH, D), 1.0)
    ctx = normal((BATCH, CTX_LEN, D), 1.0)
    c_ctx = normal((D,), 1.0)
    norm_g = 1.0 + normal((L, D), 0.02)
    w_mod = normal((L, D, 3 * D), 0.5 * D ** -0.5)
    b_mod = normal((L, 3 * D), 0.02)
    w_in = normal((L, D, IN_WIDTH), D ** -0.5)
    ssm_lam_re = -0.5 + normal((L, 2, G, N), 0.01)
    ssm_lam_im = math.pi * jnp.arange(N, dtype=jnp.float32) + normal((L, 2, G, N), 0.01)
    ssm_log_dt = jax.random.uniform(next(keys), (L, 2, G), jnp.float32, math.log(DT_MIN), math.log(DT_MAX))
    ssm_b_re = normal((L, 2, G, N, P), (2.0 * P) ** -0.5)
    ssm_b_im = normal((L, 2, G, N, P), (2.0 * P) ** -0.5)
    ssm_c_re = normal((L, 2, G, P, N), 0.5 ** 0.5)
    ssm_c_im = normal((L, 2, G, P, N), 0.5 ** 0.5)
    ssm_d = normal((L, SSM_WIDTH), 1.0)
    ssm_w_glu = normal((L, SSM_WIDTH, SSM_WIDTH), SSM_WIDTH ** -0.5)
    gla_w_a = normal((L, 2, GLA_RANK, GLA_KW), GLA_RANK ** -0.5)
    gla_b_a = normal((L, 2, GLA_KW), 0.1)
    gla_norm_g = 1.0 + normal((L, GLA_DV), 0.02)
    attn_q_g = 1.0 + normal((L, HEAD_DIM), 0.02)
    attn_k_g = 1.0 + normal((L, HEAD_DIM), 0.02)
    w_proj_ssm = normal((L, SSM_WIDTH, D), SSM_WIDTH ** -0.5)
    w_proj_gla = normal((L, GLA_VW, D), GLA_VW ** -0.5)
    w_proj_attn = normal((L, ATTN_QW, D), ATTN_QW ** -0.5)
    w_out = normal((L, D, D), D ** -0.5)
    final_g = 1.0 + normal((D,), 0.02)
    return {"x": x, "c": c, "ctx": ctx, "c_ctx": c_ctx, "norm_g": norm_g, "w_mod": w_mod, "b_mod": b_mod,
            "w_in": w_in, "ssm_lam_re": ssm_lam_re, "ssm_lam_im": ssm_lam_im, "ssm_log_dt": ssm_log_dt,
            "ssm_b_re": ssm_b_re, "ssm_b_im": ssm_b_im, "ssm_c_re": ssm_c_re, "ssm_c_im": ssm_c_im,
            "ssm_d": ssm_d, "ssm_w_glu": ssm_w_glu, "gla_w_a": gla_w_a, "gla_b_a": gla_b_a,
            "gla_norm_g": gla_norm_g, "attn_q_g": attn_q_g, "attn_k_g": attn_k_g,
            "w_proj_ssm": w_proj_ssm, "w_proj_gla": w_proj_gla, "w_proj_attn": w_proj_attn,
            "w_out": w_out, "final_g": final_g}


def reference(x, c, ctx, c_ctx, norm_g, w_mod, b_mod, w_in, ssm_lam_re, ssm_lam_im, ssm_log_dt,
              ssm_b_re, ssm_b_im, ssm_c_re, ssm_c_im, ssm_d, ssm_w_glu, gla_w_a, gla_b_a, gla_norm_g,
              attn_q_g, attn_k_g, w_proj_ssm, w_proj_gla, w_proj_attn, w_out, final_g):
    T = x.shape[1]
    rows = T // GRID_W
    cos, sin = axial_rotary(rows)
    xc = ctx
    for l in range(DEPTH):
        ctx_out = l < DEPTH - 1
        mod = jax.nn.silu(c) @ w_mod[l] + b_mod[l]
        mod_c = jax.nn.silu(c_ctx)[None] @ w_mod[l] + b_mod[l]
        shift, scale, gate = jnp.split(mod, 3, axis=-1)
        shift_c, scale_c, gate_c = jnp.split(mod_c, 3, axis=-1)
        h = rms_norm(x, norm_g[l]) * (1.0 + scale[:, None]) + shift[:, None]
        hc = rms_norm(xc, norm_g[l]) * (1.0 + scale_c[:, None]) + shift_c[:, None]
        (su, sz, gq, gk, gv, gz, glr, aq, ak, av, az, mg) = split_columns(h @ w_in[l])
        (su_c, sz_c, gq_c, gk_c, gv_c, gz_c, glr_c, aq_c, ak_c, av_c, az_c, mg_c) = split_columns(hc @ w_in[l])
        s_l, s_c = s5_branch(su, su_c, ssm_lam_re[l], ssm_lam_im[l], ssm_log_dt[l], ssm_b_re[l], ssm_b_im[l],
                             ssm_c_re[l], ssm_c_im[l], ssm_d[l], ctx_out)
        g_l, g_c = gla_branch(gq, gk, gv, glr, gq_c, gk_c, gv_c, glr_c, gla_w_a[l], gla_b_a[l],
                              gla_norm_g[l], ctx_out)
        a_l, a_c = attention_branch(aq, ak, av, aq_c, ak_c, av_c, attn_q_g[l], attn_k_g[l], cos, sin, ctx_out)
        out = merge_branches(s_l, sz, g_l, gz, a_l, az, mg, ssm_w_glu[l], w_proj_ssm[l], w_proj_gla[l],
                             w_proj_attn[l], w_out[l])
        x = x + gate[:, None] * out
        if ctx_out:
            out_c = merge_branches(s_c, sz_c, g_c, gz_c, a_c, az_c, mg_c, ssm_w_glu[l], w_proj_ssm[l],
                                   w_proj_gla[l], w_proj_attn[l], w_out[l])
            xc = xc + gate_c[:, None] * out_c
    return rms_norm(x, final_g)
```

```python
import numpy as np
import concourse.bass as bass
import concourse.mybir as mybir
from concourse.bass_utils import run_bass_kernel_spmd
from contextlib import ExitStack

F32 = mybir.dt.float32
BF16 = mybir.dt.bfloat16
AF = mybir.ActivationFunctionType
ALU = mybir.AluOpType
AX = mybir.AxisListType

SEM_WRAP = 30000


class Buf:
    __slots__ = ("t", "name", "lw", "rd", "root")

    def __init__(self, t, name="", root=None):
        self.t = t
        self.name = name
        self.lw = None
        self.rd = []
        self.root = root if root is not None else self

    def alias(self, ap):
        return Buf(ap, self.name + "_v", root=self.root)

    def __getitem__(self, idx):
        return self.t[idx]


class Prog:
    ENGS = ("pe", "act", "dve", "pool", "sp")

    def __init__(self, nc, ndma_slots=12):
        self.nc = nc
        self.stack = ExitStack()
        self.streams = {e: [] for e in self.ENGS}
        self.sems = []
        self.eng_sem = {}
        self.eng_cnt = {}
        self.waited = {e: {} for e in self.ENGS}
        self.ndma_slots = ndma_slots
        self.dma_slots = {}
        self.dma_n = {}
        self.n_ins = 0
        self.pending = {e: [] for e in self.ENGS}
        self.scopes = []
        self.banks = None

    def bank(self, i):
        if self.banks is None:
            self.banks = [self.psum(f"bank{k}", [128, 512]) for k in range(8)]
        return self.banks[i]

    def barrier(self):
        evs = []
        for e, s in self.eng_sem.items():
            if self.eng_cnt[e] > 0:
                evs.append((s, self.eng_cnt[e]))
        for e, slots in self.dma_slots.items():
            n = self.dma_n[e]
            for k, s in enumerate(slots):
                cnt = (n - k + self.ndma_slots - 1) // self.ndma_slots if n > k else 0
                if cnt > 0:
                    evs.append((s, 16 * cnt))
        evs += getattr(self, "cc_events", [])
        for e in self.ENGS:
            own = self.eng_sem.get(e)
            w = self.waited[e]
            for (s, v) in evs:
                if s == own:
                    continue
                if w.get(s, 0) < v:
                    w[s] = v
                    self.pending[e].append((s, v))

    def open_scope(self):
        self.scopes.append(ExitStack())

    def close_scope(self):
        self.barrier()
        self.emit_segment()
        self.scopes.pop().close()

    def new_sem(self, name):
        s = self.stack.enter_context(self.nc.semaphore(name))
        self.sems.append(s)
        return len(self.sems) - 1

    def sbuf(self, name, shape, dtype):
        st = self.scopes[-1] if self.scopes else self.stack
        self.uid = getattr(self, "uid", 0) + 1
        t = st.enter_context(self.nc.sbuf_tensor(f"{name}_{self.uid}", list(shape), dtype))
        return Buf(t, name)

    def psum(self, name, shape, dtype=F32):
        t = self.stack.enter_context(self.nc.psum_tensor(name, list(shape), dtype))
        return Buf(t, name)

    def dram(self, name, shape, dtype, kind="Internal"):
        t = self.nc.dram_tensor(name, list(shape), dtype, kind=kind)
        return Buf(t.ap(), name)

    def view(self, ap, name=""):
        return Buf(ap, name)

    def _collect_waits(self, eng, reads, writes):
        evs = []
        for b in reads:
            b = b.root
            if b.lw is not None:
                evs.append(b.lw)
        for b in writes:
            b = b.root
            if b.lw is not None:
                evs.append(b.lw)
            evs.extend(b.rd)
        need = {}
        w = self.waited[eng]
        own = self.eng_sem.get(eng)
        for (s, v) in evs:
            if w.get(s, 0) >= v:
                continue
            if s == own and (eng == "pe" or v > self.eng_cnt[eng]):
                continue
            if need.get(s, 0) < v:
                need[s] = v
        for s, v in need.items():
            w[s] = v
        return list(need.items())

    def _mark(self, ev, reads, writes):
        for b in writes:
            b = b.root
            b.lw = ev
            b.rd = []
        for b in reads:
            b = b.root
            b.rd.append(ev)
            if len(b.rd) > 64:
                m = {}
                for (s, v) in b.rd:
                    if m.get(s, 0) < v:
                        m[s] = v
                b.rd = list(m.items())

    def op(self, eng, fn, reads=(), writes=(), signal=True):
        waits = self._collect_waits(eng, reads, writes)
        if self.pending[eng]:
            waits = waits + self.pending[eng]
            self.pending[eng] = []
        ev = None
        inc = None
        if signal:
            if eng not in self.eng_sem or self.eng_cnt[eng] >= SEM_WRAP:
                self.eng_sem[eng] = self.new_sem(f"s_{eng}_{len(self.sems)}")
                self.eng_cnt[eng] = 0
            self.eng_cnt[eng] += 1
            s = self.eng_sem[eng]
            ev = (s, self.eng_cnt[eng])
            inc = (s, 1)
        else:
            if eng not in self.eng_sem or self.eng_cnt[eng] >= SEM_WRAP:
                self.eng_sem[eng] = self.new_sem(f"s_{eng}_{len(self.sems)}")
                self.eng_cnt[eng] = 0
            ev = (self.eng_sem[eng], self.eng_cnt[eng] + 1)
        self.streams[eng].append((fn, waits, inc))
        self._mark(ev, reads, writes)
        self.n_ins += 1
        return ev

    def dma(self, eng, out_ap, in_ap, reads=(), writes=(), **kw):
        if eng not in self.dma_slots:
            self.dma_slots[eng] = [self.new_sem(f"d_{eng}_{i}") for i in range(self.ndma_slots)]
            self.dma_n[eng] = 0
        i = self.dma_n[eng]
        self.dma_n[eng] += 1
        slot = i % self.ndma_slots
        s = self.dma_slots[eng][slot]
        gen = i // self.ndma_slots
        waits = self._collect_waits(eng, reads, writes)
        if self.pending[eng]:
            waits = waits + self.pending[eng]
            self.pending[eng] = []
        if gen > 0:
            w = self.waited[eng]
            if w.get(s, 0) < 16 * gen:
                w[s] = 16 * gen
                waits = [x for x in waits if x[0] != s] + [(s, 16 * gen)]
        ev = (s, 16 * (gen + 1))

        def fn(e, out_ap=out_ap, in_ap=in_ap, kw=kw):
            return e.dma_start(out=out_ap, in_=in_ap, **kw)

        self.streams[eng].append((fn, waits, (s, 16)))
        self._mark(ev, reads, writes)
        self.n_ins += 1
        return ev

    def collective(self, kind, op, groups, in_ap, out_ap, reads=(), writes=(), inc=1):
        eng = "pool"
        NS = 4
        if not hasattr(self, "cc_slots"):
            self.cc_slots = [self.new_sem(f"cc_{i}") for i in range(NS)]
            self.cc_n = 0
        i = self.cc_n
        self.cc_n += 1
        s = self.cc_slots[i % NS]
        gen = i // NS
        waits = self._collect_waits(eng, reads, writes)
        if self.pending[eng]:
            waits = waits + self.pending[eng]
            self.pending[eng] = []
        if gen > 0:
            w = self.waited[eng]
            if w.get(s, 0) < gen:
                w[s] = gen
                waits = [x for x in waits if x[0] != s] + [(s, gen)]
        ev = (s, gen + 1)

        def fn(e):
            return e.collective_compute(kind, op, replica_groups=groups, ins=[in_ap], outs=[out_ap])

        self.streams[eng].append((fn, waits, (s, 1)))
        self._mark(ev, reads, writes)
        self.cc_events = [(self.cc_slots[k], (self.cc_n - k + NS - 1) // NS) for k in range(NS) if self.cc_n > k]
        self.n_ins += 1
        return ev

    def finish(self, final_bufs):
        evs = []
        for b in final_bufs:
            if b.root.lw is not None:
                evs.append(b.root.lw)
        self.final_waits = evs

    def emit(self):
        self.emit_segment(final=True)
        self.stack.close()

    def emit_segment(self, final=False):
        nc = self.nc
        streams = self.streams
        self.streams = {e: [] for e in self.ENGS}
        sems = self.sems
        final_waits = getattr(self, "final_waits", []) if final else []

        def run(e, name):
            for (fn, waits, inc) in streams[name]:
                for (s, v) in waits:
                    e.wait_ge(sems[s], v)
                ins = fn(e)
                if inc is not None:
                    ins.then_inc(sems[inc[0]], inc[1])
            if name == "sp":
                for (s, v) in final_waits:
                    e.wait_ge(sems[s], v)

        with nc.Block() as block:
            @block.tensor
            def _(e):
                run(e, "pe")

            @block.scalar
            def _(e):
                run(e, "act")

            @block.vector
            def _(e):
                run(e, "dve")

            @block.gpsimd
            def _(e):
                run(e, "pool")

            @block.sync
            def _(e):
                run(e, "sp")


NCORES = 8
D = 4096
NDC = 32
EPS = 1e-6


def _nc():
    return bass.Bass("TRN2", target_bir_lowering=False)


def build_M():
    nc = _nc()
    P = Prog(nc)
    cT = P.dram("cT", [128, 32, 3], F32, "ExternalInput")
    wm = P.dram("wm", [2, 4096, 1536], F32, "ExternalInput")
    bm = P.dram("bm", [128, 2, 12], F32, "ExternalInput")
    out = P.dram("modT", [128, 2, 12, 3], F32, "ExternalOutput")
    sc = P.sbuf("sc", [128, 32, 3], F32)
    bs = P.sbuf("bs", [128, 2, 12], F32)
    acc = P.sbuf("acc", [128, 2, 12, 3], F32)
    wt = [P.sbuf(f"wt{i}", [128, 4, 1536], F32) for i in range(2)]
    ps = [P.psum(f"ps{i}", [128, 12, 3]) for i in range(2)]
    P.dma("sp", sc[:], cT[:], reads=[cT], writes=[sc])
    P.dma("sp", bs[:], bm[:], reads=[bm], writes=[bs])
    P.op("act", lambda e: e.activation(sc[:], sc[:], AF.Silu), reads=[sc], writes=[sc])
    it = 0
    for l in range(2):
        for g in range(8):
            w = wt[it % 2]
            pp = ps[it % 2]
            src = wm.t[l, g * 512:(g + 1) * 512, :].rearrange("(k p) n -> p k n", p=128)
            P.dma("sp", w[:], src, reads=[wm], writes=[w])
            for j in range(12):
                for k in range(4):
                    kc = g * 4 + k
                    P.op("pe", lambda e, pp=pp, w=w, j=j, k=k, kc=kc: e.matmul(
                        pp[:, j, :], w[:, k, j * 128:(j + 1) * 128], sc[:, kc, :],
                        start=(k == 0), stop=(k == 3)),
                        reads=[w, sc], writes=[pp], signal=(k == 3 and j == 11))
            if g == 0:
                P.op("dve", lambda e, pp=pp, l=l: e.tensor_copy(acc[:, l], pp[:]), reads=[pp], writes=[acc])
            else:
                P.op("dve", lambda e, pp=pp, l=l: e.tensor_tensor(acc[:, l], acc[:, l], pp[:], ALU.add),
                     reads=[pp, acc], writes=[acc])
            it += 1
    for l in range(2):
        for r in range(3):
            P.op("dve", lambda e, l=l, r=r: e.tensor_tensor(acc[:, l, :, r], acc[:, l, :, r], bs[:, l, :], ALU.add),
                 reads=[acc, bs], writes=[acc])
    P.dma("sp", out[:], acc[:], reads=[acc], writes=[out])
    P.finish([out])
    P.emit()
    return nc


def build_A(nlat=1024, nctx=64, out_bf16=True):
    nc = _nc()
    P = Prog(nc)
    NT = nlat + nctx
    odt = BF16 if out_bf16 else F32
    xT = P.dram("xT", [NDC, 128, NT], F32, "ExternalInput")
    ng = P.dram("ng", [128, NDC], F32, "ExternalInput")
    ms = P.dram("ms", [128, 96, 2], F32, "ExternalInput")
    hT = P.dram("hT", [NDC, 128, NT], odt, "ExternalOutput")
    emit_A(P, xT, ng, ms, hT, nlat, nctx, odt)
    P.finish([hT])
    P.emit()
    return nc


def emit_A(P, xT, ng, ms, hT, nlat, nctx, odt):
    ngs = P.sbuf("a_ng", [128, NDC], F32)
    mss = P.sbuf("a_ms", [128, 96, 2], F32)
    Av = P.sbuf("a_A", [128, 2, NDC], F32)
    Bv = P.sbuf("a_B", [128, 2, NDC], F32)
    onesm = P.sbuf("a_ones", [128, 128], F32)
    epsb = P.sbuf("a_eps", [128, 1], F32)
    xs = P.sbuf("a_xs", [128, NDC, 512], F32)
    ho = P.sbuf("a_ho", [128, NDC, 512], odt)
    sq = [P.sbuf(f"a_sq{i}", [128, 512], F32) for i in range(2)]
    tmp = [P.sbuf(f"a_tmp{i}", [128, 512], F32) for i in range(2)]
    rstd = P.sbuf("a_rstd", [128, 512], F32)
    pss = P.psum("a_pss", [128, 512])
    P.dma("sp", ngs[:], ng[:], reads=[ng], writes=[ngs])
    P.dma("sp", mss[:], ms[:], reads=[ms], writes=[mss])
    P.op("dve", lambda e: e.memset(onesm[:], 1.0 / D), writes=[onesm])
    P.op("dve", lambda e: e.memset(epsb[:], EPS), writes=[epsb])
    for t in range(2):
        P.op("dve", lambda e, t=t: e.scalar_tensor_tensor(Av[:, t, :], mss[:, 32:64, t], 1.0, ngs[:], ALU.add, ALU.mult),
             reads=[mss, ngs], writes=[Av])
        P.op("dve", lambda e, t=t: e.tensor_copy(Bv[:, t, :], mss[:, 0:32, t]), reads=[mss], writes=[Bv])
    blocks = []
    o = 0
    while o < nlat:
        n = min(512, nlat - o)
        blocks.append((o, n, 0))
        o += n
    while o < nlat + nctx:
        n = min(512, nlat + nctx - o)
        blocks.append((o, n, 1))
        o += n
    for (o, n, t) in blocks:
        P.dma("sp", xs[:, :, 0:n], xT.t[:, :, o:o + n].rearrange("c p n -> p c n"), reads=[xT], writes=[xs])
        for dc in range(NDC):
            s = sq[dc % 2]
            P.op("act", lambda e, s=s, dc=dc, n=n: e.activation(s[:, 0:n], xs[:, dc, 0:n], AF.Square), reads=[xs], writes=[s])
            P.op("pe", lambda e, s=s, dc=dc, n=n: e.matmul(pss[:, 0:n], onesm[:], s[:, 0:n], start=(dc == 0), stop=(dc == NDC - 1)),
                 reads=[onesm, s], writes=[pss])
        P.op("act", lambda e, n=n: e.activation(rstd[:, 0:n], pss[:, 0:n], AF.Sqrt, bias=epsb[:], scale=1.0), reads=[pss, epsb], writes=[rstd])
        P.op("dve", lambda e, n=n: e.reciprocal(rstd[:, 0:n], rstd[:, 0:n]), reads=[rstd], writes=[rstd])
        for dc in range(NDC):
            tm = tmp[dc % 2]
            P.op("dve", lambda e, tm=tm, dc=dc, n=n, t=t: e.scalar_tensor_tensor(
                tm[:, 0:n], xs[:, dc, 0:n], Av[:, t, dc:dc + 1], rstd[:, 0:n], ALU.mult, ALU.mult),
                reads=[xs, Av, rstd], writes=[tm])
            P.op("act", lambda e, tm=tm, dc=dc, n=n, t=t: e.activation(
                ho[:, dc, 0:n], tm[:, 0:n], AF.Identity, bias=Bv[:, t, dc:dc + 1], scale=1.0),
                reads=[tm, Bv], writes=[ho])
        P.dma("sp", hT.t[:, :, o:o + n].rearrange("c p n -> p c n"), ho[:, :, 0:n], reads=[ho], writes=[hT])


TALL = 4352
NCTX = 256
NLAT = 4096
TM_W = 384
C_GV, C_AV = 0, 256
FM_W = 2208
R_GQ, R_GK, R_GZ, R_AQ, R_AK, R_AZ, R_SU, R_SZ, R_LR = 0, 128, 256, 512, 1024, 1152, 1664, 1920, 2176


def emit_B1(P, hTb, wq, tm, fm, tall=TALL):
    hb = [P.sbuf(f"b1_h{i}", [128, NDC, 512], BF16) for i in range(2)]
    W = P.sbuf("b1_w", [128, NDC, 1024], BF16)
    stg = [P.sbuf(f"b1_s{i}", [128, 1024], F32) for i in range(3)]
    ps = [(P.bank(0), P.bank(1)), (P.bank(2), P.bank(3))]
    nblk = (tall + 511) // 512
    passes = [("tm", 0, TM_W), ("fm", TM_W, 1024), ("fm", TM_W + 1024, 1024), ("fm", TM_W + 2048, FM_W - 2048)]
    it = 0
    si = 0
    for (kind, c0, ncol) in passes:
        for half in range(2):
            P.dma("pool", W[:, half * 16:(half + 1) * 16, 0:ncol],
                  wq.t[half * 2048:(half + 1) * 2048, c0:c0 + ncol].rearrange("(k p) n -> p k n", p=128),
                  reads=[wq], writes=[W])
        for tb in range(nblk):
            o = tb * 512
            n = min(512, tall - o)
            h = hb[it % 2]
            it += 1
            rd_cpn(P, hTb, o, n, h, lambda so, pn, h=h: h[:, :, so:so + pn])
            if kind == "tm":
                for sub in range(n // 128):
                    pp = ps[si % 2]
                    st = stg[si % 3]
                    si += 1
                    for (b0, bw, bank) in ((0, ncol, 0),):
                        for kc in range(NDC):
                            P.op("pe", lambda e, pp=pp, h=h, kc=kc, sub=sub, b0=b0, bw=bw, bank=bank: e.matmul(
                                pp[bank][:, 0:bw], h[:, kc, sub * 128:(sub + 1) * 128], W[:, kc, b0:b0 + bw],
                                start=(kc == 0), stop=(kc == NDC - 1)),
                                reads=[h, W], writes=[pp[bank]], signal=(kc == NDC - 1))
                    P.op("act", lambda e, pp=pp, st=st, ncol=ncol: e.activation(st[:, 0:ncol], pp[0][:, 0:ncol], AF.Copy), reads=[pp[0]], writes=[st])
                    r0 = o + sub * 128
                    P.dma("sp", tm.t[r0:r0 + 128, 0:ncol], st[:, 0:ncol], reads=[st], writes=[tm])
            else:
                r_base = c0 - TM_W
                ncc = (ncol + 127) // 128
                for cc in range(ncc):
                    m = min(128, ncol - cc * 128)
                    pp = P.bank(si % 4)
                    st = stg[si % 3]
                    for kc in range(NDC):
                        P.op("pe", lambda e, pp=pp, h=h, kc=kc, cc=cc, m=m, n=n: e.matmul(
                            pp[0:m, 0:n], W[:, kc, cc * 128:cc * 128 + m], h[:, kc, 0:n],
                            start=(kc == 0), stop=(kc == NDC - 1)),
                            reads=[h, W], writes=[pp], signal=(kc == NDC - 1))
                    if si % 2 == 0:
                        P.op("act", lambda e, pp=pp, st=st, m=m, n=n: e.activation(st[0:m, 0:n], pp[0:m, 0:n], AF.Copy), reads=[pp], writes=[st])
                    else:
                        P.op("dve", lambda e, pp=pp, st=st, m=m, n=n: e.tensor_copy(st[0:m, 0:n], pp[0:m, 0:n]), reads=[pp], writes=[st])
                    si += 1
                    rr = r_base + cc * 128
                    P.dma("sp", fm.t[rr:rr + m, o:o + n], st[0:m, 0:n], reads=[st], writes=[fm])


def emit_attn(P, fm, tm, cosT, sinT, rmT, gqk, atT, ctx_out, tall=TALL, nctx=NCTX):
    nlat = tall - nctx
    ntile = tall // 128
    cs = P.sbuf("at_cos", [128, nlat], F32)
    sn = P.sbuf("at_sin", [128, nlat], F32)
    rm = P.sbuf("at_rm", [128, 128], F32)
    g2 = P.sbuf("at_g", [128, 2], F32)
    ones_f = P.sbuf("at_1f", [128, 128], F32)
    ones_b = P.sbuf("at_1b", [128, 128], BF16)
    epsb = P.sbuf("at_eps", [128, 1], F32)
    KT = P.sbuf("at_KT", [128, tall], BF16)
    V = P.sbuf("at_V", [128, ntile, 128], BF16)
    QT = [P.sbuf(f"at_QT{i}", [128, 512], BF16) for i in range(2)]
    xs = [P.sbuf(f"at_xs{i}", [128, 512], F32) for i in range(2)]
    sq = P.sbuf("at_sq", [128, 512], F32)
    rs = P.sbuf("at_rs", [128, 512], F32)
    xn = P.sbuf("at_xn", [128, 512], F32)
    t1 = P.sbuf("at_t1", [128, 512], F32)
    t2 = P.sbuf("at_t2", [128, 512], F32)
    pb = [P.sbuf(f"at_p{i}", [128, 512], BF16) for i in range(3)]
    az = P.sbuf("at_az", [128, 512], F32)
    rl = P.sbuf("at_rl", [128, 512], F32)
    ob = P.sbuf("at_ob", [128, 512], F32)
    oo = [P.sbuf(f"at_oo{i}", [128, 512], BF16) for i in range(2)]
    ps_ms, ps_rot = P.bank(0), P.bank(1)
    ps_s = [P.bank(2), P.bank(3), P.bank(4)]
    ps_o, ps_l = P.bank(5), P.bank(6)
    P.dma("sp", cs[:], cosT[:], reads=[cosT], writes=[cs])
    P.dma("sp", sn[:], sinT[:], reads=[sinT], writes=[sn])
    P.dma("sp", rm[:], rmT[:], reads=[rmT], writes=[rm])
    P.dma("sp", g2[:], gqk[:], reads=[gqk], writes=[g2])
    P.op("dve", lambda e: e.memset(ones_f[:], 1.0 / 128), writes=[ones_f])
    P.op("dve", lambda e: e.memset(ones_b[:], 1.0), writes=[ones_b])
    P.op("dve", lambda e: e.memset(epsb[:], EPS), writes=[epsb])
    P.dma("pool", V[:], tm.t[:, C_AV:C_AV + 128].rearrange("(t p) c -> p t c", p=128), reads=[tm], writes=[V])
    cnt = [0]

    def prep(r0, o, n, gi, rope, pos0, dst, dstb):
        x = xs[cnt[0] % 2]
        cnt[0] += 1
        P.dma("sp", x[:, 0:n], fm.t[r0:r0 + 128, o:o + n], reads=[fm], writes=[x])
        P.op("act", lambda e: e.activation(sq[:, 0:n], x[:, 0:n], AF.Square), reads=[x], writes=[sq])
        P.op("pe", lambda e: e.matmul(ps_ms[:, 0:n], ones_f[:], sq[:, 0:n], start=True, stop=True), reads=[ones_f, sq], writes=[ps_ms])
        P.op("act", lambda e: e.activation(rs[:, 0:n], ps_ms[:, 0:n], AF.Sqrt, bias=epsb[:], scale=1.0), reads=[ps_ms, epsb], writes=[rs])
        P.op("dve", lambda e: e.reciprocal(rs[:, 0:n], rs[:, 0:n]), reads=[rs], writes=[rs])
        if not rope:
            P.op("dve", lambda e: e.scalar_tensor_tensor(dst, x[:, 0:n], g2[:, gi:gi + 1], rs[:, 0:n], ALU.mult, ALU.mult),
                 reads=[x, g2, rs], writes=[dstb])
            return
        P.op("dve", lambda e: e.scalar_tensor_tensor(xn[:, 0:n], x[:, 0:n], g2[:, gi:gi + 1], rs[:, 0:n], ALU.mult, ALU.mult),
             reads=[x, g2, rs], writes=[xn])
        P.op("pe", lambda e: e.matmul(ps_rot[:, 0:n], rm[:], xn[:, 0:n], start=True, stop=True), reads=[rm, xn], writes=[ps_rot])
        P.op("pool", lambda e: e.tensor_tensor(t1[:, 0:n], xn[:, 0:n], cs[:, pos0:pos0 + n], ALU.mult), reads=[xn, cs], writes=[t1])
        P.op("dve", lambda e: e.tensor_tensor(t2[:, 0:n], ps_rot[:, 0:n], sn[:, pos0:pos0 + n], ALU.mult), reads=[ps_rot, sn], writes=[t2])
        P.op("dve", lambda e: e.tensor_tensor(dst, t1[:, 0:n], t2[:, 0:n], ALU.add), reads=[t1, t2], writes=[dstb])

    prep(R_AK, 0, nctx, 1, False, 0, KT[:, 0:nctx], KT)
    o = nctx
    while o < tall:
        n = min(512, tall - o)
        prep(R_AK, o, n, 1, True, o - nctx, KT[:, o:o + n], KT)
        o += n
    scale = 128 ** -0.5
    qi = 0
    pi = 0
    for hh in range(4):
        blocks = []
        if ctx_out:
            blocks.append((0, nctx, False, nctx // 128))
        o = nctx
        while o < tall:
            n = min(512, tall - o)
            blocks.append((o, n, True, ntile))
            o += n
        for (o, n, rope, nk) in blocks:
            q = QT[qi % 2]
            qi += 1
            prep(R_AQ + hh * 128, o, n, 0, rope, o - nctx, q[:, 0:n], q)
            tiles = []
            for kc in range(nk):
                tiles.append((ps_s[pi % 3], pb[pi % 3]))
                pi += 1

            def qk(kc):
                s_ps = tiles[kc][0]
                P.op("pe", lambda e, s_ps=s_ps, q=q, kc=kc, n=n: e.matmul(s_ps[:, 0:n], KT[:, kc * 128:(kc + 1) * 128], q[:, 0:n], start=True, stop=True),
                     reads=[KT, q], writes=[s_ps])

            qk(0)
            if nk > 1:
                qk(1)
            for kc in range(nk):
                s_ps, p_sb = tiles[kc]
                P.op("act", lambda e, s_ps=s_ps, p_sb=p_sb, n=n: e.activation(p_sb[:, 0:n], s_ps[:, 0:n], AF.Exp, scale=scale),
                     reads=[s_ps], writes=[p_sb])
                if kc + 2 < nk:
                    qk(kc + 2)
                P.op("pe", lambda e, p_sb=p_sb, kc=kc, n=n, nk=nk: e.matmul(ps_o[:, 0:n], V[:, kc, :], p_sb[:, 0:n], start=(kc == 0), stop=(kc == nk - 1)),
                     reads=[V, p_sb], writes=[ps_o], signal=False)
                P.op("pe", lambda e, p_sb=p_sb, kc=kc, n=n, nk=nk: e.matmul(ps_l[:, 0:n], ones_b[:], p_sb[:, 0:n], start=(kc == 0), stop=(kc == nk - 1)),
                     reads=[ones_b, p_sb], writes=[ps_l])
            r0 = R_AZ + hh * 128
            P.dma("sp", az[:, 0:n], fm.t[r0:r0 + 128, o:o + n], reads=[fm], writes=[az])
            P.op("act", lambda e, n=n: e.activation(az[:, 0:n], az[:, 0:n], AF.Silu), reads=[az], writes=[az])
            P.op("dve", lambda e, n=n: e.reciprocal(rl[:, 0:n], ps_l[:, 0:n]), reads=[ps_l], writes=[rl])
            P.op("dve", lambda e, n=n: e.tensor_tensor(ob[:, 0:n], ps_o[:, 0:n], rl[:, 0:n], ALU.mult), reads=[ps_o, rl], writes=[ob])
            ot = oo[qi % 2]
            P.op("pool", lambda e, n=n, ot=ot: e.tensor_tensor(ot[:, 0:n], ob[:, 0:n], az[:, 0:n], ALU.mult), reads=[ob, az], writes=[ot])
            wr_rows(P, atT, hh * 128, 128, o, n, ot, lambda so, pn, ot=ot: ot[:, so:so + pn])


def rope_tables(nlat):
    rows = nlat // 64
    nf = 32
    row = np.repeat(np.arange(rows, dtype=np.float32), 64)
    col = np.tile(np.arange(64, dtype=np.float32), rows)
    inv = (10000.0 ** (-np.arange(nf, dtype=np.float32) / nf)).astype(np.float32)
    ang = np.stack([row[:, None] * inv, col[:, None] * inv], axis=1)
    ang = np.concatenate([ang, ang], axis=-1).reshape(rows * 64, 128).astype(np.float32)
    cosT = np.ascontiguousarray(np.cos(ang).T.astype(np.float32))
    sinT = np.ascontiguousarray(np.sin(ang).T.astype(np.float32))
    rm = np.zeros((128, 128), np.float32)
    for a in range(2):
        for f in range(32):
            rm[a * 64 + 32 + f, a * 64 + f] = -1.0
            rm[a * 64 + f, a * 64 + 32 + f] = 1.0
    return cosT, sinT, rm


def build_B(tall=TALL, nctx=NCTX, ctx_out=True, parts=("b1", "attn")):
    nc = _nc()
    P = Prog(nc)
    nlat = tall - nctx
    hTb = P.dram("hTb", [NDC, 128, tall], BF16, "ExternalInput")
    wq = P.dram("wq", [D, TM_W + FM_W], F32, "ExternalInput")
    cosT = P.dram("cosT", [128, nlat], F32, "ExternalInput")
    sinT = P.dram("sinT", [128, nlat], F32, "ExternalInput")
    rmT = P.dram("rmT", [128, 128], F32, "ExternalInput")
    gqk = P.dram("gqk", [128, 2], F32, "ExternalInput")
    dbg = "dbg" in parts
    tm = P.dram("tm", [tall, TM_W], F32, "ExternalOutput" if dbg else "Internal")
    fm = P.dram("fm", [FM_W, tall], F32, "ExternalOutput" if dbg else "Internal")
    atT = P.dram("atT", [512, tall], BF16, "ExternalOutput")
    s5p = P.dram("s5p", [128, 32, 4], F32, "ExternalInput")
    s5b = P.dram("s5b", [128, 32, 2, 16], F32, "ExternalInput")
    s5c = P.dram("s5c", [128, 32, 16], F32, "ExternalInput")
    s5d = P.dram("s5d", [16, 16], F32, "ExternalInput")
    s5k = P.dram("s5k", [128, 260], F32, "ExternalInput")
    gT = P.dram("gT", [256, tall], BF16, "ExternalOutput")
    gsT = P.dram("gsT", [256, tall], BF16, "ExternalOutput")
    glw = P.dram("glw", [16, 4, 64], F32, "ExternalInput")
    glb = P.dram("glb", [64, 4], F32, "ExternalInput")
    glg = P.dram("glg", [128, 1], F32, "ExternalInput")
    glk = P.dram("glk", [128, 384], F32, "ExternalInput")
    cmk = P.dram("cmk", [64, tall], F32, "ExternalInput")
    glT = P.dram("glT", [256, tall], BF16, "ExternalOutput")
    outs = [atT, gT, gsT, glT]
    if dbg:
        outs += [tm, fm]
    P.open_scope()
    emit_B1(P, hTb, wq, tm, fm, tall)
    P.close_scope()
    if "attn" in parts:
        P.open_scope()
        emit_attn(P, fm, tm, cosT, sinT, rmT, gqk, atT, ctx_out, tall, nctx)
        P.close_scope()
    if "gla" in parts:
        P.open_scope()
        emit_gla(P, fm, tm, glw, glb, glg, glk, cmk, glT, tall, nctx)
        P.close_scope()
    if "s5" in parts:
        P.open_scope()
        emit_s5(P, fm, s5p, s5b, s5c, s5d, s5k, gT, gsT, tall, nctx)
        P.close_scope()
    P.finish(outs)
    P.emit()
    return nc


def emit_s5(P, fm, s5p, s5b, s5c, s5d, cst, gT, gsT, tall=TALL, nctx=NCTX):
    nlat = tall - nctx
    NP = 32
    c = P.sbuf("s5_cst", [128, 260], F32)
    P.dma("sp", c[:], cst[:], reads=[cst], writes=[c])
    ident, swap = c[:, 0:128], c[:, 128:256]
    sgn, m0, m1, hpi = c[:, 256:257], c[:, 257:258], c[:, 258:259], c[:, 259:260]
    prm = P.sbuf("s5_prm", [128, NP, 4], F32)
    Bd = P.sbuf("s5_B", [128, NP, 2, 16], F32)
    Cw = P.sbuf("s5_C", [128, NP, 16], F32)
    dsk = P.sbuf("s5_d", [16, 16], F32)
    P.dma("sp", prm[:], s5p[:], reads=[s5p], writes=[prm])
    P.dma("sp", Bd[:], s5b[:], reads=[s5b], writes=[Bd])
    P.dma("sp", Cw[:], s5c[:], reads=[s5c], writes=[Cw])
    P.dma("sp", dsk[:], s5d[:], reads=[s5d], writes=[dsk])
    names = ["lr", "dt", "a", "th", "mag", "s", "c", "cc", "ss", "Lr", "Li", "den", "L1", "nr", "ni", "c1r", "c1i", "t"]
    T_ = {k: P.sbuf("s5_" + k, [128, NP], F32) for k in names}
    PR = P.sbuf("s5_PR", [128, 13, NP], F32)
    PI = P.sbuf("s5_PI", [128, 13, NP], F32)
    PIs = P.sbuf("s5_PIs", [128, 13, NP], F32)

    def tt(o, a, b, op, eng="dve"):
        P.op(eng, lambda e: e.tensor_tensor(o[:], a[:], b[:], op), reads=[a, b], writes=[o])

    lre, lim, ldt = prm[:, :, 0], prm[:, :, 1], prm[:, :, 2]
    P.op("dve", lambda e: e.tensor_scalar(T_["lr"][:], lre, -1e-4, None, ALU.min), reads=[prm], writes=[T_["lr"]])
    P.op("act", lambda e: e.activation(T_["dt"][:], ldt, AF.Exp), reads=[prm], writes=[T_["dt"]])
    tt(T_["a"], T_["lr"], T_["dt"], ALU.mult)
    P.op("dve", lambda e: e.tensor_tensor(T_["th"][:], lim, T_["dt"][:], ALU.mult), reads=[prm, T_["dt"]], writes=[T_["th"]])
    P.op("act", lambda e: e.activation(T_["mag"][:], T_["a"][:], AF.Exp), reads=[T_["a"]], writes=[T_["mag"]])
    P.op("act", lambda e: e.activation(T_["s"][:], T_["th"][:], AF.Sin, scale=1.0 / 16), reads=[T_["th"]], writes=[T_["s"]])
    P.op("act", lambda e: e.activation(T_["c"][:], T_["th"][:], AF.Sin, bias=hpi, scale=1.0 / 16), reads=[T_["th"], c], writes=[T_["c"]])
    for _ in range(4):
        tt(T_["cc"], T_["c"], T_["c"], ALU.mult)
        tt(T_["ss"], T_["s"], T_["s"], ALU.mult)
        P.op("dve", lambda e: e.scalar_tensor_tensor(T_["s"][:], T_["c"][:], 2.0, T_["s"][:], ALU.mult, ALU.mult),
             reads=[T_["c"], T_["s"]], writes=[T_["s"]])
        tt(T_["c"], T_["cc"], T_["ss"], ALU.subtract)
    tt(T_["Lr"], T_["mag"], T_["c"], ALU.mult)
    tt(T_["Li"], T_["mag"], T_["s"], ALU.mult)
    tt(T_["den"], T_["lr"], T_["lr"], ALU.mult)
    P.op("dve", lambda e: e.tensor_tensor(T_["t"][:], lim, lim, ALU.mult), reads=[prm], writes=[T_["t"]])
    tt(T_["den"], T_["den"], T_["t"], ALU.add)
    P.op("dve", lambda e: e.reciprocal(T_["den"][:], T_["den"][:]), reads=[T_["den"]], writes=[T_["den"]])
    P.op("dve", lambda e: e.tensor_scalar(T_["L1"][:], T_["Lr"][:], -1.0, None, ALU.add), reads=[T_["Lr"]], writes=[T_["L1"]])
    tt(T_["nr"], T_["L1"], T_["lr"], ALU.mult)
    P.op("dve", lambda e: e.tensor_tensor(T_["t"][:], T_["Li"][:], lim, ALU.mult), reads=[prm, T_["Li"]], writes=[T_["t"]])
    tt(T_["nr"], T_["nr"], T_["t"], ALU.add)
    tt(T_["ni"], T_["Li"], T_["lr"], ALU.mult)
    P.op("dve", lambda e: e.tensor_tensor(T_["t"][:], T_["L1"][:], lim, ALU.mult), reads=[prm, T_["L1"]], writes=[T_["t"]])
    tt(T_["ni"], T_["ni"], T_["t"], ALU.subtract)
    tt(T_["c1r"], T_["nr"], T_["den"], ALU.mult)
    tt(T_["c1i"], T_["ni"], T_["den"], ALU.mult)
    P.op("dve", lambda e: e.tensor_copy(PR[:, 0, :], T_["Lr"][:]), reads=[T_["Lr"]], writes=[PR])
    P.op("dve", lambda e: e.tensor_copy(PI[:, 0, :], T_["Li"][:]), reads=[T_["Li"]], writes=[PI])
    for m in range(12):
        P.op("dve", lambda e, m=m: e.tensor_tensor(T_["cc"][:], PR[:, m, :], PR[:, m, :], ALU.mult), reads=[PR], writes=[T_["cc"]])
        P.op("dve", lambda e, m=m: e.tensor_tensor(T_["ss"][:], PI[:, m, :], PI[:, m, :], ALU.mult), reads=[PI], writes=[T_["ss"]])
        P.op("dve", lambda e, m=m: e.scalar_tensor_tensor(PI[:, m + 1, :], PR[:, m, :], 2.0, PI[:, m, :], ALU.mult, ALU.mult),
             reads=[PR, PI], writes=[PI])
        P.op("dve", lambda e, m=m: e.tensor_tensor(PR[:, m + 1, :], T_["cc"][:], T_["ss"][:], ALU.subtract),
             reads=[T_["cc"], T_["ss"], PR], writes=[PR])
    P.op("dve", lambda e: e.tensor_scalar(PIs[:], PI[:], sgn, None, ALU.mult), reads=[PI, c], writes=[PIs])
    P.op("dve", lambda e: e.tensor_scalar(Cw[:], Cw[:], sgn, None, ALU.mult), reads=[Cw, c], writes=[Cw])
    bT = P.sbuf("s5_bT", [16, NP, 128], F32)
    tb = [P.sbuf(f"s5_tb{i}", [128, 16], F32) for i in range(4)]
    for pi in range(NP):
        c1r, c1i = T_["c1r"][:, pi:pi + 1], T_["c1i"][:, pi:pi + 1]
        Bre, Bim = Bd[:, pi, 0, :], Bd[:, pi, 1, :]
        rd = [T_["c1r"], T_["c1i"], Bd]
        P.op("dve", lambda e, Bim=Bim, c1i=c1i: e.tensor_scalar(tb[0][:], Bim, c1i, None, ALU.mult), reads=rd, writes=[tb[0]])
        P.op("dve", lambda e, Bre=Bre, c1r=c1r: e.scalar_tensor_tensor(tb[1][:], Bre, c1r, tb[0][:], ALU.mult, ALU.subtract), reads=rd + [tb[0]], writes=[tb[1]])
        P.op("dve", lambda e, Bre=Bre, c1i=c1i: e.tensor_scalar(tb[2][:], Bre, c1i, None, ALU.mult), reads=rd, writes=[tb[2]])
        P.op("dve", lambda e, Bim=Bim, c1r=c1r: e.scalar_tensor_tensor(tb[3][:], Bim, c1r, tb[2][:], ALU.mult, ALU.add), reads=rd + [tb[2]], writes=[tb[3]])
        P.op("dve", lambda e: e.tensor_scalar(tb[3][:], tb[3][:], m1, None, ALU.mult), reads=[tb[3], c], writes=[tb[3]])
        P.op("dve", lambda e: e.scalar_tensor_tensor(tb[1][:], tb[1][:], m0, tb[3][:], ALU.mult, ALU.add), reads=[tb[1], tb[3], c], writes=[tb[1]])
        pb = P.bank(pi % 2)
        P.op("pe", lambda e, pb=pb: e.transpose(pb[0:16, 0:128], tb[1][:], ident), reads=[tb[1], c], writes=[pb])
        P.op("act", lambda e, pb=pb, pi=pi: e.activation(bT[:, pi, :], pb[0:16, 0:128], AF.Copy), reads=[pb], writes=[bT])
    uT = P.sbuf("s5_u", [16, tall], F32)
    zT = P.sbuf("s5_z", [16, tall], F32)
    X = [[P.sbuf(f"s5_X{d}{k}", [128, tall], F32) for k in range(2)] for d in range(2)]
    Lm = [P.sbuf(f"s5_Lm{d}", [128, 13, 128], BF16) for d in range(2)]
    Xh = [[P.sbuf(f"s5_Xh{d}{k}", [128, tall], BF16) for k in range(2)] for d in range(2)]
    stmp = [P.sbuf(f"s5_tq{i}", [128, 512], F32) for i in range(3)]
    sti = [0]
    ys = [P.sbuf(f"s5_y{i}", [16, 512], F32) for i in range(2)]
    gb = [P.sbuf(f"s5_g{i}", [16, 512], BF16) for i in range(2)]
    gsb = [P.sbuf(f"s5_gs{i}", [16, 512], BF16) for i in range(2)]
    nsteps = 0
    while (1 << nsteps) < tall:
        nsteps += 1
    bk = [0]

    def nb():
        bk[0] += 1
        return P.bank(2 + bk[0] % 6)

    blocks = [(0, nctx)]
    o = nctx
    while o < tall:
        blocks.append((o, min(512, tall - o)))
        o += 512

    def a1(o):
        return o - nctx if o >= nctx else nlat + o

    for gi in range(16):
        P.dma("sp", uT[:], fm.t[R_SU + gi * 16:R_SU + (gi + 1) * 16, :], reads=[fm], writes=[uT])
        P.dma("sp", zT[:], fm.t[R_SZ + gi * 16:R_SZ + (gi + 1) * 16, :], reads=[fm], writes=[zT])
        for d in range(2):
            pi = d * 16 + gi
            for m in range(nsteps):
                P.op("dve", lambda e, d=d, m=m, pi=pi: e.tensor_scalar(Lm[d][:, m, :], ident, PR[:, m, pi:pi + 1], None, ALU.mult),
                     reads=[PR, c], writes=[Lm[d]])
                P.op("dve", lambda e, d=d, m=m, pi=pi: e.scalar_tensor_tensor(Lm[d][:, m, :], swap, PIs[:, m, pi:pi + 1], Lm[d][:, m, :], ALU.mult, ALU.add),
                     reads=[PIs, c, Lm[d]], writes=[Lm[d]])
            for (o, n) in blocks:
                pb = nb()
                P.op("pe", lambda e, pb=pb, pi=pi, o=o, n=n: e.matmul(pb[:, 0:n], bT[:, pi, :], uT[:, o:o + n], start=True, stop=True),
                     reads=[bT, uT], writes=[pb])
                od = o if d == 0 else a1(o)
                P.op("act", lambda e, pb=pb, d=d, od=od, n=n: e.activation(X[d][0][:, od:od + n], pb[:, 0:n], AF.Copy), reads=[pb], writes=[X[d][0]])
            P.op("pool", lambda e, d=d: e.tensor_copy(Xh[d][0][:], X[d][0][:]), reads=[X[d][0]], writes=[Xh[d][0]])
        cur = 0
        for m in range(nsteps):
            s = 1 << m
            for d in range(2):
                Xa, Xb = X[d][cur], X[d][1 - cur]
                L = T_
                w = tall - s
                o = 0
                while o < w:
                    n = min(512, w - o)
                    pb = nb()
                    src = o if d == 0 else o + s
                    dst = o + s if d == 0 else o
                    Xha = Xh[d][cur]
                    P.op("pe", lambda e, pb=pb, d=d, m=m, Xha=Xha, src=src, n=n: e.matmul(pb[:, 0:n], Lm[d][:, m, :], Xha[:, src:src + n], start=True, stop=True),
                         reads=[Lm[d], Xha], writes=[pb])
                    if (o // 512) % 2 == 0:
                        P.op("dve", lambda e, pb=pb, Xa=Xa, Xb=Xb, dst=dst, n=n: e.tensor_tensor(Xb[:, dst:dst + n], Xa[:, dst:dst + n], pb[:, 0:n], ALU.add),
                             reads=[pb, Xa], writes=[Xb])
                    else:
                        tq = stmp[sti[0] % 3]
                        sti[0] += 1
                        P.op("act", lambda e, pb=pb, tq=tq, n=n: e.activation(tq[:, 0:n], pb[:, 0:n], AF.Copy), reads=[pb], writes=[tq])
                        P.op("pool", lambda e, tq=tq, Xa=Xa, Xb=Xb, dst=dst, n=n: e.tensor_tensor(Xb[:, dst:dst + n], Xa[:, dst:dst + n], tq[:, 0:n], ALU.add),
                             reads=[tq, Xa], writes=[Xb])
                    o += n
                c0 = 0 if d == 0 else w
                P.op("pool", lambda e, Xa=Xa, Xb=Xb, c0=c0, s=s: e.tensor_copy(Xb[:, c0:c0 + s], Xa[:, c0:c0 + s]), reads=[Xa], writes=[Xb])
                if m < nsteps - 1:
                    Xhb = Xh[d][1 - cur]
                    P.op("act", lambda e, Xb=Xb, Xhb=Xhb: e.activation(Xhb[:], Xb[:], AF.Copy), reads=[Xb], writes=[Xhb])
            cur = 1 - cur
        for bi, (o, n) in enumerate(blocks):
            pb = nb()
            P.op("pe", lambda e, pb=pb, gi=gi, o=o, n=n, cur=cur: e.matmul(pb[0:16, 0:n], Cw[:, gi, :], X[0][cur][:, o:o + n], start=True, stop=False),
                 reads=[Cw, X[0][cur]], writes=[pb], signal=False)
            oa = a1(o)
            P.op("pe", lambda e, pb=pb, gi=gi, oa=oa, n=n, cur=cur: e.matmul(pb[0:16, 0:n], Cw[:, 16 + gi, :], X[1][cur][:, oa:oa + n], start=False, stop=True),
                 reads=[Cw, X[1][cur]], writes=[pb])
            y, g, gs = ys[bi % 2], gb[bi % 2], gsb[bi % 2]
            P.op("dve", lambda e, pb=pb, y=y, gi=gi, o=o, n=n: e.scalar_tensor_tensor(y[:, 0:n], uT[:, o:o + n], dsk[:, gi:gi + 1], pb[0:16, 0:n], ALU.mult, ALU.add),
                 reads=[uT, dsk, pb], writes=[y])
            P.op("act", lambda e, y=y, g=g, n=n: e.activation(g[:, 0:n], y[:, 0:n], AF.Gelu_apprx_tanh), reads=[y], writes=[g])
            P.op("act", lambda e, o=o, n=n: e.activation(zT[:, o:o + n], zT[:, o:o + n], AF.Silu), reads=[zT], writes=[zT])
            P.op("dve", lambda e, g=g, gs=gs, o=o, n=n: e.tensor_tensor(gs[:, 0:n], g[:, 0:n], zT[:, o:o + n], ALU.mult), reads=[g, zT], writes=[gs])
            wr_rows(P, gT, gi * 16, 16, o, n, g, lambda so, pn, g=g: g[:, so:so + pn])
            wr_rows(P, gsT, gi * 16, 16, o, n, gs, lambda so, pn, gs=gs: gs[:, so:so + pn])


def s5_consts():
    c = np.zeros((128, 260), np.float32)
    c[:, 0:128] = np.eye(128)
    for p in range(128):
        c[p, 128 + (p + 64) % 128] = 1.0
    c[:64, 256] = 1.0
    c[64:, 256] = -1.0
    c[:64, 257] = 1.0
    c[64:, 258] = 1.0
    c[:, 259] = np.pi / 2
    return c


def s5_host_layout(lam_re, lam_im, log_dt, b_re, b_im, c_re, c_im, d_skip, j):
    gs = slice(j * 16, (j + 1) * 16)
    def dup(x):
        return np.concatenate([x, x], 0)
    lre = lam_re[:, gs].reshape(32, 64).T
    lim = lam_im[:, gs].reshape(32, 64).T
    ldt = np.broadcast_to(log_dt[:, gs].reshape(1, 32), (64, 32))
    prm = np.stack([lre, lim, ldt, np.zeros_like(lre)], -1)
    prm = np.ascontiguousarray(dup(prm)).astype(np.float32)
    bre = b_re[:, gs].reshape(32, 64, 16).transpose(1, 0, 2)
    bim = b_im[:, gs].reshape(32, 64, 16).transpose(1, 0, 2)
    bb = np.ascontiguousarray(dup(np.stack([bre, bim], 2))).astype(np.float32)
    cre = c_re[:, gs].reshape(32, 16, 64).transpose(2, 0, 1)
    cim = c_im[:, gs].reshape(32, 16, 64).transpose(2, 0, 1)
    cc = np.ascontiguousarray(np.concatenate([cre, cim], 0)).astype(np.float32)
    dd = np.ascontiguousarray(d_skip.reshape(64, 16)[gs].T).astype(np.float32)
    return prm, bb, cc, dd
    P.finish(outs)
    P.emit()
    return nc


def emit_gla(P, fm, tm, glw, glb, glg, glk, cmk, glT, tall=TALL, nctx=NCTX):
    nch = tall // 128
    ncc = nctx // 128
    kk = P.sbuf("gl_k", [128, 384], F32)
    P.dma("sp", kk[:], glk[:], reads=[glk], writes=[kk])
    cm = P.sbuf("gl_cm", [64, tall], F32)
    P.dma("sp", cm[:], cmk[:], reads=[cmk], writes=[cm])
    wa = P.sbuf("gl_wa", [16, 4, 64], F32)
    ba = P.sbuf("gl_ba", [64, 4], F32)
    go = P.sbuf("gl_go", [128, 1], F32)
    P.dma("sp", wa[:], glw[:], reads=[glw], writes=[wa])
    P.dma("sp", ba[:], glb[:], reads=[glb], writes=[ba])
    P.dma("sp", go[:], glg[:], reads=[glg], writes=[go])
    P.op("dve", lambda e: e.tensor_scalar(ba[:], ba[:], -1.0, None, ALU.mult), reads=[ba], writes=[ba])
    one1 = P.sbuf("gl_one", [128, 1], F32)
    epsb = P.sbuf("gl_eps", [128, 1], F32)
    ones_f = P.sbuf("gl_1f", [128, 128], F32)
    P.op("dve", lambda e: e.memset(one1[:], 1.0), writes=[one1])
    P.op("dve", lambda e: e.memset(epsb[:], EPS), writes=[epsb])
    P.op("dve", lambda e: e.memset(ones_f[:], 1.0 / 128), writes=[ones_f])
    qh = P.sbuf("gl_q", [64, tall], F32)
    kh = P.sbuf("gl_kh", [64, tall], F32)
    la = P.sbuf("gl_la", [64, tall], F32)
    cs = P.sbuf("gl_cs", [64, tall], F32)
    cB = P.sbuf("gl_cB", [64, tall], F32)
    ex = P.sbuf("gl_ex", [64, tall], F32)
    qe = P.sbuf("gl_qe", [64, tall], BF16)
    ke = P.sbuf("gl_ke", [64, tall], BF16)
    tot = P.sbuf("gl_tot", [64, nch], F32)
    Et = P.sbuf("gl_Et", [64, nch], F32)
    Vt = P.sbuf("gl_V", [128, nch, 128], BF16)
    oacc = P.sbuf("gl_o", [128, tall], F32)
    S = P.sbuf("gl_S", [64, 128], F32)
    Sb = P.sbuf("gl_Sb", [64, 128], BF16)
    scm = [P.sbuf(f"gl_sc{i}", [128, 128], BF16) for i in range(2)]
    kT = [P.sbuf(f"gl_kT{i}", [128, 64], BF16) for i in range(2)]
    sq = P.sbuf("gl_sq", [128, 512], F32)
    rs = P.sbuf("gl_rs", [128, 512], F32)
    gz = P.sbuf("gl_gz", [128, 512], F32)
    yo = P.sbuf("gl_yo", [128, 512], F32)
    ob = [P.sbuf(f"gl_ob{i}", [128, 512], BF16) for i in range(2)]
    bk = [0]

    def nb():
        bk[0] += 1
        return P.bank(bk[0] % 8)

    for hh in range(2):
        P.dma("sp", qh[:], fm.t[R_GQ + hh * 64:R_GQ + (hh + 1) * 64, :], reads=[fm], writes=[qh])
        P.dma("sp", kh[:], fm.t[R_GK + hh * 64:R_GK + (hh + 1) * 64, :], reads=[fm], writes=[kh])
        P.dma("pool", Vt[:], tm.t[:, C_GV + hh * 128:C_GV + (hh + 1) * 128].rearrange("(t p) c -> p t c", p=128), reads=[tm], writes=[Vt])
        for d in range(2):
            pr = d * 2 + hh
            P.dma("sp", ex[0:16, :], fm.t[R_LR + d * 16:R_LR + (d + 1) * 16, :], reads=[fm], writes=[ex])
            o = 0
            while o < tall:
                n = min(512, tall - o)
                pb = nb()
                P.op("pe", lambda e, pb=pb, pr=pr, o=o, n=n: e.matmul(pb[0:64, 0:n], wa[:, pr, :], ex[0:16, o:o + n], start=True, stop=True),
                     reads=[wa, ex], writes=[pb])
                P.op("act", lambda e, pb=pb, pr=pr, o=o, n=n: e.activation(la[:, o:o + n], pb[0:64, 0:n], AF.Exp, bias=ba[:, pr:pr + 1], scale=-1.0),
                     reads=[pb, ba], writes=[la])
                o += n
            P.op("act", lambda e: e.activation(la[:], la[:], AF.Ln, bias=one1[0:64, :], scale=1.0), reads=[la, one1], writes=[la])
            P.op("dve", lambda e: e.tensor_tensor_scan(cs[:], cm[:], la[:], 0.0, ALU.mult, ALU.add), reads=[cm, la], writes=[cs])
            P.op("dve", lambda e: e.tensor_copy(tot[:], cs[:].rearrange("p (c t) -> p c t", t=128)[:, :, 127]), reads=[cs], writes=[tot])
            if d == 0:
                P.op("dve", lambda e: e.tensor_copy(cB[:], cs[:]), reads=[cs], writes=[cB])
            else:
                for c in range(nch):
                    P.op("dve", lambda e, c=c: e.tensor_scalar(cB[:, c * 128:(c + 1) * 128], cs[:, c * 128:(c + 1) * 128], -1.0, tot[:, c:c + 1], ALU.mult, ALU.add),
                         reads=[cs, tot], writes=[cB])
                P.op("dve", lambda e: e.tensor_tensor(cB[:], cB[:], la[:], ALU.add), reads=[cB, la], writes=[cB])
            P.op("act", lambda e: e.activation(ex[:], cB[:], AF.Exp, scale=-1.0 / 16), reads=[cB], writes=[ex])
            P.op("dve", lambda e: e.scalar_tensor_tensor(qe[:], qh[:], 0.125, ex[:], ALU.mult, ALU.mult), reads=[qh, ex], writes=[qe])
            P.op("act", lambda e: e.activation(ex[:], cB[:], AF.Exp, scale=1.0 / 16), reads=[cB], writes=[ex])
            P.op("dve", lambda e: e.tensor_tensor(ke[:], kh[:], ex[:], ALU.mult), reads=[kh, ex], writes=[ke])
            for c in range(nch):
                P.op("dve", lambda e, c=c: e.tensor_scalar(cs[:, c * 128:(c + 1) * 128], cB[:, c * 128:(c + 1) * 128], -1.0, tot[:, c:c + 1], ALU.mult, ALU.add),
                     reads=[cB, tot], writes=[cs])
            P.op("act", lambda e: e.activation(ex[:], cs[:], AF.Exp, scale=-1.0 / 16), reads=[cs], writes=[ex])
            P.op("dve", lambda e: e.tensor_tensor(la[:], kh[:], ex[:], ALU.mult), reads=[kh, ex], writes=[la])
            P.op("act", lambda e: e.activation(Et[:], tot[:], AF.Exp, scale=-1.0 / 16), reads=[tot], writes=[Et])
            P.op("dve", lambda e: e.memset(S[:], 0.0), writes=[S])
            P.op("dve", lambda e: e.memset(Sb[:], 0.0), writes=[Sb])
            if d == 0:
                order = list(range(nch))
            else:
                order = list(range(ncc - 1, -1, -1)) + list(range(nch - 1, ncc - 1, -1))
            mk = kk[:, 0:128] if d == 0 else kk[:, 128:256]
            for i, c in enumerate(order):
                sl = slice(c * 128, (c + 1) * 128)
                p1, p2, p3, p4 = nb(), nb(), nb(), nb()
                sc = scm[i % 2]
                kt = kT[i % 2]
                P.op("pe", lambda e, p1=p1, sl=sl: e.matmul(p1[:, 0:128], ke[:, sl], qe[:, sl], start=True, stop=True), reads=[ke, qe], writes=[p1])
                P.op("dve", lambda e, p1=p1, sc=sc, mk=mk: e.tensor_tensor(sc[:], p1[:, 0:128], mk, ALU.mult), reads=[p1, kk], writes=[sc])
                P.op("pe", lambda e, p2=p2, sc=sc, c=c: e.matmul(p2[:, 0:128], Vt[:, c, :], sc[:], start=True, stop=False), reads=[Vt, sc], writes=[p2], signal=False)
                P.op("pe", lambda e, p2=p2, sl=sl: e.matmul(p2[:, 0:128], Sb[:], qe[:, sl], start=False, stop=True), reads=[Sb, qe], writes=[p2])
                if d == 0:
                    P.op("act", lambda e, p2=p2, sl=sl: e.activation(oacc[:, sl], p2[:, 0:128], AF.Copy), reads=[p2], writes=[oacc])
                else:
                    P.op("dve", lambda e, p2=p2, sl=sl: e.tensor_tensor(oacc[:, sl], oacc[:, sl], p2[:, 0:128], ALU.add), reads=[p2, oacc], writes=[oacc])
                P.op("pe", lambda e, p3=p3, sl=sl: e.transpose(p3[:, 0:64], la[:, sl], kk[0:64, 256:320]), reads=[la, kk], writes=[p3])
                P.op("act", lambda e, p3=p3, kt=kt: e.activation(kt[:], p3[:, 0:64], AF.Copy), reads=[p3], writes=[kt])
                P.op("pe", lambda e, p4=p4, kt=kt, c=c: e.matmul(p4[0:64, 0:128], kt[:], Vt[:, c, :], start=True, stop=True), reads=[kt, Vt], writes=[p4])
                P.op("dve", lambda e, p4=p4, c=c: e.scalar_tensor_tensor(S[:], S[:], Et[:, c:c + 1], p4[0:64, 0:128], ALU.mult, ALU.add), reads=[S, Et, p4], writes=[S])
                P.op("act", lambda e: e.activation(Sb[:], S[:], AF.Copy), reads=[S], writes=[Sb])
        o = 0
        bi = 0
        while o < tall:
            n = min(512, tall - o)
            pb = nb()
            P.op("act", lambda e, o=o, n=n: e.activation(sq[:, 0:n], oacc[:, o:o + n], AF.Square), reads=[oacc], writes=[sq])
            P.op("pe", lambda e, pb=pb, n=n: e.matmul(pb[:, 0:n], ones_f[:], sq[:, 0:n], start=True, stop=True), reads=[ones_f, sq], writes=[pb])
            P.op("act", lambda e, pb=pb, n=n: e.activation(rs[:, 0:n], pb[:, 0:n], AF.Sqrt, bias=epsb[:], scale=1.0), reads=[pb, epsb], writes=[rs])
            P.op("dve", lambda e, n=n: e.reciprocal(rs[:, 0:n], rs[:, 0:n]), reads=[rs], writes=[rs])
            P.op("dve", lambda e, o=o, n=n: e.scalar_tensor_tensor(yo[:, 0:n], oacc[:, o:o + n], go[:, 0:1], rs[:, 0:n], ALU.mult, ALU.mult), reads=[oacc, go, rs], writes=[yo])
            P.dma("sp", gz[:, 0:n], fm.t[R_GZ + hh * 128:R_GZ + (hh + 1) * 128, o:o + n], reads=[fm], writes=[gz])
            P.op("act", lambda e, n=n: e.activation(gz[:, 0:n], gz[:, 0:n], AF.Silu), reads=[gz], writes=[gz])
            ot = ob[bi % 2]
            bi += 1
            P.op("pool", lambda e, ot=ot, n=n: e.tensor_tensor(ot[:, 0:n], yo[:, 0:n], gz[:, 0:n], ALU.mult), reads=[yo, gz], writes=[ot])
            wr_rows(P, glT, hh * 128, 128, o, n, ot, lambda so, pn, ot=ot: ot[:, so:so + pn])
            o += n


def gla_consts(tall):
    k = np.zeros((128, 384), np.float32)
    s = np.arange(128)[:, None]
    t = np.arange(128)[None, :]
    k[:, 0:128] = (s <= t)
    k[:, 128:256] = (s >= t)
    k[:, 256:384] = np.eye(128)
    cm = np.ones((64, tall), np.float32)
    cm[:, ::128] = 0.0
    return k, cm


def build_C(nlat=1024, nctx=64):
    nc = _nc()
    P = Prog(nc)
    NT = nlat + nctx
    gT = P.dram("gT", [1024, NT], BF16, "ExternalInput")
    gsT = P.dram("gsT", [1024, NT], BF16, "ExternalInput")
    glT = P.dram("glT", [1024, NT], BF16, "ExternalInput")
    atT = P.dram("atT", [2048, NT], BF16, "ExternalInput")
    hT = P.dram("hT", [NDC, 128, NT], BF16, "ExternalInput")
    xT = P.dram("xT", [NDC, 128, NT], F32, "ExternalInput")
    wglu = P.dram("wglu", [1024, 1024], F32, "ExternalInput")
    wps = P.dram("wps", [1024, D], F32, "ExternalInput")
    wpg = P.dram("wpg", [1024, D], F32, "ExternalInput")
    wpa = P.dram("wpa", [2048, D], F32, "ExternalInput")
    wmg = P.dram("wmg", [D, 3 * D], F32, "ExternalInput")
    wo = P.dram("wo", [D, D], F32, "ExternalInput")
    gsel = P.dram("gsel", [128, NDC, 2], F32, "ExternalInput")
    xo = P.dram("xo", [NDC, 128, NT], F32, "ExternalOutput")
    gates = P.dram("gates", [96, 128, NT], BF16, "Internal")
    blocks = []
    o = 0
    while o < nlat:
        n = min(512, nlat - o)
        blocks.append((o, n, 0))
        o += n
    if nctx:
        blocks.append((nlat, nctx, 1))
    bk = [0]

    def nb():
        bk[0] += 1
        return P.bank(bk[0] % 8)

    mT = P.sbuf("c_m", [128, NDC, NT], BF16)
    P.open_scope()
    hs = P.sbuf("c_h", [128, NDC, NT], BF16)
    P.dma("sp", hs[:], hT.t.rearrange("c p n -> p c n"), reads=[hT], writes=[hs])
    Wg = [P.sbuf(f"c_wg{i}", [128, NDC, 128], BF16) for i in range(3)]
    gst = [P.sbuf(f"c_gst{i}", [128, NT], BF16) for i in range(2)]
    for mc in range(96):
        W = Wg[mc % 3]
        P.dma("pool", W[:], wmg.t[:, mc * 128:(mc + 1) * 128].rearrange("(k p) n -> p k n", p=128), reads=[wmg], writes=[W])
        st = gst[mc % 2]
        for (o, n, t) in blocks:
            pb = nb()
            for kc in range(NDC):
                P.op("pe", lambda e, pb=pb, W=W, kc=kc, o=o, n=n: e.matmul(pb[:, 0:n], W[:, kc, :], hs[:, kc, o:o + n], start=(kc == 0), stop=(kc == NDC - 1)),
                     reads=[W, hs], writes=[pb], signal=(kc == NDC - 1))
            P.op("act", lambda e, pb=pb, st=st, o=o, n=n: e.activation(st[:, o:o + n], pb[:, 0:n], AF.Sigmoid), reads=[pb], writes=[st])
        P.dma("sp", gates.t[mc], st[:], reads=[st], writes=[gates])
    P.close_scope()
    P.open_scope()
    gs_ = P.sbuf("c_g", [128, 8, NT], BF16)
    ss_ = P.sbuf("c_s", [128, 8, NT], BF16)
    gl_ = P.sbuf("c_gl", [128, 8, NT], BF16)
    at_ = P.sbuf("c_at", [128, 16, NT], BF16)
    P.dma("sp", gs_[:], gT.t.rearrange("(c p) n -> p c n", p=128), reads=[gT], writes=[gs_])
    P.dma("sp", ss_[:], gsT.t.rearrange("(c p) n -> p c n", p=128), reads=[gsT], writes=[ss_])
    P.dma("sp", gl_[:], glT.t.rearrange("(c p) n -> p c n", p=128), reads=[glT], writes=[gl_])
    P.dma("sp", at_[:], atT.t.rearrange("(c p) n -> p c n", p=128), reads=[atT], writes=[at_])
    wgl = P.sbuf("c_wglu", [128, 8, 1024], BF16)
    P.dma("pool", wgl[:], wglu.t.rearrange("(k p) n -> p k n", p=128), reads=[wglu], writes=[wgl])
    sg = [P.sbuf(f"c_sg{i}", [128, 512], BF16) for i in range(2)]
    i = 0
    for oc in range(8):
        for (o, n, t) in blocks:
            pb = nb()
            for kc in range(8):
                P.op("pe", lambda e, pb=pb, kc=kc, oc=oc, o=o, n=n: e.matmul(pb[:, 0:n], wgl[:, kc, oc * 128:(oc + 1) * 128], gs_[:, kc, o:o + n], start=(kc == 0), stop=(kc == 7)),
                     reads=[wgl, gs_], writes=[pb], signal=(kc == 7))
            s_ = sg[i % 2]
            i += 1
            P.op("act", lambda e, pb=pb, s_=s_, n=n: e.activation(s_[:, 0:n], pb[:, 0:n], AF.Sigmoid), reads=[pb], writes=[s_])
            P.op("dve", lambda e, s_=s_, oc=oc, o=o, n=n: e.tensor_tensor(ss_[:, oc, o:o + n], ss_[:, oc, o:o + n], s_[:, 0:n], ALU.mult), reads=[ss_, s_], writes=[ss_])
    w1 = [P.sbuf(f"c_w1{i}", [128, 8, 128], BF16) for i in range(2)]
    w2 = [P.sbuf(f"c_w2{i}", [128, 8, 128], BF16) for i in range(2)]
    w3 = [P.sbuf(f"c_w3{i}", [128, 16, 128], BF16) for i in range(2)]
    g3 = [P.sbuf(f"c_g3{i}", [128, 3, NT], BF16) for i in range(2)]
    m1 = P.sbuf("c_m1", [128, 512], F32)
    m2 = P.sbuf("c_m2", [128, 512], F32)
    m3 = P.sbuf("c_m3", [128, 512], F32)
    for dc in range(NDC):
        a, b_, c_, g_ = w1[dc % 2], w2[dc % 2], w3[dc % 2], g3[dc % 2]
        cs_ = slice(dc * 128, (dc + 1) * 128)
        P.dma("pool", a[:], wps.t[:, cs_].rearrange("(k p) n -> p k n", p=128), reads=[wps], writes=[a])
        P.dma("pool", b_[:], wpg.t[:, cs_].rearrange("(k p) n -> p k n", p=128), reads=[wpg], writes=[b_])
        P.dma("pool", c_[:], wpa.t[:, cs_].rearrange("(k p) n -> p k n", p=128), reads=[wpa], writes=[c_])
        for br in range(3):
            P.dma("sp", g_[:, br, :], gates.t[br * 32 + dc], reads=[gates], writes=[g_])
        for (o, n, t) in blocks:
            p1, p2, p3 = nb(), nb(), nb()
            for kc in range(8):
                P.op("pe", lambda e, p1=p1, a=a, kc=kc, o=o, n=n: e.matmul(p1[:, 0:n], a[:, kc, :], ss_[:, kc, o:o + n], start=(kc == 0), stop=(kc == 7)),
                     reads=[a, ss_], writes=[p1], signal=(kc == 7))
            for kc in range(8):
                P.op("pe", lambda e, p2=p2, b_=b_, kc=kc, o=o, n=n: e.matmul(p2[:, 0:n], b_[:, kc, :], gl_[:, kc, o:o + n], start=(kc == 0), stop=(kc == 7)),
                     reads=[b_, gl_], writes=[p2], signal=(kc == 7))
            for kc in range(16):
                P.op("pe", lambda e, p3=p3, c_=c_, kc=kc, o=o, n=n: e.matmul(p3[:, 0:n], c_[:, kc, :], at_[:, kc, o:o + n], start=(kc == 0), stop=(kc == 15)),
                     reads=[c_, at_], writes=[p3], signal=(kc == 15))
            P.op("dve", lambda e, p1=p1, g_=g_, o=o, n=n: e.tensor_tensor(m1[:, 0:n], p1[:, 0:n], g_[:, 0, o:o + n], ALU.mult), reads=[p1, g_], writes=[m1])
            P.op("dve", lambda e, p2=p2, g_=g_, o=o, n=n: e.tensor_tensor(m2[:, 0:n], p2[:, 0:n], g_[:, 1, o:o + n], ALU.mult), reads=[p2, g_], writes=[m2])
            P.op("dve", lambda e, p3=p3, g_=g_, o=o, n=n: e.tensor_tensor(m3[:, 0:n], p3[:, 0:n], g_[:, 2, o:o + n], ALU.mult), reads=[p3, g_], writes=[m3])
            P.op("pool", lambda e, n=n: e.tensor_tensor(m1[:, 0:n], m1[:, 0:n], m2[:, 0:n], ALU.add), reads=[m1, m2], writes=[m1])
            P.op("pool", lambda e, dc=dc, o=o, n=n: e.tensor_tensor(mT[:, dc, o:o + n], m1[:, 0:n], m3[:, 0:n], ALU.add), reads=[m1, m3], writes=[mT])
    P.close_scope()
    P.open_scope()
    gv = P.sbuf("c_gv", [128, NDC, 2], F32)
    P.dma("sp", gv[:], gsel[:], reads=[gsel], writes=[gv])
    wo_ = [P.sbuf(f"c_wo{i}", [128, NDC, 128], BF16) for i in range(2)]
    xin = [P.sbuf(f"c_xi{i}", [128, NT], F32) for i in range(2)]
    xot = [P.sbuf(f"c_xo{i}", [128, NT], F32) for i in range(2)]
    for dc in range(NDC):
        W = wo_[dc % 2]
        xi, xn = xin[dc % 2], xot[dc % 2]
        P.dma("pool", W[:], wo.t[:, dc * 128:(dc + 1) * 128].rearrange("(k p) n -> p k n", p=128), reads=[wo], writes=[W])
        P.dma("sp", xi[:], xT.t[dc], reads=[xT], writes=[xi])
        for (o, n, t) in blocks:
            pb = nb()
            for kc in range(NDC):
                P.op("pe", lambda e, pb=pb, W=W, kc=kc, o=o, n=n: e.matmul(pb[:, 0:n], W[:, kc, :], mT[:, kc, o:o + n], start=(kc == 0), stop=(kc == NDC - 1)),
                     reads=[W, mT], writes=[pb], signal=(kc == NDC - 1))
            P.op("dve", lambda e, pb=pb, xi=xi, xn=xn, dc=dc, o=o, n=n, t=t: e.scalar_tensor_tensor(
                xn[:, o:o + n], pb[:, 0:n], gv[:, dc, t:t + 1], xi[:, o:o + n], ALU.mult, ALU.add), reads=[pb, gv, xi], writes=[xn])
        P.dma("sp", xo.t[dc], xn[:], reads=[xn], writes=[xo])
    P.close_scope()
    P.finish([xo])
    P.emit()
    return nc


_OFF = dict(su=0, sz=1024, gq=2048, gk=2560, gv=3072, gz=4096, glr=5120, aq=5152, ak=7200, av=7712, az=8224, mg=10272)


def _cols(j):
    r = lambda k, w: list(range(_OFF[k] + j * w, _OFF[k] + (j + 1) * w))
    return (r("gv", 256) + r("av", 128) + r("gq", 128) + r("gk", 128) + r("gz", 256) + r("aq", 512) + r("ak", 128)
            + r("az", 512) + r("su", 256) + r("sz", 256) + list(range(_OFF["glr"], _OFF["glr"] + 32)))


def _fm(a):
    return np.ascontiguousarray(a.T.reshape(NDC, 128, a.shape[0]))


def _run(nc, ins):
    return run_bass_kernel_spmd(nc, ins, core_ids=list(range(NCORES))).results


def kernel(x, c, ctx, c_ctx, norm_g, w_mod, b_mod, w_in, ssm_lam_re, ssm_lam_im, ssm_log_dt, ssm_b_re, ssm_b_im,
           ssm_c_re, ssm_c_im, ssm_d, ssm_w_glu, gla_w_a, gla_b_a, gla_norm_g, attn_q_g, attn_k_g,
           w_proj_ssm, w_proj_gla, w_proj_attn, w_out, final_g):
    f = lambda a: np.asarray(a, dtype=np.float32)
    x, c, ctx, c_ctx = f(x), f(c), f(ctx), f(c_ctx)
    cs3 = np.stack([c[0], c[1], c_ctx], 0)
    cT = np.ascontiguousarray(cs3.reshape(3, 32, 128).transpose(2, 1, 0))
    w_mod, b_mod = f(w_mod), f(b_mod)
    ins = []
    for i in range(8):
        sl = slice(i * 1536, (i + 1) * 1536)
        ins.append({"cT": cT, "wm": np.ascontiguousarray(w_mod[:, :, sl]),
                    "bm": np.ascontiguousarray(b_mod[:, sl].reshape(2, 12, 128).transpose(2, 0, 1))})
    res = _run(build_M(), ins)
    modT = np.concatenate([r["modT"] for r in res], axis=2)
    cores = [(i // 4, i % 4) for i in range(8)]
    xT = []
    for (b, j) in cores:
        loc = np.concatenate([x[b, j * 1024:(j + 1) * 1024], ctx[b, j * 64:(j + 1) * 64]], 0)
        xT.append(_fm(loc))
    cosT, sinT, rm = rope_tables(NLAT)
    glk, cmk = gla_consts(TALL)
    s5k = s5_consts()
    ncA = build_A(1024, 64, True)
    ncB = build_B(TALL, NCTX, True, ("b1", "attn", "gla", "s5"))
    ncC = build_C(1024, 64)
    w_in = f(w_in)
    for l in range(2):
        ngl = np.ascontiguousarray(f(norm_g)[l].reshape(32, 128).T)
        ins = [{"xT": xT[i], "ng": ngl, "ms": np.ascontiguousarray(modT[:, l][:, :, [b, 2]])} for i, (b, j) in enumerate(cores)]
        resA = _run(ncA, ins)
        hT = [r["hT"] for r in resA]
        hTb = []
        for b in range(2):
            hTb.append(np.ascontiguousarray(np.concatenate(
                [hT[b * 4 + j][:, :, 1024:1088] for j in range(4)] + [hT[b * 4 + j][:, :, 0:1024] for j in range(4)], axis=2)))
        ins = []
        for i, (b, j) in enumerate(cores):
            prm, bb, cc, dd = s5_host_layout(f(ssm_lam_re)[l], f(ssm_lam_im)[l], f(ssm_log_dt)[l], f(ssm_b_re)[l], f(ssm_b_im)[l],
                                             f(ssm_c_re)[l], f(ssm_c_im)[l], f(ssm_d)[l], j)
            WA = f(gla_w_a)[l]
            BA = f(gla_b_a)[l]
            glw = np.ascontiguousarray(WA[:, :, 128 * j:128 * (j + 1)].reshape(2, 16, 2, 64).transpose(1, 0, 2, 3).reshape(16, 4, 64))
            glb = np.ascontiguousarray(BA[:, 128 * j:128 * (j + 1)].reshape(4, 64).T)
            ins.append({"hTb": hTb[b], "wq": np.ascontiguousarray(w_in[l][:, _cols(j)]), "cosT": cosT, "sinT": sinT, "rmT": rm,
                        "gqk": np.ascontiguousarray(np.stack([f(attn_q_g)[l], f(attn_k_g)[l]], 1)),
                        "s5p": prm, "s5b": bb, "s5c": cc, "s5d": dd, "s5k": s5k,
                        "glw": glw, "glb": glb, "glg": np.ascontiguousarray(f(gla_norm_g)[l].reshape(128, 1)), "glk": glk, "cmk": cmk})
        resB = _run(ncB, ins)
        wmg = np.ascontiguousarray(w_in[l][:, _OFF["mg"]:])
        ins = []
        for i, (b, j) in enumerate(cores):
            tok = list(range(256 + j * 1024, 256 + (j + 1) * 1024)) + list(range(j * 64, (j + 1) * 64))
            cat = lambda k: np.ascontiguousarray(np.concatenate([resB[b * 4 + jj][k] for jj in range(4)], 0)[:, tok])
            ins.append({"gT": cat("gT"), "gsT": cat("gsT"), "glT": cat("glT"), "atT": cat("atT"), "hT": hT[i], "xT": xT[i],
                        "wglu": f(ssm_w_glu)[l], "wps": f(w_proj_ssm)[l], "wpg": f(w_proj_gla)[l], "wpa": f(w_proj_attn)[l],
                        "wmg": wmg, "wo": f(w_out)[l],
                        "gsel": np.ascontiguousarray(modT[:, l, 64:96][:, :, [b, 2]])})
        resC = _run(ncC, ins)
        xT = [r["xo"] for r in resC]
    ncF = build_A(1024, 64, False)
    fgl = np.ascontiguousarray(f(final_g).reshape(32, 128).T)
    zero = np.zeros((128, 96, 2), np.float32)
    resF = _run(ncF, [{"xT": xT[i], "ng": fgl, "ms": zero} for i in range(8)])
    out = np.zeros((2, 4096, 4096), np.float32)
    for i, (b, j) in enumerate(cores):
        o = resF[i]["hT"].reshape(4096, 1088)[:, 0:1024]
        out[b, j * 1024:(j + 1) * 1024] = o.T
    return out


GROUPS = [[0, 1, 2, 3], [4, 5, 6, 7]]
NK = 8


def _blocks(nctx, tall, step=512):
    bl = [(0, nctx, 1)]
    o = nctx
    while o < tall:
        n = min(step, tall - o)
        bl.append((o, n, 0))
        o += n
    return bl


CW = 256


class TT:
    def __init__(self, P, name, rows, tall, dtype, r0=0, bufs=None):
        self.rows, self.tall, self.r0 = rows, tall, r0
        self.bufs = bufs if bufs is not None else [P.dram(f"{name}_{i}", [rows, CW], dtype) for i in range(tall // CW)]

    def sub(self, r0):
        return TT(None, None, self.rows, self.tall, None, self.r0 + r0, self.bufs)

    def pieces(self, o, n):
        out = []
        so = 0
        while n > 0:
            ci, lo = o // CW, o % CW
            pn = min(n, CW - lo)
            out.append((self.bufs[ci], lo, pn, so))
            o += pn
            n -= pn
            so += pn
        return out


def wr_rows(P, dst, r0, nr, o, n, srcbuf, src_fn, eng="sp"):
    if isinstance(dst, TT):
        for (b, lo, pn, so) in dst.pieces(o, n):
            P.dma(eng, b.t[dst.r0 + r0:dst.r0 + r0 + nr, lo:lo + pn], src_fn(so, pn), reads=[srcbuf], writes=[b])
    else:
        P.dma(eng, dst.t[r0:r0 + nr, o:o + n], src_fn(0, n), reads=[srcbuf], writes=[dst])


def wr_cpn(P, dst, c0, ncn, o, n, srcbuf, src_fn, eng="sp"):
    if isinstance(dst, TT):
        for (b, lo, pn, so) in dst.pieces(o, n):
            P.dma(eng, b.t[dst.r0 + c0 * 128:dst.r0 + (c0 + ncn) * 128, lo:lo + pn].rearrange("(c p) n -> p c n", p=128), src_fn(so, pn),
                  reads=[srcbuf], writes=[b])
    else:
        P.dma(eng, dst.t[c0 * 128:(c0 + ncn) * 128, o:o + n].rearrange("(c p) n -> p c n", p=128), src_fn(0, n), reads=[srcbuf], writes=[dst])


def rd_cpn(P, src, o, n, dstbuf, dst_fn, eng="sp", c0=0, ncn=None):
    if isinstance(src, TT):
        ncn_ = ncn if ncn is not None else src.rows // 128
        for (b, lo, pn, so) in src.pieces(o, n):
            P.dma(eng, dst_fn(so, pn), b.t[src.r0 + c0 * 128:src.r0 + (c0 + ncn_) * 128, lo:lo + pn].rearrange("(c p) n -> p c n", p=128),
                  reads=[b], writes=[dstbuf])
    else:
        P.dma(eng, dst_fn(0, n), src.t[:, :, o:o + n].rearrange("c p n -> p c n"), reads=[src], writes=[dstbuf])


def emit_M2(P, cT, wm, bm, modS):
    sc = P.sbuf("m_sc", [128, 32, 2], F32)
    bs = P.sbuf("m_bs", [128, 2, 24], F32)
    wt = [P.sbuf(f"m_wt{i}", [128, 4, 3072], F32) for i in range(2)]
    P.dma("sp", sc[:], cT[:], reads=[cT], writes=[sc])
    P.dma("sp", bs[:], bm[:], reads=[bm], writes=[bs])
    P.op("act", lambda e: e.activation(sc[:], sc[:], AF.Silu), reads=[sc], writes=[sc])
    it = 0
    for l in range(2):
        for g in range(8):
            w = wt[it % 2]
            pp = P.bank(it % 2)
            it += 1
            P.dma("sp", w[:], wm.t[l, g * 512:(g + 1) * 512, :].rearrange("(k p) n -> p k n", p=128), reads=[wm], writes=[w])
            for j in range(24):
                for k in range(4):
                    kc = g * 4 + k
                    P.op("pe", lambda e, pp=pp, w=w, j=j, k=k, kc=kc: e.matmul(
                        pp[:, j * 2:j * 2 + 2], w[:, k, j * 128:(j + 1) * 128], sc[:, kc, :], start=(k == 0), stop=(k == 3)),
                        reads=[w, sc], writes=[pp], signal=(k == 3 and j == 23))
            dst = modS[:, l].rearrange("p a b -> p (a b)")
            if g == 0:
                P.op("dve", lambda e, pp=pp, dst=dst: e.tensor_copy(dst, pp[:, 0:48]), reads=[pp], writes=[modS])
            else:
                P.op("dve", lambda e, pp=pp, dst=dst: e.tensor_tensor(dst, dst, pp[:, 0:48], ALU.add), reads=[pp, modS], writes=[modS])
    for l in range(2):
        for r in range(2):
            P.op("dve", lambda e, l=l, r=r: e.tensor_tensor(modS[:, l, :, r], modS[:, l, :, r], bs[:, l, :], ALU.add),
                 reads=[modS, bs], writes=[modS])


def emit_A2(P, xs, Av, Bv, arin, arout, dst, dst_dt, tall, nctx, lat_only=False, ag_out=None):
    onesm = P.sbuf("a_ones", [128, 128], F32)
    epsb = P.sbuf("a_eps", [128, 1], F32)
    ss = P.sbuf("a_ss", [128, tall], F32)
    xb = [P.sbuf(f"a_xb{i}", [128, NK, 512], F32) for i in range(2)]
    sq = [P.sbuf(f"a_sq{i}", [128, 512], F32) for i in range(2)]
    tmp = [P.sbuf(f"a_tmp{i}", [128, 512], F32) for i in range(2)]
    ho = [P.sbuf(f"a_ho{i}", [128, NK, 512], dst_dt) for i in range(2)]
    P.op("dve", lambda e: e.memset(onesm[:], 1.0 / D), writes=[onesm])
    P.op("dve", lambda e: e.memset(epsb[:], EPS), writes=[epsb])
    bl = _blocks(nctx, tall)
    for bi, (o, n, t) in enumerate(bl):
        x = xb[bi % 2]
        pb = P.bank(bi % 2)
        P.dma("sp", x[:, :, 0:n], xs.t[:, :, o:o + n].rearrange("c p n -> p c n"), reads=[xs], writes=[x])
        for k in range(NK):
            s = sq[k % 2]
            P.op("act", lambda e, s=s, x=x, k=k, n=n: e.activation(s[:, 0:n], x[:, k, 0:n], AF.Square), reads=[x], writes=[s])
            P.op("pe", lambda e, s=s, pb=pb, k=k, n=n: e.matmul(pb[:, 0:n], onesm[:], s[:, 0:n], start=(k == 0), stop=(k == NK - 1)),
                 reads=[onesm, s], writes=[pb])
        P.op("dve", lambda e, pb=pb, o=o, n=n: e.tensor_copy(ss[:, o:o + n], pb[:, 0:n]), reads=[pb], writes=[ss])
    qw = tall // 4
    for q in range(4):
        P.dma("sp", arin[q][:], ss[:, q * qw:(q + 1) * qw], reads=[ss], writes=[arin[q]])
        P.collective("AllReduce", ALU.add, GROUPS, arin[q][:], arout[q][:], reads=[arin[q]], writes=[arout[q]])
    for q in range(4):
        P.dma("sp", ss[:, q * qw:(q + 1) * qw], arout[q][:], reads=[arout[q]], writes=[ss])
    P.op("act", lambda e: e.activation(ss[:], ss[:], AF.Sqrt, bias=epsb[:], scale=1.0), reads=[ss, epsb], writes=[ss])
    P.op("dve", lambda e: e.reciprocal(ss[:], ss[:]), reads=[ss], writes=[ss])
    for bi, (o, n, t) in enumerate(bl):
        if lat_only and t == 1:
            continue
        x = xb[bi % 2]
        h = ho[bi % 2]
        P.dma("sp", x[:, :, 0:n], xs.t[:, :, o:o + n].rearrange("c p n -> p c n"), reads=[xs], writes=[x])
        for k in range(NK):
            tm_ = tmp[k % 2]
            P.op("dve", lambda e, tm_=tm_, x=x, k=k, n=n, t=t, o=o: e.scalar_tensor_tensor(
                tm_[:, 0:n], x[:, k, 0:n], Av[:, t, k:k + 1], ss[:, o:o + n], ALU.mult, ALU.mult), reads=[x, Av, ss], writes=[tm_])
            P.op("act", lambda e, tm_=tm_, h=h, k=k, n=n, t=t: e.activation(
                h[:, k, 0:n], tm_[:, 0:n], AF.Identity, bias=Bv[:, t, k:k + 1], scale=1.0), reads=[tm_, Bv], writes=[h])
        oo = o - nctx if lat_only else o
        wr_cpn(P, dst, 0, NK, oo, n, h, lambda so, pn, h=h: h[:, :, so:so + pn])
        if ag_out is not None:
            for ci in range(oo // CW, (oo + n) // CW):
                P.collective("AllGather", ALU.bypass, GROUPS, dst.bufs[ci][:], ag_out.bufs[ci][:], reads=[dst.bufs[ci]], writes=[ag_out.bufs[ci]])


def emit_C2(P, hTb, agbout, wglu, wmg, wps, wpg, wpa, wo, gate, xs_in, xs_out, agmin, agmout, tall, nctx):
    bk = [0]

    def nb():
        bk[0] += 1
        return P.bank(bk[0] % 8)

    ags_o, agl_o, aga_o = agbout

    def agv_rd(src, qn, r, o, n, dstbuf, dst_fn):
        for (b, lo, pn, so) in src.pieces(o, n):
            P.dma("sp", dst_fn(so, pn), b.t.rearrange("(r q p) n -> p r q n", r=4, q=qn, p=128)[:, r, :, lo:lo + pn], reads=[b], writes=[dstbuf])

    sTd = P.dram(f"sTd{P.n_ins}", [1024, tall], BF16)
    P.open_scope()
    wgl = P.sbuf("c_wglu", [128, 8, 1024], BF16)
    P.dma("pool", wgl[:], wglu.t.rearrange("(k p) n -> p k n", p=128), reads=[wglu], writes=[wgl])
    gb_ = [P.sbuf(f"c_gb{i}", [128, 4, 4, 512], BF16) for i in range(2)]
    so_ = [P.sbuf(f"c_so{i}", [128, 8, 512], BF16) for i in range(2)]
    sg = [P.sbuf(f"c_sg{i}", [128, 512], BF16) for i in range(2)]
    o = 0
    it = 0
    while o < tall:
        n = min(512, tall - o)
        a = gb_[it % 2]
        so = so_[it % 2]
        it += 1
        for r in range(4):
            agv_rd(ags_o, 4, r, o, n, a, lambda so, pn, a=a, r=r: a[:, r, :, so:so + pn])
        for oc in range(8):
            pb = nb()
            for kc in range(8):
                P.op("pe", lambda e, pb=pb, a=a, kc=kc, oc=oc, n=n: e.matmul(pb[:, 0:n], wgl[:, kc, oc * 128:(oc + 1) * 128], a[:, kc // 2, kc % 2, 0:n],
                                                                           start=(kc == 0), stop=(kc == 7)),
                     reads=[wgl, a], writes=[pb], signal=(kc == 7))
            s_ = sg[oc % 2]
            P.op("act", lambda e, pb=pb, s_=s_, n=n: e.activation(s_[:, 0:n], pb[:, 0:n], AF.Sigmoid), reads=[pb], writes=[s_])
            P.op("dve", lambda e, s_=s_, a=a, so=so, oc=oc, n=n: e.tensor_tensor(so[:, oc, 0:n], a[:, oc // 2, 2 + oc % 2, 0:n], s_[:, 0:n], ALU.mult),
                 reads=[a, s_], writes=[so])
        P.dma("sp", sTd.t[:, o:o + n].rearrange("(c p) n -> p c n", p=128), so[:, :, 0:n], reads=[so], writes=[sTd])
        o += n
    P.close_scope()
    P.open_scope()
    wmgS = P.sbuf("c_wmg", [128, NDC, 3, 384], BF16)
    wprS = P.sbuf("c_wpr", [128, NDC, 384], BF16)
    hb = [P.sbuf(f"c_hb{i}", [128, NDC, 256], BF16) for i in range(2)]
    sbk = [P.sbuf(f"c_sb{i}", [128, 8, 256], BF16) for i in range(2)]
    ab = [P.sbuf(f"c_ab{i}", [128, 4, 6, 256], BF16) for i in range(2)]
    gs3 = [P.sbuf(f"c_g3{i}", [128, 3, 256], F32) for i in range(2)]
    m1 = P.sbuf("c_m1", [128, 256], F32)
    m2 = P.sbuf("c_m2", [128, 256], F32)
    m3 = P.sbuf("c_m3", [128, 256], F32)
    mo = [P.sbuf(f"c_mo{i}", [128, 3, 256], BF16) for i in range(2)]
    it = 0
    for (k0, nk) in ((0, 3), (3, 3), (6, 2)):
        c0, c1 = k0 * 128, (k0 + nk) * 128
        for br in range(3):
            P.dma("pool", wmgS[:, :, br, 0:nk * 128], wmg.t[:, br * 1024 + c0:br * 1024 + c1].rearrange("(k p) n -> p k n", p=128),
                  reads=[wmg], writes=[wmgS])
        P.dma("pool", wprS[:, 0:8, 0:nk * 128], wps.t[:, c0:c1].rearrange("(k p) n -> p k n", p=128), reads=[wps], writes=[wprS])
        P.dma("pool", wprS[:, 8:16, 0:nk * 128], wpg.t[:, c0:c1].rearrange("(k p) n -> p k n", p=128), reads=[wpg], writes=[wprS])
        P.dma("pool", wprS[:, 16:32, 0:nk * 128], wpa.t[:, c0:c1].rearrange("(k p) n -> p k n", p=128), reads=[wpa], writes=[wprS])
        o = 0
        while o < tall:
            n = min(256, tall - o)
            h = hb[it % 2]
            a = ab[it % 2]
            sb = sbk[it % 2]
            mout = mo[it % 2]
            it += 1
            rd_cpn(P, hTb, o, n, h, lambda so, pn, h=h: h[:, :, so:so + pn])
            P.dma("sp", sb[:, :, 0:n], sTd.t[:, o:o + n].rearrange("(c p) n -> p c n", p=128), reads=[sTd], writes=[sb])
            for r in range(4):
                agv_rd(agl_o, 2, r, o, n, a, lambda so, pn, a=a, r=r: a[:, r, 0:2, so:so + pn])
                agv_rd(aga_o, 4, r, o, n, a, lambda so, pn, a=a, r=r: a[:, r, 2:6, so:so + pn])
            for kk in range(nk):
                g3 = gs3[kk % 2]
                cw = slice(kk * 128, (kk + 1) * 128)
                for br in range(3):
                    pg = nb()
                    for kc in range(NDC):
                        P.op("pe", lambda e, pg=pg, h=h, kc=kc, br=br, cw=cw, n=n: e.matmul(pg[:, 0:n], wmgS[:, kc, br, cw], h[:, kc, 0:n],
                                                                                          start=(kc == 0), stop=(kc == NDC - 1)),
                             reads=[wmgS, h], writes=[pg], signal=(kc == NDC - 1))
                    P.op("act", lambda e, pg=pg, g3=g3, br=br, n=n: e.activation(g3[:, br, 0:n], pg[:, 0:n], AF.Sigmoid), reads=[pg], writes=[g3])
                p1, p2, p3 = nb(), nb(), nb()
                for kc in range(8):
                    P.op("pe", lambda e, p1=p1, sb=sb, kc=kc, cw=cw, n=n: e.matmul(p1[:, 0:n], wprS[:, kc, cw], sb[:, kc, 0:n], start=(kc == 0), stop=(kc == 7)),
                         reads=[wprS, sb], writes=[p1], signal=(kc == 7))
                for kc in range(8):
                    P.op("pe", lambda e, p2=p2, a=a, kc=kc, cw=cw, n=n: e.matmul(p2[:, 0:n], wprS[:, 8 + kc, cw], a[:, kc // 2, kc % 2, 0:n], start=(kc == 0), stop=(kc == 7)),
                         reads=[wprS, a], writes=[p2], signal=(kc == 7))
                for kc in range(16):
                    P.op("pe", lambda e, p3=p3, a=a, kc=kc, cw=cw, n=n: e.matmul(p3[:, 0:n], wprS[:, 16 + kc, cw], a[:, kc // 4, 2 + kc % 4, 0:n], start=(kc == 0), stop=(kc == 15)),
                         reads=[wprS, a], writes=[p3], signal=(kc == 15))
                P.op("dve", lambda e, p1=p1, g3=g3, n=n: e.tensor_tensor(m1[:, 0:n], p1[:, 0:n], g3[:, 0, 0:n], ALU.mult), reads=[p1, g3], writes=[m1])
                P.op("dve", lambda e, p2=p2, g3=g3, n=n: e.tensor_tensor(m2[:, 0:n], p2[:, 0:n], g3[:, 1, 0:n], ALU.mult), reads=[p2, g3], writes=[m2])
                P.op("dve", lambda e, p3=p3, g3=g3, n=n: e.tensor_tensor(m3[:, 0:n], p3[:, 0:n], g3[:, 2, 0:n], ALU.mult), reads=[p3, g3], writes=[m3])
                P.op("pool", lambda e, n=n: e.tensor_tensor(m1[:, 0:n], m1[:, 0:n], m2[:, 0:n], ALU.add), reads=[m1, m2], writes=[m1])
                P.op("pool", lambda e, mout=mout, kk=kk, n=n: e.tensor_tensor(mout[:, kk, 0:n], m1[:, 0:n], m3[:, 0:n], ALU.add), reads=[m1, m3], writes=[mout])
            wr_cpn(P, agmin, k0, nk, o, n, mout, lambda so, pn, mout=mout, nk=nk: mout[:, 0:nk, so:so + pn])
            o += n
    P.close_scope()
    for ci in range(len(agmin.bufs)):
        P.collective("AllGather", ALU.bypass, GROUPS, agmin.bufs[ci][:], agmout.bufs[ci][:], reads=[agmin.bufs[ci]], writes=[agmout.bufs[ci]])
    P.open_scope()
    woS = P.sbuf("c_wo", [128, NDC, 1024], BF16)
    for hf in range(2):
        P.dma("pool", woS[:, hf * 16:(hf + 1) * 16, :], wo.t[hf * 2048:(hf + 1) * 2048, :].rearrange("(k p) n -> p k n", p=128), reads=[wo], writes=[woS])
    mb = [P.sbuf(f"c_mb{i}", [128, NDC, 512], BF16) for i in range(2)]
    xi = [P.sbuf(f"c_xi{i}", [128, NK, 512], F32) for i in range(2)]
    xo = [P.sbuf(f"c_xo{i}", [128, NK, 512], F32) for i in range(2)]
    for bi, (o, n, t) in enumerate(_blocks(nctx, tall)):
        m_, xin, xout = mb[bi % 2], xi[bi % 2], xo[bi % 2]
        rd_cpn(P, agmout, o, n, m_, lambda so, pn, m_=m_: m_[:, :, so:so + pn])
        P.dma("sp", xin[:, :, 0:n], xs_in.t[:, :, o:o + n].rearrange("c p n -> p c n"), reads=[xs_in], writes=[xin])
        for k in range(NK):
            pb = nb()
            for kc in range(NDC):
                P.op("pe", lambda e, pb=pb, m_=m_, kc=kc, k=k, n=n: e.matmul(pb[:, 0:n], woS[:, kc, k * 128:(k + 1) * 128], m_[:, kc, 0:n],
                                                                          start=(kc == 0), stop=(kc == NDC - 1)),
                     reads=[woS, m_], writes=[pb], signal=(kc == NDC - 1))
            P.op("dve", lambda e, pb=pb, xin=xin, xout=xout, k=k, n=n, t=t: e.scalar_tensor_tensor(
                xout[:, k, 0:n], pb[:, 0:n], gate[:, t, k:k + 1], xin[:, k, 0:n], ALU.mult, ALU.add), reads=[pb, gate, xin], writes=[xout])
        P.dma("sp", xs_out.t[:, :, o:o + n].rearrange("c p n -> p c n"), xout[:, :, 0:n], reads=[xout], writes=[xs_out])
    P.close_scope()


def build_fused(nlat=NLAT, nctx=NCTX, nlayers=2, stop=99):
    nc = _nc()
    P = Prog(nc)
    tall = nlat + nctx
    I = "ExternalInput"
    xs0 = P.dram("xs", [NK, 128, tall], F32, I)
    cT = P.dram("cT", [128, 32, 2], F32, I)
    wm = P.dram("wm", [2, D, 3072], F32, I)
    bm = P.dram("bm", [128, 2, 24], F32, I)
    ngd = P.dram("ng", [128, 3, NK], F32, I)
    cosT = P.dram("cosT", [128, nlat], F32, I)
    sinT = P.dram("sinT", [128, nlat], F32, I)
    rmT = P.dram("rmT", [128, 128], F32, I)
    s5k = P.dram("s5k", [128, 260], F32, I)
    glk = P.dram("glk", [128, 384], F32, I)
    cmk = P.dram("cmk", [64, tall], F32, I)
    L = []
    for l in range(nlayers):
        d = {}
        for (nm, shp) in (("wq", [D, TM_W + FM_W]), ("wmg", [D, 3072]), ("wglu", [1024, 1024]), ("wps", [1024, 1024]),
                          ("wpg", [1024, 1024]), ("wpa", [2048, 1024]), ("wo", [D, 1024]), ("gqk", [128, 2]),
                          ("s5p", [128, 32, 4]), ("s5b", [128, 32, 2, 16]), ("s5c", [128, 32, 16]), ("s5d", [16, 16]),
                          ("glw", [16, 4, 64]), ("glb", [64, 4]), ("glg", [128, 1])):
            d[nm] = P.dram(f"{nm}{l}", shp, F32, I)
        L.append(d)
    out = P.dram("out", [NK * 128, nlat], F32, "ExternalOutput")
    modS = P.sbuf("modS", [128, 2, 24, 2], F32)
    ngs = P.sbuf("ngs", [128, 3, NK], F32)
    Av = P.sbuf("Av", [128, 2, NK], F32)
    Bv = P.sbuf("Bv", [128, 2, NK], F32)
    Gv = P.sbuf("Gv", [128, 2, NK], F32)
    P.dma("sp", ngs[:], ngd[:], reads=[ngd], writes=[ngs])
    P.open_scope()
    emit_M2(P, cT, wm, bm, modS)
    P.close_scope()
    xs = xs0
    for l in range(nlayers):
        W = L[l]
        for t in range(2):
            P.op("dve", lambda e, t=t, l=l: e.scalar_tensor_tensor(Av[:, t, :], modS[:, l, 8:16, t], 1.0, ngs[:, l, :], ALU.add, ALU.mult),
                 reads=[modS, ngs], writes=[Av])
            P.op("dve", lambda e, t=t, l=l: e.tensor_copy(Bv[:, t, :], modS[:, l, 0:8, t]), reads=[modS], writes=[Bv])
            P.op("dve", lambda e, t=t, l=l: e.tensor_copy(Gv[:, t, :], modS[:, l, 16:24, t]), reads=[modS], writes=[Gv])
        arin = [P.dram(f"arin{l}_{q}", [128, tall // 4], F32) for q in range(4)]
        arout = [P.dram(f"arout{l}_{q}", [128, tall // 4], F32) for q in range(4)]
        aghin = TT(P, f"aghin{l}", NK * 128, tall, BF16)
        aghout = TT(P, f"aghout{l}", NDC * 128, tall, BF16)
        P.open_scope()
        emit_A2(P, xs, Av, Bv, arin, arout, aghin, BF16, tall, nctx, ag_out=aghout)
        P.close_scope()
        if stop == 2:
            break
        hTb = aghout
        tm = P.dram(f"tm{l}", [tall, TM_W], F32)
        fm = P.dram(f"fm{l}", [FM_W, tall], F32)
        ags_i, ags_o = TT(P, f"agsi{l}", 512, tall, BF16), TT(P, f"agso{l}", 4 * 512, tall, BF16)
        agl_i, agl_o = TT(P, f"agli{l}", 256, tall, BF16), TT(P, f"aglo{l}", 4 * 256, tall, BF16)
        aga_i, aga_o = TT(P, f"agai{l}", 512, tall, BF16), TT(P, f"agao{l}", 4 * 512, tall, BF16)
        gT, gsT, glT, atT = ags_i.sub(0), ags_i.sub(256), agl_i, aga_i

        def ag_all(ti, to):
            for ci in range(len(ti.bufs)):
                P.collective("AllGather", ALU.bypass, GROUPS, ti.bufs[ci][:], to.bufs[ci][:], reads=[ti.bufs[ci]], writes=[to.bufs[ci]])
        P.open_scope()
        emit_B1(P, hTb, W["wq"], tm, fm, tall)
        P.close_scope()
        P.open_scope()
        emit_s5(P, fm, W["s5p"], W["s5b"], W["s5c"], W["s5d"], s5k, gT, gsT, tall, nctx)
        P.close_scope()
        ag_all(ags_i, ags_o)
        P.open_scope()
        emit_gla(P, fm, tm, W["glw"], W["glb"], W["glg"], glk, cmk, glT, tall, nctx)
        P.close_scope()
        ag_all(agl_i, agl_o)
        P.open_scope()
        emit_attn(P, fm, tm, cosT, sinT, rmT, W["gqk"], atT, l < nlayers - 1, tall, nctx)
        P.close_scope()
        ag_all(aga_i, aga_o)
        agmin = TT(P, f"agmin{l}", NK * 128, tall, BF16)
        agmout = TT(P, f"agmout{l}", NDC * 128, tall, BF16)
        xs_new = P.dram(f"xs{l + 1}", [NK, 128, tall], F32)
        emit_C2(P, hTb, (ags_o, agl_o, aga_o), W["wglu"], W["wmg"], W["wps"], W["wpg"], W["wpa"], W["wo"], Gv, xs, xs_new, agmin, agmout, tall, nctx)
        xs = xs_new
    if stop < 99:
        P.finish([])
        P.emit()
        return nc
    P.op("dve", lambda e: e.memset(Bv[:], 0.0), writes=[Bv])
    for t in range(2):
        P.op("dve", lambda e, t=t: e.tensor_copy(Av[:, t, :], ngs[:, 2, :]), reads=[ngs], writes=[Av])
    arin = [P.dram(f"arinF_{q}", [128, tall // 4], F32) for q in range(4)]
    arout = [P.dram(f"aroutF_{q}", [128, tall // 4], F32) for q in range(4)]
    P.open_scope()
    emit_A2(P, xs, Av, Bv, arin, arout, out, F32, tall, nctx, lat_only=True)
    P.close_scope()
    P.finish([out])
    P.emit()
    return nc


def _fused_inputs(x, c, ctx, c_ctx, norm_g, w_mod, b_mod, w_in, ssm_lam_re, ssm_lam_im, ssm_log_dt, ssm_b_re, ssm_b_im,
                  ssm_c_re, ssm_c_im, ssm_d, ssm_w_glu, gla_w_a, gla_b_a, gla_norm_g, attn_q_g, attn_k_g,
                  w_proj_ssm, w_proj_gla, w_proj_attn, w_out, final_g):
    f = lambda a: np.asarray(a, dtype=np.float32)
    x, c, ctx, c_ctx = f(x), f(c), f(ctx), f(c_ctx)
    nlat, nctx = x.shape[1], ctx.shape[1]
    tall = nlat + nctx
    w_mod, b_mod, w_in, norm_g, final_g = f(w_mod), f(b_mod), f(w_in), f(norm_g), f(final_g)
    cosT, sinT, rm = rope_tables(nlat)
    glk, cmk = gla_consts(tall)
    s5k = s5_consts()
    ins = []
    for i in range(NCORES):
        b, j = i // 4, i % 4
        dsl = slice(j * 1024, (j + 1) * 1024)
        xa = np.concatenate([ctx[b], x[b]], 0)
        d = {"xs": np.ascontiguousarray(xa[:, dsl].T.reshape(NK, 128, tall)),
             "cT": np.ascontiguousarray(np.stack([c[b], c_ctx], 0).reshape(2, 32, 128).transpose(2, 1, 0)),
             "cosT": cosT, "sinT": sinT, "rmT": rm, "s5k": s5k, "glk": glk, "cmk": cmk}
        mcols = np.concatenate([np.arange(p * 4096 + j * 1024, p * 4096 + (j + 1) * 1024) for p in range(3)])
        d["wm"] = np.ascontiguousarray(w_mod[:, :, mcols])
        d["bm"] = np.ascontiguousarray(b_mod[:, mcols].reshape(2, 24, 128).transpose(2, 0, 1))
        d["ng"] = np.ascontiguousarray(np.stack([norm_g[0][dsl], norm_g[1][dsl], final_g[dsl]], 0).reshape(3, NK, 128).transpose(2, 0, 1))
        for l in range(2):
            gcols = np.concatenate([np.arange(_OFF["mg"] + br * 4096 + j * 1024, _OFF["mg"] + br * 4096 + (j + 1) * 1024) for br in range(3)])
            prm, bb, cc, dd = s5_host_layout(f(ssm_lam_re)[l], f(ssm_lam_im)[l], f(ssm_log_dt)[l], f(ssm_b_re)[l], f(ssm_b_im)[l],
                                             f(ssm_c_re)[l], f(ssm_c_im)[l], f(ssm_d)[l], j)
            WA, BA = f(gla_w_a)[l], f(gla_b_a)[l]
            d[f"wq{l}"] = np.ascontiguousarray(w_in[l][:, _cols(j)])
            d[f"wmg{l}"] = np.ascontiguousarray(w_in[l][:, gcols])
            d[f"wglu{l}"] = np.ascontiguousarray(f(ssm_w_glu)[l])
            d[f"wps{l}"] = np.ascontiguousarray(f(w_proj_ssm)[l][:, dsl])
            d[f"wpg{l}"] = np.ascontiguousarray(f(w_proj_gla)[l][:, dsl])
            d[f"wpa{l}"] = np.ascontiguousarray(f(w_proj_attn)[l][:, dsl])
            d[f"wo{l}"] = np.ascontiguousarray(f(w_out)[l][:, dsl])
            d[f"gqk{l}"] = np.ascontiguousarray(np.stack([f(attn_q_g)[l], f(attn_k_g)[l]], 1))
            d[f"s5p{l}"], d[f"s5b{l}"], d[f"s5c{l}"], d[f"s5d{l}"] = prm, bb, cc, dd
            d[f"glw{l}"] = np.ascontiguousarray(WA[:, :, 128 * j:128 * (j + 1)].reshape(2, 16, 2, 64).transpose(1, 0, 2, 3).reshape(16, 4, 64))
            d[f"glb{l}"] = np.ascontiguousarray(BA[:, 128 * j:128 * (j + 1)].reshape(4, 64).T)
            d[f"glg{l}"] = np.ascontiguousarray(f(gla_norm_g)[l].reshape(128, 1))
        ins.append(d)
    return ins, nlat, nctx


def kernel_fused(**inputs):
    ins, nlat, nctx = _fused_inputs(**inputs)
    import os
    nc = build_fused(nlat, nctx, stop=int(os.environ.get("FSTOP", "99")))
    res = _run(nc, ins)
    out = np.zeros((2, nlat, D), np.float32)
    for i in range(NCORES):
        b, j = i // 4, i % 4
        out[b][:, j * 1024:(j + 1) * 1024] = res[i]["out"].T
    return out


kernel_unfused = kernel


def kernel(**inputs):
    return kernel_fused(**inputs)
```

```python
import numpy as np
import concourse.bass as bass
import concourse.mybir as mybir
from concourse.bass_utils import run_bass_kernel_spmd
from contextlib import ExitStack

F32 = mybir.dt.float32
BF16 = mybir.dt.bfloat16
AF = mybir.ActivationFunctionType
ALU = mybir.AluOpType
AX = mybir.AxisListType

SEM_WRAP = 30000


class Buf:
    __slots__ = ("t", "name", "lw", "rd", "root")

    def __init__(self, t, name="", root=None):
        self.t = t
        self.name = name
        self.lw = None
        self.rd = []
        self.root = root if root is not None else self

    def alias(self, ap):
        return Buf(ap, self.name + "_v", root=self.root)

    def __getitem__(self, idx):
        return self.t[idx]


class Prog:
    ENGS = ("pe", "act", "dve", "pool", "sp")

    def __init__(self, nc, ndma_slots=12):
        self.nc = nc
        self.stack = ExitStack()
        self.streams = {e: [] for e in self.ENGS}
        self.sems = []
        self.eng_sem = {}
        self.eng_cnt = {}
        self.waited = {e: {} for e in self.ENGS}
        self.ndma_slots = ndma_slots
        self.dma_slots = {}
        self.dma_n = {}
        self.n_ins = 0
        self.pending = {e: [] for e in self.ENGS}
        self.scopes = []
        self.banks = None

    def bank(self, i):
        if self.banks is None:
            self.banks = [self.psum(f"bank{k}", [128, 512]) for k in range(8)]
        return self.banks[i]

    def barrier(self):
        evs = []
        for e, s in self.eng_sem.items():
            if self.eng_cnt[e] > 0:
                evs.append((s, self.eng_cnt[e]))
        for e, slots in self.dma_slots.items():
            n = self.dma_n[e]
            for k, s in enumerate(slots):
                cnt = (n - k + self.ndma_slots - 1) // self.ndma_slots if n > k else 0
                if cnt > 0:
                    evs.append((s, 16 * cnt))
        evs += getattr(self, "cc_events", [])
        for e in self.ENGS:
            own = self.eng_sem.get(e)
            w = self.waited[e]
            for (s, v) in evs:
                if s == own:
                    continue
                if w.get(s, 0) < v:
                    w[s] = v
                    self.pending[e].append((s, v))

    def open_scope(self):
        self.scopes.append(ExitStack())

    def close_scope(self):
        self.barrier()
        self.emit_segment()
        self.scopes.pop().close()

    def new_sem(self, name):
        s = self.stack.enter_context(self.nc.semaphore(name))
        self.sems.append(s)
        return len(self.sems) - 1

    def sbuf(self, name, shape, dtype):
        st = self.scopes[-1] if self.scopes else self.stack
        self.uid = getattr(self, "uid", 0) + 1
        t = st.enter_context(self.nc.sbuf_tensor(f"{name}_{self.uid}", list(shape), dtype))
        return Buf(t, name)

    def psum(self, name, shape, dtype=F32):
        t = self.stack.enter_context(self.nc.psum_tensor(name, list(shape), dtype))
        return Buf(t, name)

    def dram(self, name, shape, dtype, kind="Internal"):
        t = self.nc.dram_tensor(name, list(shape), dtype, kind=kind)
        return Buf(t.ap(), name)

    def view(self, ap, name=""):
        return Buf(ap, name)

    def _collect_waits(self, eng, reads, writes):
        evs = []
        for b in reads:
            b = b.root
            if b.lw is not None:
                evs.append(b.lw)
        for b in writes:
            b = b.root
            if b.lw is not None:
                evs.append(b.lw)
            evs.extend(b.rd)
        need = {}
        w = self.waited[eng]
        own = self.eng_sem.get(eng)
        for (s, v) in evs:
            if w.get(s, 0) >= v:
                continue
            if s == own and (eng == "pe" or v > self.eng_cnt[eng]):
                continue
            if need.get(s, 0) < v:
                need[s] = v
        for s, v in need.items():
            w[s] = v
        return list(need.items())

    def _mark(self, ev, reads, writes):
        for b in writes:
            b = b.root
            b.lw = ev
            b.rd = []
        for b in reads:
            b = b.root
            b.rd.append(ev)
            if len(b.rd) > 64:
                m = {}
                for (s, v) in b.rd:
                    if m.get(s, 0) < v:
                        m[s] = v
                b.rd = list(m.items())

    def op(self, eng, fn, reads=(), writes=(), signal=True):
        waits = self._collect_waits(eng, reads, writes)
        if self.pending[eng]:
            waits = waits + self.pending[eng]
            self.pending[eng] = []
        ev = None
        inc = None
        if signal:
            if eng not in self.eng_sem or self.eng_cnt[eng] >= SEM_WRAP:
                self.eng_sem[eng] = self.new_sem(f"s_{eng}_{len(self.sems)}")
                self.eng_cnt[eng] = 0
            self.eng_cnt[eng] += 1
            s = self.eng_sem[eng]
            ev = (s, self.eng_cnt[eng])
            inc = (s, 1)
        else:
            if eng not in self.eng_sem or self.eng_cnt[eng] >= SEM_WRAP:
                self.eng_sem[eng] = self.new_sem(f"s_{eng}_{len(self.sems)}")
                self.eng_cnt[eng] = 0
            ev = (self.eng_sem[eng], self.eng_cnt[eng] + 1)
        self.streams[eng].append((fn, waits, inc))
        self._mark(ev, reads, writes)
        self.n_ins += 1
        return ev

    def dma(self, eng, out_ap, in_ap, reads=(), writes=(), **kw):
        if eng not in self.dma_slots:
            self.dma_slots[eng] = [self.new_sem(f"d_{eng}_{i}") for i in range(self.ndma_slots)]
            self.dma_n[eng] = 0
        i = self.dma_n[eng]
        self.dma_n[eng] += 1
        slot = i % self.ndma_slots
        s = self.dma_slots[eng][slot]
        gen = i // self.ndma_slots
        waits = self._collect_waits(eng, reads, writes)
        if self.pending[eng]:
            waits = waits + self.pending[eng]
            self.pending[eng] = []
        if gen > 0:
            w = self.waited[eng]
            if w.get(s, 0) < 16 * gen:
                w[s] = 16 * gen
                waits = [x for x in waits if x[0] != s] + [(s, 16 * gen)]
        ev = (s, 16 * (gen + 1))

        def fn(e, out_ap=out_ap, in_ap=in_ap, kw=kw):
            return e.dma_start(out=out_ap, in_=in_ap, **kw)

        self.streams[eng].append((fn, waits, (s, 16)))
        self._mark(ev, reads, writes)
        self.n_ins += 1
        return ev

    def collective(self, kind, op, groups, in_ap, out_ap, reads=(), writes=(), inc=1):
        eng = "pool"
        NS = 4
        if not hasattr(self, "cc_slots"):
            self.cc_slots = [self.new_sem(f"cc_{i}") for i in range(NS)]
            self.cc_n = 0
        i = self.cc_n
        self.cc_n += 1
        s = self.cc_slots[i % NS]
        gen = i // NS
        waits = self._collect_waits(eng, reads, writes)
        if self.pending[eng]:
            waits = waits + self.pending[eng]
            self.pending[eng] = []
        if gen > 0:
            w = self.waited[eng]
            if w.get(s, 0) < gen:
                w[s] = gen
                waits = [x for x in waits if x[0] != s] + [(s, gen)]
        ev = (s, gen + 1)

        def fn(e):
            return e.collective_compute(kind, op, replica_groups=groups, ins=[in_ap], outs=[out_ap])

        self.streams[eng].append((fn, waits, (s, 1)))
        self._mark(ev, reads, writes)
        self.cc_events = [(self.cc_slots[k], (self.cc_n - k + NS - 1) // NS) for k in range(NS) if self.cc_n > k]
        self.n_ins += 1
        return ev

    def finish(self, final_bufs):
        evs = []
        for b in final_bufs:
            if b.root.lw is not None:
                evs.append(b.root.lw)
        self.final_waits = evs

    def emit(self):
        self.emit_segment(final=True)
        self.stack.close()

    def emit_segment(self, final=False):
        nc = self.nc
        streams = self.streams
        self.streams = {e: [] for e in self.ENGS}
        sems = self.sems
        final_waits = getattr(self, "final_waits", []) if final else []

        def run(e, name):
            for (fn, waits, inc) in streams[name]:
                for (s, v) in waits:
                    e.wait_ge(sems[s], v)
                ins = fn(e)
                if inc is not None:
                    ins.then_inc(sems[inc[0]], inc[1])
            if name == "sp":
                for (s, v) in final_waits:
                    e.wait_ge(sems[s], v)

        with nc.Block() as block:
            @block.tensor
            def _(e):
                run(e, "pe")

            @block.scalar
            def _(e):
                run(e, "act")

            @block.vector
            def _(e):
                run(e, "dve")

            @block.gpsimd
            def _(e):
                run(e, "pool")

            @block.sync
            def _(e):
                run(e, "sp")


NCORES = 8
D = 4096
NDC = 32
EPS = 1e-6


def _nc():
    return bass.Bass("TRN2", target_bir_lowering=False)


def build_M():
    nc = _nc()
    P = Prog(nc)
    cT = P.dram("cT", [128, 32, 3], F32, "ExternalInput")
    wm = P.dram("wm", [2, 4096, 1536], F32, "ExternalInput")
    bm = P.dram("bm", [128, 2, 12], F32, "ExternalInput")
    out = P.dram("modT", [128, 2, 12, 3], F32, "ExternalOutput")
    sc = P.sbuf("sc", [128, 32, 3], F32)
    bs = P.sbuf("bs", [128, 2, 12], F32)
    acc = P.sbuf("acc", [128, 2, 12, 3], F32)
    wt = [P.sbuf(f"wt{i}", [128, 4, 1536], F32) for i in range(2)]
    ps = [P.psum(f"ps{i}", [128, 12, 3]) for i in range(2)]
    P.dma("sp", sc[:], cT[:], reads=[cT], writes=[sc])
    P.dma("sp", bs[:], bm[:], reads=[bm], writes=[bs])
    P.op("act", lambda e: e.activation(sc[:], sc[:], AF.Silu), reads=[sc], writes=[sc])
    it = 0
    for l in range(2):
        for g in range(8):
            w = wt[it % 2]
            pp = ps[it % 2]
            src = wm.t[l, g * 512:(g + 1) * 512, :].rearrange("(k p) n -> p k n", p=128)
            P.dma("sp", w[:], src, reads=[wm], writes=[w])
            for j in range(12):
                for k in range(4):
                    kc = g * 4 + k
                    P.op("pe", lambda e, pp=pp, w=w, j=j, k=k, kc=kc: e.matmul(
                        pp[:, j, :], w[:, k, j * 128:(j + 1) * 128], sc[:, kc, :],
                        start=(k == 0), stop=(k == 3)),
                        reads=[w, sc], writes=[pp], signal=(k == 3 and j == 11))
            if g == 0:
                P.op("dve", lambda e, pp=pp, l=l: e.tensor_copy(acc[:, l], pp[:]), reads=[pp], writes=[acc])
            else:
                P.op("dve", lambda e, pp=pp, l=l: e.tensor_tensor(acc[:, l], acc[:, l], pp[:], ALU.add),
                     reads=[pp, acc], writes=[acc])
            it += 1
    for l in range(2):
        for r in range(3):
            P.op("dve", lambda e, l=l, r=r: e.tensor_tensor(acc[:, l, :, r], acc[:, l, :, r], bs[:, l, :], ALU.add),
                 reads=[acc, bs], writes=[acc])
    P.dma("sp", out[:], acc[:], reads=[acc], writes=[out])
    P.finish([out])
    P.emit()
    return nc


def build_A(nlat=1024, nctx=64, out_bf16=True):
    nc = _nc()
    P = Prog(nc)
    NT = nlat + nctx
    odt = BF16 if out_bf16 else F32
    xT = P.dram("xT", [NDC, 128, NT], F32, "ExternalInput")
    ng = P.dram("ng", [128, NDC], F32, "ExternalInput")
    ms = P.dram("ms", [128, 96, 2], F32, "ExternalInput")
    hT = P.dram("hT", [NDC, 128, NT], odt, "ExternalOutput")
    emit_A(P, xT, ng, ms, hT, nlat, nctx, odt)
    P.finish([hT])
    P.emit()
    return nc


def emit_A(P, xT, ng, ms, hT, nlat, nctx, odt):
    ngs = P.sbuf("a_ng", [128, NDC], F32)
    mss = P.sbuf("a_ms", [128, 96, 2], F32)
    Av = P.sbuf("a_A", [128, 2, NDC], F32)
    Bv = P.sbuf("a_B", [128, 2, NDC], F32)
    onesm = P.sbuf("a_ones", [128, 128], F32)
    epsb = P.sbuf("a_eps", [128, 1], F32)
    xs = P.sbuf("a_xs", [128, NDC, 512], F32)
    ho = P.sbuf("a_ho", [128, NDC, 512], odt)
    sq = [P.sbuf(f"a_sq{i}", [128, 512], F32) for i in range(2)]
    tmp = [P.sbuf(f"a_tmp{i}", [128, 512], F32) for i in range(2)]
    rstd = P.sbuf("a_rstd", [128, 512], F32)
    pss = P.psum("a_pss", [128, 512])
    P.dma("sp", ngs[:], ng[:], reads=[ng], writes=[ngs])
    P.dma("sp", mss[:], ms[:], reads=[ms], writes=[mss])
    P.op("dve", lambda e: e.memset(onesm[:], 1.0 / D), writes=[onesm])
    P.op("dve", lambda e: e.memset(epsb[:], EPS), writes=[epsb])
    for t in range(2):
        P.op("dve", lambda e, t=t: e.scalar_tensor_tensor(Av[:, t, :], mss[:, 32:64, t], 1.0, ngs[:], ALU.add, ALU.mult),
             reads=[mss, ngs], writes=[Av])
        P.op("dve", lambda e, t=t: e.tensor_copy(Bv[:, t, :], mss[:, 0:32, t]), reads=[mss], writes=[Bv])
    blocks = []
    o = 0
    while o < nlat:
        n = min(512, nlat - o)
        blocks.append((o, n, 0))
        o += n
    while o < nlat + nctx:
        n = min(512, nlat + nctx - o)
        blocks.append((o, n, 1))
        o += n
    for (o, n, t) in blocks:
        P.dma("sp", xs[:, :, 0:n], xT.t[:, :, o:o + n].rearrange("c p n -> p c n"), reads=[xT], writes=[xs])
        for dc in range(NDC):
            s = sq[dc % 2]
            P.op("act", lambda e, s=s, dc=dc, n=n: e.activation(s[:, 0:n], xs[:, dc, 0:n], AF.Square), reads=[xs], writes=[s])
            P.op("pe", lambda e, s=s, dc=dc, n=n: e.matmul(pss[:, 0:n], onesm[:], s[:, 0:n], start=(dc == 0), stop=(dc == NDC - 1)),
                 reads=[onesm, s], writes=[pss])
        P.op("act", lambda e, n=n: e.activation(rstd[:, 0:n], pss[:, 0:n], AF.Sqrt, bias=epsb[:], scale=1.0), reads=[pss, epsb], writes=[rstd])
        P.op("dve", lambda e, n=n: e.reciprocal(rstd[:, 0:n], rstd[:, 0:n]), reads=[rstd], writes=[rstd])
        for dc in range(NDC):
            tm = tmp[dc % 2]
            P.op("dve", lambda e, tm=tm, dc=dc, n=n, t=t: e.scalar_tensor_tensor(
                tm[:, 0:n], xs[:, dc, 0:n], Av[:, t, dc:dc + 1], rstd[:, 0:n], ALU.mult, ALU.mult),
                reads=[xs, Av, rstd], writes=[tm])
            P.op("act", lambda e, tm=tm, dc=dc, n=n, t=t: e.activation(
                ho[:, dc, 0:n], tm[:, 0:n], AF.Identity, bias=Bv[:, t, dc:dc + 1], scale=1.0),
                reads=[tm, Bv], writes=[ho])
        P.dma("sp", hT.t[:, :, o:o + n].rearrange("c p n -> p c n"), ho[:, :, 0:n], reads=[ho], writes=[hT])


TALL = 4352
NCTX = 256
NLAT = 4096
TM_W = 384
C_GV, C_AV = 0, 256
FM_W = 2208
R_GQ, R_GK, R_GZ, R_AQ, R_AK, R_AZ, R_SU, R_SZ, R_LR = 0, 128, 256, 512, 1024, 1152, 1664, 1920, 2176


def emit_B1(P, hTb, wq, tm, fm, tall=TALL):
    hb = [P.sbuf(f"b1_h{i}", [128, NDC, 512], BF16) for i in range(2)]
    W = P.sbuf("b1_w", [128, NDC, 1024], BF16)
    stg = [P.sbuf(f"b1_s{i}", [128, 1024], F32) for i in range(3)]
    ps = [(P.bank(0), P.bank(1)), (P.bank(2), P.bank(3))]
    nblk = (tall + 511) // 512
    passes = [("tm", 0, TM_W), ("fm", TM_W, 1024), ("fm", TM_W + 1024, 1024), ("fm", TM_W + 2048, FM_W - 2048)]
    it = 0
    si = 0
    for (kind, c0, ncol) in passes:
        for half in range(2):
            P.dma("pool", W[:, half * 16:(half + 1) * 16, 0:ncol],
                  wq.t[half * 2048:(half + 1) * 2048, c0:c0 + ncol].rearrange("(k p) n -> p k n", p=128),
                  reads=[wq], writes=[W])
        for tb in range(nblk):
            o = tb * 512
            n = min(512, tall - o)
            h = hb[it % 2]
            it += 1
            rd_cpn(P, hTb, o, n, h, lambda so, pn, h=h: h[:, :, so:so + pn])
            if kind == "tm":
                for sub in range(n // 128):
                    pp = ps[si % 2]
                    st = stg[si % 3]
                    si += 1
                    for (b0, bw, bank) in ((0, ncol, 0),):
                        for kc in range(NDC):
                            P.op("pe", lambda e, pp=pp, h=h, kc=kc, sub=sub, b0=b0, bw=bw, bank=bank: e.matmul(
                                pp[bank][:, 0:bw], h[:, kc, sub * 128:(sub + 1) * 128], W[:, kc, b0:b0 + bw],
                                start=(kc == 0), stop=(kc == NDC - 1)),
                                reads=[h, W], writes=[pp[bank]], signal=(kc == NDC - 1))
                    P.op("act", lambda e, pp=pp, st=st, ncol=ncol: e.activation(st[:, 0:ncol], pp[0][:, 0:ncol], AF.Copy), reads=[pp[0]], writes=[st])
                    r0 = o + sub * 128
                    P.dma("sp", tm.t[r0:r0 + 128, 0:ncol], st[:, 0:ncol], reads=[st], writes=[tm])
            else:
                r_base = c0 - TM_W
                ncc = (ncol + 127) // 128
                for cc in range(ncc):
                    m = min(128, ncol - cc * 128)
                    pp = P.bank(si % 4)
                    st = stg[si % 3]
                    for kc in range(NDC):
                        P.op("pe", lambda e, pp=pp, h=h, kc=kc, cc=cc, m=m, n=n: e.matmul(
                            pp[0:m, 0:n], W[:, kc, cc * 128:cc * 128 + m], h[:, kc, 0:n],
                            start=(kc == 0), stop=(kc == NDC - 1)),
                            reads=[h, W], writes=[pp], signal=(kc == NDC - 1))
                    if si % 2 == 0:
                        P.op("act", lambda e, pp=pp, st=st, m=m, n=n: e.activation(st[0:m, 0:n], pp[0:m, 0:n], AF.Copy), reads=[pp], writes=[st])
                    else:
                        P.op("dve", lambda e, pp=pp, st=st, m=m, n=n: e.tensor_copy(st[0:m, 0:n], pp[0:m, 0:n]), reads=[pp], writes=[st])
                    si += 1
                    rr = r_base + cc * 128
                    P.dma("sp", fm.t[rr:rr + m, o:o + n], st[0:m, 0:n], reads=[st], writes=[fm])


def emit_attn(P, fm, tm, cosT, sinT, rmT, gqk, atT, ctx_out, tall=TALL, nctx=NCTX):
    nlat = tall - nctx
    ntile = tall // 128
    cs = P.sbuf("at_cos", [128, nlat], F32)
    sn = P.sbuf("at_sin", [128, nlat], F32)
    rm = P.sbuf("at_rm", [128, 128], F32)
    g2 = P.sbuf("at_g", [128, 2], F32)
    ones_f = P.sbuf("at_1f", [128, 128], F32)
    ones_b = P.sbuf("at_1b", [128, 128], BF16)
    epsb = P.sbuf("at_eps", [128, 1], F32)
    KT = P.sbuf("at_KT", [128, tall], BF16)
    V = P.sbuf("at_V", [128, ntile, 128], BF16)
    QT = [P.sbuf(f"at_QT{i}", [128, 512], BF16) for i in range(2)]
    xs = [P.sbuf(f"at_xs{i}", [128, 512], F32) for i in range(2)]
    sq = P.sbuf("at_sq", [128, 512], F32)
    rs = P.sbuf("at_rs", [128, 512], F32)
    xn = P.sbuf("at_xn", [128, 512], F32)
    t1 = P.sbuf("at_t1", [128, 512], F32)
    t2 = P.sbuf("at_t2", [128, 512], F32)
    pb = [P.sbuf(f"at_p{i}", [128, 512], BF16) for i in range(3)]
    az = P.sbuf("at_az", [128, 512], F32)
    rl = P.sbuf("at_rl", [128, 512], F32)
    ob = P.sbuf("at_ob", [128, 512], F32)
    oo = [P.sbuf(f"at_oo{i}", [128, 512], BF16) for i in range(2)]
    ps_ms, ps_rot = P.bank(0), P.bank(1)
    ps_s = [P.bank(2), P.bank(3), P.bank(4)]
    ps_o, ps_l = P.bank(5), P.bank(6)
    P.dma("sp", cs[:], cosT[:], reads=[cosT], writes=[cs])
    P.dma("sp", sn[:], sinT[:], reads=[sinT], writes=[sn])
    P.dma("sp", rm[:], rmT[:], reads=[rmT], writes=[rm])
    P.dma("sp", g2[:], gqk[:], reads=[gqk], writes=[g2])
    P.op("dve", lambda e: e.memset(ones_f[:], 1.0 / 128), writes=[ones_f])
    P.op("dve", lambda e: e.memset(ones_b[:], 1.0), writes=[ones_b])
    P.op("dve", lambda e: e.memset(epsb[:], EPS), writes=[epsb])
    P.dma("pool", V[:], tm.t[:, C_AV:C_AV + 128].rearrange("(t p) c -> p t c", p=128), reads=[tm], writes=[V])
    cnt = [0]

    def prep(r0, o, n, gi, rope, pos0, dst, dstb):
        x = xs[cnt[0] % 2]
        cnt[0] += 1
        P.dma("sp", x[:, 0:n], fm.t[r0:r0 + 128, o:o + n], reads=[fm], writes=[x])
        P.op("act", lambda e: e.activation(sq[:, 0:n], x[:, 0:n], AF.Square), reads=[x], writes=[sq])
        P.op("pe", lambda e: e.matmul(ps_ms[:, 0:n], ones_f[:], sq[:, 0:n], start=True, stop=True), reads=[ones_f, sq], writes=[ps_ms])
        P.op("act", lambda e: e.activation(rs[:, 0:n], ps_ms[:, 0:n], AF.Sqrt, bias=epsb[:], scale=1.0), reads=[ps_ms, epsb], writes=[rs])
        P.op("dve", lambda e: e.reciprocal(rs[:, 0:n], rs[:, 0:n]), reads=[rs], writes=[rs])
        if not rope:
            P.op("dve", lambda e: e.scalar_tensor_tensor(dst, x[:, 0:n], g2[:, gi:gi + 1], rs[:, 0:n], ALU.mult, ALU.mult),
                 reads=[x, g2, rs], writes=[dstb])
            return
        P.op("dve", lambda e: e.scalar_tensor_tensor(xn[:, 0:n], x[:, 0:n], g2[:, gi:gi + 1], rs[:, 0:n], ALU.mult, ALU.mult),
             reads=[x, g2, rs], writes=[xn])
        P.op("pe", lambda e: e.matmul(ps_rot[:, 0:n], rm[:], xn[:, 0:n], start=True, stop=True), reads=[rm, xn], writes=[ps_rot])
        P.op("pool", lambda e: e.tensor_tensor(t1[:, 0:n], xn[:, 0:n], cs[:, pos0:pos0 + n], ALU.mult), reads=[xn, cs], writes=[t1])
        P.op("dve", lambda e: e.tensor_tensor(t2[:, 0:n], ps_rot[:, 0:n], sn[:, pos0:pos0 + n], ALU.mult), reads=[ps_rot, sn], writes=[t2])
        P.op("dve", lambda e: e.tensor_tensor(dst, t1[:, 0:n], t2[:, 0:n], ALU.add), reads=[t1, t2], writes=[dstb])

    prep(R_AK, 0, nctx, 1, False, 0, KT[:, 0:nctx], KT)
    o = nctx
    while o < tall:
        n = min(512, tall - o)
        prep(R_AK, o, n, 1, True, o - nctx, KT[:, o:o + n], KT)
        o += n
    scale = 128 ** -0.5
    qi = 0
    pi = 0
    for hh in range(4):
        blocks = []
        if ctx_out:
            blocks.append((0, nctx, False, nctx // 128))
        o = nctx
        while o < tall:
            n = min(512, tall - o)
            blocks.append((o, n, True, ntile))
            o += n
        for (o, n, rope, nk) in blocks:
            q = QT[qi % 2]
            qi += 1
            prep(R_AQ + hh * 128, o, n, 0, rope, o - nctx, q[:, 0:n], q)
            tiles = []
            for kc in range(nk):
                tiles.append((ps_s[pi % 3], pb[pi % 3]))
                pi += 1

            def qk(kc):
                s_ps = tiles[kc][0]
                P.op("pe", lambda e, s_ps=s_ps, q=q, kc=kc, n=n: e.matmul(s_ps[:, 0:n], KT[:, kc * 128:(kc + 1) * 128], q[:, 0:n], start=True, stop=True),
                     reads=[KT, q], writes=[s_ps])

            qk(0)
            if nk > 1:
                qk(1)
            for kc in range(nk):
                s_ps, p_sb = tiles[kc]
                P.op("act", lambda e, s_ps=s_ps, p_sb=p_sb, n=n: e.activation(p_sb[:, 0:n], s_ps[:, 0:n], AF.Exp, scale=scale),
                     reads=[s_ps], writes=[p_sb])
                if kc + 2 < nk:
                    qk(kc + 2)
                P.op("pe", lambda e, p_sb=p_sb, kc=kc, n=n, nk=nk: e.matmul(ps_o[:, 0:n], V[:, kc, :], p_sb[:, 0:n], start=(kc == 0), stop=(kc == nk - 1)),
                     reads=[V, p_sb], writes=[ps_o], signal=False)
                P.op("pe", lambda e, p_sb=p_sb, kc=kc, n=n, nk=nk: e.matmul(ps_l[:, 0:n], ones_b[:], p_sb[:, 0:n], start=(kc == 0), stop=(kc == nk - 1)),
                     reads=[ones_b, p_sb], writes=[ps_l])
            r0 = R_AZ + hh * 128
            P.dma("sp", az[:, 0:n], fm.t[r0:r0 + 128, o:o + n], reads=[fm], writes=[az])
            P.op("act", lambda e, n=n: e.activation(az[:, 0:n], az[:, 0:n], AF.Silu), reads=[az], writes=[az])
            P.op("dve", lambda e, n=n: e.reciprocal(rl[:, 0:n], ps_l[:, 0:n]), reads=[ps_l], writes=[rl])
            P.op("dve", lambda e, n=n: e.tensor_tensor(ob[:, 0:n], ps_o[:, 0:n], rl[:, 0:n], ALU.mult), reads=[ps_o, rl], writes=[ob])
            ot = oo[qi % 2]
            P.op("pool", lambda e, n=n, ot=ot: e.tensor_tensor(ot[:, 0:n], ob[:, 0:n], az[:, 0:n], ALU.mult), reads=[ob, az], writes=[ot])
            wr_rows(P, atT, hh * 128, 128, o, n, ot, lambda so, pn, ot=ot: ot[:, so:so + pn])


def rope_tables(nlat):
    rows = nlat // 64
    nf = 32
    row = np.repeat(np.arange(rows, dtype=np.float32), 64)
    col = np.tile(np.arange(64, dtype=np.float32), rows)
    inv = (10000.0 ** (-np.arange(nf, dtype=np.float32) / nf)).astype(np.float32)
    ang = np.stack([row[:, None] * inv, col[:, None] * inv], axis=1)
    ang = np.concatenate([ang, ang], axis=-1).reshape(rows * 64, 128).astype(np.float32)
    cosT = np.ascontiguousarray(np.cos(ang).T.astype(np.float32))
    sinT = np.ascontiguousarray(np.sin(ang).T.astype(np.float32))
    rm = np.zeros((128, 128), np.float32)
    for a in range(2):
        for f in range(32):
            rm[a * 64 + 32 + f, a * 64 + f] = -1.0
            rm[a * 64 + f, a * 64 + 32 + f] = 1.0
    return cosT, sinT, rm


def build_B(tall=TALL, nctx=NCTX, ctx_out=True, parts=("b1", "attn")):
    nc = _nc()
    P = Prog(nc)
    nlat = tall - nctx
    hTb = P.dram("hTb", [NDC, 128, tall], BF16, "ExternalInput")
    wq = P.dram("wq", [D, TM_W + FM_W], F32, "ExternalInput")
    cosT = P.dram("cosT", [128, nlat], F32, "ExternalInput")
    sinT = P.dram("sinT", [128, nlat], F32, "ExternalInput")
    rmT = P.dram("rmT", [128, 128], F32, "ExternalInput")
    gqk = P.dram("gqk", [128, 2], F32, "ExternalInput")
    dbg = "dbg" in parts
    tm = P.dram("tm", [tall, TM_W], F32, "ExternalOutput" if dbg else "Internal")
    fm = P.dram("fm", [FM_W, tall], F32, "ExternalOutput" if dbg else "Internal")
    atT = P.dram("atT", [512, tall], BF16, "ExternalOutput")
    s5p = P.dram("s5p", [128, 32, 4], F32, "ExternalInput")
    s5b = P.dram("s5b", [128, 32, 2, 16], F32, "ExternalInput")
    s5c = P.dram("s5c", [128, 32, 16], F32, "ExternalInput")
    s5d = P.dram("s5d", [16, 16], F32, "ExternalInput")
    s5k = P.dram("s5k", [128, 260], F32, "ExternalInput")
    gT = P.dram("gT", [256, tall], BF16, "ExternalOutput")
    gsT = P.dram("gsT", [256, tall], BF16, "ExternalOutput")
    glw = P.dram("glw", [16, 4, 64], F32, "ExternalInput")
    glb = P.dram("glb", [64, 4], F32, "ExternalInput")
    glg = P.dram("glg", [128, 1], F32, "ExternalInput")
    glk = P.dram("glk", [128, 384], F32, "ExternalInput")
    cmk = P.dram("cmk", [64, tall], F32, "ExternalInput")
    glT = P.dram("glT", [256, tall], BF16, "ExternalOutput")
    outs = [atT, gT, gsT, glT]
    if dbg:
        outs += [tm, fm]
    P.open_scope()
    emit_B1(P, hTb, wq, tm, fm, tall)
    P.close_scope()
    if "attn" in parts:
        P.open_scope()
        emit_attn(P, fm, tm, cosT, sinT, rmT, gqk, atT, ctx_out, tall, nctx)
        P.close_scope()
    if "gla" in parts:
        P.open_scope()
        emit_gla(P, fm, tm, glw, glb, glg, glk, cmk, glT, tall, nctx)
        P.close_scope()
    if "s5" in parts:
        P.open_scope()
        emit_s5(P, fm, s5p, s5b, s5c, s5d, s5k, gT, gsT, tall, nctx)
        P.close_scope()
    P.finish(outs)
    P.emit()
    return nc


def emit_s5(P, fm, s5p, s5b, s5c, s5d, cst, gT, gsT, tall=TALL, nctx=NCTX):
    nlat = tall - nctx
    NP = 32
    c = P.sbuf("s5_cst", [128, 260], F32)
    P.dma("sp", c[:], cst[:], reads=[cst], writes=[c])
    ident, swap = c[:, 0:128], c[:, 128:256]
    sgn, m0, m1, hpi = c[:, 256:257], c[:, 257:258], c[:, 258:259], c[:, 259:260]
    prm = P.sbuf("s5_prm", [128, NP, 4], F32)
    Bd = P.sbuf("s5_B", [128, NP, 2, 16], F32)
    Cw = P.sbuf("s5_C", [128, NP, 16], F32)
    dsk = P.sbuf("s5_d", [16, 16], F32)
    P.dma("sp", prm[:], s5p[:], reads=[s5p], writes=[prm])
    P.dma("sp", Bd[:], s5b[:], reads=[s5b], writes=[Bd])
    P.dma("sp", Cw[:], s5c[:], reads=[s5c], writes=[Cw])
    P.dma("sp", dsk[:], s5d[:], reads=[s5d], writes=[dsk])
    names = ["lr", "dt", "a", "th", "mag", "s", "c", "cc", "ss", "Lr", "Li", "den", "L1", "nr", "ni", "c1r", "c1i", "t"]
    T_ = {k: P.sbuf("s5_" + k, [128, NP], F32) for k in names}
    PR = P.sbuf("s5_PR", [128, 13, NP], F32)
    PI = P.sbuf("s5_PI", [128, 13, NP], F32)
    PIs = P.sbuf("s5_PIs", [128, 13, NP], F32)

    def tt(o, a, b, op, eng="dve"):
        P.op(eng, lambda e: e.tensor_tensor(o[:], a[:], b[:], op), reads=[a, b], writes=[o])

    lre, lim, ldt = prm[:, :, 0], prm[:, :, 1], prm[:, :, 2]
    P.op("dve", lambda e: e.tensor_scalar(T_["lr"][:], lre, -1e-4, None, ALU.min), reads=[prm], writes=[T_["lr"]])
    P.op("act", lambda e: e.activation(T_["dt"][:], ldt, AF.Exp), reads=[prm], writes=[T_["dt"]])
    tt(T_["a"], T_["lr"], T_["dt"], ALU.mult)
    P.op("dve", lambda e: e.tensor_tensor(T_["th"][:], lim, T_["dt"][:], ALU.mult), reads=[prm, T_["dt"]], writes=[T_["th"]])
    P.op("act", lambda e: e.activation(T_["mag"][:], T_["a"][:], AF.Exp), reads=[T_["a"]], writes=[T_["mag"]])
    P.op("act", lambda e: e.activation(T_["s"][:], T_["th"][:], AF.Sin, scale=1.0 / 16), reads=[T_["th"]], writes=[T_["s"]])
    P.op("act", lambda e: e.activation(T_["c"][:], T_["th"][:], AF.Sin, bias=hpi, scale=1.0 / 16), reads=[T_["th"], c], writes=[T_["c"]])
    for _ in range(4):
        tt(T_["cc"], T_["c"], T_["c"], ALU.mult)
        tt(T_["ss"], T_["s"], T_["s"], ALU.mult)
        P.op("dve", lambda e: e.scalar_tensor_tensor(T_["s"][:], T_["c"][:], 2.0, T_["s"][:], ALU.mult, ALU.mult),
             reads=[T_["c"], T_["s"]], writes=[T_["s"]])
        tt(T_["c"], T_["cc"], T_["ss"], ALU.subtract)
    tt(T_["Lr"], T_["mag"], T_["c"], ALU.mult)
    tt(T_["Li"], T_["mag"], T_["s"], ALU.mult)
    tt(T_["den"], T_["lr"], T_["lr"], ALU.mult)
    P.op("dve", lambda e: e.tensor_tensor(T_["t"][:], lim, lim, ALU.mult), reads=[prm], writes=[T_["t"]])
    tt(T_["den"], T_["den"], T_["t"], ALU.add)
    P.op("dve", lambda e: e.reciprocal(T_["den"][:], T_["den"][:]), reads=[T_["den"]], writes=[T_["den"]])
    P.op("dve", lambda e: e.tensor_scalar(T_["L1"][:], T_["Lr"][:], -1.0, None, ALU.add), reads=[T_["Lr"]], writes=[T_["L1"]])
    tt(T_["nr"], T_["L1"], T_["lr"], ALU.mult)
    P.op("dve", lambda e: e.tensor_tensor(T_["t"][:], T_["Li"][:], lim, ALU.mult), reads=[prm, T_["Li"]], writes=[T_["t"]])
    tt(T_["nr"], T_["nr"], T_["t"], ALU.add)
    tt(T_["ni"], T_["Li"], T_["lr"], ALU.mult)
    P.op("dve", lambda e: e.tensor_tensor(T_["t"][:], T_["L1"][:], lim, ALU.mult), reads=[prm, T_["L1"]], writes=[T_["t"]])
    tt(T_["ni"], T_["ni"], T_["t"], ALU.subtract)
    tt(T_["c1r"], T_["nr"], T_["den"], ALU.mult)
    tt(T_["c1i"], T_["ni"], T_["den"], ALU.mult)
    P.op("dve", lambda e: e.tensor_copy(PR[:, 0, :], T_["Lr"][:]), reads=[T_["Lr"]], writes=[PR])
    P.op("dve", lambda e: e.tensor_copy(PI[:, 0, :], T_["Li"][:]), reads=[T_["Li"]], writes=[PI])
    for m in range(12):
        P.op("dve", lambda e, m=m: e.tensor_tensor(T_["cc"][:], PR[:, m, :], PR[:, m, :], ALU.mult), reads=[PR], writes=[T_["cc"]])
        P.op("dve", lambda e, m=m: e.tensor_tensor(T_["ss"][:], PI[:, m, :], PI[:, m, :], ALU.mult), reads=[PI], writes=[T_["ss"]])
        P.op("dve", lambda e, m=m: e.scalar_tensor_tensor(PI[:, m + 1, :], PR[:, m, :], 2.0, PI[:, m, :], ALU.mult, ALU.mult),
             reads=[PR, PI], writes=[PI])
        P.op("dve", lambda e, m=m: e.tensor_tensor(PR[:, m + 1, :], T_["cc"][:], T_["ss"][:], ALU.subtract),
             reads=[T_["cc"], T_["ss"], PR], writes=[PR])
    P.op("dve", lambda e: e.tensor_scalar(PIs[:], PI[:], sgn, None, ALU.mult), reads=[PI, c], writes=[PIs])
    P.op("dve", lambda e: e.tensor_scalar(Cw[:], Cw[:], sgn, None, ALU.mult), reads=[Cw, c], writes=[Cw])
    bT = P.sbuf("s5_bT", [16, NP, 128], F32)
    tb = [P.sbuf(f"s5_tb{i}", [128, 16], F32) for i in range(4)]
    for pi in range(NP):
        c1r, c1i = T_["c1r"][:, pi:pi + 1], T_["c1i"][:, pi:pi + 1]
        Bre, Bim = Bd[:, pi, 0, :], Bd[:, pi, 1, :]
        rd = [T_["c1r"], T_["c1i"], Bd]
        P.op("dve", lambda e, Bim=Bim, c1i=c1i: e.tensor_scalar(tb[0][:], Bim, c1i, None, ALU.mult), reads=rd, writes=[tb[0]])
        P.op("dve", lambda e, Bre=Bre, c1r=c1r: e.scalar_tensor_tensor(tb[1][:], Bre, c1r, tb[0][:], ALU.mult, ALU.subtract), reads=rd + [tb[0]], writes=[tb[1]])
        P.op("dve", lambda e, Bre=Bre, c1i=c1i: e.tensor_scalar(tb[2][:], Bre, c1i, None, ALU.mult), reads=rd, writes=[tb[2]])
        P.op("dve", lambda e, Bim=Bim, c1r=c1r: e.scalar_tensor_tensor(tb[3][:], Bim, c1r, tb[2][:], ALU.mult, ALU.add), reads=rd + [tb[2]], writes=[tb[3]])
        P.op("dve", lambda e: e.tensor_scalar(tb[3][:], tb[3][:], m1, None, ALU.mult), reads=[tb[3], c], writes=[tb[3]])
        P.op("dve", lambda e: e.scalar_tensor_tensor(tb[1][:], tb[1][:], m0, tb[3][:], ALU.mult, ALU.add), reads=[tb[1], tb[3], c], writes=[tb[1]])
        pb = P.bank(pi % 2)
        P.op("pe", lambda e, pb=pb: e.transpose(pb[0:16, 0:128], tb[1][:], ident), reads=[tb[1], c], writes=[pb])
        P.op("act", lambda e, pb=pb, pi=pi: e.activation(bT[:, pi, :], pb[0:16, 0:128], AF.Copy), reads=[pb], writes=[bT])
    uT = P.sbuf("s5_u", [16, tall], F32)
    zT = P.sbuf("s5_z", [16, tall], F32)
    identb = P.sbuf("s5_identb", [128, 128], BF16)
    Cwb = P.sbuf("s5_Cwb", [128, NP, 16], BF16)
    P.op("dve", lambda e: e.tensor_copy(identb[:], ident), reads=[c], writes=[identb])
    P.op("dve", lambda e: e.tensor_copy(Cwb[:], Cw[:]), reads=[Cw], writes=[Cwb])
    Lm = [P.sbuf(f"s5_Lm{d}", [128, 13, 128], BF16) for d in range(2)]
    Xh = [[P.sbuf(f"s5_Xh{d}{k}", [128, tall], BF16) for k in range(2)] for d in range(2)]
    ys = [P.sbuf(f"s5_y{i}", [16, 512], F32) for i in range(2)]
    gb = [P.sbuf(f"s5_g{i}", [16, 512], BF16) for i in range(2)]
    gsb = [P.sbuf(f"s5_gs{i}", [16, 512], BF16) for i in range(2)]
    nsteps = 0
    while (1 << nsteps) < tall:
        nsteps += 1
    bk = [0]

    def nb():
        bk[0] += 1
        return P.bank(2 + bk[0] % 6)

    blocks = [(0, nctx)]
    o = nctx
    while o < tall:
        blocks.append((o, min(512, tall - o)))
        o += 512

    def a1(o):
        return o - nctx if o >= nctx else nlat + o

    for gi in range(16):
        P.dma("sp", uT[:], fm.t[R_SU + gi * 16:R_SU + (gi + 1) * 16, :], reads=[fm], writes=[uT])
        P.dma("sp", zT[:], fm.t[R_SZ + gi * 16:R_SZ + (gi + 1) * 16, :], reads=[fm], writes=[zT])
        for d in range(2):
            pi = d * 16 + gi
            for m in range(nsteps):
                P.op("dve", lambda e, d=d, m=m, pi=pi: e.tensor_scalar(Lm[d][:, m, :], ident, PR[:, m, pi:pi + 1], None, ALU.mult),
                     reads=[PR, c], writes=[Lm[d]])
                P.op("dve", lambda e, d=d, m=m, pi=pi: e.scalar_tensor_tensor(Lm[d][:, m, :], swap, PIs[:, m, pi:pi + 1], Lm[d][:, m, :], ALU.mult, ALU.add),
                     reads=[PIs, c, Lm[d]], writes=[Lm[d]])
            for (o, n) in blocks:
                pb = nb()
                P.op("pe", lambda e, pb=pb, pi=pi, o=o, n=n: e.matmul(pb[:, 0:n], bT[:, pi, :], uT[:, o:o + n], start=True, stop=True),
                     reads=[bT, uT], writes=[pb])
                od = o if d == 0 else a1(o)
                P.op("act", lambda e, pb=pb, d=d, od=od, n=n: e.activation(Xh[d][0][:, od:od + n], pb[:, 0:n], AF.Copy), reads=[pb], writes=[Xh[d][0]])
        cur = 0
        ev = 0
        for m in range(nsteps):
            s = 1 << m
            for d in range(2):
                Xa, Xb = Xh[d][cur], Xh[d][1 - cur]
                w = tall - s
                o = 0
                while o < w:
                    n = min(512, w - o)
                    pb = nb()
                    src = o if d == 0 else o + s
                    dst = o + s if d == 0 else o
                    P.op("pe", lambda e, pb=pb, Xa=Xa, dst=dst, n=n: e.matmul(pb[:, 0:n], identb[:], Xa[:, dst:dst + n], start=True, stop=False),
                         reads=[identb, Xa], writes=[pb], signal=False)
                    P.op("pe", lambda e, pb=pb, d=d, m=m, Xa=Xa, src=src, n=n: e.matmul(pb[:, 0:n], Lm[d][:, m, :], Xa[:, src:src + n], start=False, stop=True),
                         reads=[Lm[d], Xa], writes=[pb])
                    if ev % 2 == 0:
                        P.op("act", lambda e, pb=pb, Xb=Xb, dst=dst, n=n: e.activation(Xb[:, dst:dst + n], pb[:, 0:n], AF.Copy), reads=[pb], writes=[Xb])
                    else:
                        P.op("dve", lambda e, pb=pb, Xb=Xb, dst=dst, n=n: e.tensor_copy(Xb[:, dst:dst + n], pb[:, 0:n]), reads=[pb], writes=[Xb])
                    ev += 1
                    o += n
                c0 = 0 if d == 0 else w
                P.op("pool", lambda e, Xa=Xa, Xb=Xb, c0=c0, s=s: e.tensor_copy(Xb[:, c0:c0 + s], Xa[:, c0:c0 + s]), reads=[Xa], writes=[Xb])
            cur = 1 - cur
        for bi, (o, n) in enumerate(blocks):
            pb = nb()
            P.op("pe", lambda e, pb=pb, gi=gi, o=o, n=n, cur=cur: e.matmul(pb[0:16, 0:n], Cwb[:, gi, :], Xh[0][cur][:, o:o + n], start=True, stop=False),
                 reads=[Cwb, Xh[0][cur]], writes=[pb], signal=False)
            oa = a1(o)
            P.op("pe", lambda e, pb=pb, gi=gi, oa=oa, n=n, cur=cur: e.matmul(pb[0:16, 0:n], Cwb[:, 16 + gi, :], Xh[1][cur][:, oa:oa + n], start=False, stop=True),
                 reads=[Cwb, Xh[1][cur]], writes=[pb])
            y, g, gs = ys[bi % 2], gb[bi % 2], gsb[bi % 2]
            P.op("dve", lambda e, pb=pb, y=y, gi=gi, o=o, n=n: e.scalar_tensor_tensor(y[:, 0:n], uT[:, o:o + n], dsk[:, gi:gi + 1], pb[0:16, 0:n], ALU.mult, ALU.add),
                 reads=[uT, dsk, pb], writes=[y])
            P.op("act", lambda e, y=y, g=g, n=n: e.activation(g[:, 0:n], y[:, 0:n], AF.Gelu_apprx_tanh), reads=[y], writes=[g])
            P.op("act", lambda e, o=o, n=n: e.activation(zT[:, o:o + n], zT[:, o:o + n], AF.Silu), reads=[zT], writes=[zT])
            P.op("dve", lambda e, g=g, gs=gs, o=o, n=n: e.tensor_tensor(gs[:, 0:n], g[:, 0:n], zT[:, o:o + n], ALU.mult), reads=[g, zT], writes=[gs])
            wr_rows(P, gT, gi * 16, 16, o, n, g, lambda so, pn, g=g: g[:, so:so + pn])
            wr_rows(P, gsT, gi * 16, 16, o, n, gs, lambda so, pn, gs=gs: gs[:, so:so + pn])


def s5_consts():
    c = np.zeros((128, 260), np.float32)
    c[:, 0:128] = np.eye(128)
    for p in range(128):
        c[p, 128 + (p + 64) % 128] = 1.0
    c[:64, 256] = 1.0
    c[64:, 256] = -1.0
    c[:64, 257] = 1.0
    c[64:, 258] = 1.0
    c[:, 259] = np.pi / 2
    return c


def s5_host_layout(lam_re, lam_im, log_dt, b_re, b_im, c_re, c_im, d_skip, j):
    gs = slice(j * 16, (j + 1) * 16)
    def dup(x):
        return np.concatenate([x, x], 0)
    lre = lam_re[:, gs].reshape(32, 64).T
    lim = lam_im[:, gs].reshape(32, 64).T
    ldt = np.broadcast_to(log_dt[:, gs].reshape(1, 32), (64, 32))
    prm = np.stack([lre, lim, ldt, np.zeros_like(lre)], -1)
    prm = np.ascontiguousarray(dup(prm)).astype(np.float32)
    bre = b_re[:, gs].reshape(32, 64, 16).transpose(1, 0, 2)
    bim = b_im[:, gs].reshape(32, 64, 16).transpose(1, 0, 2)
    bb = np.ascontiguousarray(dup(np.stack([bre, bim], 2))).astype(np.float32)
    cre = c_re[:, gs].reshape(32, 16, 64).transpose(2, 0, 1)
    cim = c_im[:, gs].reshape(32, 16, 64).transpose(2, 0, 1)
    cc = np.ascontiguousarray(np.concatenate([cre, cim], 0)).astype(np.float32)
    dd = np.ascontiguousarray(d_skip.reshape(64, 16)[gs].T).astype(np.float32)
    return prm, bb, cc, dd
    P.finish(outs)
    P.emit()
    return nc


def emit_gla(P, fm, tm, glw, glb, glg, glk, cmk, glT, tall=TALL, nctx=NCTX):
    nch = tall // 128
    ncc = nctx // 128
    kk = P.sbuf("gl_k", [128, 384], F32)
    P.dma("sp", kk[:], glk[:], reads=[glk], writes=[kk])
    cm = P.sbuf("gl_cm", [64, tall], F32)
    P.dma("sp", cm[:], cmk[:], reads=[cmk], writes=[cm])
    wa = P.sbuf("gl_wa", [16, 4, 64], F32)
    ba = P.sbuf("gl_ba", [64, 4], F32)
    go = P.sbuf("gl_go", [128, 1], F32)
    P.dma("sp", wa[:], glw[:], reads=[glw], writes=[wa])
    P.dma("sp", ba[:], glb[:], reads=[glb], writes=[ba])
    P.dma("sp", go[:], glg[:], reads=[glg], writes=[go])
    P.op("dve", lambda e: e.tensor_scalar(ba[:], ba[:], -1.0, None, ALU.mult), reads=[ba], writes=[ba])
    one1 = P.sbuf("gl_one", [128, 1], F32)
    epsb = P.sbuf("gl_eps", [128, 1], F32)
    ones_f = P.sbuf("gl_1f", [128, 128], F32)
    P.op("dve", lambda e: e.memset(one1[:], 1.0), writes=[one1])
    P.op("dve", lambda e: e.memset(epsb[:], EPS), writes=[epsb])
    P.op("dve", lambda e: e.memset(ones_f[:], 1.0 / 128), writes=[ones_f])
    qh = P.sbuf("gl_q", [64, tall], F32)
    kh = P.sbuf("gl_kh", [64, tall], F32)
    la = P.sbuf("gl_la", [64, tall], F32)
    cs = P.sbuf("gl_cs", [64, tall], F32)
    cB = P.sbuf("gl_cB", [64, tall], F32)
    ex = P.sbuf("gl_ex", [64, tall], F32)
    qe = P.sbuf("gl_qe", [64, tall], BF16)
    ke = P.sbuf("gl_ke", [64, tall], BF16)
    tot = P.sbuf("gl_tot", [64, nch], F32)
    Et = P.sbuf("gl_Et", [64, nch], F32)
    Vt = P.sbuf("gl_V", [128, nch, 128], BF16)
    oacc = P.sbuf("gl_o", [128, tall], F32)
    S = P.sbuf("gl_S", [64, 128], F32)
    Sb = P.sbuf("gl_Sb", [64, 128], BF16)
    scm = [P.sbuf(f"gl_sc{i}", [128, 128], BF16) for i in range(2)]
    kT = [P.sbuf(f"gl_kT{i}", [128, 64], BF16) for i in range(2)]
    sq = P.sbuf("gl_sq", [128, 512], F32)
    rs = P.sbuf("gl_rs", [128, 512], F32)
    gz = P.sbuf("gl_gz", [128, 512], F32)
    yo = P.sbuf("gl_yo", [128, 512], F32)
    ob = [P.sbuf(f"gl_ob{i}", [128, 512], BF16) for i in range(2)]
    bk = [0]

    def nb():
        bk[0] += 1
        return P.bank(bk[0] % 8)

    for hh in range(2):
        P.dma("sp", qh[:], fm.t[R_GQ + hh * 64:R_GQ + (hh + 1) * 64, :], reads=[fm], writes=[qh])
        P.dma("sp", kh[:], fm.t[R_GK + hh * 64:R_GK + (hh + 1) * 64, :], reads=[fm], writes=[kh])
        P.dma("pool", Vt[:], tm.t[:, C_GV + hh * 128:C_GV + (hh + 1) * 128].rearrange("(t p) c -> p t c", p=128), reads=[tm], writes=[Vt])
        for d in range(2):
            pr = d * 2 + hh
            P.dma("sp", ex[0:16, :], fm.t[R_LR + d * 16:R_LR + (d + 1) * 16, :], reads=[fm], writes=[ex])
            o = 0
            while o < tall:
                n = min(512, tall - o)
                pb = nb()
                P.op("pe", lambda e, pb=pb, pr=pr, o=o, n=n: e.matmul(pb[0:64, 0:n], wa[:, pr, :], ex[0:16, o:o + n], start=True, stop=True),
                     reads=[wa, ex], writes=[pb])
                P.op("act", lambda e, pb=pb, pr=pr, o=o, n=n: e.activation(la[:, o:o + n], pb[0:64, 0:n], AF.Exp, bias=ba[:, pr:pr + 1], scale=-1.0),
                     reads=[pb, ba], writes=[la])
                o += n
            P.op("act", lambda e: e.activation(la[:], la[:], AF.Ln, bias=one1[0:64, :], scale=1.0), reads=[la, one1], writes=[la])
            P.op("dve", lambda e: e.tensor_tensor_scan(cs[:], cm[:], la[:], 0.0, ALU.mult, ALU.add), reads=[cm, la], writes=[cs])
            P.op("dve", lambda e: e.tensor_copy(tot[:], cs[:].rearrange("p (c t) -> p c t", t=128)[:, :, 127]), reads=[cs], writes=[tot])
            if d == 0:
                P.op("dve", lambda e: e.tensor_copy(cB[:], cs[:]), reads=[cs], writes=[cB])
            else:
                for c in range(nch):
                    P.op("dve", lambda e, c=c: e.tensor_scalar(cB[:, c * 128:(c + 1) * 128], cs[:, c * 128:(c + 1) * 128], -1.0, tot[:, c:c + 1], ALU.mult, ALU.add),
                         reads=[cs, tot], writes=[cB])
                P.op("dve", lambda e: e.tensor_tensor(cB[:], cB[:], la[:], ALU.add), reads=[cB, la], writes=[cB])
            P.op("act", lambda e: e.activation(ex[:], cB[:], AF.Exp, scale=-1.0 / 16), reads=[cB], writes=[ex])
            P.op("dve", lambda e: e.scalar_tensor_tensor(qe[:], qh[:], 0.125, ex[:], ALU.mult, ALU.mult), reads=[qh, ex], writes=[qe])
            P.op("act", lambda e: e.activation(ex[:], cB[:], AF.Exp, scale=1.0 / 16), reads=[cB], writes=[ex])
            P.op("dve", lambda e: e.tensor_tensor(ke[:], kh[:], ex[:], ALU.mult), reads=[kh, ex], writes=[ke])
            for c in range(nch):
                P.op("dve", lambda e, c=c: e.tensor_scalar(cs[:, c * 128:(c + 1) * 128], cB[:, c * 128:(c + 1) * 128], -1.0, tot[:, c:c + 1], ALU.mult, ALU.add),
                     reads=[cB, tot], writes=[cs])
            P.op("act", lambda e: e.activation(ex[:], cs[:], AF.Exp, scale=-1.0 / 16), reads=[cs], writes=[ex])
            P.op("dve", lambda e: e.tensor_tensor(la[:], kh[:], ex[:], ALU.mult), reads=[kh, ex], writes=[la])
            P.op("act", lambda e: e.activation(Et[:], tot[:], AF.Exp, scale=-1.0 / 16), reads=[tot], writes=[Et])
            P.op("dve", lambda e: e.memset(S[:], 0.0), writes=[S])
            P.op("dve", lambda e: e.memset(Sb[:], 0.0), writes=[Sb])
            if d == 0:
                order = list(range(nch))
            else:
                order = list(range(ncc - 1, -1, -1)) + list(range(nch - 1, ncc - 1, -1))
            mk = kk[:, 0:128] if d == 0 else kk[:, 128:256]
            for i, c in enumerate(order):
                sl = slice(c * 128, (c + 1) * 128)
                p1, p2, p3, p4 = nb(), nb(), nb(), nb()
                sc = scm[i % 2]
                kt = kT[i % 2]
                P.op("pe", lambda e, p1=p1, sl=sl: e.matmul(p1[:, 0:128], ke[:, sl], qe[:, sl], start=True, stop=True), reads=[ke, qe], writes=[p1])
                P.op("dve", lambda e, p1=p1, sc=sc, mk=mk: e.tensor_tensor(sc[:], p1[:, 0:128], mk, ALU.mult), reads=[p1, kk], writes=[sc])
                P.op("pe", lambda e, p2=p2, sc=sc, c=c: e.matmul(p2[:, 0:128], Vt[:, c, :], sc[:], start=True, stop=False), reads=[Vt, sc], writes=[p2], signal=False)
                P.op("pe", lambda e, p2=p2, sl=sl: e.matmul(p2[:, 0:128], Sb[:], qe[:, sl], start=False, stop=True), reads=[Sb, qe], writes=[p2])
                if d == 0:
                    P.op("act", lambda e, p2=p2, sl=sl: e.activation(oacc[:, sl], p2[:, 0:128], AF.Copy), reads=[p2], writes=[oacc])
                else:
                    P.op("dve", lambda e, p2=p2, sl=sl: e.tensor_tensor(oacc[:, sl], oacc[:, sl], p2[:, 0:128], ALU.add), reads=[p2, oacc], writes=[oacc])
                P.op("pe", lambda e, p3=p3, sl=sl: e.transpose(p3[:, 0:64], la[:, sl], kk[0:64, 256:320]), reads=[la, kk], writes=[p3])
                P.op("act", lambda e, p3=p3, kt=kt: e.activation(kt[:], p3[:, 0:64], AF.Copy), reads=[p3], writes=[kt])
                P.op("pe", lambda e, p4=p4, kt=kt, c=c: e.matmul(p4[0:64, 0:128], kt[:], Vt[:, c, :], start=True, stop=True), reads=[kt, Vt], writes=[p4])
                P.op("dve", lambda e, p4=p4, c=c: e.scalar_tensor_tensor(S[:], S[:], Et[:, c:c + 1], p4[0:64, 0:128], ALU.mult, ALU.add), reads=[S, Et, p4], writes=[S])
                P.op("act", lambda e: e.activation(Sb[:], S[:], AF.Copy), reads=[S], writes=[Sb])
        o = 0
        bi = 0
        while o < tall:
            n = min(512, tall - o)
            pb = nb()
            P.op("act", lambda e, o=o, n=n: e.activation(sq[:, 0:n], oacc[:, o:o + n], AF.Square), reads=[oacc], writes=[sq])
            P.op("pe", lambda e, pb=pb, n=n: e.matmul(pb[:, 0:n], ones_f[:], sq[:, 0:n], start=True, stop=True), reads=[ones_f, sq], writes=[pb])
            P.op("act", lambda e, pb=pb, n=n: e.activation(rs[:, 0:n], pb[:, 0:n], AF.Sqrt, bias=epsb[:], scale=1.0), reads=[pb, epsb], writes=[rs])
            P.op("dve", lambda e, n=n: e.reciprocal(rs[:, 0:n], rs[:, 0:n]), reads=[rs], writes=[rs])
            P.op("dve", lambda e, o=o, n=n: e.scalar_tensor_tensor(yo[:, 0:n], oacc[:, o:o + n], go[:, 0:1], rs[:, 0:n], ALU.mult, ALU.mult), reads=[oacc, go, rs], writes=[yo])
            P.dma("sp", gz[:, 0:n], fm.t[R_GZ + hh * 128:R_GZ + (hh + 1) * 128, o:o + n], reads=[fm], writes=[gz])
            P.op("act", lambda e, n=n: e.activation(gz[:, 0:n], gz[:, 0:n], AF.Silu), reads=[gz], writes=[gz])
            ot = ob[bi % 2]
            bi += 1
            P.op("pool", lambda e, ot=ot, n=n: e.tensor_tensor(ot[:, 0:n], yo[:, 0:n], gz[:, 0:n], ALU.mult), reads=[yo, gz], writes=[ot])
            wr_rows(P, glT, hh * 128, 128, o, n, ot, lambda so, pn, ot=ot: ot[:, so:so + pn])
            o += n


def gla_consts(tall):
    k = np.zeros((128, 384), np.float32)
    s = np.arange(128)[:, None]
    t = np.arange(128)[None, :]
    k[:, 0:128] = (s <= t)
    k[:, 128:256] = (s >= t)
    k[:, 256:384] = np.eye(128)
    cm = np.ones((64, tall), np.float32)
    cm[:, ::128] = 0.0
    return k, cm


def build_C(nlat=1024, nctx=64):
    nc = _nc()
    P = Prog(nc)
    NT = nlat + nctx
    gT = P.dram("gT", [1024, NT], BF16, "ExternalInput")
    gsT = P.dram("gsT", [1024, NT], BF16, "ExternalInput")
    glT = P.dram("glT", [1024, NT], BF16, "ExternalInput")
    atT = P.dram("atT", [2048, NT], BF16, "ExternalInput")
    hT = P.dram("hT", [NDC, 128, NT], BF16, "ExternalInput")
    xT = P.dram("xT", [NDC, 128, NT], F32, "ExternalInput")
    wglu = P.dram("wglu", [1024, 1024], F32, "ExternalInput")
    wps = P.dram("wps", [1024, D], F32, "ExternalInput")
    wpg = P.dram("wpg", [1024, D], F32, "ExternalInput")
    wpa = P.dram("wpa", [2048, D], F32, "ExternalInput")
    wmg = P.dram("wmg", [D, 3 * D], F32, "ExternalInput")
    wo = P.dram("wo", [D, D], F32, "ExternalInput")
    gsel = P.dram("gsel", [128, NDC, 2], F32, "ExternalInput")
    xo = P.dram("xo", [NDC, 128, NT], F32, "ExternalOutput")
    gates = P.dram("gates", [96, 128, NT], BF16, "Internal")
    blocks = []
    o = 0
    while o < nlat:
        n = min(512, nlat - o)
        blocks.append((o, n, 0))
        o += n
    if nctx:
        blocks.append((nlat, nctx, 1))
    bk = [0]

    def nb():
        bk[0] += 1
        return P.bank(bk[0] % 8)

    mT = P.sbuf("c_m", [128, NDC, NT], BF16)
    P.open_scope()
    hs = P.sbuf("c_h", [128, NDC, NT], BF16)
    P.dma("sp", hs[:], hT.t.rearrange("c p n -> p c n"), reads=[hT], writes=[hs])
    Wg = [P.sbuf(f"c_wg{i}", [128, NDC, 128], BF16) for i in range(3)]
    gst = [P.sbuf(f"c_gst{i}", [128, NT], BF16) for i in range(2)]
    for mc in range(96):
        W = Wg[mc % 3]
        P.dma("pool", W[:], wmg.t[:, mc * 128:(mc + 1) * 128].rearrange("(k p) n -> p k n", p=128), reads=[wmg], writes=[W])
        st = gst[mc % 2]
        for (o, n, t) in blocks:
            pb = nb()
            for kc in range(NDC):
                P.op("pe", lambda e, pb=pb, W=W, kc=kc, o=o, n=n: e.matmul(pb[:, 0:n], W[:, kc, :], hs[:, kc, o:o + n], start=(kc == 0), stop=(kc == NDC - 1)),
                     reads=[W, hs], writes=[pb], signal=(kc == NDC - 1))
            P.op("act", lambda e, pb=pb, st=st, o=o, n=n: e.activation(st[:, o:o + n], pb[:, 0:n], AF.Sigmoid), reads=[pb], writes=[st])
        P.dma("sp", gates.t[mc], st[:], reads=[st], writes=[gates])
    P.close_scope()
    P.open_scope()
    gs_ = P.sbuf("c_g", [128, 8, NT], BF16)
    ss_ = P.sbuf("c_s", [128, 8, NT], BF16)
    gl_ = P.sbuf("c_gl", [128, 8, NT], BF16)
    at_ = P.sbuf("c_at", [128, 16, NT], BF16)
    P.dma("sp", gs_[:], gT.t.rearrange("(c p) n -> p c n", p=128), reads=[gT], writes=[gs_])
    P.dma("sp", ss_[:], gsT.t.rearrange("(c p) n -> p c n", p=128), reads=[gsT], writes=[ss_])
    P.dma("sp", gl_[:], glT.t.rearrange("(c p) n -> p c n", p=128), reads=[glT], writes=[gl_])
    P.dma("sp", at_[:], atT.t.rearrange("(c p) n -> p c n", p=128), reads=[atT], writes=[at_])
    wgl = P.sbuf("c_wglu", [128, 8, 1024], BF16)
    P.dma("pool", wgl[:], wglu.t.rearrange("(k p) n -> p k n", p=128), reads=[wglu], writes=[wgl])
    sg = [P.sbuf(f"c_sg{i}", [128, 512], BF16) for i in range(2)]
    i = 0
    for oc in range(8):
        for (o, n, t) in blocks:
            pb = nb()
            for kc in range(8):
                P.op("pe", lambda e, pb=pb, kc=kc, oc=oc, o=o, n=n: e.matmul(pb[:, 0:n], wgl[:, kc, oc * 128:(oc + 1) * 128], gs_[:, kc, o:o + n], start=(kc == 0), stop=(kc == 7)),
                     reads=[wgl, gs_], writes=[pb], signal=(kc == 7))
            s_ = sg[i % 2]
            i += 1
            P.op("act", lambda e, pb=pb, s_=s_, n=n: e.activation(s_[:, 0:n], pb[:, 0:n], AF.Sigmoid), reads=[pb], writes=[s_])
            P.op("dve", lambda e, s_=s_, oc=oc, o=o, n=n: e.tensor_tensor(ss_[:, oc, o:o + n], ss_[:, oc, o:o + n], s_[:, 0:n], ALU.mult), reads=[ss_, s_], writes=[ss_])
    w1 = [P.sbuf(f"c_w1{i}", [128, 8, 128], BF16) for i in range(2)]
    w2 = [P.sbuf(f"c_w2{i}", [128, 8, 128], BF16) for i in range(2)]
    w3 = [P.sbuf(f"c_w3{i}", [128, 16, 128], BF16) for i in range(2)]
    g3 = [P.sbuf(f"c_g3{i}", [128, 3, NT], BF16) for i in range(2)]
    m1 = P.sbuf("c_m1", [128, 512], F32)
    m2 = P.sbuf("c_m2", [128, 512], F32)
    m3 = P.sbuf("c_m3", [128, 512], F32)
    for dc in range(NDC):
        a, b_, c_, g_ = w1[dc % 2], w2[dc % 2], w3[dc % 2], g3[dc % 2]
        cs_ = slice(dc * 128, (dc + 1) * 128)
        P.dma("pool", a[:], wps.t[:, cs_].rearrange("(k p) n -> p k n", p=128), reads=[wps], writes=[a])
        P.dma("pool", b_[:], wpg.t[:, cs_].rearrange("(k p) n -> p k n", p=128), reads=[wpg], writes=[b_])
        P.dma("pool", c_[:], wpa.t[:, cs_].rearrange("(k p) n -> p k n", p=128), reads=[wpa], writes=[c_])
        for br in range(3):
            P.dma("sp", g_[:, br, :], gates.t[br * 32 + dc], reads=[gates], writes=[g_])
        for (o, n, t) in blocks:
            p1, p2, p3 = nb(), nb(), nb()
            for kc in range(8):
                P.op("pe", lambda e, p1=p1, a=a, kc=kc, o=o, n=n: e.matmul(p1[:, 0:n], a[:, kc, :], ss_[:, kc, o:o + n], start=(kc == 0), stop=(kc == 7)),
                     reads=[a, ss_], writes=[p1], signal=(kc == 7))
            for kc in range(8):
                P.op("pe", lambda e, p2=p2, b_=b_, kc=kc, o=o, n=n: e.matmul(p2[:, 0:n], b_[:, kc, :], gl_[:, kc, o:o + n], start=(kc == 0), stop=(kc == 7)),
                     reads=[b_, gl_], writes=[p2], signal=(kc == 7))
            for kc in range(16):
                P.op("pe", lambda e, p3=p3, c_=c_, kc=kc, o=o, n=n: e.matmul(p3[:, 0:n], c_[:, kc, :], at_[:, kc, o:o + n], start=(kc == 0), stop=(kc == 15)),
                     reads=[c_, at_], writes=[p3], signal=(kc == 15))
            P.op("dve", lambda e, p1=p1, g_=g_, o=o, n=n: e.tensor_tensor(m1[:, 0:n], p1[:, 0:n], g_[:, 0, o:o + n], ALU.mult), reads=[p1, g_], writes=[m1])
            P.op("dve", lambda e, p2=p2, g_=g_, o=o, n=n: e.tensor_tensor(m2[:, 0:n], p2[:, 0:n], g_[:, 1, o:o + n], ALU.mult), reads=[p2, g_], writes=[m2])
            P.op("dve", lambda e, p3=p3, g_=g_, o=o, n=n: e.tensor_tensor(m3[:, 0:n], p3[:, 0:n], g_[:, 2, o:o + n], ALU.mult), reads=[p3, g_], writes=[m3])
            P.op("pool", lambda e, n=n: e.tensor_tensor(m1[:, 0:n], m1[:, 0:n], m2[:, 0:n], ALU.add), reads=[m1, m2], writes=[m1])
            P.op("pool", lambda e, dc=dc, o=o, n=n: e.tensor_tensor(mT[:, dc, o:o + n], m1[:, 0:n], m3[:, 0:n], ALU.add), reads=[m1, m3], writes=[mT])
    P.close_scope()
    P.open_scope()
    gv = P.sbuf("c_gv", [128, NDC, 2], F32)
    P.dma("sp", gv[:], gsel[:], reads=[gsel], writes=[gv])
    wo_ = [P.sbuf(f"c_wo{i}", [128, NDC, 128], BF16) for i in range(2)]
    xin = [P.sbuf(f"c_xi{i}", [128, NT], F32) for i in range(2)]
    xot = [P.sbuf(f"c_xo{i}", [128, NT], F32) for i in range(2)]
    for dc in range(NDC):
        W = wo_[dc % 2]
        xi, xn = xin[dc % 2], xot[dc % 2]
        P.dma("pool", W[:], wo.t[:, dc * 128:(dc + 1) * 128].rearrange("(k p) n -> p k n", p=128), reads=[wo], writes=[W])
        P.dma("sp", xi[:], xT.t[dc], reads=[xT], writes=[xi])
        for (o, n, t) in blocks:
            pb = nb()
            for kc in range(NDC):
                P.op("pe", lambda e, pb=pb, W=W, kc=kc, o=o, n=n: e.matmul(pb[:, 0:n], W[:, kc, :], mT[:, kc, o:o + n], start=(kc == 0), stop=(kc == NDC - 1)),
                     reads=[W, mT], writes=[pb], signal=(kc == NDC - 1))
            P.op("dve", lambda e, pb=pb, xi=xi, xn=xn, dc=dc, o=o, n=n, t=t: e.scalar_tensor_tensor(
                xn[:, o:o + n], pb[:, 0:n], gv[:, dc, t:t + 1], xi[:, o:o + n], ALU.mult, ALU.add), reads=[pb, gv, xi], writes=[xn])
        P.dma("sp", xo.t[dc], xn[:], reads=[xn], writes=[xo])
    P.close_scope()
    P.finish([xo])
    P.emit()
    return nc


_OFF = dict(su=0, sz=1024, gq=2048, gk=2560, gv=3072, gz=4096, glr=5120, aq=5152, ak=7200, av=7712, az=8224, mg=10272)


def _cols(j):
    r = lambda k, w: list(range(_OFF[k] + j * w, _OFF[k] + (j + 1) * w))
    return (r("gv", 256) + r("av", 128) + r("gq", 128) + r("gk", 128) + r("gz", 256) + r("aq", 512) + r("ak", 128)
            + r("az", 512) + r("su", 256) + r("sz", 256) + list(range(_OFF["glr"], _OFF["glr"] + 32)))


def _fm(a):
    return np.ascontiguousarray(a.T.reshape(NDC, 128, a.shape[0]))


def _run(nc, ins):
    return run_bass_kernel_spmd(nc, ins, core_ids=list(range(NCORES))).results


def kernel(x, c, ctx, c_ctx, norm_g, w_mod, b_mod, w_in, ssm_lam_re, ssm_lam_im, ssm_log_dt, ssm_b_re, ssm_b_im,
           ssm_c_re, ssm_c_im, ssm_d, ssm_w_glu, gla_w_a, gla_b_a, gla_norm_g, attn_q_g, attn_k_g,
           w_proj_ssm, w_proj_gla, w_proj_attn, w_out, final_g):
    f = lambda a: np.asarray(a, dtype=np.float32)
    x, c, ctx, c_ctx = f(x), f(c), f(ctx), f(c_ctx)
    cs3 = np.stack([c[0], c[1], c_ctx], 0)
    cT = np.ascontiguousarray(cs3.reshape(3, 32, 128).transpose(2, 1, 0))
    w_mod, b_mod = f(w_mod), f(b_mod)
    ins = []
    for i in range(8):
        sl = slice(i * 1536, (i + 1) * 1536)
        ins.append({"cT": cT, "wm": np.ascontiguousarray(w_mod[:, :, sl]),
                    "bm": np.ascontiguousarray(b_mod[:, sl].reshape(2, 12, 128).transpose(2, 0, 1))})
    res = _run(build_M(), ins)
    modT = np.concatenate([r["modT"] for r in res], axis=2)
    cores = [(i // 4, i % 4) for i in range(8)]
    xT = []
    for (b, j) in cores:
        loc = np.concatenate([x[b, j * 1024:(j + 1) * 1024], ctx[b, j * 64:(j + 1) * 64]], 0)
        xT.append(_fm(loc))
    cosT, sinT, rm = rope_tables(NLAT)
    glk, cmk = gla_consts(TALL)
    s5k = s5_consts()
    ncA = build_A(1024, 64, True)
    ncB = build_B(TALL, NCTX, True, ("b1", "attn", "gla", "s5"))
    ncC = build_C(1024, 64)
    w_in = f(w_in)
    for l in range(2):
        ngl = np.ascontiguousarray(f(norm_g)[l].reshape(32, 128).T)
        ins = [{"xT": xT[i], "ng": ngl, "ms": np.ascontiguousarray(modT[:, l][:, :, [b, 2]])} for i, (b, j) in enumerate(cores)]
        resA = _run(ncA, ins)
        hT = [r["hT"] for r in resA]
        hTb = []
        for b in range(2):
            hTb.append(np.ascontiguousarray(np.concatenate(
                [hT[b * 4 + j][:, :, 1024:1088] for j in range(4)] + [hT[b * 4 + j][:, :, 0:1024] for j in range(4)], axis=2)))
        ins = []
        for i, (b, j) in enumerate(cores):
            prm, bb, cc, dd = s5_host_layout(f(ssm_lam_re)[l], f(ssm_lam_im)[l], f(ssm_log_dt)[l], f(ssm_b_re)[l], f(ssm_b_im)[l],
                                             f(ssm_c_re)[l], f(ssm_c_im)[l], f(ssm_d)[l], j)
            WA = f(gla_w_a)[l]
            BA = f(gla_b_a)[l]
            glw = np.ascontiguousarray(WA[:, :, 128 * j:128 * (j + 1)].reshape(2, 16, 2, 64).transpose(1, 0, 2, 3).reshape(16, 4, 64))
            glb = np.ascontiguousarray(BA[:, 128 * j:128 * (j + 1)].reshape(4, 64).T)
            ins.append({"hTb": hTb[b], "wq": np.ascontiguousarray(w_in[l][:, _cols(j)]), "cosT": cosT, "sinT": sinT, "rmT": rm,
                        "gqk": np.ascontiguousarray(np.stack([f(attn_q_g)[l], f(attn_k_g)[l]], 1)),
                        "s5p": prm, "s5b": bb, "s5c": cc, "s5d": dd, "s5k": s5k,
                        "glw": glw, "glb": glb, "glg": np.ascontiguousarray(f(gla_norm_g)[l].reshape(128, 1)), "glk": glk, "cmk": cmk})
        resB = _run(ncB, ins)
        wmg = np.ascontiguousarray(w_in[l][:, _OFF["mg"]:])
        ins = []
        for i, (b, j) in enumerate(cores):
            tok = list(range(256 + j * 1024, 256 + (j + 1) * 1024)) + list(range(j * 64, (j + 1) * 64))
            cat = lambda k: np.ascontiguousarray(np.concatenate([resB[b * 4 + jj][k] for jj in range(4)], 0)[:, tok])
            ins.append({"gT": cat("gT"), "gsT": cat("gsT"), "glT": cat("glT"), "atT": cat("atT"), "hT": hT[i], "xT": xT[i],
                        "wglu": f(ssm_w_glu)[l], "wps": f(w_proj_ssm)[l], "wpg": f(w_proj_gla)[l], "wpa": f(w_proj_attn)[l],
                        "wmg": wmg, "wo": f(w_out)[l],
                        "gsel": np.ascontiguousarray(modT[:, l, 64:96][:, :, [b, 2]])})
        resC = _run(ncC, ins)
        xT = [r["xo"] for r in resC]
    ncF = build_A(1024, 64, False)
    fgl = np.ascontiguousarray(f(final_g).reshape(32, 128).T)
    zero = np.zeros((128, 96, 2), np.float32)
    resF = _run(ncF, [{"xT": xT[i], "ng": fgl, "ms": zero} for i in range(8)])
    out = np.zeros((2, 4096, 4096), np.float32)
    for i, (b, j) in enumerate(cores):
        o = resF[i]["hT"].reshape(4096, 1088)[:, 0:1024]
        out[b, j * 1024:(j + 1) * 1024] = o.T
    return out


GROUPS = [[0, 1, 2, 3], [4, 5, 6, 7]]
NK = 8


def _blocks(nctx, tall, step=512):
    bl = [(0, nctx, 1)]
    o = nctx
    while o < tall:
        n = min(step, tall - o)
        bl.append((o, n, 0))
        o += n
    return bl


CW = 256


class TT:
    def __init__(self, P, name, rows, tall, dtype, r0=0, bufs=None):
        self.rows, self.tall, self.r0 = rows, tall, r0
        self.bufs = bufs if bufs is not None else [P.dram(f"{name}_{i}", [rows, CW], dtype) for i in range(tall // CW)]

    def sub(self, r0):
        return TT(None, None, self.rows, self.tall, None, self.r0 + r0, self.bufs)

    def pieces(self, o, n):
        out = []
        so = 0
        while n > 0:
            ci, lo = o // CW, o % CW
            pn = min(n, CW - lo)
            out.append((self.bufs[ci], lo, pn, so))
            o += pn
            n -= pn
            so += pn
        return out


def wr_rows(P, dst, r0, nr, o, n, srcbuf, src_fn, eng="sp"):
    if isinstance(dst, TT):
        for (b, lo, pn, so) in dst.pieces(o, n):
            P.dma(eng, b.t[dst.r0 + r0:dst.r0 + r0 + nr, lo:lo + pn], src_fn(so, pn), reads=[srcbuf], writes=[b])
    else:
        P.dma(eng, dst.t[r0:r0 + nr, o:o + n], src_fn(0, n), reads=[srcbuf], writes=[dst])


def wr_cpn(P, dst, c0, ncn, o, n, srcbuf, src_fn, eng="sp"):
    if isinstance(dst, TT):
        for (b, lo, pn, so) in dst.pieces(o, n):
            P.dma(eng, b.t[dst.r0 + c0 * 128:dst.r0 + (c0 + ncn) * 128, lo:lo + pn].rearrange("(c p) n -> p c n", p=128), src_fn(so, pn),
                  reads=[srcbuf], writes=[b])
    else:
        P.dma(eng, dst.t[c0 * 128:(c0 + ncn) * 128, o:o + n].rearrange("(c p) n -> p c n", p=128), src_fn(0, n), reads=[srcbuf], writes=[dst])


def rd_cpn(P, src, o, n, dstbuf, dst_fn, eng="sp", c0=0, ncn=None):
    if isinstance(src, TT):
        ncn_ = ncn if ncn is not None else src.rows // 128
        for (b, lo, pn, so) in src.pieces(o, n):
            P.dma(eng, dst_fn(so, pn), b.t[src.r0 + c0 * 128:src.r0 + (c0 + ncn_) * 128, lo:lo + pn].rearrange("(c p) n -> p c n", p=128),
                  reads=[b], writes=[dstbuf])
    else:
        P.dma(eng, dst_fn(0, n), src.t[:, :, o:o + n].rearrange("c p n -> p c n"), reads=[src], writes=[dstbuf])


def emit_M2(P, cT, wm, bm, modS):
    sc = P.sbuf("m_sc", [128, 32, 2], F32)
    bs = P.sbuf("m_bs", [128, 2, 24], F32)
    wt = [P.sbuf(f"m_wt{i}", [128, 4, 3072], F32) for i in range(2)]
    P.dma("sp", sc[:], cT[:], reads=[cT], writes=[sc])
    P.dma("sp", bs[:], bm[:], reads=[bm], writes=[bs])
    P.op("act", lambda e: e.activation(sc[:], sc[:], AF.Silu), reads=[sc], writes=[sc])
    it = 0
    for l in range(2):
        for g in range(8):
            w = wt[it % 2]
            pp = P.bank(it % 2)
            it += 1
            P.dma("sp", w[:], wm.t[l, g * 512:(g + 1) * 512, :].rearrange("(k p) n -> p k n", p=128), reads=[wm], writes=[w])
            for j in range(24):
                for k in range(4):
                    kc = g * 4 + k
                    P.op("pe", lambda e, pp=pp, w=w, j=j, k=k, kc=kc: e.matmul(
                        pp[:, j * 2:j * 2 + 2], w[:, k, j * 128:(j + 1) * 128], sc[:, kc, :], start=(k == 0), stop=(k == 3)),
                        reads=[w, sc], writes=[pp], signal=(k == 3 and j == 23))
            dst = modS[:, l].rearrange("p a b -> p (a b)")
            if g == 0:
                P.op("dve", lambda e, pp=pp, dst=dst: e.tensor_copy(dst, pp[:, 0:48]), reads=[pp], writes=[modS])
            else:
                P.op("dve", lambda e, pp=pp, dst=dst: e.tensor_tensor(dst, dst, pp[:, 0:48], ALU.add), reads=[pp, modS], writes=[modS])
    for l in range(2):
        for r in range(2):
            P.op("dve", lambda e, l=l, r=r: e.tensor_tensor(modS[:, l, :, r], modS[:, l, :, r], bs[:, l, :], ALU.add),
                 reads=[modS, bs], writes=[modS])


def emit_A2(P, xs, Av, Bv, arin, arout, dst, dst_dt, tall, nctx, lat_only=False, ag_out=None):
    onesm = P.sbuf("a_ones", [128, 128], F32)
    epsb = P.sbuf("a_eps", [128, 1], F32)
    ss = P.sbuf("a_ss", [128, tall], F32)
    xb = [P.sbuf(f"a_xb{i}", [128, NK, 512], F32) for i in range(2)]
    sq = [P.sbuf(f"a_sq{i}", [128, 512], F32) for i in range(2)]
    tmp = [P.sbuf(f"a_tmp{i}", [128, 512], F32) for i in range(2)]
    ho = [P.sbuf(f"a_ho{i}", [128, NK, 512], dst_dt) for i in range(2)]
    P.op("dve", lambda e: e.memset(onesm[:], 1.0 / D), writes=[onesm])
    P.op("dve", lambda e: e.memset(epsb[:], EPS), writes=[epsb])
    bl = _blocks(nctx, tall)
    for bi, (o, n, t) in enumerate(bl):
        x = xb[bi % 2]
        pb = P.bank(bi % 2)
        P.dma("sp", x[:, :, 0:n], xs.t[:, :, o:o + n].rearrange("c p n -> p c n"), reads=[xs], writes=[x])
        for k in range(NK):
            s = sq[k % 2]
            P.op("act", lambda e, s=s, x=x, k=k, n=n: e.activation(s[:, 0:n], x[:, k, 0:n], AF.Square), reads=[x], writes=[s])
            P.op("pe", lambda e, s=s, pb=pb, k=k, n=n: e.matmul(pb[:, 0:n], onesm[:], s[:, 0:n], start=(k == 0), stop=(k == NK - 1)),
                 reads=[onesm, s], writes=[pb])
        P.op("dve", lambda e, pb=pb, o=o, n=n: e.tensor_copy(ss[:, o:o + n], pb[:, 0:n]), reads=[pb], writes=[ss])
    qw = tall // 4
    for q in range(4):
        P.dma("sp", arin[q][:], ss[:, q * qw:(q + 1) * qw], reads=[ss], writes=[arin[q]])
        P.collective("AllReduce", ALU.add, GROUPS, arin[q][:], arout[q][:], reads=[arin[q]], writes=[arout[q]])
    for q in range(4):
        P.dma("sp", ss[:, q * qw:(q + 1) * qw], arout[q][:], reads=[arout[q]], writes=[ss])
    P.op("act", lambda e: e.activation(ss[:], ss[:], AF.Sqrt, bias=epsb[:], scale=1.0), reads=[ss, epsb], writes=[ss])
    P.op("dve", lambda e: e.reciprocal(ss[:], ss[:]), reads=[ss], writes=[ss])
    for bi, (o, n, t) in enumerate(bl):
        if lat_only and t == 1:
            continue
        x = xb[bi % 2]
        h = ho[bi % 2]
        P.dma("sp", x[:, :, 0:n], xs.t[:, :, o:o + n].rearrange("c p n -> p c n"), reads=[xs], writes=[x])
        for k in range(NK):
            tm_ = tmp[k % 2]
            P.op("dve", lambda e, tm_=tm_, x=x, k=k, n=n, t=t, o=o: e.scalar_tensor_tensor(
                tm_[:, 0:n], x[:, k, 0:n], Av[:, t, k:k + 1], ss[:, o:o + n], ALU.mult, ALU.mult), reads=[x, Av, ss], writes=[tm_])
            P.op("act", lambda e, tm_=tm_, h=h, k=k, n=n, t=t: e.activation(
                h[:, k, 0:n], tm_[:, 0:n], AF.Identity, bias=Bv[:, t, k:k + 1], scale=1.0), reads=[tm_, Bv], writes=[h])
        oo = o - nctx if lat_only else o
        wr_cpn(P, dst, 0, NK, oo, n, h, lambda so, pn, h=h: h[:, :, so:so + pn])
        if ag_out is not None:
            for ci in range(oo // CW, (oo + n) // CW):
                P.collective("AllGather", ALU.bypass, GROUPS, dst.bufs[ci][:], ag_out.bufs[ci][:], reads=[dst.bufs[ci]], writes=[ag_out.bufs[ci]])


def emit_C2(P, hTb, agbout, wglu, wmg, wps, wpg, wpa, wo, gate, xs_in, xs_out, agmin, agmout, tall, nctx):
    bk = [0]

    def nb():
        bk[0] += 1
        return P.bank(bk[0] % 8)

    ags_o, agl_o, aga_o = agbout

    def agv_rd(src, qn, r, o, n, dstbuf, dst_fn):
        for (b, lo, pn, so) in src.pieces(o, n):
            P.dma("sp", dst_fn(so, pn), b.t.rearrange("(r q p) n -> p r q n", r=4, q=qn, p=128)[:, r, :, lo:lo + pn], reads=[b], writes=[dstbuf])

    sTd = P.dram(f"sTd{P.n_ins}", [1024, tall], BF16)
    P.open_scope()
    wgl = P.sbuf("c_wglu", [128, 8, 1024], BF16)
    P.dma("pool", wgl[:], wglu.t.rearrange("(k p) n -> p k n", p=128), reads=[wglu], writes=[wgl])
    gb_ = [P.sbuf(f"c_gb{i}", [128, 4, 4, 512], BF16) for i in range(2)]
    so_ = [P.sbuf(f"c_so{i}", [128, 8, 512], BF16) for i in range(2)]
    sg = [P.sbuf(f"c_sg{i}", [128, 512], BF16) for i in range(2)]
    o = 0
    it = 0
    while o < tall:
        n = min(512, tall - o)
        a = gb_[it % 2]
        so = so_[it % 2]
        it += 1
        for r in range(4):
            agv_rd(ags_o, 4, r, o, n, a, lambda so, pn, a=a, r=r: a[:, r, :, so:so + pn])
        for oc in range(8):
            pb = nb()
            for kc in range(8):
                P.op("pe", lambda e, pb=pb, a=a, kc=kc, oc=oc, n=n: e.matmul(pb[:, 0:n], wgl[:, kc, oc * 128:(oc + 1) * 128], a[:, kc // 2, kc % 2, 0:n],
                                                                           start=(kc == 0), stop=(kc == 7)),
                     reads=[wgl, a], writes=[pb], signal=(kc == 7))
            s_ = sg[oc % 2]
            P.op("act", lambda e, pb=pb, s_=s_, n=n: e.activation(s_[:, 0:n], pb[:, 0:n], AF.Sigmoid), reads=[pb], writes=[s_])
            P.op("dve", lambda e, s_=s_, a=a, so=so, oc=oc, n=n: e.tensor_tensor(so[:, oc, 0:n], a[:, oc // 2, 2 + oc % 2, 0:n], s_[:, 0:n], ALU.mult),
                 reads=[a, s_], writes=[so])
        P.dma("sp", sTd.t[:, o:o + n].rearrange("(c p) n -> p c n", p=128), so[:, :, 0:n], reads=[so], writes=[sTd])
        o += n
    P.close_scope()
    P.open_scope()
    wmgS = P.sbuf("c_wmg", [128, NDC, 3, 384], BF16)
    wprS = P.sbuf("c_wpr", [128, NDC, 384], BF16)
    hb = [P.sbuf(f"c_hb{i}", [128, NDC, 256], BF16) for i in range(2)]
    sbk = [P.sbuf(f"c_sb{i}", [128, 8, 256], BF16) for i in range(2)]
    ab = [P.sbuf(f"c_ab{i}", [128, 4, 6, 256], BF16) for i in range(2)]
    gs3 = [P.sbuf(f"c_g3{i}", [128, 3, 256], F32) for i in range(2)]
    m1 = P.sbuf("c_m1", [128, 256], F32)
    m2 = P.sbuf("c_m2", [128, 256], F32)
    m3 = P.sbuf("c_m3", [128, 256], F32)
    mo = [P.sbuf(f"c_mo{i}", [128, 3, 256], BF16) for i in range(2)]
    it = 0
    for (k0, nk) in ((0, 3), (3, 3), (6, 2)):
        c0, c1 = k0 * 128, (k0 + nk) * 128
        for br in range(3):
            P.dma("pool", wmgS[:, :, br, 0:nk * 128], wmg.t[:, br * 1024 + c0:br * 1024 + c1].rearrange("(k p) n -> p k n", p=128),
                  reads=[wmg], writes=[wmgS])
        P.dma("pool", wprS[:, 0:8, 0:nk * 128], wps.t[:, c0:c1].rearrange("(k p) n -> p k n", p=128), reads=[wps], writes=[wprS])
        P.dma("pool", wprS[:, 8:16, 0:nk * 128], wpg.t[:, c0:c1].rearrange("(k p) n -> p k n", p=128), reads=[wpg], writes=[wprS])
        P.dma("pool", wprS[:, 16:32, 0:nk * 128], wpa.t[:, c0:c1].rearrange("(k p) n -> p k n", p=128), reads=[wpa], writes=[wprS])
        o = 0
        while o < tall:
            n = min(256, tall - o)
            h = hb[it % 2]
            a = ab[it % 2]
            sb = sbk[it % 2]
            mout = mo[it % 2]
            it += 1
            rd_cpn(P, hTb, o, n, h, lambda so, pn, h=h: h[:, :, so:so + pn])
            P.dma("sp", sb[:, :, 0:n], sTd.t[:, o:o + n].rearrange("(c p) n -> p c n", p=128), reads=[sTd], writes=[sb])
            for r in range(4):
                agv_rd(agl_o, 2, r, o, n, a, lambda so, pn, a=a, r=r: a[:, r, 0:2, so:so + pn])
                agv_rd(aga_o, 4, r, o, n, a, lambda so, pn, a=a, r=r: a[:, r, 2:6, so:so + pn])
            for kk in range(nk):
                g3 = gs3[kk % 2]
                cw = slice(kk * 128, (kk + 1) * 128)
                for br in range(3):
                    pg = nb()
                    for kc in range(NDC):
                        P.op("pe", lambda e, pg=pg, h=h, kc=kc, br=br, cw=cw, n=n: e.matmul(pg[:, 0:n], wmgS[:, kc, br, cw], h[:, kc, 0:n],
                                                                                          start=(kc == 0), stop=(kc == NDC - 1)),
                             reads=[wmgS, h], writes=[pg], signal=(kc == NDC - 1))
                    P.op("act", lambda e, pg=pg, g3=g3, br=br, n=n: e.activation(g3[:, br, 0:n], pg[:, 0:n], AF.Sigmoid), reads=[pg], writes=[g3])
                p1, p2, p3 = nb(), nb(), nb()
                for kc in range(8):
                    P.op("pe", lambda e, p1=p1, sb=sb, kc=kc, cw=cw, n=n: e.matmul(p1[:, 0:n], wprS[:, kc, cw], sb[:, kc, 0:n], start=(kc == 0), stop=(kc == 7)),
                         reads=[wprS, sb], writes=[p1], signal=(kc == 7))
                for kc in range(8):
                    P.op("pe", lambda e, p2=p2, a=a, kc=kc, cw=cw, n=n: e.matmul(p2[:, 0:n], wprS[:, 8 + kc, cw], a[:, kc // 2, kc % 2, 0:n], start=(kc == 0), stop=(kc == 7)),
                         reads=[wprS, a], writes=[p2], signal=(kc == 7))
                for kc in range(16):
                    P.op("pe", lambda e, p3=p3, a=a, kc=kc, cw=cw, n=n: e.matmul(p3[:, 0:n], wprS[:, 16 + kc, cw], a[:, kc // 4, 2 + kc % 4, 0:n], start=(kc == 0), stop=(kc == 15)),
                         reads=[wprS, a], writes=[p3], signal=(kc == 15))
                P.op("dve", lambda e, p1=p1, g3=g3, n=n: e.tensor_tensor(m1[:, 0:n], p1[:, 0:n], g3[:, 0, 0:n], ALU.mult), reads=[p1, g3], writes=[m1])
                P.op("dve", lambda e, p2=p2, g3=g3, n=n: e.tensor_tensor(m2[:, 0:n], p2[:, 0:n], g3[:, 1, 0:n], ALU.mult), reads=[p2, g3], writes=[m2])
                P.op("dve", lambda e, p3=p3, g3=g3, n=n: e.tensor_tensor(m3[:, 0:n], p3[:, 0:n], g3[:, 2, 0:n], ALU.mult), reads=[p3, g3], writes=[m3])
                P.op("pool", lambda e, n=n: e.tensor_tensor(m1[:, 0:n], m1[:, 0:n], m2[:, 0:n], ALU.add), reads=[m1, m2], writes=[m1])
                P.op("pool", lambda e, mout=mout, kk=kk, n=n: e.tensor_tensor(mout[:, kk, 0:n], m1[:, 0:n], m3[:, 0:n], ALU.add), reads=[m1, m3], writes=[mout])
            wr_cpn(P, agmin, k0, nk, o, n, mout, lambda so, pn, mout=mout, nk=nk: mout[:, 0:nk, so:so + pn])
            o += n
    P.close_scope()
    for ci in range(len(agmin.bufs)):
        P.collective("AllGather", ALU.bypass, GROUPS, agmin.bufs[ci][:], agmout.bufs[ci][:], reads=[agmin.bufs[ci]], writes=[agmout.bufs[ci]])
    P.open_scope()
    woS = P.sbuf("c_wo", [128, NDC, 1024], BF16)
    for hf in range(2):
        P.dma("pool", woS[:, hf * 16:(hf + 1) * 16, :], wo.t[hf * 2048:(hf + 1) * 2048, :].rearrange("(k p) n -> p k n", p=128), reads=[wo], writes=[woS])
    mb = [P.sbuf(f"c_mb{i}", [128, NDC, 512], BF16) for i in range(2)]
    xi = [P.sbuf(f"c_xi{i}", [128, NK, 512], F32) for i in range(2)]
    xo = [P.sbuf(f"c_xo{i}", [128, NK, 512], F32) for i in range(2)]
    for bi, (o, n, t) in enumerate(_blocks(nctx, tall)):
        m_, xin, xout = mb[bi % 2], xi[bi % 2], xo[bi % 2]
        rd_cpn(P, agmout, o, n, m_, lambda so, pn, m_=m_: m_[:, :, so:so + pn])
        P.dma("sp", xin[:, :, 0:n], xs_in.t[:, :, o:o + n].rearrange("c p n -> p c n"), reads=[xs_in], writes=[xin])
        for k in range(NK):
            pb = nb()
            for kc in range(NDC):
                P.op("pe", lambda e, pb=pb, m_=m_, kc=kc, k=k, n=n: e.matmul(pb[:, 0:n], woS[:, kc, k * 128:(k + 1) * 128], m_[:, kc, 0:n],
                                                                          start=(kc == 0), stop=(kc == NDC - 1)),
                     reads=[woS, m_], writes=[pb], signal=(kc == NDC - 1))
            P.op("dve", lambda e, pb=pb, xin=xin, xout=xout, k=k, n=n, t=t: e.scalar_tensor_tensor(
                xout[:, k, 0:n], pb[:, 0:n], gate[:, t, k:k + 1], xin[:, k, 0:n], ALU.mult, ALU.add), reads=[pb, gate, xin], writes=[xout])
        P.dma("sp", xs_out.t[:, :, o:o + n].rearrange("c p n -> p c n"), xout[:, :, 0:n], reads=[xout], writes=[xs_out])
    P.close_scope()


def build_fused(nlat=NLAT, nctx=NCTX, nlayers=2, stop=99):
    nc = _nc()
    P = Prog(nc)
    tall = nlat + nctx
    I = "ExternalInput"
    xs0 = P.dram("xs", [NK, 128, tall], F32, I)
    cT = P.dram("cT", [128, 32, 2], F32, I)
    wm = P.dram("wm", [2, D, 3072], F32, I)
    bm = P.dram("bm", [128, 2, 24], F32, I)
    ngd = P.dram("ng", [128, 3, NK], F32, I)
    cosT = P.dram("cosT", [128, nlat], F32, I)
    sinT = P.dram("sinT", [128, nlat], F32, I)
    rmT = P.dram("rmT", [128, 128], F32, I)
    s5k = P.dram("s5k", [128, 260], F32, I)
    glk = P.dram("glk", [128, 384], F32, I)
    cmk = P.dram("cmk", [64, tall], F32, I)
    L = []
    for l in range(nlayers):
        d = {}
        for (nm, shp) in (("wq", [D, TM_W + FM_W]), ("wmg", [D, 3072]), ("wglu", [1024, 1024]), ("wps", [1024, 1024]),
                          ("wpg", [1024, 1024]), ("wpa", [2048, 1024]), ("wo", [D, 1024]), ("gqk", [128, 2]),
                          ("s5p", [128, 32, 4]), ("s5b", [128, 32, 2, 16]), ("s5c", [128, 32, 16]), ("s5d", [16, 16]),
                          ("glw", [16, 4, 64]), ("glb", [64, 4]), ("glg", [128, 1])):
            d[nm] = P.dram(f"{nm}{l}", shp, F32, I)
        L.append(d)
    out = P.dram("out", [NK * 128, nlat], F32, "ExternalOutput")
    modS = P.sbuf("modS", [128, 2, 24, 2], F32)
    ngs = P.sbuf("ngs", [128, 3, NK], F32)
    Av = P.sbuf("Av", [128, 2, NK], F32)
    Bv = P.sbuf("Bv", [128, 2, NK], F32)
    Gv = P.sbuf("Gv", [128, 2, NK], F32)
    P.dma("sp", ngs[:], ngd[:], reads=[ngd], writes=[ngs])
    P.open_scope()
    emit_M2(P, cT, wm, bm, modS)
    P.close_scope()
    xs = xs0
    for l in range(nlayers):
        W = L[l]
        for t in range(2):
            P.op("dve", lambda e, t=t, l=l: e.scalar_tensor_tensor(Av[:, t, :], modS[:, l, 8:16, t], 1.0, ngs[:, l, :], ALU.add, ALU.mult),
                 reads=[modS, ngs], writes=[Av])
            P.op("dve", lambda e, t=t, l=l: e.tensor_copy(Bv[:, t, :], modS[:, l, 0:8, t]), reads=[modS], writes=[Bv])
            P.op("dve", lambda e, t=t, l=l: e.tensor_copy(Gv[:, t, :], modS[:, l, 16:24, t]), reads=[modS], writes=[Gv])
        arin = [P.dram(f"arin{l}_{q}", [128, tall // 4], F32) for q in range(4)]
        arout = [P.dram(f"arout{l}_{q}", [128, tall // 4], F32) for q in range(4)]
        aghin = TT(P, f"aghin{l}", NK * 128, tall, BF16)
        aghout = TT(P, f"aghout{l}", NDC * 128, tall, BF16)
        P.open_scope()
        emit_A2(P, xs, Av, Bv, arin, arout, aghin, BF16, tall, nctx, ag_out=aghout)
        P.close_scope()
        if stop == 2:
            break
        hTb = aghout
        tm = P.dram(f"tm{l}", [tall, TM_W], F32)
        fm = P.dram(f"fm{l}", [FM_W, tall], F32)
        ags_i, ags_o = TT(P, f"agsi{l}", 512, tall, BF16), TT(P, f"agso{l}", 4 * 512, tall, BF16)
        agl_i, agl_o = TT(P, f"agli{l}", 256, tall, BF16), TT(P, f"aglo{l}", 4 * 256, tall, BF16)
        aga_i, aga_o = TT(P, f"agai{l}", 512, tall, BF16), TT(P, f"agao{l}", 4 * 512, tall, BF16)
        gT, gsT, glT, atT = ags_i.sub(0), ags_i.sub(256), agl_i, aga_i

        def ag_all(ti, to):
            for ci in range(len(ti.bufs)):
                P.collective("AllGather", ALU.bypass, GROUPS, ti.bufs[ci][:], to.bufs[ci][:], reads=[ti.bufs[ci]], writes=[to.bufs[ci]])
        P.open_scope()
        emit_B1(P, hTb, W["wq"], tm, fm, tall)
        P.close_scope()
        P.open_scope()
        emit_s5(P, fm, W["s5p"], W["s5b"], W["s5c"], W["s5d"], s5k, gT, gsT, tall, nctx)
        P.close_scope()
        ag_all(ags_i, ags_o)
        P.open_scope()
        emit_gla(P, fm, tm, W["glw"], W["glb"], W["glg"], glk, cmk, glT, tall, nctx)
        P.close_scope()
        ag_all(agl_i, agl_o)
        P.open_scope()
        emit_attn(P, fm, tm, cosT, sinT, rmT, W["gqk"], atT, l < nlayers - 1, tall, nctx)
        P.close_scope()
        ag_all(aga_i, aga_o)
        agmin = TT(P, f"agmin{l}", NK * 128, tall, BF16)
        agmout = TT(P, f"agmout{l}", NDC * 128, tall, BF16)
        xs_new = P.dram(f"xs{l + 1}", [NK, 128, tall], F32)
        emit_C2(P, hTb, (ags_o, agl_o, aga_o), W["wglu"], W["wmg"], W["wps"], W["wpg"], W["wpa"], W["wo"], Gv, xs, xs_new, agmin, agmout, tall, nctx)
        xs = xs_new
    if stop < 99:
        P.finish([])
        P.emit()
        return nc
    P.op("dve", lambda e: e.memset(Bv[:], 0.0), writes=[Bv])
    for t in range(2):
        P.op("dve", lambda e, t=t: e.tensor_copy(Av[:, t, :], ngs[:, 2, :]), reads=[ngs], writes=[Av])
    arin = [P.dram(f"arinF_{q}", [128, tall // 4], F32) for q in range(4)]
    arout = [P.dram(f"aroutF_{q}", [128, tall // 4], F32) for q in range(4)]
    P.open_scope()
    emit_A2(P, xs, Av, Bv, arin, arout, out, F32, tall, nctx, lat_only=True)
    P.close_scope()
    P.finish([out])
    P.emit()
    return nc


def _fused_inputs(x, c, ctx, c_ctx, norm_g, w_mod, b_mod, w_in, ssm_lam_re, ssm_lam_im, ssm_log_dt, ssm_b_re, ssm_b_im,
                  ssm_c_re, ssm_c_im, ssm_d, ssm_w_glu, gla_w_a, gla_b_a, gla_norm_g, attn_q_g, attn_k_g,
                  w_proj_ssm, w_proj_gla, w_proj_attn, w_out, final_g):
    f = lambda a: np.asarray(a, dtype=np.float32)
    x, c, ctx, c_ctx = f(x), f(c), f(ctx), f(c_ctx)
    nlat, nctx = x.shape[1], ctx.shape[1]
    tall = nlat + nctx
    w_mod, b_mod, w_in, norm_g, final_g = f(w_mod), f(b_mod), f(w_in), f(norm_g), f(final_g)
    cosT, sinT, rm = rope_tables(nlat)
    glk, cmk = gla_consts(tall)
    s5k = s5_consts()
    ins = []
    for i in range(NCORES):
        b, j = i // 4, i % 4
        dsl = slice(j * 1024, (j + 1) * 1024)
        xa = np.concatenate([ctx[b], x[b]], 0)
        d = {"xs": np.ascontiguousarray(xa[:, dsl].T.reshape(NK, 128, tall)),
             "cT": np.ascontiguousarray(np.stack([c[b], c_ctx], 0).reshape(2, 32, 128).transpose(2, 1, 0)),
             "cosT": cosT, "sinT": sinT, "rmT": rm, "s5k": s5k, "glk": glk, "cmk": cmk}
        mcols = np.concatenate([np.arange(p * 4096 + j * 1024, p * 4096 + (j + 1) * 1024) for p in range(3)])
        d["wm"] = np.ascontiguousarray(w_mod[:, :, mcols])
        d["bm"] = np.ascontiguousarray(b_mod[:, mcols].reshape(2, 24, 128).transpose(2, 0, 1))
        d["ng"] = np.ascontiguousarray(np.stack([norm_g[0][dsl], norm_g[1][dsl], final_g[dsl]], 0).reshape(3, NK, 128).transpose(2, 0, 1))
        for l in range(2):
            gcols = np.concatenate([np.arange(_OFF["mg"] + br * 4096 + j * 1024, _OFF["mg"] + br * 4096 + (j + 1) * 1024) for br in range(3)])
            prm, bb, cc, dd = s5_host_layout(f(ssm_lam_re)[l], f(ssm_lam_im)[l], f(ssm_log_dt)[l], f(ssm_b_re)[l], f(ssm_b_im)[l],
                                             f(ssm_c_re)[l], f(ssm_c_im)[l], f(ssm_d)[l], j)
            WA, BA = f(gla_w_a)[l], f(gla_b_a)[l]
            d[f"wq{l}"] = np.ascontiguousarray(w_in[l][:, _cols(j)])
            d[f"wmg{l}"] = np.ascontiguousarray(w_in[l][:, gcols])
            d[f"wglu{l}"] = np.ascontiguousarray(f(ssm_w_glu)[l])
            d[f"wps{l}"] = np.ascontiguousarray(f(w_proj_ssm)[l][:, dsl])
            d[f"wpg{l}"] = np.ascontiguousarray(f(w_proj_gla)[l][:, dsl])
            d[f"wpa{l}"] = np.ascontiguousarray(f(w_proj_attn)[l][:, dsl])
            d[f"wo{l}"] = np.ascontiguousarray(f(w_out)[l][:, dsl])
            d[f"gqk{l}"] = np.ascontiguousarray(np.stack([f(attn_q_g)[l], f(attn_k_g)[l]], 1))
            d[f"s5p{l}"], d[f"s5b{l}"], d[f"s5c{l}"], d[f"s5d{l}"] = prm, bb, cc, dd
            d[f"glw{l}"] = np.ascontiguousarray(WA[:, :, 128 * j:128 * (j + 1)].reshape(2, 16, 2, 64).transpose(1, 0, 2, 3).reshape(16, 4, 64))
            d[f"glb{l}"] = np.ascontiguousarray(BA[:, 128 * j:128 * (j + 1)].reshape(4, 64).T)
            d[f"glg{l}"] = np.ascontiguousarray(f(gla_norm_g)[l].reshape(128, 1))
        ins.append(d)
    return ins, nlat, nctx


def kernel_fused(**inputs):
    ins, nlat, nctx = _fused_inputs(**inputs)
    import os
    nc = build_fused(nlat, nctx, stop=int(os.environ.get("FSTOP", "99")))
    res = _run(nc, ins)
    out = np.zeros((2, nlat, D), np.float32)
    for i in range(NCORES):
        b, j = i // 4, i % 4
        out[b][:, j * 1024:(j + 1) * 1024] = res[i]["out"].T
    return out


kernel_unfused = kernel


def kernel(**inputs):
    return kernel_fused(**inputs)
```

```python
import numpy as np
import concourse.bass as bass
import concourse.mybir as mybir
from concourse.bass_utils import run_bass_kernel_spmd
from contextlib import ExitStack

F32 = mybir.dt.float32
BF16 = mybir.dt.bfloat16
AF = mybir.ActivationFunctionType
ALU = mybir.AluOpType
AX = mybir.AxisListType

SEM_WRAP = 30000


class Buf:
    __slots__ = ("t", "name", "lw", "rd", "root")

    def __init__(self, t, name="", root=None):
        self.t = t
        self.name = name
        self.lw = None
        self.rd = []
        self.root = root if root is not None else self

    def alias(self, ap):
        return Buf(ap, self.name + "_v", root=self.root)

    def __getitem__(self, idx):
        return self.t[idx]


class Prog:
    ENGS = ("pe", "act", "dve", "pool", "sp")

    def __init__(self, nc, ndma_slots=12):
        self.nc = nc
        self.stack = ExitStack()
        self.streams = {e: [] for e in self.ENGS}
        self.sems = []
        self.eng_sem = {}
        self.eng_cnt = {}
        self.waited = {e: {} for e in self.ENGS}
        self.ndma_slots = ndma_slots
        self.dma_slots = {}
        self.dma_n = {}
        self.n_ins = 0
        self.pending = {e: [] for e in self.ENGS}
        self.scopes = []
        self.banks = None

    def bank(self, i):
        if self.banks is None:
            self.banks = [self.psum(f"bank{k}", [128, 512]) for k in range(8)]
        return self.banks[i]

    def barrier(self):
        evs = []
        for e, s in self.eng_sem.items():
            if self.eng_cnt[e] > 0:
                evs.append((s, self.eng_cnt[e]))
        for e, slots in self.dma_slots.items():
            n = self.dma_n[e]
            for k, s in enumerate(slots):
                cnt = (n - k + self.ndma_slots - 1) // self.ndma_slots if n > k else 0
                if cnt > 0:
                    evs.append((s, 16 * cnt))
        evs += getattr(self, "cc_events", [])
        for e in self.ENGS:
            own = self.eng_sem.get(e)
            w = self.waited[e]
            for (s, v) in evs:
                if s == own:
                    continue
                if w.get(s, 0) < v:
                    w[s] = v
                    self.pending[e].append((s, v))

    def open_scope(self):
        self.scopes.append(ExitStack())

    def close_scope(self):
        self.barrier()
        self.emit_segment()
        self.scopes.pop().close()

    def new_sem(self, name):
        s = self.stack.enter_context(self.nc.semaphore(name))
        self.sems.append(s)
        return len(self.sems) - 1

    def sbuf(self, name, shape, dtype):
        st = self.scopes[-1] if self.scopes else self.stack
        self.uid = getattr(self, "uid", 0) + 1
        t = st.enter_context(self.nc.sbuf_tensor(f"{name}_{self.uid}", list(shape), dtype))
        return Buf(t, name)

    def psum(self, name, shape, dtype=F32):
        t = self.stack.enter_context(self.nc.psum_tensor(name, list(shape), dtype))
        return Buf(t, name)

    def dram(self, name, shape, dtype, kind="Internal"):
        t = self.nc.dram_tensor(name, list(shape), dtype, kind=kind)
        return Buf(t.ap(), name)

    def view(self, ap, name=""):
        return Buf(ap, name)

    def _collect_waits(self, eng, reads, writes):
        evs = []
        for b in reads:
            b = b.root
            if b.lw is not None:
                evs.append(b.lw)
        for b in writes:
            b = b.root
            if b.lw is not None:
                evs.append(b.lw)
            evs.extend(b.rd)
        need = {}
        w = self.waited[eng]
        own = self.eng_sem.get(eng)
        for (s, v) in evs:
            if w.get(s, 0) >= v:
                continue
            if s == own and (eng == "pe" or v > self.eng_cnt[eng]):
                continue
            if need.get(s, 0) < v:
                need[s] = v
        for s, v in need.items():
            w[s] = v
        return list(need.items())

    def _mark(self, ev, reads, writes):
        for b in writes:
            b = b.root
            b.lw = ev
            b.rd = []
        for b in reads:
            b = b.root
            b.rd.append(ev)
            if len(b.rd) > 64:
                m = {}
                for (s, v) in b.rd:
                    if m.get(s, 0) < v:
                        m[s] = v
                b.rd = list(m.items())

    def op(self, eng, fn, reads=(), writes=(), signal=True):
        waits = self._collect_waits(eng, reads, writes)
        if self.pending[eng]:
            waits = waits + self.pending[eng]
            self.pending[eng] = []
        ev = None
        inc = None
        if signal:
            if eng not in self.eng_sem or self.eng_cnt[eng] >= SEM_WRAP:
                self.eng_sem[eng] = self.new_sem(f"s_{eng}_{len(self.sems)}")
                self.eng_cnt[eng] = 0
            self.eng_cnt[eng] += 1
            s = self.eng_sem[eng]
            ev = (s, self.eng_cnt[eng])
            inc = (s, 1)
        else:
            if eng not in self.eng_sem or self.eng_cnt[eng] >= SEM_WRAP:
                self.eng_sem[eng] = self.new_sem(f"s_{eng}_{len(self.sems)}")
                self.eng_cnt[eng] = 0
            ev = (self.eng_sem[eng], self.eng_cnt[eng] + 1)
        self.streams[eng].append((fn, waits, inc))
        self._mark(ev, reads, writes)
        self.n_ins += 1
        return ev

    def dma(self, eng, out_ap, in_ap, reads=(), writes=(), **kw):
        if eng not in self.dma_slots:
            self.dma_slots[eng] = [self.new_sem(f"d_{eng}_{i}") for i in range(self.ndma_slots)]
            self.dma_n[eng] = 0
        i = self.dma_n[eng]
        self.dma_n[eng] += 1
        slot = i % self.ndma_slots
        s = self.dma_slots[eng][slot]
        gen = i // self.ndma_slots
        waits = self._collect_waits(eng, reads, writes)
        if self.pending[eng]:
            waits = waits + self.pending[eng]
            self.pending[eng] = []
        if gen > 0:
            w = self.waited[eng]
            if w.get(s, 0) < 16 * gen:
                w[s] = 16 * gen
                waits = [x for x in waits if x[0] != s] + [(s, 16 * gen)]
        ev = (s, 16 * (gen + 1))

        def fn(e, out_ap=out_ap, in_ap=in_ap, kw=kw):
            return e.dma_start(out=out_ap, in_=in_ap, **kw)

        self.streams[eng].append((fn, waits, (s, 16)))
        self._mark(ev, reads, writes)
        self.n_ins += 1
        return ev

    def collective(self, kind, op, groups, in_ap, out_ap, reads=(), writes=(), inc=1):
        eng = "pool"
        NS = 4
        if not hasattr(self, "cc_slots"):
            self.cc_slots = [self.new_sem(f"cc_{i}") for i in range(NS)]
            self.cc_n = 0
        i = self.cc_n
        self.cc_n += 1
        s = self.cc_slots[i % NS]
        gen = i // NS
        waits = self._collect_waits(eng, reads, writes)
        if self.pending[eng]:
            waits = waits + self.pending[eng]
            self.pending[eng] = []
        if gen > 0:
            w = self.waited[eng]
            if w.get(s, 0) < gen:
                w[s] = gen
                waits = [x for x in waits if x[0] != s] + [(s, gen)]
        ev = (s, gen + 1)

        def fn(e):
            return e.collective_compute(kind, op, replica_groups=groups, ins=[in_ap], outs=[out_ap])

        self.streams[eng].append((fn, waits, (s, 1)))
        self._mark(ev, reads, writes)
        self.cc_events = [(self.cc_slots[k], (self.cc_n - k + NS - 1) // NS) for k in range(NS) if self.cc_n > k]
        self.n_ins += 1
        return ev

    def finish(self, final_bufs):
        evs = []
        for b in final_bufs:
            if b.root.lw is not None:
                evs.append(b.root.lw)
        self.final_waits = evs

    def emit(self):
        self.emit_segment(final=True)
        self.stack.close()

    def emit_segment(self, final=False):
        nc = self.nc
        streams = self.streams
        self.streams = {e: [] for e in self.ENGS}
        sems = self.sems
        final_waits = getattr(self, "final_waits", []) if final else []

        def run(e, name):
            for (fn, waits, inc) in streams[name]:
                for (s, v) in waits:
                    e.wait_ge(sems[s], v)
                ins = fn(e)
                if inc is not None:
                    ins.then_inc(sems[inc[0]], inc[1])
            if name == "sp":
                for (s, v) in final_waits:
                    e.wait_ge(sems[s], v)

        with nc.Block() as block:
            @block.tensor
            def _(e):
                run(e, "pe")

            @block.scalar
            def _(e):
                run(e, "act")

            @block.vector
            def _(e):
                run(e, "dve")

            @block.gpsimd
            def _(e):
                run(e, "pool")

            @block.sync
            def _(e):
                run(e, "sp")


NCORES = 8
D = 4096
NDC = 32
EPS = 1e-6


def _nc():
    return bass.Bass("TRN2", target_bir_lowering=False)


def build_M():
    nc = _nc()
    P = Prog(nc)
    cT = P.dram("cT", [128, 32, 3], F32, "ExternalInput")
    wm = P.dram("wm", [2, 4096, 1536], F32, "ExternalInput")
    bm = P.dram("bm", [128, 2, 12], F32, "ExternalInput")
    out = P.dram("modT", [128, 2, 12, 3], F32, "ExternalOutput")
    sc = P.sbuf("sc", [128, 32, 3], F32)
    bs = P.sbuf("bs", [128, 2, 12], F32)
    acc = P.sbuf("acc", [128, 2, 12, 3], F32)
    wt = [P.sbuf(f"wt{i}", [128, 4, 1536], F32) for i in range(2)]
    ps = [P.psum(f"ps{i}", [128, 12, 3]) for i in range(2)]
    P.dma("sp", sc[:], cT[:], reads=[cT], writes=[sc])
    P.dma("sp", bs[:], bm[:], reads=[bm], writes=[bs])
    P.op("act", lambda e: e.activation(sc[:], sc[:], AF.Silu), reads=[sc], writes=[sc])
    it = 0
    for l in range(2):
        for g in range(8):
            w = wt[it % 2]
            pp = ps[it % 2]
            src = wm.t[l, g * 512:(g + 1) * 512, :].rearrange("(k p) n -> p k n", p=128)
            P.dma("sp", w[:], src, reads=[wm], writes=[w])
            for j in range(12):
                for k in range(4):
                    kc = g * 4 + k
                    P.op("pe", lambda e, pp=pp, w=w, j=j, k=k, kc=kc: e.matmul(
                        pp[:, j, :], w[:, k, j * 128:(j + 1) * 128], sc[:, kc, :],
                        start=(k == 0), stop=(k == 3)),
                        reads=[w, sc], writes=[pp], signal=(k == 3 and j == 11))
            if g == 0:
                P.op("dve", lambda e, pp=pp, l=l: e.tensor_copy(acc[:, l], pp[:]), reads=[pp], writes=[acc])
            else:
                P.op("dve", lambda e, pp=pp, l=l: e.tensor_tensor(acc[:, l], acc[:, l], pp[:], ALU.add),
                     reads=[pp, acc], writes=[acc])
            it += 1
    for l in range(2):
        for r in range(3):
            P.op("dve", lambda e, l=l, r=r: e.tensor_tensor(acc[:, l, :, r], acc[:, l, :, r], bs[:, l, :], ALU.add),
                 reads=[acc, bs], writes=[acc])
    P.dma("sp", out[:], acc[:], reads=[acc], writes=[out])
    P.finish([out])
    P.emit()
    return nc


def build_A(nlat=1024, nctx=64, out_bf16=True):
    nc = _nc()
    P = Prog(nc)
    NT = nlat + nctx
    odt = BF16 if out_bf16 else F32
    xT = P.dram("xT", [NDC, 128, NT], F32, "ExternalInput")
    ng = P.dram("ng", [128, NDC], F32, "ExternalInput")
    ms = P.dram("ms", [128, 96, 2], F32, "ExternalInput")
    hT = P.dram("hT", [NDC, 128, NT], odt, "ExternalOutput")
    emit_A(P, xT, ng, ms, hT, nlat, nctx, odt)
    P.finish([hT])
    P.emit()
    return nc


def emit_A(P, xT, ng, ms, hT, nlat, nctx, odt):
    ngs = P.sbuf("a_ng", [128, NDC], F32)
    mss = P.sbuf("a_ms", [128, 96, 2], F32)
    Av = P.sbuf("a_A", [128, 2, NDC], F32)
    Bv = P.sbuf("a_B", [128, 2, NDC], F32)
    onesm = P.sbuf("a_ones", [128, 128], F32)
    epsb = P.sbuf("a_eps", [128, 1], F32)
    xs = P.sbuf("a_xs", [128, NDC, 512], F32)
    ho = P.sbuf("a_ho", [128, NDC, 512], odt)
    sq = [P.sbuf(f"a_sq{i}", [128, 512], F32) for i in range(2)]
    tmp = [P.sbuf(f"a_tmp{i}", [128, 512], F32) for i in range(2)]
    rstd = P.sbuf("a_rstd", [128, 512], F32)
    pss = P.psum("a_pss", [128, 512])
    P.dma("sp", ngs[:], ng[:], reads=[ng], writes=[ngs])
    P.dma("sp", mss[:], ms[:], reads=[ms], writes=[mss])
    P.op("dve", lambda e: e.memset(onesm[:], 1.0 / D), writes=[onesm])
    P.op("dve", lambda e: e.memset(epsb[:], EPS), writes=[epsb])
    for t in range(2):
        P.op("dve", lambda e, t=t: e.scalar_tensor_tensor(Av[:, t, :], mss[:, 32:64, t], 1.0, ngs[:], ALU.add, ALU.mult),
             reads=[mss, ngs], writes=[Av])
        P.op("dve", lambda e, t=t: e.tensor_copy(Bv[:, t, :], mss[:, 0:32, t]), reads=[mss], writes=[Bv])
    blocks = []
    o = 0
    while o < nlat:
        n = min(512, nlat - o)
        blocks.append((o, n, 0))
        o += n
    while o < nlat + nctx:
        n = min(512, nlat + nctx - o)
        blocks.append((o, n, 1))
        o += n
    for (o, n, t) in blocks:
        P.dma("sp", xs[:, :, 0:n], xT.t[:, :, o:o + n].rearrange("c p n -> p c n"), reads=[xT], writes=[xs])
        for dc in range(NDC):
            s = sq[dc % 2]
            P.op("act", lambda e, s=s, dc=dc, n=n: e.activation(s[:, 0:n], xs[:, dc, 0:n], AF.Square), reads=[xs], writes=[s])
            P.op("pe", lambda e, s=s, dc=dc, n=n: e.matmul(pss[:, 0:n], onesm[:], s[:, 0:n], start=(dc == 0), stop=(dc == NDC - 1)),
                 reads=[onesm, s], writes=[pss])
        P.op("act", lambda e, n=n: e.activation(rstd[:, 0:n], pss[:, 0:n], AF.Sqrt, bias=epsb[:], scale=1.0), reads=[pss, epsb], writes=[rstd])
        P.op("dve", lambda e, n=n: e.reciprocal(rstd[:, 0:n], rstd[:, 0:n]), reads=[rstd], writes=[rstd])
        for dc in range(NDC):
            tm = tmp[dc % 2]
            P.op("dve", lambda e, tm=tm, dc=dc, n=n, t=t: e.scalar_tensor_tensor(
                tm[:, 0:n], xs[:, dc, 0:n], Av[:, t, dc:dc + 1], rstd[:, 0:n], ALU.mult, ALU.mult),
                reads=[xs, Av, rstd], writes=[tm])
            P.op("act", lambda e, tm=tm, dc=dc, n=n, t=t: e.activation(
                ho[:, dc, 0:n], tm[:, 0:n], AF.Identity, bias=Bv[:, t, dc:dc + 1], scale=1.0),
                reads=[tm, Bv], writes=[ho])
        P.dma("sp", hT.t[:, :, o:o + n].rearrange("c p n -> p c n"), ho[:, :, 0:n], reads=[ho], writes=[hT])


TALL = 4352
NCTX = 256
NLAT = 4096
TM_W = 384
C_GV, C_AV = 0, 256
FM_W = 2208
R_GQ, R_GK, R_GZ, R_AQ, R_AK, R_AZ, R_SU, R_SZ, R_LR = 0, 128, 256, 512, 1024, 1152, 1664, 1920, 2176


def emit_B1(P, hTb, wq, tm, fm, tall=TALL):
    hb = [P.sbuf(f"b1_h{i}", [128, NDC, 512], BF16) for i in range(2)]
    Ws = [P.sbuf(f"b1_w{i}", [128, NDC, 1024], BF16) for i in range(2)]
    stg = [P.sbuf(f"b1_s{i}", [128, 512], F32) for i in range(3)]
    ps = [(P.bank(0), P.bank(1)), (P.bank(2), P.bank(3))]
    nblk = (tall + 511) // 512
    passes = [("tm", 0, TM_W), ("fm", TM_W, 1024), ("fm", TM_W + 1024, 1024), ("fm", TM_W + 2048, FM_W - 2048)]
    it = 0
    si = 0
    for pi_, (kind, c0, ncol) in enumerate(passes):
        W = Ws[pi_ % 2]
        for half in range(2):
            P.dma("pool", W[:, half * 16:(half + 1) * 16, 0:ncol],
                  wq.t[half * 2048:(half + 1) * 2048, c0:c0 + ncol].rearrange("(k p) n -> p k n", p=128),
                  reads=[wq], writes=[W])
        for tb in range(nblk):
            o = tb * 512
            n = min(512, tall - o)
            h = hb[it % 2]
            it += 1
            rd_cpn(P, hTb, o, n, h, lambda so, pn, h=h: h[:, :, so:so + pn])
            if kind == "tm":
                for sub in range(n // 128):
                    pp = ps[si % 2]
                    st = stg[si % 3]
                    si += 1
                    for (b0, bw, bank) in ((0, ncol, 0),):
                        for kc in range(NDC):
                            P.op("pe", lambda e, pp=pp, h=h, kc=kc, sub=sub, b0=b0, bw=bw, bank=bank, W=W: e.matmul(
                                pp[bank][:, 0:bw], h[:, kc, sub * 128:(sub + 1) * 128], W[:, kc, b0:b0 + bw],
                                start=(kc == 0), stop=(kc == NDC - 1)),
                                reads=[h, W], writes=[pp[bank]], signal=(kc == NDC - 1))
                    P.op("act", lambda e, pp=pp, st=st, ncol=ncol: e.activation(st[:, 0:ncol], pp[0][:, 0:ncol], AF.Copy), reads=[pp[0]], writes=[st])
                    r0 = o + sub * 128
                    P.dma("sp", tm.t[r0:r0 + 128, 0:ncol], st[:, 0:ncol], reads=[st], writes=[tm])
            else:
                r_base = c0 - TM_W
                ncc = (ncol + 127) // 128
                for cc in range(ncc):
                    m = min(128, ncol - cc * 128)
                    pp = P.bank(si % 4)
                    st = stg[si % 3]
                    for kc in range(NDC):
                        P.op("pe", lambda e, pp=pp, h=h, kc=kc, cc=cc, m=m, n=n, W=W: e.matmul(
                            pp[0:m, 0:n], W[:, kc, cc * 128:cc * 128 + m], h[:, kc, 0:n],
                            start=(kc == 0), stop=(kc == NDC - 1)),
                            reads=[h, W], writes=[pp], signal=(kc == NDC - 1))
                    if si % 2 == 0:
                        P.op("act", lambda e, pp=pp, st=st, m=m, n=n: e.activation(st[0:m, 0:n], pp[0:m, 0:n], AF.Copy), reads=[pp], writes=[st])
                    else:
                        P.op("dve", lambda e, pp=pp, st=st, m=m, n=n: e.tensor_copy(st[0:m, 0:n], pp[0:m, 0:n]), reads=[pp], writes=[st])
                    si += 1
                    rr = r_base + cc * 128
                    P.dma("sp", fm.t[rr:rr + m, o:o + n], st[0:m, 0:n], reads=[st], writes=[fm])


def emit_attn(P, fm, tm, cosT, sinT, rmT, gqk, atT, ctx_out, tall=TALL, nctx=NCTX):
    nlat = tall - nctx
    ntile = tall // 128
    cs = P.sbuf("at_cos", [128, nlat], F32)
    sn = P.sbuf("at_sin", [128, nlat], F32)
    rm = P.sbuf("at_rm", [128, 128], F32)
    g2 = P.sbuf("at_g", [128, 2], F32)
    ones_f = P.sbuf("at_1f", [128, 128], F32)
    ones_b = P.sbuf("at_1b", [128, 128], BF16)
    epsb = P.sbuf("at_eps", [128, 1], F32)
    KT = P.sbuf("at_KT", [128, tall], BF16)
    V = P.sbuf("at_V", [128, ntile, 128], BF16)
    QT = [P.sbuf(f"at_QT{i}", [128, 512], BF16) for i in range(2)]
    xs = [P.sbuf(f"at_xs{i}", [128, 512], F32) for i in range(2)]
    sq = P.sbuf("at_sq", [128, 512], F32)
    rs = P.sbuf("at_rs", [128, 512], F32)
    xn = P.sbuf("at_xn", [128, 512], F32)
    t1 = P.sbuf("at_t1", [128, 512], F32)
    t2 = P.sbuf("at_t2", [128, 512], F32)
    pb = [P.sbuf(f"at_p{i}", [128, 512], BF16) for i in range(3)]
    az = P.sbuf("at_az", [128, 512], F32)
    rl = P.sbuf("at_rl", [128, 512], F32)
    ob = P.sbuf("at_ob", [128, 512], F32)
    oo = [P.sbuf(f"at_oo{i}", [128, 512], BF16) for i in range(2)]
    ps_ms, ps_rot = P.bank(0), P.bank(1)
    ps_s = [P.bank(2), P.bank(3), P.bank(4)]
    ps_o, ps_l = P.bank(5), P.bank(6)
    P.dma("sp", cs[:], cosT[:], reads=[cosT], writes=[cs])
    P.dma("sp", sn[:], sinT[:], reads=[sinT], writes=[sn])
    P.dma("sp", rm[:], rmT[:], reads=[rmT], writes=[rm])
    P.dma("sp", g2[:], gqk[:], reads=[gqk], writes=[g2])
    P.op("dve", lambda e: e.memset(ones_f[:], 1.0 / 128), writes=[ones_f])
    P.op("dve", lambda e: e.memset(ones_b[:], 1.0), writes=[ones_b])
    P.op("dve", lambda e: e.memset(epsb[:], EPS), writes=[epsb])
    P.dma("pool", V[:], tm.t[:, C_AV:C_AV + 128].rearrange("(t p) c -> p t c", p=128), reads=[tm], writes=[V])
    cnt = [0]

    def prep(r0, o, n, gi, rope, pos0, dst, dstb):
        x = xs[cnt[0] % 2]
        cnt[0] += 1
        P.dma("sp", x[:, 0:n], fm.t[r0:r0 + 128, o:o + n], reads=[fm], writes=[x])
        P.op("act", lambda e: e.activation(sq[:, 0:n], x[:, 0:n], AF.Square), reads=[x], writes=[sq])
        P.op("pe", lambda e: e.matmul(ps_ms[:, 0:n], ones_f[:], sq[:, 0:n], start=True, stop=True), reads=[ones_f, sq], writes=[ps_ms])
        P.op("act", lambda e: e.activation(rs[:, 0:n], ps_ms[:, 0:n], AF.Sqrt, bias=epsb[:], scale=1.0), reads=[ps_ms, epsb], writes=[rs])
        P.op("dve", lambda e: e.reciprocal(rs[:, 0:n], rs[:, 0:n]), reads=[rs], writes=[rs])
        if not rope:
            P.op("dve", lambda e: e.scalar_tensor_tensor(dst, x[:, 0:n], g2[:, gi:gi + 1], rs[:, 0:n], ALU.mult, ALU.mult),
                 reads=[x, g2, rs], writes=[dstb])
            return
        P.op("dve", lambda e: e.scalar_tensor_tensor(xn[:, 0:n], x[:, 0:n], g2[:, gi:gi + 1], rs[:, 0:n], ALU.mult, ALU.mult),
             reads=[x, g2, rs], writes=[xn])
        P.op("pe", lambda e: e.matmul(ps_rot[:, 0:n], rm[:], xn[:, 0:n], start=True, stop=True), reads=[rm, xn], writes=[ps_rot])
        P.op("pool", lambda e: e.tensor_tensor(t1[:, 0:n], xn[:, 0:n], cs[:, pos0:pos0 + n], ALU.mult), reads=[xn, cs], writes=[t1])
        P.op("dve", lambda e: e.tensor_tensor(t2[:, 0:n], ps_rot[:, 0:n], sn[:, pos0:pos0 + n], ALU.mult), reads=[ps_rot, sn], writes=[t2])
        P.op("dve", lambda e: e.tensor_tensor(dst, t1[:, 0:n], t2[:, 0:n], ALU.add), reads=[t1, t2], writes=[dstb])

    prep(R_AK, 0, nctx, 1, False, 0, KT[:, 0:nctx], KT)
    o = nctx
    while o < tall:
        n = min(512, tall - o)
        prep(R_AK, o, n, 1, True, o - nctx, KT[:, o:o + n], KT)
        o += n
    scale = 128 ** -0.5
    qi = 0
    pi = 0
    for hh in range(4):
        blocks = []
        if ctx_out:
            blocks.append((0, nctx, False, nctx // 128))
        o = nctx
        while o < tall:
            n = min(512, tall - o)
            blocks.append((o, n, True, ntile))
            o += n
        for (o, n, rope, nk) in blocks:
            q = QT[qi % 2]
            qi += 1
            prep(R_AQ + hh * 128, o, n, 0, rope, o - nctx, q[:, 0:n], q)
            tiles = []
            for kc in range(nk):
                tiles.append((ps_s[pi % 3], pb[pi % 3]))
                pi += 1

            def qk(kc):
                s_ps = tiles[kc][0]
                P.op("pe", lambda e, s_ps=s_ps, q=q, kc=kc, n=n: e.matmul(s_ps[:, 0:n], KT[:, kc * 128:(kc + 1) * 128], q[:, 0:n], start=True, stop=True),
                     reads=[KT, q], writes=[s_ps])

            qk(0)
            if nk > 1:
                qk(1)
            for kc in range(nk):
                s_ps, p_sb = tiles[kc]
                P.op("act", lambda e, s_ps=s_ps, p_sb=p_sb, n=n: e.activation(p_sb[:, 0:n], s_ps[:, 0:n], AF.Exp, scale=scale),
                     reads=[s_ps], writes=[p_sb])
                if kc + 2 < nk:
                    qk(kc + 2)
                P.op("pe", lambda e, p_sb=p_sb, kc=kc, n=n, nk=nk: e.matmul(ps_o[:, 0:n], V[:, kc, :], p_sb[:, 0:n], start=(kc == 0), stop=(kc == nk - 1)),
                     reads=[V, p_sb], writes=[ps_o], signal=False)
                P.op("pe", lambda e, p_sb=p_sb, kc=kc, n=n, nk=nk: e.matmul(ps_l[:, 0:n], ones_b[:], p_sb[:, 0:n], start=(kc == 0), stop=(kc == nk - 1)),
                     reads=[ones_b, p_sb], writes=[ps_l])
            r0 = R_AZ + hh * 128
            P.dma("sp", az[:, 0:n], fm.t[r0:r0 + 128, o:o + n], reads=[fm], writes=[az])
            P.op("act", lambda e, n=n: e.activation(az[:, 0:n], az[:, 0:n], AF.Silu), reads=[az], writes=[az])
            P.op("dve", lambda e, n=n: e.reciprocal(rl[:, 0:n], ps_l[:, 0:n]), reads=[ps_l], writes=[rl])
            P.op("dve", lambda e, n=n: e.tensor_tensor(ob[:, 0:n], ps_o[:, 0:n], rl[:, 0:n], ALU.mult), reads=[ps_o, rl], writes=[ob])
            ot = oo[qi % 2]
            P.op("pool", lambda e, n=n, ot=ot: e.tensor_tensor(ot[:, 0:n], ob[:, 0:n], az[:, 0:n], ALU.mult), reads=[ob, az], writes=[ot])
            wr_rows(P, atT, hh * 128, 128, o, n, ot, lambda so, pn, ot=ot: ot[:, so:so + pn])


def rope_tables(nlat):
    rows = nlat // 64
    nf = 32
    row = np.repeat(np.arange(rows, dtype=np.float32), 64)
    col = np.tile(np.arange(64, dtype=np.float32), rows)
    inv = (10000.0 ** (-np.arange(nf, dtype=np.float32) / nf)).astype(np.float32)
    ang = np.stack([row[:, None] * inv, col[:, None] * inv], axis=1)
    ang = np.concatenate([ang, ang], axis=-1).reshape(rows * 64, 128).astype(np.float32)
    cosT = np.ascontiguousarray(np.cos(ang).T.astype(np.float32))
    sinT = np.ascontiguousarray(np.sin(ang).T.astype(np.float32))
    rm = np.zeros((128, 128), np.float32)
    for a in range(2):
        for f in range(32):
            rm[a * 64 + 32 + f, a * 64 + f] = -1.0
            rm[a * 64 + f, a * 64 + 32 + f] = 1.0
    return cosT, sinT, rm


def build_B(tall=TALL, nctx=NCTX, ctx_out=True, parts=("b1", "attn")):
    nc = _nc()
    P = Prog(nc)
    nlat = tall - nctx
    hTb = P.dram("hTb", [NDC, 128, tall], BF16, "ExternalInput")
    wq = P.dram("wq", [D, TM_W + FM_W], F32, "ExternalInput")
    cosT = P.dram("cosT", [128, nlat], F32, "ExternalInput")
    sinT = P.dram("sinT", [128, nlat], F32, "ExternalInput")
    rmT = P.dram("rmT", [128, 128], F32, "ExternalInput")
    gqk = P.dram("gqk", [128, 2], F32, "ExternalInput")
    dbg = "dbg" in parts
    tm = P.dram("tm", [tall, TM_W], F32, "ExternalOutput" if dbg else "Internal")
    fm = P.dram("fm", [FM_W, tall], F32, "ExternalOutput" if dbg else "Internal")
    atT = P.dram("atT", [512, tall], BF16, "ExternalOutput")
    s5p = P.dram("s5p", [128, 32, 4], F32, "ExternalInput")
    s5b = P.dram("s5b", [128, 32, 2, 16], F32, "ExternalInput")
    s5c = P.dram("s5c", [128, 32, 16], F32, "ExternalInput")
    s5d = P.dram("s5d", [16, 16], F32, "ExternalInput")
    s5k = P.dram("s5k", [128, 260], F32, "ExternalInput")
    gT = P.dram("gT", [256, tall], BF16, "ExternalOutput")
    gsT = P.dram("gsT", [256, tall], BF16, "ExternalOutput")
    glw = P.dram("glw", [16, 4, 64], F32, "ExternalInput")
    glb = P.dram("glb", [64, 4], F32, "ExternalInput")
    glg = P.dram("glg", [128, 1], F32, "ExternalInput")
    glk = P.dram("glk", [128, 384], F32, "ExternalInput")
    cmk = P.dram("cmk", [64, tall], F32, "ExternalInput")
    glT = P.dram("glT", [256, tall], BF16, "ExternalOutput")
    outs = [atT, gT, gsT, glT]
    if dbg:
        outs += [tm, fm]
    P.open_scope()
    emit_B1(P, hTb, wq, tm, fm, tall)
    P.close_scope()
    if "attn" in parts:
        P.open_scope()
        emit_attn(P, fm, tm, cosT, sinT, rmT, gqk, atT, ctx_out, tall, nctx)
        P.close_scope()
    if "gla" in parts:
        P.open_scope()
        emit_gla(P, fm, tm, glw, glb, glg, glk, cmk, glT, tall, nctx)
        P.close_scope()
    if "s5" in parts:
        P.open_scope()
        emit_s5(P, fm, s5p, s5b, s5c, s5d, s5k, gT, gsT, tall, nctx)
        P.close_scope()
    P.finish(outs)
    P.emit()
    return nc


def emit_s5(P, fm, s5p, s5b, s5c, s5d, cst, gT, gsT, tall=TALL, nctx=NCTX):
    nlat = tall - nctx
    NP = 32
    c = P.sbuf("s5_cst", [128, 260], F32)
    P.dma("sp", c[:], cst[:], reads=[cst], writes=[c])
    ident, swap = c[:, 0:128], c[:, 128:256]
    sgn, m0, m1, hpi = c[:, 256:257], c[:, 257:258], c[:, 258:259], c[:, 259:260]
    prm = P.sbuf("s5_prm", [128, NP, 4], F32)
    Bd = P.sbuf("s5_B", [128, NP, 2, 16], F32)
    Cw = P.sbuf("s5_C", [128, NP, 16], F32)
    dsk = P.sbuf("s5_d", [16, 16], F32)
    P.dma("sp", prm[:], s5p[:], reads=[s5p], writes=[prm])
    P.dma("sp", Bd[:], s5b[:], reads=[s5b], writes=[Bd])
    P.dma("sp", Cw[:], s5c[:], reads=[s5c], writes=[Cw])
    P.dma("sp", dsk[:], s5d[:], reads=[s5d], writes=[dsk])
    names = ["lr", "dt", "a", "th", "mag", "s", "c", "cc", "ss", "Lr", "Li", "den", "L1", "nr", "ni", "c1r", "c1i", "t"]
    T_ = {k: P.sbuf("s5_" + k, [128, NP], F32) for k in names}
    PR = P.sbuf("s5_PR", [128, 13, NP], F32)
    PI = P.sbuf("s5_PI", [128, 13, NP], F32)
    PIs = P.sbuf("s5_PIs", [128, 13, NP], F32)

    def tt(o, a, b, op, eng="dve"):
        P.op(eng, lambda e: e.tensor_tensor(o[:], a[:], b[:], op), reads=[a, b], writes=[o])

    lre, lim, ldt = prm[:, :, 0], prm[:, :, 1], prm[:, :, 2]
    P.op("dve", lambda e: e.tensor_scalar(T_["lr"][:], lre, -1e-4, None, ALU.min), reads=[prm], writes=[T_["lr"]])
    P.op("act", lambda e: e.activation(T_["dt"][:], ldt, AF.Exp), reads=[prm], writes=[T_["dt"]])
    tt(T_["a"], T_["lr"], T_["dt"], ALU.mult)
    P.op("dve", lambda e: e.tensor_tensor(T_["th"][:], lim, T_["dt"][:], ALU.mult), reads=[prm, T_["dt"]], writes=[T_["th"]])
    P.op("act", lambda e: e.activation(T_["mag"][:], T_["a"][:], AF.Exp), reads=[T_["a"]], writes=[T_["mag"]])
    P.op("act", lambda e: e.activation(T_["s"][:], T_["th"][:], AF.Sin, scale=1.0 / 16), reads=[T_["th"]], writes=[T_["s"]])
    P.op("act", lambda e: e.activation(T_["c"][:], T_["th"][:], AF.Sin, bias=hpi, scale=1.0 / 16), reads=[T_["th"], c], writes=[T_["c"]])
    for _ in range(4):
        tt(T_["cc"], T_["c"], T_["c"], ALU.mult)
        tt(T_["ss"], T_["s"], T_["s"], ALU.mult)
        P.op("dve", lambda e: e.scalar_tensor_tensor(T_["s"][:], T_["c"][:], 2.0, T_["s"][:], ALU.mult, ALU.mult),
             reads=[T_["c"], T_["s"]], writes=[T_["s"]])
        tt(T_["c"], T_["cc"], T_["ss"], ALU.subtract)
    tt(T_["Lr"], T_["mag"], T_["c"], ALU.mult)
    tt(T_["Li"], T_["mag"], T_["s"], ALU.mult)
    tt(T_["den"], T_["lr"], T_["lr"], ALU.mult)
    P.op("dve", lambda e: e.tensor_tensor(T_["t"][:], lim, lim, ALU.mult), reads=[prm], writes=[T_["t"]])
    tt(T_["den"], T_["den"], T_["t"], ALU.add)
    P.op("dve", lambda e: e.reciprocal(T_["den"][:], T_["den"][:]), reads=[T_["den"]], writes=[T_["den"]])
    P.op("dve", lambda e: e.tensor_scalar(T_["L1"][:], T_["Lr"][:], -1.0, None, ALU.add), reads=[T_["Lr"]], writes=[T_["L1"]])
    tt(T_["nr"], T_["L1"], T_["lr"], ALU.mult)
    P.op("dve", lambda e: e.tensor_tensor(T_["t"][:], T_["Li"][:], lim, ALU.mult), reads=[prm, T_["Li"]], writes=[T_["t"]])
    tt(T_["nr"], T_["nr"], T_["t"], ALU.add)
    tt(T_["ni"], T_["Li"], T_["lr"], ALU.mult)
    P.op("dve", lambda e: e.tensor_tensor(T_["t"][:], T_["L1"][:], lim, ALU.mult), reads=[prm, T_["L1"]], writes=[T_["t"]])
    tt(T_["ni"], T_["ni"], T_["t"], ALU.subtract)
    tt(T_["c1r"], T_["nr"], T_["den"], ALU.mult)
    tt(T_["c1i"], T_["ni"], T_["den"], ALU.mult)
    P.op("dve", lambda e: e.tensor_copy(PR[:, 0, :], T_["Lr"][:]), reads=[T_["Lr"]], writes=[PR])
    P.op("dve", lambda e: e.tensor_copy(PI[:, 0, :], T_["Li"][:]), reads=[T_["Li"]], writes=[PI])
    for m in range(12):
        P.op("dve", lambda e, m=m: e.tensor_tensor(T_["cc"][:], PR[:, m, :], PR[:, m, :], ALU.mult), reads=[PR], writes=[T_["cc"]])
        P.op("dve", lambda e, m=m: e.tensor_tensor(T_["ss"][:], PI[:, m, :], PI[:, m, :], ALU.mult), reads=[PI], writes=[T_["ss"]])
        P.op("dve", lambda e, m=m: e.scalar_tensor_tensor(PI[:, m + 1, :], PR[:, m, :], 2.0, PI[:, m, :], ALU.mult, ALU.mult),
             reads=[PR, PI], writes=[PI])
        P.op("dve", lambda e, m=m: e.tensor_tensor(PR[:, m + 1, :], T_["cc"][:], T_["ss"][:], ALU.subtract),
             reads=[T_["cc"], T_["ss"], PR], writes=[PR])
    P.op("dve", lambda e: e.tensor_scalar(PIs[:], PI[:], sgn, None, ALU.mult), reads=[PI, c], writes=[PIs])
    P.op("dve", lambda e: e.tensor_scalar(Cw[:], Cw[:], sgn, None, ALU.mult), reads=[Cw, c], writes=[Cw])
    bT = P.sbuf("s5_bT", [16, NP, 128], F32)
    tb = [P.sbuf(f"s5_tb{i}", [128, 16], F32) for i in range(4)]
    for pi in range(NP):
        c1r, c1i = T_["c1r"][:, pi:pi + 1], T_["c1i"][:, pi:pi + 1]
        Bre, Bim = Bd[:, pi, 0, :], Bd[:, pi, 1, :]
        rd = [T_["c1r"], T_["c1i"], Bd]
        P.op("dve", lambda e, Bim=Bim, c1i=c1i: e.tensor_scalar(tb[0][:], Bim, c1i, None, ALU.mult), reads=rd, writes=[tb[0]])
        P.op("dve", lambda e, Bre=Bre, c1r=c1r: e.scalar_tensor_tensor(tb[1][:], Bre, c1r, tb[0][:], ALU.mult, ALU.subtract), reads=rd + [tb[0]], writes=[tb[1]])
        P.op("dve", lambda e, Bre=Bre, c1i=c1i: e.tensor_scalar(tb[2][:], Bre, c1i, None, ALU.mult), reads=rd, writes=[tb[2]])
        P.op("dve", lambda e, Bim=Bim, c1r=c1r: e.scalar_tensor_tensor(tb[3][:], Bim, c1r, tb[2][:], ALU.mult, ALU.add), reads=rd + [tb[2]], writes=[tb[3]])
        P.op("dve", lambda e: e.tensor_scalar(tb[3][:], tb[3][:], m1, None, ALU.mult), reads=[tb[3], c], writes=[tb[3]])
        P.op("dve", lambda e: e.scalar_tensor_tensor(tb[1][:], tb[1][:], m0, tb[3][:], ALU.mult, ALU.add), reads=[tb[1], tb[3], c], writes=[tb[1]])
        pb = P.bank(pi % 2)
        P.op("pe", lambda e, pb=pb: e.transpose(pb[0:16, 0:128], tb[1][:], ident), reads=[tb[1], c], writes=[pb])
        P.op("act", lambda e, pb=pb, pi=pi: e.activation(bT[:, pi, :], pb[0:16, 0:128], AF.Copy), reads=[pb], writes=[bT])
    NSL = 2
    identb = P.sbuf("s5_identb", [128, 128], BF16)
    Cwb = P.sbuf("s5_Cwb", [128, NP, 16], BF16)
    P.op("dve", lambda e: e.tensor_copy(identb[:], ident), reads=[c], writes=[identb])
    P.op("dve", lambda e: e.tensor_copy(Cwb[:], Cw[:]), reads=[Cw], writes=[Cwb])
    uTs = [P.sbuf(f"s5_u{k}", [16, tall], F32) for k in range(NSL)]
    zTs = [P.sbuf(f"s5_z{k}", [16, tall], F32) for k in range(NSL)]
    Lms = [[P.sbuf(f"s5_Lm{k}{d}", [128, 13, 128], BF16) for d in range(2)] for k in range(NSL)]
    Xhs = [[[P.sbuf(f"s5_Xh{k}{d}{q}", [128, tall], BF16) for q in range(2)] for d in range(2)] for k in range(NSL)]
    ys = [P.sbuf(f"s5_y{i}", [16, 512], F32) for i in range(2)]
    gb = [P.sbuf(f"s5_g{i}", [16, 512], BF16) for i in range(2)]
    gsb = [P.sbuf(f"s5_gs{i}", [16, 512], BF16) for i in range(2)]
    nsteps = 0
    while (1 << nsteps) < tall:
        nsteps += 1
    bk = [0]

    def nb():
        bk[0] += 1
        return P.bank(bk[0] % 8)

    blocks = [(0, nctx)]
    o = nctx
    while o < tall:
        blocks.append((o, min(512, tall - o)))
        o += 512

    def a1(o):
        return o - nctx if o >= nctx else nlat + o

    evc = [0]

    def st_load(gi, k):
        uT, zT, Lm, Xh = uTs[k], zTs[k], Lms[k], Xhs[k]
        P.dma("sp", uT[:], fm.t[R_SU + gi * 16:R_SU + (gi + 1) * 16, :], reads=[fm], writes=[uT])
        P.dma("sp", zT[:], fm.t[R_SZ + gi * 16:R_SZ + (gi + 1) * 16, :], reads=[fm], writes=[zT])
        for d in range(2):
            pi = d * 16 + gi
            for m in range(nsteps):
                P.op("dve", lambda e, d=d, m=m, pi=pi: e.tensor_scalar(Lm[d][:, m, :], ident, PR[:, m, pi:pi + 1], None, ALU.mult),
                     reads=[PR, c], writes=[Lm[d]])
                P.op("dve", lambda e, d=d, m=m, pi=pi: e.scalar_tensor_tensor(Lm[d][:, m, :], swap, PIs[:, m, pi:pi + 1], Lm[d][:, m, :], ALU.mult, ALU.add),
                     reads=[PIs, c, Lm[d]], writes=[Lm[d]])
            for (o, n) in blocks:
                pb = nb()
                P.op("pe", lambda e, pb=pb, pi=pi, o=o, n=n: e.matmul(pb[:, 0:n], bT[:, pi, :], uT[:, o:o + n], start=True, stop=True),
                     reads=[bT, uT], writes=[pb])
                od = o if d == 0 else a1(o)
                P.op("act", lambda e, pb=pb, d=d, od=od, n=n: e.activation(Xh[d][0][:, od:od + n], pb[:, 0:n], AF.Copy), reads=[pb], writes=[Xh[d][0]])

    def st_step(k, m, cur):
        Lm, Xh = Lms[k], Xhs[k]
        s = 1 << m
        for d in range(2):
            Xa, Xb = Xh[d][cur], Xh[d][1 - cur]
            w = tall - s
            o = 0
            while o < w:
                n = min(512, w - o)
                pb = nb()
                src = o if d == 0 else o + s
                dst = o + s if d == 0 else o
                P.op("pe", lambda e, pb=pb, Xa=Xa, dst=dst, n=n: e.matmul(pb[:, 0:n], identb[:], Xa[:, dst:dst + n], start=True, stop=False),
                     reads=[identb, Xa], writes=[pb], signal=False)
                P.op("pe", lambda e, pb=pb, d=d, m=m, Xa=Xa, src=src, n=n: e.matmul(pb[:, 0:n], Lm[d][:, m, :], Xa[:, src:src + n], start=False, stop=True),
                     reads=[Lm[d], Xa], writes=[pb])
                if evc[0] % 2 == 0:
                    P.op("act", lambda e, pb=pb, Xb=Xb, dst=dst, n=n: e.activation(Xb[:, dst:dst + n], pb[:, 0:n], AF.Copy), reads=[pb], writes=[Xb])
                else:
                    P.op("dve", lambda e, pb=pb, Xb=Xb, dst=dst, n=n: e.tensor_copy(Xb[:, dst:dst + n], pb[:, 0:n]), reads=[pb], writes=[Xb])
                evc[0] += 1
                o += n
            c0 = 0 if d == 0 else w
            P.op("pool", lambda e, Xa=Xa, Xb=Xb, c0=c0, s=s: e.tensor_copy(Xb[:, c0:c0 + s], Xa[:, c0:c0 + s]), reads=[Xa], writes=[Xb])

    def st_out(gi, k, cur):
        uT, zT, Xh = uTs[k], zTs[k], Xhs[k]
        for bi, (o, n) in enumerate(blocks):
            pb = nb()
            P.op("pe", lambda e, pb=pb, o=o, n=n: e.matmul(pb[0:16, 0:n], Cwb[:, gi, :], Xh[0][cur][:, o:o + n], start=True, stop=False),
                 reads=[Cwb, Xh[0][cur]], writes=[pb], signal=False)
            oa = a1(o)
            P.op("pe", lambda e, pb=pb, oa=oa, n=n: e.matmul(pb[0:16, 0:n], Cwb[:, 16 + gi, :], Xh[1][cur][:, oa:oa + n], start=False, stop=True),
                 reads=[Cwb, Xh[1][cur]], writes=[pb])
            y, g, gs = ys[bi % 2], gb[bi % 2], gsb[bi % 2]
            P.op("dve", lambda e, pb=pb, y=y, o=o, n=n: e.scalar_tensor_tensor(y[:, 0:n], uT[:, o:o + n], dsk[:, gi:gi + 1], pb[0:16, 0:n], ALU.mult, ALU.add),
                 reads=[uT, dsk, pb], writes=[y])
            P.op("act", lambda e, y=y, g=g, n=n: e.activation(g[:, 0:n], y[:, 0:n], AF.Gelu_apprx_tanh), reads=[y], writes=[g])
            P.op("act", lambda e, o=o, n=n: e.activation(zT[:, o:o + n], zT[:, o:o + n], AF.Silu), reads=[zT], writes=[zT])
            P.op("dve", lambda e, g=g, gs=gs, o=o, n=n: e.tensor_tensor(gs[:, 0:n], g[:, 0:n], zT[:, o:o + n], ALU.mult), reads=[g, zT], writes=[gs])
            wr_rows(P, gT, gi * 16, 16, o, n, g, lambda so, pn, g=g: g[:, so:so + pn])
            wr_rows(P, gsT, gi * 16, 16, o, n, gs, lambda so, pn, gs=gs: gs[:, so:so + pn])

    for g0 in range(0, 16, NSL):
        for k in range(NSL):
            st_load(g0 + k, k)
        cur = 0
        for m in range(nsteps):
            for k in range(NSL):
                st_step(k, m, cur)
            cur = 1 - cur
        for k in range(NSL):
            st_out(g0 + k, k, cur)


def s5_consts():
    c = np.zeros((128, 260), np.float32)
    c[:, 0:128] = np.eye(128)
    for p in range(128):
        c[p, 128 + (p + 64) % 128] = 1.0
    c[:64, 256] = 1.0
    c[64:, 256] = -1.0
    c[:64, 257] = 1.0
    c[64:, 258] = 1.0
    c[:, 259] = np.pi / 2
    return c


def s5_host_layout(lam_re, lam_im, log_dt, b_re, b_im, c_re, c_im, d_skip, j):
    gs = slice(j * 16, (j + 1) * 16)
    def dup(x):
        return np.concatenate([x, x], 0)
    lre = lam_re[:, gs].reshape(32, 64).T
    lim = lam_im[:, gs].reshape(32, 64).T
    ldt = np.broadcast_to(log_dt[:, gs].reshape(1, 32), (64, 32))
    prm = np.stack([lre, lim, ldt, np.zeros_like(lre)], -1)
    prm = np.ascontiguousarray(dup(prm)).astype(np.float32)
    bre = b_re[:, gs].reshape(32, 64, 16).transpose(1, 0, 2)
    bim = b_im[:, gs].reshape(32, 64, 16).transpose(1, 0, 2)
    bb = np.ascontiguousarray(dup(np.stack([bre, bim], 2))).astype(np.float32)
    cre = c_re[:, gs].reshape(32, 16, 64).transpose(2, 0, 1)
    cim = c_im[:, gs].reshape(32, 16, 64).transpose(2, 0, 1)
    cc = np.ascontiguousarray(np.concatenate([cre, cim], 0)).astype(np.float32)
    dd = np.ascontiguousarray(d_skip.reshape(64, 16)[gs].T).astype(np.float32)
    return prm, bb, cc, dd
    P.finish(outs)
    P.emit()
    return nc


def emit_gla(P, fm, tm, glw, glb, glg, glk, cmk, glT, tall=TALL, nctx=NCTX):
    nch = tall // 128
    ncc = nctx // 128
    kk = P.sbuf("gl_k", [128, 384], F32)
    P.dma("sp", kk[:], glk[:], reads=[glk], writes=[kk])
    cm = P.sbuf("gl_cm", [64, tall], F32)
    P.dma("sp", cm[:], cmk[:], reads=[cmk], writes=[cm])
    wa = P.sbuf("gl_wa", [16, 4, 64], F32)
    ba = P.sbuf("gl_ba", [64, 4], F32)
    go = P.sbuf("gl_go", [128, 1], F32)
    P.dma("sp", wa[:], glw[:], reads=[glw], writes=[wa])
    P.dma("sp", ba[:], glb[:], reads=[glb], writes=[ba])
    P.dma("sp", go[:], glg[:], reads=[glg], writes=[go])
    P.op("dve", lambda e: e.tensor_scalar(ba[:], ba[:], -1.0, None, ALU.mult), reads=[ba], writes=[ba])
    one1 = P.sbuf("gl_one", [128, 1], F32)
    epsb = P.sbuf("gl_eps", [128, 1], F32)
    ones_f = P.sbuf("gl_1f", [128, 128], F32)
    P.op("dve", lambda e: e.memset(one1[:], 1.0), writes=[one1])
    P.op("dve", lambda e: e.memset(epsb[:], EPS), writes=[epsb])
    P.op("dve", lambda e: e.memset(ones_f[:], 1.0 / 128), writes=[ones_f])
    qh = P.sbuf("gl_q", [64, tall], F32)
    kh = P.sbuf("gl_kh", [64, tall], F32)
    la = P.sbuf("gl_la", [64, tall], F32)
    cs = P.sbuf("gl_cs", [64, tall], F32)
    cB = P.sbuf("gl_cB", [64, tall], F32)
    ex = P.sbuf("gl_ex", [64, tall], F32)
    qe = P.sbuf("gl_qe", [64, tall], BF16)
    ke = P.sbuf("gl_ke", [64, tall], BF16)
    tot = P.sbuf("gl_tot", [64, nch], F32)
    Et = P.sbuf("gl_Et", [64, nch], F32)
    Vt = P.sbuf("gl_V", [128, nch, 128], BF16)
    oacc = P.sbuf("gl_o", [128, tall], F32)
    S = P.sbuf("gl_S", [64, 128], F32)
    Sb = P.sbuf("gl_Sb", [64, 128], BF16)
    scm = [P.sbuf(f"gl_sc{i}", [128, 128], BF16) for i in range(2)]
    kT = [P.sbuf(f"gl_kT{i}", [128, 64], BF16) for i in range(2)]
    sq = P.sbuf("gl_sq", [128, 512], F32)
    rs = P.sbuf("gl_rs", [128, 512], F32)
    gz = P.sbuf("gl_gz", [128, 512], F32)
    yo = P.sbuf("gl_yo", [128, 512], F32)
    ob = [P.sbuf(f"gl_ob{i}", [128, 512], BF16) for i in range(2)]
    bk = [0]

    def nb():
        bk[0] += 1
        return P.bank(bk[0] % 8)

    for hh in range(2):
        P.dma("sp", qh[:], fm.t[R_GQ + hh * 64:R_GQ + (hh + 1) * 64, :], reads=[fm], writes=[qh])
        P.dma("sp", kh[:], fm.t[R_GK + hh * 64:R_GK + (hh + 1) * 64, :], reads=[fm], writes=[kh])
        P.dma("pool", Vt[:], tm.t[:, C_GV + hh * 128:C_GV + (hh + 1) * 128].rearrange("(t p) c -> p t c", p=128), reads=[tm], writes=[Vt])
        for d in range(2):
            pr = d * 2 + hh
            P.dma("sp", ex[0:16, :], fm.t[R_LR + d * 16:R_LR + (d + 1) * 16, :], reads=[fm], writes=[ex])
            o = 0
            while o < tall:
                n = min(512, tall - o)
                pb = nb()
                P.op("pe", lambda e, pb=pb, pr=pr, o=o, n=n: e.matmul(pb[0:64, 0:n], wa[:, pr, :], ex[0:16, o:o + n], start=True, stop=True),
                     reads=[wa, ex], writes=[pb])
                P.op("act", lambda e, pb=pb, pr=pr, o=o, n=n: e.activation(la[:, o:o + n], pb[0:64, 0:n], AF.Exp, bias=ba[:, pr:pr + 1], scale=-1.0),
                     reads=[pb, ba], writes=[la])
                o += n
            P.op("act", lambda e: e.activation(la[:], la[:], AF.Ln, bias=one1[0:64, :], scale=1.0), reads=[la, one1], writes=[la])
            P.op("dve", lambda e: e.tensor_tensor_scan(cs[:], cm[:], la[:], 0.0, ALU.mult, ALU.add), reads=[cm, la], writes=[cs])
            P.op("dve", lambda e: e.tensor_copy(tot[:], cs[:].rearrange("p (c t) -> p c t", t=128)[:, :, 127]), reads=[cs], writes=[tot])
            if d == 0:
                P.op("dve", lambda e: e.tensor_copy(cB[:], cs[:]), reads=[cs], writes=[cB])
            else:
                for c in range(nch):
                    P.op("dve", lambda e, c=c: e.tensor_scalar(cB[:, c * 128:(c + 1) * 128], cs[:, c * 128:(c + 1) * 128], -1.0, tot[:, c:c + 1], ALU.mult, ALU.add),
                         reads=[cs, tot], writes=[cB])
                P.op("dve", lambda e: e.tensor_tensor(cB[:], cB[:], la[:], ALU.add), reads=[cB, la], writes=[cB])
            P.op("act", lambda e: e.activation(ex[:], cB[:], AF.Exp, scale=-1.0 / 16), reads=[cB], writes=[ex])
            P.op("dve", lambda e: e.scalar_tensor_tensor(qe[:], qh[:], 0.125, ex[:], ALU.mult, ALU.mult), reads=[qh, ex], writes=[qe])
            P.op("act", lambda e: e.activation(ex[:], cB[:], AF.Exp, scale=1.0 / 16), reads=[cB], writes=[ex])
            P.op("dve", lambda e: e.tensor_tensor(ke[:], kh[:], ex[:], ALU.mult), reads=[kh, ex], writes=[ke])
            for c in range(nch):
                P.op("dve", lambda e, c=c: e.tensor_scalar(cs[:, c * 128:(c + 1) * 128], cB[:, c * 128:(c + 1) * 128], -1.0, tot[:, c:c + 1], ALU.mult, ALU.add),
                     reads=[cB, tot], writes=[cs])
            P.op("act", lambda e: e.activation(ex[:], cs[:], AF.Exp, scale=-1.0 / 16), reads=[cs], writes=[ex])
            P.op("dve", lambda e: e.tensor_tensor(la[:], kh[:], ex[:], ALU.mult), reads=[kh, ex], writes=[la])
            P.op("act", lambda e: e.activation(Et[:], tot[:], AF.Exp, scale=-1.0 / 16), reads=[tot], writes=[Et])
            P.op("dve", lambda e: e.memset(S[:], 0.0), writes=[S])
            P.op("dve", lambda e: e.memset(Sb[:], 0.0), writes=[Sb])
            if d == 0:
                order = list(range(nch))
            else:
                order = list(range(ncc - 1, -1, -1)) + list(range(nch - 1, ncc - 1, -1))
            mk = kk[:, 0:128] if d == 0 else kk[:, 128:256]
            for i, c in enumerate(order):
                sl = slice(c * 128, (c + 1) * 128)
                p1, p2, p3, p4 = nb(), nb(), nb(), nb()
                sc = scm[i % 2]
                kt = kT[i % 2]
                P.op("pe", lambda e, p1=p1, sl=sl: e.matmul(p1[:, 0:128], ke[:, sl], qe[:, sl], start=True, stop=True), reads=[ke, qe], writes=[p1])
                P.op("dve", lambda e, p1=p1, sc=sc, mk=mk: e.tensor_tensor(sc[:], p1[:, 0:128], mk, ALU.mult), reads=[p1, kk], writes=[sc])
                P.op("pe", lambda e, p2=p2, sc=sc, c=c: e.matmul(p2[:, 0:128], Vt[:, c, :], sc[:], start=True, stop=False), reads=[Vt, sc], writes=[p2], signal=False)
                P.op("pe", lambda e, p2=p2, sl=sl: e.matmul(p2[:, 0:128], Sb[:], qe[:, sl], start=False, stop=True), reads=[Sb, qe], writes=[p2])
                if d == 0:
                    P.op("act", lambda e, p2=p2, sl=sl: e.activation(oacc[:, sl], p2[:, 0:128], AF.Copy), reads=[p2], writes=[oacc])
                else:
                    P.op("dve", lambda e, p2=p2, sl=sl: e.tensor_tensor(oacc[:, sl], oacc[:, sl], p2[:, 0:128], ALU.add), reads=[p2, oacc], writes=[oacc])
                P.op("pe", lambda e, p3=p3, sl=sl: e.transpose(p3[:, 0:64], la[:, sl], kk[0:64, 256:320]), reads=[la, kk], writes=[p3])
                P.op("act", lambda e, p3=p3, kt=kt: e.activation(kt[:], p3[:, 0:64], AF.Copy), reads=[p3], writes=[kt])
                P.op("pe", lambda e, p4=p4, kt=kt, c=c: e.matmul(p4[0:64, 0:128], kt[:], Vt[:, c, :], start=True, stop=True), reads=[kt, Vt], writes=[p4])
                P.op("dve", lambda e, p4=p4, c=c: e.scalar_tensor_tensor(S[:], S[:], Et[:, c:c + 1], p4[0:64, 0:128], ALU.mult, ALU.add), reads=[S, Et, p4], writes=[S])
                P.op("act", lambda e: e.activation(Sb[:], S[:], AF.Copy), reads=[S], writes=[Sb])
        o = 0
        bi = 0
        while o < tall:
            n = min(512, tall - o)
            pb = nb()
            P.op("act", lambda e, o=o, n=n: e.activation(sq[:, 0:n], oacc[:, o:o + n], AF.Square), reads=[oacc], writes=[sq])
            P.op("pe", lambda e, pb=pb, n=n: e.matmul(pb[:, 0:n], ones_f[:], sq[:, 0:n], start=True, stop=True), reads=[ones_f, sq], writes=[pb])
            P.op("act", lambda e, pb=pb, n=n: e.activation(rs[:, 0:n], pb[:, 0:n], AF.Sqrt, bias=epsb[:], scale=1.0), reads=[pb, epsb], writes=[rs])
            P.op("dve", lambda e, n=n: e.reciprocal(rs[:, 0:n], rs[:, 0:n]), reads=[rs], writes=[rs])
            P.op("dve", lambda e, o=o, n=n: e.scalar_tensor_tensor(yo[:, 0:n], oacc[:, o:o + n], go[:, 0:1], rs[:, 0:n], ALU.mult, ALU.mult), reads=[oacc, go, rs], writes=[yo])
            P.dma("sp", gz[:, 0:n], fm.t[R_GZ + hh * 128:R_GZ + (hh + 1) * 128, o:o + n], reads=[fm], writes=[gz])
            P.op("act", lambda e, n=n: e.activation(gz[:, 0:n], gz[:, 0:n], AF.Silu), reads=[gz], writes=[gz])
            ot = ob[bi % 2]
            bi += 1
            P.op("pool", lambda e, ot=ot, n=n: e.tensor_tensor(ot[:, 0:n], yo[:, 0:n], gz[:, 0:n], ALU.mult), reads=[yo, gz], writes=[ot])
            wr_rows(P, glT, hh * 128, 128, o, n, ot, lambda so, pn, ot=ot: ot[:, so:so + pn])
            o += n


def gla_consts(tall):
    k = np.zeros((128, 384), np.float32)
    s = np.arange(128)[:, None]
    t = np.arange(128)[None, :]
    k[:, 0:128] = (s <= t)
    k[:, 128:256] = (s >= t)
    k[:, 256:384] = np.eye(128)
    cm = np.ones((64, tall), np.float32)
    cm[:, ::128] = 0.0
    return k, cm


def build_C(nlat=1024, nctx=64):
    nc = _nc()
    P = Prog(nc)
    NT = nlat + nctx
    gT = P.dram("gT", [1024, NT], BF16, "ExternalInput")
    gsT = P.dram("gsT", [1024, NT], BF16, "ExternalInput")
    glT = P.dram("glT", [1024, NT], BF16, "ExternalInput")
    atT = P.dram("atT", [2048, NT], BF16, "ExternalInput")
    hT = P.dram("hT", [NDC, 128, NT], BF16, "ExternalInput")
    xT = P.dram("xT", [NDC, 128, NT], F32, "ExternalInput")
    wglu = P.dram("wglu", [1024, 1024], F32, "ExternalInput")
    wps = P.dram("wps", [1024, D], F32, "ExternalInput")
    wpg = P.dram("wpg", [1024, D], F32, "ExternalInput")
    wpa = P.dram("wpa", [2048, D], F32, "ExternalInput")
    wmg = P.dram("wmg", [D, 3 * D], F32, "ExternalInput")
    wo = P.dram("wo", [D, D], F32, "ExternalInput")
    gsel = P.dram("gsel", [128, NDC, 2], F32, "ExternalInput")
    xo = P.dram("xo", [NDC, 128, NT], F32, "ExternalOutput")
    gates = P.dram("gates", [96, 128, NT], BF16, "Internal")
    blocks = []
    o = 0
    while o < nlat:
        n = min(512, nlat - o)
        blocks.append((o, n, 0))
        o += n
    if nctx:
        blocks.append((nlat, nctx, 1))
    bk = [0]

    def nb():
        bk[0] += 1
        return P.bank(bk[0] % 8)

    mT = P.sbuf("c_m", [128, NDC, NT], BF16)
    P.open_scope()
    hs = P.sbuf("c_h", [128, NDC, NT], BF16)
    P.dma("sp", hs[:], hT.t.rearrange("c p n -> p c n"), reads=[hT], writes=[hs])
    Wg = [P.sbuf(f"c_wg{i}", [128, NDC, 128], BF16) for i in range(3)]
    gst = [P.sbuf(f"c_gst{i}", [128, NT], BF16) for i in range(2)]
    for mc in range(96):
        W = Wg[mc % 3]
        P.dma("pool", W[:], wmg.t[:, mc * 128:(mc + 1) * 128].rearrange("(k p) n -> p k n", p=128), reads=[wmg], writes=[W])
        st = gst[mc % 2]
        for (o, n, t) in blocks:
            pb = nb()
            for kc in range(NDC):
                P.op("pe", lambda e, pb=pb, W=W, kc=kc, o=o, n=n: e.matmul(pb[:, 0:n], W[:, kc, :], hs[:, kc, o:o + n], start=(kc == 0), stop=(kc == NDC - 1)),
                     reads=[W, hs], writes=[pb], signal=(kc == NDC - 1))
            P.op("act", lambda e, pb=pb, st=st, o=o, n=n: e.activation(st[:, o:o + n], pb[:, 0:n], AF.Sigmoid), reads=[pb], writes=[st])
        P.dma("sp", gates.t[mc], st[:], reads=[st], writes=[gates])
    P.close_scope()
    P.open_scope()
    gs_ = P.sbuf("c_g", [128, 8, NT], BF16)
    ss_ = P.sbuf("c_s", [128, 8, NT], BF16)
    gl_ = P.sbuf("c_gl", [128, 8, NT], BF16)
    at_ = P.sbuf("c_at", [128, 16, NT], BF16)
    P.dma("sp", gs_[:], gT.t.rearrange("(c p) n -> p c n", p=128), reads=[gT], writes=[gs_])
    P.dma("sp", ss_[:], gsT.t.rearrange("(c p) n -> p c n", p=128), reads=[gsT], writes=[ss_])
    P.dma("sp", gl_[:], glT.t.rearrange("(c p) n -> p c n", p=128), reads=[glT], writes=[gl_])
    P.dma("sp", at_[:], atT.t.rearrange("(c p) n -> p c n", p=128), reads=[atT], writes=[at_])
    wgl = P.sbuf("c_wglu", [128, 8, 1024], BF16)
    P.dma("pool", wgl[:], wglu.t.rearrange("(k p) n -> p k n", p=128), reads=[wglu], writes=[wgl])
    sg = [P.sbuf(f"c_sg{i}", [128, 512], BF16) for i in range(2)]
    i = 0
    for oc in range(8):
        for (o, n, t) in blocks:
            pb = nb()
            for kc in range(8):
                P.op("pe", lambda e, pb=pb, kc=kc, oc=oc, o=o, n=n: e.matmul(pb[:, 0:n], wgl[:, kc, oc * 128:(oc + 1) * 128], gs_[:, kc, o:o + n], start=(kc == 0), stop=(kc == 7)),
                     reads=[wgl, gs_], writes=[pb], signal=(kc == 7))
            s_ = sg[i % 2]
            i += 1
            P.op("act", lambda e, pb=pb, s_=s_, n=n: e.activation(s_[:, 0:n], pb[:, 0:n], AF.Sigmoid), reads=[pb], writes=[s_])
            P.op("dve", lambda e, s_=s_, oc=oc, o=o, n=n: e.tensor_tensor(ss_[:, oc, o:o + n], ss_[:, oc, o:o + n], s_[:, 0:n], ALU.mult), reads=[ss_, s_], writes=[ss_])
    w1 = [P.sbuf(f"c_w1{i}", [128, 8, 128], BF16) for i in range(2)]
    w2 = [P.sbuf(f"c_w2{i}", [128, 8, 128], BF16) for i in range(2)]
    w3 = [P.sbuf(f"c_w3{i}", [128, 16, 128], BF16) for i in range(2)]
    g3 = [P.sbuf(f"c_g3{i}", [128, 3, NT], BF16) for i in range(2)]
    m1 = P.sbuf("c_m1", [128, 512], F32)
    m2 = P.sbuf("c_m2", [128, 512], F32)
    m3 = P.sbuf("c_m3", [128, 512], F32)
    for dc in range(NDC):
        a, b_, c_, g_ = w1[dc % 2], w2[dc % 2], w3[dc % 2], g3[dc % 2]
        cs_ = slice(dc * 128, (dc + 1) * 128)
        P.dma("pool", a[:], wps.t[:, cs_].rearrange("(k p) n -> p k n", p=128), reads=[wps], writes=[a])
        P.dma("pool", b_[:], wpg.t[:, cs_].rearrange("(k p) n -> p k n", p=128), reads=[wpg], writes=[b_])
        P.dma("pool", c_[:], wpa.t[:, cs_].rearrange("(k p) n -> p k n", p=128), reads=[wpa], writes=[c_])
        for br in range(3):
            P.dma("sp", g_[:, br, :], gates.t[br * 32 + dc], reads=[gates], writes=[g_])
        for (o, n, t) in blocks:
            p1, p2, p3 = nb(), nb(), nb()
            for kc in range(8):
                P.op("pe", lambda e, p1=p1, a=a, kc=kc, o=o, n=n: e.matmul(p1[:, 0:n], a[:, kc, :], ss_[:, kc, o:o + n], start=(kc == 0), stop=(kc == 7)),
                     reads=[a, ss_], writes=[p1], signal=(kc == 7))
            for kc in range(8):
                P.op("pe", lambda e, p2=p2, b_=b_, kc=kc, o=o, n=n: e.matmul(p2[:, 0:n], b_[:, kc, :], gl_[:, kc, o:o + n], start=(kc == 0), stop=(kc == 7)),
                     reads=[b_, gl_], writes=[p2], signal=(kc == 7))
            for kc in range(16):
                P.op("pe", lambda e, p3=p3, c_=c_, kc=kc, o=o, n=n: e.matmul(p3[:, 0:n], c_[:, kc, :], at_[:, kc, o:o + n], start=(kc == 0), stop=(kc == 15)),
                     reads=[c_, at_], writes=[p3], signal=(kc == 15))
            P.op("dve", lambda e, p1=p1, g_=g_, o=o, n=n: e.tensor_tensor(m1[:, 0:n], p1[:, 0:n], g_[:, 0, o:o + n], ALU.mult), reads=[p1, g_], writes=[m1])
            P.op("dve", lambda e, p2=p2, g_=g_, o=o, n=n: e.tensor_tensor(m2[:, 0:n], p2[:, 0:n], g_[:, 1, o:o + n], ALU.mult), reads=[p2, g_], writes=[m2])
            P.op("dve", lambda e, p3=p3, g_=g_, o=o, n=n: e.tensor_tensor(m3[:, 0:n], p3[:, 0:n], g_[:, 2, o:o + n], ALU.mult), reads=[p3, g_], writes=[m3])
            P.op("pool", lambda e, n=n: e.tensor_tensor(m1[:, 0:n], m1[:, 0:n], m2[:, 0:n], ALU.add), reads=[m1, m2], writes=[m1])
            P.op("pool", lambda e, dc=dc, o=o, n=n: e.tensor_tensor(mT[:, dc, o:o + n], m1[:, 0:n], m3[:, 0:n], ALU.add), reads=[m1, m3], writes=[mT])
    P.close_scope()
    P.open_scope()
    gv = P.sbuf("c_gv", [128, NDC, 2], F32)
    P.dma("sp", gv[:], gsel[:], reads=[gsel], writes=[gv])
    wo_ = [P.sbuf(f"c_wo{i}", [128, NDC, 128], BF16) for i in range(2)]
    xin = [P.sbuf(f"c_xi{i}", [128, NT], F32) for i in range(2)]
    xot = [P.sbuf(f"c_xo{i}", [128, NT], F32) for i in range(2)]
    for dc in range(NDC):
        W = wo_[dc % 2]
        xi, xn = xin[dc % 2], xot[dc % 2]
        P.dma("pool", W[:], wo.t[:, dc * 128:(dc + 1) * 128].rearrange("(k p) n -> p k n", p=128), reads=[wo], writes=[W])
        P.dma("sp", xi[:], xT.t[dc], reads=[xT], writes=[xi])
        for (o, n, t) in blocks:
            pb = nb()
            for kc in range(NDC):
                P.op("pe", lambda e, pb=pb, W=W, kc=kc, o=o, n=n: e.matmul(pb[:, 0:n], W[:, kc, :], mT[:, kc, o:o + n], start=(kc == 0), stop=(kc == NDC - 1)),
                     reads=[W, mT], writes=[pb], signal=(kc == NDC - 1))
            P.op("dve", lambda e, pb=pb, xi=xi, xn=xn, dc=dc, o=o, n=n, t=t: e.scalar_tensor_tensor(
                xn[:, o:o + n], pb[:, 0:n], gv[:, dc, t:t + 1], xi[:, o:o + n], ALU.mult, ALU.add), reads=[pb, gv, xi], writes=[xn])
        P.dma("sp", xo.t[dc], xn[:], reads=[xn], writes=[xo])
    P.close_scope()
    P.finish([xo])
    P.emit()
    return nc


_OFF = dict(su=0, sz=1024, gq=2048, gk=2560, gv=3072, gz=4096, glr=5120, aq=5152, ak=7200, av=7712, az=8224, mg=10272)


def _cols(j):
    r = lambda k, w: list(range(_OFF[k] + j * w, _OFF[k] + (j + 1) * w))
    return (r("gv", 256) + r("av", 128) + r("gq", 128) + r("gk", 128) + r("gz", 256) + r("aq", 512) + r("ak", 128)
            + r("az", 512) + r("su", 256) + r("sz", 256) + list(range(_OFF["glr"], _OFF["glr"] + 32)))


def _fm(a):
    return np.ascontiguousarray(a.T.reshape(NDC, 128, a.shape[0]))


def _run(nc, ins):
    return run_bass_kernel_spmd(nc, ins, core_ids=list(range(NCORES))).results


def kernel(x, c, ctx, c_ctx, norm_g, w_mod, b_mod, w_in, ssm_lam_re, ssm_lam_im, ssm_log_dt, ssm_b_re, ssm_b_im,
           ssm_c_re, ssm_c_im, ssm_d, ssm_w_glu, gla_w_a, gla_b_a, gla_norm_g, attn_q_g, attn_k_g,
           w_proj_ssm, w_proj_gla, w_proj_attn, w_out, final_g):
    f = lambda a: np.asarray(a, dtype=np.float32)
    x, c, ctx, c_ctx = f(x), f(c), f(ctx), f(c_ctx)
    cs3 = np.stack([c[0], c[1], c_ctx], 0)
    cT = np.ascontiguousarray(cs3.reshape(3, 32, 128).transpose(2, 1, 0))
    w_mod, b_mod = f(w_mod), f(b_mod)
    ins = []
    for i in range(8):
        sl = slice(i * 1536, (i + 1) * 1536)
        ins.append({"cT": cT, "wm": np.ascontiguousarray(w_mod[:, :, sl]),
                    "bm": np.ascontiguousarray(b_mod[:, sl].reshape(2, 12, 128).transpose(2, 0, 1))})
    res = _run(build_M(), ins)
    modT = np.concatenate([r["modT"] for r in res], axis=2)
    cores = [(i // 4, i % 4) for i in range(8)]
    xT = []
    for (b, j) in cores:
        loc = np.concatenate([x[b, j * 1024:(j + 1) * 1024], ctx[b, j * 64:(j + 1) * 64]], 0)
        xT.append(_fm(loc))
    cosT, sinT, rm = rope_tables(NLAT)
    glk, cmk = gla_consts(TALL)
    s5k = s5_consts()
    ncA = build_A(1024, 64, True)
    ncB = build_B(TALL, NCTX, True, ("b1", "attn", "gla", "s5"))
    ncC = build_C(1024, 64)
    w_in = f(w_in)
    for l in range(2):
        ngl = np.ascontiguousarray(f(norm_g)[l].reshape(32, 128).T)
        ins = [{"xT": xT[i], "ng": ngl, "ms": np.ascontiguousarray(modT[:, l][:, :, [b, 2]])} for i, (b, j) in enumerate(cores)]
        resA = _run(ncA, ins)
        hT = [r["hT"] for r in resA]
        hTb = []
        for b in range(2):
            hTb.append(np.ascontiguousarray(np.concatenate(
                [hT[b * 4 + j][:, :, 1024:1088] for j in range(4)] + [hT[b * 4 + j][:, :, 0:1024] for j in range(4)], axis=2)))
        ins = []
        for i, (b, j) in enumerate(cores):
            prm, bb, cc, dd = s5_host_layout(f(ssm_lam_re)[l], f(ssm_lam_im)[l], f(ssm_log_dt)[l], f(ssm_b_re)[l], f(ssm_b_im)[l],
                                             f(ssm_c_re)[l], f(ssm_c_im)[l], f(ssm_d)[l], j)
            WA = f(gla_w_a)[l]
            BA = f(gla_b_a)[l]
            glw = np.ascontiguousarray(WA[:, :, 128 * j:128 * (j + 1)].reshape(2, 16, 2, 64).transpose(1, 0, 2, 3).reshape(16, 4, 64))
            glb = np.ascontiguousarray(BA[:, 128 * j:128 * (j + 1)].reshape(4, 64).T)
            ins.append({"hTb": hTb[b], "wq": np.ascontiguousarray(w_in[l][:, _cols(j)]), "cosT": cosT, "sinT": sinT, "rmT": rm,
                        "gqk": np.ascontiguousarray(np.stack([f(attn_q_g)[l], f(attn_k_g)[l]], 1)),
                        "s5p": prm, "s5b": bb, "s5c": cc, "s5d": dd, "s5k": s5k,
                        "glw": glw, "glb": glb, "glg": np.ascontiguousarray(f(gla_norm_g)[l].reshape(128, 1)), "glk": glk, "cmk": cmk})
        resB = _run(ncB, ins)
        wmg = np.ascontiguousarray(w_in[l][:, _OFF["mg"]:])
        ins = []
        for i, (b, j) in enumerate(cores):
            tok = list(range(256 + j * 1024, 256 + (j + 1) * 1024)) + list(range(j * 64, (j + 1) * 64))
            cat = lambda k: np.ascontiguousarray(np.concatenate([resB[b * 4 + jj][k] for jj in range(4)], 0)[:, tok])
            ins.append({"gT": cat("gT"), "gsT": cat("gsT"), "glT": cat("glT"), "atT": cat("atT"), "hT": hT[i], "xT": xT[i],
                        "wglu": f(ssm_w_glu)[l], "wps": f(w_proj_ssm)[l], "wpg": f(w_proj_gla)[l], "wpa": f(w_proj_attn)[l],
                        "wmg": wmg, "wo": f(w_out)[l],
                        "gsel": np.ascontiguousarray(modT[:, l, 64:96][:, :, [b, 2]])})
        resC = _run(ncC, ins)
        xT = [r["xo"] for r in resC]
    ncF = build_A(1024, 64, False)
    fgl = np.ascontiguousarray(f(final_g).reshape(32, 128).T)
    zero = np.zeros((128, 96, 2), np.float32)
    resF = _run(ncF, [{"xT": xT[i], "ng": fgl, "ms": zero} for i in range(8)])
    out = np.zeros((2, 4096, 4096), np.float32)
    for i, (b, j) in enumerate(cores):
        o = resF[i]["hT"].reshape(4096, 1088)[:, 0:1024]
        out[b, j * 1024:(j + 1) * 1024] = o.T
    return out


GROUPS = [[0, 1, 2, 3], [4, 5, 6, 7]]
NK = 8


def _blocks(nctx, tall, step=512):
    bl = [(0, nctx, 1)]
    o = nctx
    while o < tall:
        n = min(step, tall - o)
        bl.append((o, n, 0))
        o += n
    return bl


CW = 256


class TT:
    def __init__(self, P, name, rows, tall, dtype, r0=0, bufs=None):
        self.rows, self.tall, self.r0 = rows, tall, r0
        self.bufs = bufs if bufs is not None else [P.dram(f"{name}_{i}", [rows, CW], dtype) for i in range(tall // CW)]

    def sub(self, r0):
        return TT(None, None, self.rows, self.tall, None, self.r0 + r0, self.bufs)

    def pieces(self, o, n):
        out = []
        so = 0
        while n > 0:
            ci, lo = o // CW, o % CW
            pn = min(n, CW - lo)
            out.append((self.bufs[ci], lo, pn, so))
            o += pn
            n -= pn
            so += pn
        return out


def wr_rows(P, dst, r0, nr, o, n, srcbuf, src_fn, eng="sp"):
    if isinstance(dst, TT):
        for (b, lo, pn, so) in dst.pieces(o, n):
            P.dma(eng, b.t[dst.r0 + r0:dst.r0 + r0 + nr, lo:lo + pn], src_fn(so, pn), reads=[srcbuf], writes=[b])
    else:
        P.dma(eng, dst.t[r0:r0 + nr, o:o + n], src_fn(0, n), reads=[srcbuf], writes=[dst])


def wr_cpn(P, dst, c0, ncn, o, n, srcbuf, src_fn, eng="sp"):
    if isinstance(dst, TT):
        for (b, lo, pn, so) in dst.pieces(o, n):
            P.dma(eng, b.t[dst.r0 + c0 * 128:dst.r0 + (c0 + ncn) * 128, lo:lo + pn].rearrange("(c p) n -> p c n", p=128), src_fn(so, pn),
                  reads=[srcbuf], writes=[b])
    else:
        P.dma(eng, dst.t[c0 * 128:(c0 + ncn) * 128, o:o + n].rearrange("(c p) n -> p c n", p=128), src_fn(0, n), reads=[srcbuf], writes=[dst])


def rd_cpn(P, src, o, n, dstbuf, dst_fn, eng="sp", c0=0, ncn=None):
    if isinstance(src, TT):
        ncn_ = ncn if ncn is not None else src.rows // 128
        for (b, lo, pn, so) in src.pieces(o, n):
            P.dma(eng, dst_fn(so, pn), b.t[src.r0 + c0 * 128:src.r0 + (c0 + ncn_) * 128, lo:lo + pn].rearrange("(c p) n -> p c n", p=128),
                  reads=[b], writes=[dstbuf])
    else:
        P.dma(eng, dst_fn(0, n), src.t[:, :, o:o + n].rearrange("c p n -> p c n"), reads=[src], writes=[dstbuf])


def emit_M2(P, cT, wm, bm, modS):
    sc = P.sbuf("m_sc", [128, 32, 2], F32)
    bs = P.sbuf("m_bs", [128, 2, 24], F32)
    wt = [P.sbuf(f"m_wt{i}", [128, 4, 3072], F32) for i in range(2)]
    P.dma("sp", sc[:], cT[:], reads=[cT], writes=[sc])
    P.dma("sp", bs[:], bm[:], reads=[bm], writes=[bs])
    P.op("act", lambda e: e.activation(sc[:], sc[:], AF.Silu), reads=[sc], writes=[sc])
    it = 0
    for l in range(2):
        for g in range(8):
            w = wt[it % 2]
            pp = P.bank(it % 2)
            it += 1
            P.dma("sp", w[:], wm.t[l, g * 512:(g + 1) * 512, :].rearrange("(k p) n -> p k n", p=128), reads=[wm], writes=[w])
            for j in range(24):
                for k in range(4):
                    kc = g * 4 + k
                    P.op("pe", lambda e, pp=pp, w=w, j=j, k=k, kc=kc: e.matmul(
                        pp[:, j * 2:j * 2 + 2], w[:, k, j * 128:(j + 1) * 128], sc[:, kc, :], start=(k == 0), stop=(k == 3)),
                        reads=[w, sc], writes=[pp], signal=(k == 3 and j == 23))
            dst = modS[:, l].rearrange("p a b -> p (a b)")
            if g == 0:
                P.op("dve", lambda e, pp=pp, dst=dst: e.tensor_copy(dst, pp[:, 0:48]), reads=[pp], writes=[modS])
            else:
                P.op("dve", lambda e, pp=pp, dst=dst: e.tensor_tensor(dst, dst, pp[:, 0:48], ALU.add), reads=[pp, modS], writes=[modS])
    for l in range(2):
        for r in range(2):
            P.op("dve", lambda e, l=l, r=r: e.tensor_tensor(modS[:, l, :, r], modS[:, l, :, r], bs[:, l, :], ALU.add),
                 reads=[modS, bs], writes=[modS])


def emit_A2(P, xs, Av, Bv, arin, arout, dst, dst_dt, tall, nctx, lat_only=False, ag_out=None):
    onesm = P.sbuf("a_ones", [128, 128], F32)
    epsb = P.sbuf("a_eps", [128, 1], F32)
    ss = P.sbuf("a_ss", [128, tall], F32)
    xb = [P.sbuf(f"a_xb{i}", [128, NK, 512], F32) for i in range(2)]
    sq = [P.sbuf(f"a_sq{i}", [128, 512], F32) for i in range(2)]
    tmp = [P.sbuf(f"a_tmp{i}", [128, 512], F32) for i in range(2)]
    ho = [P.sbuf(f"a_ho{i}", [128, NK, 512], dst_dt) for i in range(2)]
    P.op("dve", lambda e: e.memset(onesm[:], 1.0 / D), writes=[onesm])
    P.op("dve", lambda e: e.memset(epsb[:], EPS), writes=[epsb])
    bl = _blocks(nctx, tall)
    for bi, (o, n, t) in enumerate(bl):
        x = xb[bi % 2]
        pb = P.bank(bi % 2)
        P.dma("sp", x[:, :, 0:n], xs.t[:, :, o:o + n].rearrange("c p n -> p c n"), reads=[xs], writes=[x])
        for k in range(NK):
            s = sq[k % 2]
            P.op("act", lambda e, s=s, x=x, k=k, n=n: e.activation(s[:, 0:n], x[:, k, 0:n], AF.Square), reads=[x], writes=[s])
            P.op("pe", lambda e, s=s, pb=pb, k=k, n=n: e.matmul(pb[:, 0:n], onesm[:], s[:, 0:n], start=(k == 0), stop=(k == NK - 1)),
                 reads=[onesm, s], writes=[pb])
        P.op("dve", lambda e, pb=pb, o=o, n=n: e.tensor_copy(ss[:, o:o + n], pb[:, 0:n]), reads=[pb], writes=[ss])
    qw = tall // 4
    for q in range(4):
        P.dma("sp", arin[q][:], ss[:, q * qw:(q + 1) * qw], reads=[ss], writes=[arin[q]])
        P.collective("AllReduce", ALU.add, GROUPS, arin[q][:], arout[q][:], reads=[arin[q]], writes=[arout[q]])
    for q in range(4):
        P.dma("sp", ss[:, q * qw:(q + 1) * qw], arout[q][:], reads=[arout[q]], writes=[ss])
    P.op("act", lambda e: e.activation(ss[:], ss[:], AF.Sqrt, bias=epsb[:], scale=1.0), reads=[ss, epsb], writes=[ss])
    P.op("dve", lambda e: e.reciprocal(ss[:], ss[:]), reads=[ss], writes=[ss])
    for bi, (o, n, t) in enumerate(bl):
        if lat_only and t == 1:
            continue
        x = xb[bi % 2]
        h = ho[bi % 2]
        P.dma("sp", x[:, :, 0:n], xs.t[:, :, o:o + n].rearrange("c p n -> p c n"), reads=[xs], writes=[x])
        for k in range(NK):
            tm_ = tmp[k % 2]
            P.op("dve", lambda e, tm_=tm_, x=x, k=k, n=n, t=t, o=o: e.scalar_tensor_tensor(
                tm_[:, 0:n], x[:, k, 0:n], Av[:, t, k:k + 1], ss[:, o:o + n], ALU.mult, ALU.mult), reads=[x, Av, ss], writes=[tm_])
            P.op("act", lambda e, tm_=tm_, h=h, k=k, n=n, t=t: e.activation(
                h[:, k, 0:n], tm_[:, 0:n], AF.Identity, bias=Bv[:, t, k:k + 1], scale=1.0), reads=[tm_, Bv], writes=[h])
        oo = o - nctx if lat_only else o
        wr_cpn(P, dst, 0, NK, oo, n, h, lambda so, pn, h=h: h[:, :, so:so + pn])
        if ag_out is not None:
            for ci in range(oo // CW, (oo + n) // CW):
                P.collective("AllGather", ALU.bypass, GROUPS, dst.bufs[ci][:], ag_out.bufs[ci][:], reads=[dst.bufs[ci]], writes=[ag_out.bufs[ci]])


def emit_C2(P, hTb, agbout, wglu, wmg, wps, wpg, wpa, wo, gate, xs_in, xs_out, agmin, agmout, tall, nctx):
    bk = [0]

    def nb():
        bk[0] += 1
        return P.bank(bk[0] % 8)

    ags_o, agl_o, aga_o = agbout

    def agv_rd(src, qn, r, o, n, dstbuf, dst_fn):
        for (b, lo, pn, so) in src.pieces(o, n):
            P.dma("sp", dst_fn(so, pn), b.t.rearrange("(r q p) n -> p r q n", r=4, q=qn, p=128)[:, r, :, lo:lo + pn], reads=[b], writes=[dstbuf])

    sTd = P.dram(f"sTd{P.n_ins}", [1024, tall], BF16)
    P.open_scope()
    wgl = P.sbuf("c_wglu", [128, 8, 1024], BF16)
    P.dma("pool", wgl[:], wglu.t.rearrange("(k p) n -> p k n", p=128), reads=[wglu], writes=[wgl])
    gb_ = [P.sbuf(f"c_gb{i}", [128, 4, 4, 512], BF16) for i in range(2)]
    so_ = [P.sbuf(f"c_so{i}", [128, 8, 512], BF16) for i in range(2)]
    sg = [P.sbuf(f"c_sg{i}", [128, 512], BF16) for i in range(2)]
    o = 0
    it = 0
    while o < tall:
        n = min(512, tall - o)
        a = gb_[it % 2]
        so = so_[it % 2]
        it += 1
        for r in range(4):
            agv_rd(ags_o, 4, r, o, n, a, lambda so, pn, a=a, r=r: a[:, r, :, so:so + pn])
        for oc in range(8):
            pb = nb()
            for kc in range(8):
                P.op("pe", lambda e, pb=pb, a=a, kc=kc, oc=oc, n=n: e.matmul(pb[:, 0:n], wgl[:, kc, oc * 128:(oc + 1) * 128], a[:, kc // 2, kc % 2, 0:n],
                                                                           start=(kc == 0), stop=(kc == 7)),
                     reads=[wgl, a], writes=[pb], signal=(kc == 7))
            s_ = sg[oc % 2]
            P.op("act", lambda e, pb=pb, s_=s_, n=n: e.activation(s_[:, 0:n], pb[:, 0:n], AF.Sigmoid), reads=[pb], writes=[s_])
            P.op("dve", lambda e, s_=s_, a=a, so=so, oc=oc, n=n: e.tensor_tensor(so[:, oc, 0:n], a[:, oc // 2, 2 + oc % 2, 0:n], s_[:, 0:n], ALU.mult),
                 reads=[a, s_], writes=[so])
        P.dma("sp", sTd.t[:, o:o + n].rearrange("(c p) n -> p c n", p=128), so[:, :, 0:n], reads=[so], writes=[sTd])
        o += n
    P.close_scope()
    P.open_scope()
    wmgS = P.sbuf("c_wmg", [128, NDC, 3, 384], BF16)
    wprS = P.sbuf("c_wpr", [128, NDC, 384], BF16)
    hb = [P.sbuf(f"c_hb{i}", [128, NDC, 256], BF16) for i in range(2)]
    sbk = [P.sbuf(f"c_sb{i}", [128, 8, 256], BF16) for i in range(2)]
    ab = [P.sbuf(f"c_ab{i}", [128, 4, 6, 256], BF16) for i in range(2)]
    gs3 = [P.sbuf(f"c_g3{i}", [128, 3, 256], F32) for i in range(2)]
    m1 = P.sbuf("c_m1", [128, 256], F32)
    m2 = P.sbuf("c_m2", [128, 256], F32)
    m3 = P.sbuf("c_m3", [128, 256], F32)
    mo = [P.sbuf(f"c_mo{i}", [128, 3, 256], BF16) for i in range(2)]
    it = 0
    for (k0, nk) in ((0, 3), (3, 3), (6, 2)):
        c0, c1 = k0 * 128, (k0 + nk) * 128
        for br in range(3):
            P.dma("pool", wmgS[:, :, br, 0:nk * 128], wmg.t[:, br * 1024 + c0:br * 1024 + c1].rearrange("(k p) n -> p k n", p=128),
                  reads=[wmg], writes=[wmgS])
        P.dma("pool", wprS[:, 0:8, 0:nk * 128], wps.t[:, c0:c1].rearrange("(k p) n -> p k n", p=128), reads=[wps], writes=[wprS])
        P.dma("pool", wprS[:, 8:16, 0:nk * 128], wpg.t[:, c0:c1].rearrange("(k p) n -> p k n", p=128), reads=[wpg], writes=[wprS])
        P.dma("pool", wprS[:, 16:32, 0:nk * 128], wpa.t[:, c0:c1].rearrange("(k p) n -> p k n", p=128), reads=[wpa], writes=[wprS])
        o = 0
        while o < tall:
            n = min(256, tall - o)
            h = hb[it % 2]
            a = ab[it % 2]
            sb = sbk[it % 2]
            mout = mo[it % 2]
            it += 1
            rd_cpn(P, hTb, o, n, h, lambda so, pn, h=h: h[:, :, so:so + pn])
            P.dma("sp", sb[:, :, 0:n], sTd.t[:, o:o + n].rearrange("(c p) n -> p c n", p=128), reads=[sTd], writes=[sb])
            for r in range(4):
                agv_rd(agl_o, 2, r, o, n, a, lambda so, pn, a=a, r=r: a[:, r, 0:2, so:so + pn])
                agv_rd(aga_o, 4, r, o, n, a, lambda so, pn, a=a, r=r: a[:, r, 2:6, so:so + pn])
            for kk in range(nk):
                g3 = gs3[kk % 2]
                cw = slice(kk * 128, (kk + 1) * 128)
                for br in range(3):
                    pg = nb()
                    for kc in range(NDC):
                        P.op("pe", lambda e, pg=pg, h=h, kc=kc, br=br, cw=cw, n=n: e.matmul(pg[:, 0:n], wmgS[:, kc, br, cw], h[:, kc, 0:n],
                                                                                          start=(kc == 0), stop=(kc == NDC - 1)),
                             reads=[wmgS, h], writes=[pg], signal=(kc == NDC - 1))
                    P.op("act", lambda e, pg=pg, g3=g3, br=br, n=n: e.activation(g3[:, br, 0:n], pg[:, 0:n], AF.Sigmoid), reads=[pg], writes=[g3])
                p1, p2, p3 = nb(), nb(), nb()
                for kc in range(8):
                    P.op("pe", lambda e, p1=p1, sb=sb, kc=kc, cw=cw, n=n: e.matmul(p1[:, 0:n], wprS[:, kc, cw], sb[:, kc, 0:n], start=(kc == 0), stop=(kc == 7)),
                         reads=[wprS, sb], writes=[p1], signal=(kc == 7))
                for kc in range(8):
                    P.op("pe", lambda e, p2=p2, a=a, kc=kc, cw=cw, n=n: e.matmul(p2[:, 0:n], wprS[:, 8 + kc, cw], a[:, kc // 2, kc % 2, 0:n], start=(kc == 0), stop=(kc == 7)),
                         reads=[wprS, a], writes=[p2], signal=(kc == 7))
                for kc in range(16):
                    P.op("pe", lambda e, p3=p3, a=a, kc=kc, cw=cw, n=n: e.matmul(p3[:, 0:n], wprS[:, 16 + kc, cw], a[:, kc // 4, 2 + kc % 4, 0:n], start=(kc == 0), stop=(kc == 15)),
                         reads=[wprS, a], writes=[p3], signal=(kc == 15))
                P.op("dve", lambda e, p1=p1, g3=g3, n=n: e.tensor_tensor(m1[:, 0:n], p1[:, 0:n], g3[:, 0, 0:n], ALU.mult), reads=[p1, g3], writes=[m1])
                P.op("dve", lambda e, p2=p2, g3=g3, n=n: e.tensor_tensor(m2[:, 0:n], p2[:, 0:n], g3[:, 1, 0:n], ALU.mult), reads=[p2, g3], writes=[m2])
                P.op("dve", lambda e, p3=p3, g3=g3, n=n: e.tensor_tensor(m3[:, 0:n], p3[:, 0:n], g3[:, 2, 0:n], ALU.mult), reads=[p3, g3], writes=[m3])
                P.op("pool", lambda e, n=n: e.tensor_tensor(m1[:, 0:n], m1[:, 0:n], m2[:, 0:n], ALU.add), reads=[m1, m2], writes=[m1])
                P.op("pool", lambda e, mout=mout, kk=kk, n=n: e.tensor_tensor(mout[:, kk, 0:n], m1[:, 0:n], m3[:, 0:n], ALU.add), reads=[m1, m3], writes=[mout])
            wr_cpn(P, agmin, k0, nk, o, n, mout, lambda so, pn, mout=mout, nk=nk: mout[:, 0:nk, so:so + pn])
            o += n
    P.close_scope()
    for ci in range(len(agmin.bufs)):
        P.collective("AllGather", ALU.bypass, GROUPS, agmin.bufs[ci][:], agmout.bufs[ci][:], reads=[agmin.bufs[ci]], writes=[agmout.bufs[ci]])
    P.open_scope()
    woS = P.sbuf("c_wo", [128, NDC, 1024], BF16)
    for hf in range(2):
        P.dma("pool", woS[:, hf * 16:(hf + 1) * 16, :], wo.t[hf * 2048:(hf + 1) * 2048, :].rearrange("(k p) n -> p k n", p=128), reads=[wo], writes=[woS])
    mb = [P.sbuf(f"c_mb{i}", [128, NDC, 512], BF16) for i in range(2)]
    xi = [P.sbuf(f"c_xi{i}", [128, NK, 512], F32) for i in range(2)]
    xo = [P.sbuf(f"c_xo{i}", [128, NK, 512], F32) for i in range(2)]
    for bi, (o, n, t) in enumerate(_blocks(nctx, tall)):
        m_, xin, xout = mb[bi % 2], xi[bi % 2], xo[bi % 2]
        rd_cpn(P, agmout, o, n, m_, lambda so, pn, m_=m_: m_[:, :, so:so + pn])
        P.dma("sp", xin[:, :, 0:n], xs_in.t[:, :, o:o + n].rearrange("c p n -> p c n"), reads=[xs_in], writes=[xin])
        for k in range(NK):
            pb = nb()
            for kc in range(NDC):
                P.op("pe", lambda e, pb=pb, m_=m_, kc=kc, k=k, n=n: e.matmul(pb[:, 0:n], woS[:, kc, k * 128:(k + 1) * 128], m_[:, kc, 0:n],
                                                                          start=(kc == 0), stop=(kc == NDC - 1)),
                     reads=[woS, m_], writes=[pb], signal=(kc == NDC - 1))
            P.op("dve", lambda e, pb=pb, xin=xin, xout=xout, k=k, n=n, t=t: e.scalar_tensor_tensor(
                xout[:, k, 0:n], pb[:, 0:n], gate[:, t, k:k + 1], xin[:, k, 0:n], ALU.mult, ALU.add), reads=[pb, gate, xin], writes=[xout])
        P.dma("sp", xs_out.t[:, :, o:o + n].rearrange("c p n -> p c n"), xout[:, :, 0:n], reads=[xout], writes=[xs_out])
    P.close_scope()


def build_fused(nlat=NLAT, nctx=NCTX, nlayers=2, stop=99):
    nc = _nc()
    P = Prog(nc)
    tall = nlat + nctx
    I = "ExternalInput"
    xs0 = P.dram("xs", [NK, 128, tall], F32, I)
    cT = P.dram("cT", [128, 32, 2], F32, I)
    wm = P.dram("wm", [2, D, 3072], F32, I)
    bm = P.dram("bm", [128, 2, 24], F32, I)
    ngd = P.dram("ng", [128, 3, NK], F32, I)
    cosT = P.dram("cosT", [128, nlat], F32, I)
    sinT = P.dram("sinT", [128, nlat], F32, I)
    rmT = P.dram("rmT", [128, 128], F32, I)
    s5k = P.dram("s5k", [128, 260], F32, I)
    glk = P.dram("glk", [128, 384], F32, I)
    cmk = P.dram("cmk", [64, tall], F32, I)
    L = []
    for l in range(nlayers):
        d = {}
        for (nm, shp) in (("wq", [D, TM_W + FM_W]), ("wmg", [D, 3072]), ("wglu", [1024, 1024]), ("wps", [1024, 1024]),
                          ("wpg", [1024, 1024]), ("wpa", [2048, 1024]), ("wo", [D, 1024]), ("gqk", [128, 2]),
                          ("s5p", [128, 32, 4]), ("s5b", [128, 32, 2, 16]), ("s5c", [128, 32, 16]), ("s5d", [16, 16]),
                          ("glw", [16, 4, 64]), ("glb", [64, 4]), ("glg", [128, 1])):
            d[nm] = P.dram(f"{nm}{l}", shp, F32, I)
        L.append(d)
    out = P.dram("out", [NK * 128, nlat], F32, "ExternalOutput")
    modS = P.sbuf("modS", [128, 2, 24, 2], F32)
    ngs = P.sbuf("ngs", [128, 3, NK], F32)
    Av = P.sbuf("Av", [128, 2, NK], F32)
    Bv = P.sbuf("Bv", [128, 2, NK], F32)
    Gv = P.sbuf("Gv", [128, 2, NK], F32)
    P.dma("sp", ngs[:], ngd[:], reads=[ngd], writes=[ngs])
    P.open_scope()
    emit_M2(P, cT, wm, bm, modS)
    P.close_scope()
    xs = xs0
    for l in range(nlayers):
        W = L[l]
        for t in range(2):
            P.op("dve", lambda e, t=t, l=l: e.scalar_tensor_tensor(Av[:, t, :], modS[:, l, 8:16, t], 1.0, ngs[:, l, :], ALU.add, ALU.mult),
                 reads=[modS, ngs], writes=[Av])
            P.op("dve", lambda e, t=t, l=l: e.tensor_copy(Bv[:, t, :], modS[:, l, 0:8, t]), reads=[modS], writes=[Bv])
            P.op("dve", lambda e, t=t, l=l: e.tensor_copy(Gv[:, t, :], modS[:, l, 16:24, t]), reads=[modS], writes=[Gv])
        arin = [P.dram(f"arin{l}_{q}", [128, tall // 4], F32) for q in range(4)]
        arout = [P.dram(f"arout{l}_{q}", [128, tall // 4], F32) for q in range(4)]
        aghin = TT(P, f"aghin{l}", NK * 128, tall, BF16)
        aghout = TT(P, f"aghout{l}", NDC * 128, tall, BF16)
        P.open_scope()
        emit_A2(P, xs, Av, Bv, arin, arout, aghin, BF16, tall, nctx, ag_out=aghout)
        P.close_scope()
        if stop == 2:
            break
        hTb = aghout
        tm = P.dram(f"tm{l}", [tall, TM_W], F32)
        fm = P.dram(f"fm{l}", [FM_W, tall], F32)
        ags_i, ags_o = TT(P, f"agsi{l}", 512, tall, BF16), TT(P, f"agso{l}", 4 * 512, tall, BF16)
        agl_i, agl_o = TT(P, f"agli{l}", 256, tall, BF16), TT(P, f"aglo{l}", 4 * 256, tall, BF16)
        aga_i, aga_o = TT(P, f"agai{l}", 512, tall, BF16), TT(P, f"agao{l}", 4 * 512, tall, BF16)
        gT, gsT, glT, atT = ags_i.sub(0), ags_i.sub(256), agl_i, aga_i

        def ag_all(ti, to):
            for ci in range(len(ti.bufs)):
                P.collective("AllGather", ALU.bypass, GROUPS, ti.bufs[ci][:], to.bufs[ci][:], reads=[ti.bufs[ci]], writes=[to.bufs[ci]])
        P.open_scope()
        emit_B1(P, hTb, W["wq"], tm, fm, tall)
        P.close_scope()
        P.open_scope()
        emit_s5(P, fm, W["s5p"], W["s5b"], W["s5c"], W["s5d"], s5k, gT, gsT, tall, nctx)
        P.close_scope()
        ag_all(ags_i, ags_o)
        P.open_scope()
        emit_gla(P, fm, tm, W["glw"], W["glb"], W["glg"], glk, cmk, glT, tall, nctx)
        P.close_scope()
        ag_all(agl_i, agl_o)
        P.open_scope()
        emit_attn(P, fm, tm, cosT, sinT, rmT, W["gqk"], atT, l < nlayers - 1, tall, nctx)
        P.close_scope()
        ag_all(aga_i, aga_o)
        agmin = TT(P, f"agmin{l}", NK * 128, tall, BF16)
        agmout = TT(P, f"agmout{l}", NDC * 128, tall, BF16)
        xs_new = P.dram(f"xs{l + 1}", [NK, 128, tall], F32)
        emit_C2(P, hTb, (ags_o, agl_o, aga_o), W["wglu"], W["wmg"], W["wps"], W["wpg"], W["wpa"], W["wo"], Gv, xs, xs_new, agmin, agmout, tall, nctx)
        xs = xs_new
    if stop < 99:
        P.finish([])
        P.emit()
        return nc
    P.op("dve", lambda e: e.memset(Bv[:], 0.0), writes=[Bv])
    for t in range(2):
        P.op("dve", lambda e, t=t: e.tensor_copy(Av[:, t, :], ngs[:, 2, :]), reads=[ngs], writes=[Av])
    arin = [P.dram(f"arinF_{q}", [128, tall // 4], F32) for q in range(4)]
    arout = [P.dram(f"aroutF_{q}", [128, tall // 4], F32) for q in range(4)]
    P.open_scope()
    emit_A2(P, xs, Av, Bv, arin, arout, out, F32, tall, nctx, lat_only=True)
    P.close_scope()
    P.finish([out])
    P.emit()
    return nc


def _fused_inputs(x, c, ctx, c_ctx, norm_g, w_mod, b_mod, w_in, ssm_lam_re, ssm_lam_im, ssm_log_dt, ssm_b_re, ssm_b_im,
                  ssm_c_re, ssm_c_im, ssm_d, ssm_w_glu, gla_w_a, gla_b_a, gla_norm_g, attn_q_g, attn_k_g,
                  w_proj_ssm, w_proj_gla, w_proj_attn, w_out, final_g):
    f = lambda a: np.asarray(a, dtype=np.float32)
    x, c, ctx, c_ctx = f(x), f(c), f(ctx), f(c_ctx)
    nlat, nctx = x.shape[1], ctx.shape[1]
    tall = nlat + nctx
    w_mod, b_mod, w_in, norm_g, final_g = f(w_mod), f(b_mod), f(w_in), f(norm_g), f(final_g)
    cosT, sinT, rm = rope_tables(nlat)
    glk, cmk = gla_consts(tall)
    s5k = s5_consts()
    ins = []
    for i in range(NCORES):
        b, j = i // 4, i % 4
        dsl = slice(j * 1024, (j + 1) * 1024)
        xa = np.concatenate([ctx[b], x[b]], 0)
        d = {"xs": np.ascontiguousarray(xa[:, dsl].T.reshape(NK, 128, tall)),
             "cT": np.ascontiguousarray(np.stack([c[b], c_ctx], 0).reshape(2, 32, 128).transpose(2, 1, 0)),
             "cosT": cosT, "sinT": sinT, "rmT": rm, "s5k": s5k, "glk": glk, "cmk": cmk}
        mcols = np.concatenate([np.arange(p * 4096 + j * 1024, p * 4096 + (j + 1) * 1024) for p in range(3)])
        d["wm"] = np.ascontiguousarray(w_mod[:, :, mcols])
        d["bm"] = np.ascontiguousarray(b_mod[:, mcols].reshape(2, 24, 128).transpose(2, 0, 1))
        d["ng"] = np.ascontiguousarray(np.stack([norm_g[0][dsl], norm_g[1][dsl], final_g[dsl]], 0).reshape(3, NK, 128).transpose(2, 0, 1))
        for l in range(2):
            gcols = np.concatenate([np.arange(_OFF["mg"] + br * 4096 + j * 1024, _OFF["mg"] + br * 4096 + (j + 1) * 1024) for br in range(3)])
            prm, bb, cc, dd = s5_host_layout(f(ssm_lam_re)[l], f(ssm_lam_im)[l], f(ssm_log_dt)[l], f(ssm_b_re)[l], f(ssm_b_im)[l],
                                             f(ssm_c_re)[l], f(ssm_c_im)[l], f(ssm_d)[l], j)
            WA, BA = f(gla_w_a)[l], f(gla_b_a)[l]
            d[f"wq{l}"] = np.ascontiguousarray(w_in[l][:, _cols(j)])
            d[f"wmg{l}"] = np.ascontiguousarray(w_in[l][:, gcols])
            d[f"wglu{l}"] = np.ascontiguousarray(f(ssm_w_glu)[l])
            d[f"wps{l}"] = np.ascontiguousarray(f(w_proj_ssm)[l][:, dsl])
            d[f"wpg{l}"] = np.ascontiguousarray(f(w_proj_gla)[l][:, dsl])
            d[f"wpa{l}"] = np.ascontiguousarray(f(w_proj_attn)[l][:, dsl])
            d[f"wo{l}"] = np.ascontiguousarray(f(w_out)[l][:, dsl])
            d[f"gqk{l}"] = np.ascontiguousarray(np.stack([f(attn_q_g)[l], f(attn_k_g)[l]], 1))
            d[f"s5p{l}"], d[f"s5b{l}"], d[f"s5c{l}"], d[f"s5d{l}"] = prm, bb, cc, dd
            d[f"glw{l}"] = np.ascontiguousarray(WA[:, :, 128 * j:128 * (j + 1)].reshape(2, 16, 2, 64).transpose(1, 0, 2, 3).reshape(16, 4, 64))
            d[f"glb{l}"] = np.ascontiguousarray(BA[:, 128 * j:128 * (j + 1)].reshape(4, 64).T)
            d[f"glg{l}"] = np.ascontiguousarray(f(gla_norm_g)[l].reshape(128, 1))
        ins.append(d)
    return ins, nlat, nctx


def kernel_fused(**inputs):
    ins, nlat, nctx = _fused_inputs(**inputs)
    import os
    nc = build_fused(nlat, nctx, stop=int(os.environ.get("FSTOP", "99")))
    res = _run(nc, ins)
    out = np.zeros((2, nlat, D), np.float32)
    for i in range(NCORES):
        b, j = i // 4, i % 4
        out[b][:, j * 1024:(j + 1) * 1024] = res[i]["out"].T
    return out


kernel_unfused = kernel


def kernel(**inputs):
    return kernel_fused(**inputs)
```

```python
import numpy as np
import concourse.bass as bass
import concourse.mybir as mybir
from concourse.bass_utils import run_bass_kernel_spmd
from contextlib import ExitStack

F32 = mybir.dt.float32
BF16 = mybir.dt.bfloat16
AF = mybir.ActivationFunctionType
ALU = mybir.AluOpType
AX = mybir.AxisListType

SEM_WRAP = 30000


class Buf:
    __slots__ = ("t", "name", "lw", "rd", "root")

    def __init__(self, t, name="", root=None):
        self.t = t
        self.name = name
        self.lw = None
        self.rd = []
        self.root = root if root is not None else self

    def alias(self, ap):
        return Buf(ap, self.name + "_v", root=self.root)

    def __getitem__(self, idx):
        return self.t[idx]


class Prog:
    ENGS = ("pe", "act", "dve", "pool", "sp")

    def __init__(self, nc, ndma_slots=12):
        self.nc = nc
        self.stack = ExitStack()
        self.streams = {e: [] for e in self.ENGS}
        self.sems = []
        self.eng_sem = {}
        self.eng_cnt = {}
        self.waited = {e: {} for e in self.ENGS}
        self.ndma_slots = ndma_slots
        self.dma_slots = {}
        self.dma_n = {}
        self.n_ins = 0
        self.pending = {e: [] for e in self.ENGS}
        self.scopes = []
        self.banks = None

    def bank(self, i):
        if self.banks is None:
            self.banks = [self.psum(f"bank{k}", [128, 512]) for k in range(8)]
        return self.banks[i]

    def barrier(self):
        evs = []
        for e, s in self.eng_sem.items():
            if self.eng_cnt[e] > 0:
                evs.append((s, self.eng_cnt[e]))
        for e, slots in self.dma_slots.items():
            n = self.dma_n[e]
            for k, s in enumerate(slots):
                cnt = (n - k + self.ndma_slots - 1) // self.ndma_slots if n > k else 0
                if cnt > 0:
                    evs.append((s, 16 * cnt))
        evs += getattr(self, "cc_events", [])
        for e in self.ENGS:
            own = self.eng_sem.get(e)
            w = self.waited[e]
            for (s, v) in evs:
                if s == own:
                    continue
                if w.get(s, 0) < v:
                    w[s] = v
                    self.pending[e].append((s, v))

    def open_scope(self):
        self.scopes.append(ExitStack())

    def close_scope(self):
        self.barrier()
        self.emit_segment()
        self.scopes.pop().close()

    def new_sem(self, name):
        s = self.stack.enter_context(self.nc.semaphore(name))
        self.sems.append(s)
        return len(self.sems) - 1

    def sbuf(self, name, shape, dtype):
        st = self.scopes[-1] if self.scopes else self.stack
        self.uid = getattr(self, "uid", 0) + 1
        t = st.enter_context(self.nc.sbuf_tensor(f"{name}_{self.uid}", list(shape), dtype))
        return Buf(t, name)

    def psum(self, name, shape, dtype=F32):
        t = self.stack.enter_context(self.nc.psum_tensor(name, list(shape), dtype))
        return Buf(t, name)

    def dram(self, name, shape, dtype, kind="Internal"):
        t = self.nc.dram_tensor(name, list(shape), dtype, kind=kind)
        return Buf(t.ap(), name)

    def view(self, ap, name=""):
        return Buf(ap, name)

    def _collect_waits(self, eng, reads, writes):
        evs = []
        for b in reads:
            b = b.root
            if b.lw is not None:
                evs.append(b.lw)
        for b in writes:
            b = b.root
            if b.lw is not None:
                evs.append(b.lw)
            evs.extend(b.rd)
        need = {}
        w = self.waited[eng]
        own = self.eng_sem.get(eng)
        for (s, v) in evs:
            if w.get(s, 0) >= v:
                continue
            if s == own and (eng == "pe" or v > self.eng_cnt[eng]):
                continue
            if need.get(s, 0) < v:
                need[s] = v
        for s, v in need.items():
            w[s] = v
        return list(need.items())

    def _mark(self, ev, reads, writes):
        for b in writes:
            b = b.root
            b.lw = ev
            b.rd = []
        for b in reads:
            b = b.root
            b.rd.append(ev)
            if len(b.rd) > 64:
                m = {}
                for (s, v) in b.rd:
                    if m.get(s, 0) < v:
                        m[s] = v
                b.rd = list(m.items())

    def op(self, eng, fn, reads=(), writes=(), signal=True):
        waits = self._collect_waits(eng, reads, writes)
        if self.pending[eng]:
            waits = waits + self.pending[eng]
            self.pending[eng] = []
        ev = None
        inc = None
        if signal:
            if eng not in self.eng_sem or self.eng_cnt[eng] >= SEM_WRAP:
                self.eng_sem[eng] = self.new_sem(f"s_{eng}_{len(self.sems)}")
                self.eng_cnt[eng] = 0
            self.eng_cnt[eng] += 1
            s = self.eng_sem[eng]
            ev = (s, self.eng_cnt[eng])
            inc = (s, 1)
        else:
            if eng not in self.eng_sem or self.eng_cnt[eng] >= SEM_WRAP:
                self.eng_sem[eng] = self.new_sem(f"s_{eng}_{len(self.sems)}")
                self.eng_cnt[eng] = 0
            ev = (self.eng_sem[eng], self.eng_cnt[eng] + 1)
        self.streams[eng].append((fn, waits, inc))
        self._mark(ev, reads, writes)
        self.n_ins += 1
        return ev

    def dma(self, eng, out_ap, in_ap, reads=(), writes=(), **kw):
        if eng not in self.dma_slots:
            self.dma_slots[eng] = [self.new_sem(f"d_{eng}_{i}") for i in range(self.ndma_slots)]
            self.dma_n[eng] = 0
        i = self.dma_n[eng]
        self.dma_n[eng] += 1
        slot = i % self.ndma_slots
        s = self.dma_slots[eng][slot]
        gen = i // self.ndma_slots
        waits = self._collect_waits(eng, reads, writes)
        if self.pending[eng]:
            waits = waits + self.pending[eng]
            self.pending[eng] = []
        if gen > 0:
            w = self.waited[eng]
            if w.get(s, 0) < 16 * gen:
                w[s] = 16 * gen
                waits = [x for x in waits if x[0] != s] + [(s, 16 * gen)]
        ev = (s, 16 * (gen + 1))

        def fn(e, out_ap=out_ap, in_ap=in_ap, kw=kw):
            return e.dma_start(out=out_ap, in_=in_ap, **kw)

        self.streams[eng].append((fn, waits, (s, 16)))
        self._mark(ev, reads, writes)
        self.n_ins += 1
        return ev

    def collective(self, kind, op, groups, in_ap, out_ap, reads=(), writes=(), inc=1):
        eng = "pool"
        NS = 4
        if not hasattr(self, "cc_slots"):
            self.cc_slots = [self.new_sem(f"cc_{i}") for i in range(NS)]
            self.cc_n = 0
        i = self.cc_n
        self.cc_n += 1
        s = self.cc_slots[i % NS]
        gen = i // NS
        waits = self._collect_waits(eng, reads, writes)
        if self.pending[eng]:
            waits = waits + self.pending[eng]
            self.pending[eng] = []
        if gen > 0:
            w = self.waited[eng]
            if w.get(s, 0) < gen:
                w[s] = gen
                waits = [x for x in waits if x[0] != s] + [(s, gen)]
        ev = (s, gen + 1)

        def fn(e):
            return e.collective_compute(kind, op, replica_groups=groups, ins=[in_ap], outs=[out_ap])

        self.streams[eng].append((fn, waits, (s, 1)))
        self._mark(ev, reads, writes)
        self.cc_events = [(self.cc_slots[k], (self.cc_n - k + NS - 1) // NS) for k in range(NS) if self.cc_n > k]
        self.n_ins += 1
        return ev

    def finish(self, final_bufs):
        evs = []
        for b in final_bufs:
            if b.root.lw is not None:
                evs.append(b.root.lw)
        self.final_waits = evs

    def emit(self):
        self.emit_segment(final=True)
        self.stack.close()

    def emit_segment(self, final=False):
        nc = self.nc
        streams = self.streams
        self.streams = {e: [] for e in self.ENGS}
        sems = self.sems
        final_waits = getattr(self, "final_waits", []) if final else []

        def run(e, name):
            for (fn, waits, inc) in streams[name]:
                for (s, v) in waits:
                    e.wait_ge(sems[s], v)
                ins = fn(e)
                if inc is not None:
                    ins.then_inc(sems[inc[0]], inc[1])
            if name == "sp":
                for (s, v) in final_waits:
                    e.wait_ge(sems[s], v)

        with nc.Block() as block:
            @block.tensor
            def _(e):
                run(e, "pe")

            @block.scalar
            def _(e):
                run(e, "act")

            @block.vector
            def _(e):
                run(e, "dve")

            @block.gpsimd
            def _(e):
                run(e, "pool")

            @block.sync
            def _(e):
                run(e, "sp")


NCORES = 8
D = 4096
NDC = 32
EPS = 1e-6


def _nc():
    return bass.Bass("TRN2", target_bir_lowering=False)


def build_M():
    nc = _nc()
    P = Prog(nc)
    cT = P.dram("cT", [128, 32, 3], F32, "ExternalInput")
    wm = P.dram("wm", [2, 4096, 1536], F32, "ExternalInput")
    bm = P.dram("bm", [128, 2, 12], F32, "ExternalInput")
    out = P.dram("modT", [128, 2, 12, 3], F32, "ExternalOutput")
    sc = P.sbuf("sc", [128, 32, 3], F32)
    bs = P.sbuf("bs", [128, 2, 12], F32)
    acc = P.sbuf("acc", [128, 2, 12, 3], F32)
    wt = [P.sbuf(f"wt{i}", [128, 4, 1536], F32) for i in range(2)]
    ps = [P.psum(f"ps{i}", [128, 12, 3]) for i in range(2)]
    P.dma("sp", sc[:], cT[:], reads=[cT], writes=[sc])
    P.dma("sp", bs[:], bm[:], reads=[bm], writes=[bs])
    P.op("act", lambda e: e.activation(sc[:], sc[:], AF.Silu), reads=[sc], writes=[sc])
    it = 0
    for l in range(2):
        for g in range(8):
            w = wt[it % 2]
            pp = ps[it % 2]
            src = wm.t[l, g * 512:(g + 1) * 512, :].rearrange("(k p) n -> p k n", p=128)
            P.dma("sp", w[:], src, reads=[wm], writes=[w])
            for j in range(12):
                for k in range(4):
                    kc = g * 4 + k
                    P.op("pe", lambda e, pp=pp, w=w, j=j, k=k, kc=kc: e.matmul(
                        pp[:, j, :], w[:, k, j * 128:(j + 1) * 128], sc[:, kc, :],
                        start=(k == 0), stop=(k == 3)),
                        reads=[w, sc], writes=[pp], signal=(k == 3 and j == 11))
            if g == 0:
                P.op("dve", lambda e, pp=pp, l=l: e.tensor_copy(acc[:, l], pp[:]), reads=[pp], writes=[acc])
            else:
                P.op("dve", lambda e, pp=pp, l=l: e.tensor_tensor(acc[:, l], acc[:, l], pp[:], ALU.add),
                     reads=[pp, acc], writes=[acc])
            it += 1
    for l in range(2):
        for r in range(3):
            P.op("dve", lambda e, l=l, r=r: e.tensor_tensor(acc[:, l, :, r], acc[:, l, :, r], bs[:, l, :], ALU.add),
                 reads=[acc, bs], writes=[acc])
    P.dma("sp", out[:], acc[:], reads=[acc], writes=[out])
    P.finish([out])
    P.emit()
    return nc


def build_A(nlat=1024, nctx=64, out_bf16=True):
    nc = _nc()
    P = Prog(nc)
    NT = nlat + nctx
    odt = BF16 if out_bf16 else F32
    xT = P.dram("xT", [NDC, 128, NT], F32, "ExternalInput")
    ng = P.dram("ng", [128, NDC], F32, "ExternalInput")
    ms = P.dram("ms", [128, 96, 2], F32, "ExternalInput")
    hT = P.dram("hT", [NDC, 128, NT], odt, "ExternalOutput")
    emit_A(P, xT, ng, ms, hT, nlat, nctx, odt)
    P.finish([hT])
    P.emit()
    return nc


def emit_A(P, xT, ng, ms, hT, nlat, nctx, odt):
    ngs = P.sbuf("a_ng", [128, NDC], F32)
    mss = P.sbuf("a_ms", [128, 96, 2], F32)
    Av = P.sbuf("a_A", [128, 2, NDC], F32)
    Bv = P.sbuf("a_B", [128, 2, NDC], F32)
    onesm = P.sbuf("a_ones", [128, 128], F32)
    epsb = P.sbuf("a_eps", [128, 1], F32)
    xs = P.sbuf("a_xs", [128, NDC, 512], F32)
    ho = P.sbuf("a_ho", [128, NDC, 512], odt)
    sq = [P.sbuf(f"a_sq{i}", [128, 512], F32) for i in range(2)]
    tmp = [P.sbuf(f"a_tmp{i}", [128, 512], F32) for i in range(2)]
    rstd = P.sbuf("a_rstd", [128, 512], F32)
    pss = P.psum("a_pss", [128, 512])
    P.dma("sp", ngs[:], ng[:], reads=[ng], writes=[ngs])
    P.dma("sp", mss[:], ms[:], reads=[ms], writes=[mss])
    P.op("dve", lambda e: e.memset(onesm[:], 1.0 / D), writes=[onesm])
    P.op("dve", lambda e: e.memset(epsb[:], EPS), writes=[epsb])
    for t in range(2):
        P.op("dve", lambda e, t=t: e.scalar_tensor_tensor(Av[:, t, :], mss[:, 32:64, t], 1.0, ngs[:], ALU.add, ALU.mult),
             reads=[mss, ngs], writes=[Av])
        P.op("dve", lambda e, t=t: e.tensor_copy(Bv[:, t, :], mss[:, 0:32, t]), reads=[mss], writes=[Bv])
    blocks = []
    o = 0
    while o < nlat:
        n = min(512, nlat - o)
        blocks.append((o, n, 0))
        o += n
    while o < nlat + nctx:
        n = min(512, nlat + nctx - o)
        blocks.append((o, n, 1))
        o += n
    for (o, n, t) in blocks:
        P.dma("sp", xs[:, :, 0:n], xT.t[:, :, o:o + n].rearrange("c p n -> p c n"), reads=[xT], writes=[xs])
        for dc in range(NDC):
            s = sq[dc % 2]
            P.op("act", lambda e, s=s, dc=dc, n=n: e.activation(s[:, 0:n], xs[:, dc, 0:n], AF.Square), reads=[xs], writes=[s])
            P.op("pe", lambda e, s=s, dc=dc, n=n: e.matmul(pss[:, 0:n], onesm[:], s[:, 0:n], start=(dc == 0), stop=(dc == NDC - 1)),
                 reads=[onesm, s], writes=[pss])
        P.op("act", lambda e, n=n: e.activation(rstd[:, 0:n], pss[:, 0:n], AF.Sqrt, bias=epsb[:], scale=1.0), reads=[pss, epsb], writes=[rstd])
        P.op("dve", lambda e, n=n: e.reciprocal(rstd[:, 0:n], rstd[:, 0:n]), reads=[rstd], writes=[rstd])
        for dc in range(NDC):
            tm = tmp[dc % 2]
            P.op("dve", lambda e, tm=tm, dc=dc, n=n, t=t: e.scalar_tensor_tensor(
                tm[:, 0:n], xs[:, dc, 0:n], Av[:, t, dc:dc + 1], rstd[:, 0:n], ALU.mult, ALU.mult),
                reads=[xs, Av, rstd], writes=[tm])
            P.op("act", lambda e, tm=tm, dc=dc, n=n, t=t: e.activation(
                ho[:, dc, 0:n], tm[:, 0:n], AF.Identity, bias=Bv[:, t, dc:dc + 1], scale=1.0),
                reads=[tm, Bv], writes=[ho])
        P.dma("sp", hT.t[:, :, o:o + n].rearrange("c p n -> p c n"), ho[:, :, 0:n], reads=[ho], writes=[hT])


TALL = 4352
NCTX = 256
NLAT = 4096
TM_W = 384
C_GV, C_AV = 0, 256
FM_W = 2208
R_GQ, R_GK, R_GZ, R_AQ, R_AK, R_AZ, R_SU, R_SZ, R_LR = 0, 128, 256, 512, 1024, 1152, 1664, 1920, 2176


def emit_B1(P, hTb, wq, tm, fm, tall=TALL):
    hb = [P.sbuf(f"b1_h{i}", [128, NDC, 512], BF16) for i in range(2)]
    Ws = [P.sbuf(f"b1_w{i}", [128, NDC, 1024], BF16) for i in range(2)]
    stg = [P.sbuf(f"b1_s{i}", [128, 512], F32) for i in range(3)]
    ps = [(P.bank(0), P.bank(1)), (P.bank(2), P.bank(3))]
    nblk = (tall + 511) // 512
    passes = [("tm", 0, TM_W), ("fm", TM_W, 1024), ("fm", TM_W + 1024, 1024), ("fm", TM_W + 2048, FM_W - 2048)]
    it = 0
    si = 0
    for pi_, (kind, c0, ncol) in enumerate(passes):
        W = Ws[pi_ % 2]
        for half in range(2):
            P.dma("pool", W[:, half * 16:(half + 1) * 16, 0:ncol],
                  wq.t[half * 2048:(half + 1) * 2048, c0:c0 + ncol].rearrange("(k p) n -> p k n", p=128),
                  reads=[wq], writes=[W])
        for tb in range(nblk):
            o = tb * 512
            n = min(512, tall - o)
            h = hb[it % 2]
            it += 1
            rd_cpn(P, hTb, o, n, h, lambda so, pn, h=h: h[:, :, so:so + pn])
            if kind == "tm":
                for sub in range(n // 128):
                    pp = ps[si % 2]
                    st = stg[si % 3]
                    si += 1
                    for (b0, bw, bank) in ((0, ncol, 0),):
                        for kc in range(NDC):
                            P.op("pe", lambda e, pp=pp, h=h, kc=kc, sub=sub, b0=b0, bw=bw, bank=bank, W=W: e.matmul(
                                pp[bank][:, 0:bw], h[:, kc, sub * 128:(sub + 1) * 128], W[:, kc, b0:b0 + bw],
                                start=(kc == 0), stop=(kc == NDC - 1)),
                                reads=[h, W], writes=[pp[bank]], signal=(kc == NDC - 1))
                    P.op("act", lambda e, pp=pp, st=st, ncol=ncol: e.activation(st[:, 0:ncol], pp[0][:, 0:ncol], AF.Copy), reads=[pp[0]], writes=[st])
                    r0 = o + sub * 128
                    P.dma("sp", tm.t[r0:r0 + 128, 0:ncol], st[:, 0:ncol], reads=[st], writes=[tm])
            else:
                r_base = c0 - TM_W
                ncc = (ncol + 127) // 128
                for cc in range(ncc):
                    m = min(128, ncol - cc * 128)
                    pp = P.bank(si % 4)
                    st = stg[si % 3]
                    for kc in range(NDC):
                        P.op("pe", lambda e, pp=pp, h=h, kc=kc, cc=cc, m=m, n=n, W=W: e.matmul(
                            pp[0:m, 0:n], W[:, kc, cc * 128:cc * 128 + m], h[:, kc, 0:n],
                            start=(kc == 0), stop=(kc == NDC - 1)),
                            reads=[h, W], writes=[pp], signal=(kc == NDC - 1))
                    if si % 2 == 0:
                        P.op("act", lambda e, pp=pp, st=st, m=m, n=n: e.activation(st[0:m, 0:n], pp[0:m, 0:n], AF.Copy), reads=[pp], writes=[st])
                    else:
                        P.op("dve", lambda e, pp=pp, st=st, m=m, n=n: e.tensor_copy(st[0:m, 0:n], pp[0:m, 0:n]), reads=[pp], writes=[st])
                    si += 1
                    rr = r_base + cc * 128
                    P.dma("sp", fm.t[rr:rr + m, o:o + n], st[0:m, 0:n], reads=[st], writes=[fm])


def emit_attn(P, fm, tm, cosT, sinT, rmT, gqk, atT, ctx_out, tall=TALL, nctx=NCTX):
    nlat = tall - nctx
    ntile = tall // 128
    cs = P.sbuf("at_cos", [128, nlat], F32)
    sn = P.sbuf("at_sin", [128, nlat], F32)
    rm = P.sbuf("at_rm", [128, 128], F32)
    g2 = P.sbuf("at_g", [128, 2], F32)
    ones_f = P.sbuf("at_1f", [128, 128], F32)
    ones_b = P.sbuf("at_1b", [128, 128], BF16)
    epsb = P.sbuf("at_eps", [128, 1], F32)
    KT = P.sbuf("at_KT", [128, tall], BF16)
    V = P.sbuf("at_V", [128, ntile, 128], BF16)
    QT = [P.sbuf(f"at_QT{i}", [128, 512], BF16) for i in range(2)]
    xs = [P.sbuf(f"at_xs{i}", [128, 512], F32) for i in range(2)]
    sq = P.sbuf("at_sq", [128, 512], F32)
    rs = P.sbuf("at_rs", [128, 512], F32)
    xn = P.sbuf("at_xn", [128, 512], F32)
    t1 = P.sbuf("at_t1", [128, 512], F32)
    t2 = P.sbuf("at_t2", [128, 512], F32)
    pb = [P.sbuf(f"at_p{i}", [128, 512], BF16) for i in range(3)]
    az = P.sbuf("at_az", [128, 512], F32)
    rl = P.sbuf("at_rl", [128, 512], F32)
    ob = P.sbuf("at_ob", [128, 512], F32)
    oo = [P.sbuf(f"at_oo{i}", [128, 512], BF16) for i in range(2)]
    ps_ms, ps_rot = P.bank(0), P.bank(1)
    ps_s = [P.bank(2), P.bank(3), P.bank(4)]
    ps_o, ps_l = P.bank(5), P.bank(6)
    P.dma("sp", cs[:], cosT[:], reads=[cosT], writes=[cs])
    P.dma("sp", sn[:], sinT[:], reads=[sinT], writes=[sn])
    P.dma("sp", rm[:], rmT[:], reads=[rmT], writes=[rm])
    P.dma("sp", g2[:], gqk[:], reads=[gqk], writes=[g2])
    P.op("dve", lambda e: e.memset(ones_f[:], 1.0 / 128), writes=[ones_f])
    P.op("dve", lambda e: e.memset(ones_b[:], 1.0), writes=[ones_b])
    P.op("dve", lambda e: e.memset(epsb[:], EPS), writes=[epsb])
    P.dma("pool", V[:], tm.t[:, C_AV:C_AV + 128].rearrange("(t p) c -> p t c", p=128), reads=[tm], writes=[V])
    cnt = [0]

    def prep(r0, o, n, gi, rope, pos0, dst, dstb):
        x = xs[cnt[0] % 2]
        cnt[0] += 1
        P.dma("sp", x[:, 0:n], fm.t[r0:r0 + 128, o:o + n], reads=[fm], writes=[x])
        P.op("act", lambda e: e.activation(sq[:, 0:n], x[:, 0:n], AF.Square), reads=[x], writes=[sq])
        P.op("pe", lambda e: e.matmul(ps_ms[:, 0:n], ones_f[:], sq[:, 0:n], start=True, stop=True), reads=[ones_f, sq], writes=[ps_ms])
        P.op("act", lambda e: e.activation(rs[:, 0:n], ps_ms[:, 0:n], AF.Sqrt, bias=epsb[:], scale=1.0), reads=[ps_ms, epsb], writes=[rs])
        P.op("dve", lambda e: e.reciprocal(rs[:, 0:n], rs[:, 0:n]), reads=[rs], writes=[rs])
        if not rope:
            P.op("dve", lambda e: e.scalar_tensor_tensor(dst, x[:, 0:n], g2[:, gi:gi + 1], rs[:, 0:n], ALU.mult, ALU.mult),
                 reads=[x, g2, rs], writes=[dstb])
            return
        P.op("dve", lambda e: e.scalar_tensor_tensor(xn[:, 0:n], x[:, 0:n], g2[:, gi:gi + 1], rs[:, 0:n], ALU.mult, ALU.mult),
             reads=[x, g2, rs], writes=[xn])
        P.op("pe", lambda e: e.matmul(ps_rot[:, 0:n], rm[:], xn[:, 0:n], start=True, stop=True), reads=[rm, xn], writes=[ps_rot])
        P.op("pool", lambda e: e.tensor_tensor(t1[:, 0:n], xn[:, 0:n], cs[:, pos0:pos0 + n], ALU.mult), reads=[xn, cs], writes=[t1])
        P.op("dve", lambda e: e.tensor_tensor(t2[:, 0:n], ps_rot[:, 0:n], sn[:, pos0:pos0 + n], ALU.mult), reads=[ps_rot, sn], writes=[t2])
        P.op("dve", lambda e: e.tensor_tensor(dst, t1[:, 0:n], t2[:, 0:n], ALU.add), reads=[t1, t2], writes=[dstb])

    prep(R_AK, 0, nctx, 1, False, 0, KT[:, 0:nctx], KT)
    o = nctx
    while o < tall:
        n = min(512, tall - o)
        prep(R_AK, o, n, 1, True, o - nctx, KT[:, o:o + n], KT)
        o += n
    scale = 128 ** -0.5
    qi = 0
    pi = 0
    for hh in range(4):
        blocks = []
        if ctx_out:
            blocks.append((0, nctx, False, nctx // 128))
        o = nctx
        while o < tall:
            n = min(512, tall - o)
            blocks.append((o, n, True, ntile))
            o += n
        for (o, n, rope, nk) in blocks:
            q = QT[qi % 2]
            qi += 1
            prep(R_AQ + hh * 128, o, n, 0, rope, o - nctx, q[:, 0:n], q)
            tiles = []
            for kc in range(nk):
                tiles.append((ps_s[pi % 3], pb[pi % 3]))
                pi += 1

            def qk(kc):
                s_ps = tiles[kc][0]
                P.op("pe", lambda e, s_ps=s_ps, q=q, kc=kc, n=n: e.matmul(s_ps[:, 0:n], KT[:, kc * 128:(kc + 1) * 128], q[:, 0:n], start=True, stop=True),
                     reads=[KT, q], writes=[s_ps])

            qk(0)
            if nk > 1:
                qk(1)
            for kc in range(nk):
                s_ps, p_sb = tiles[kc]
                P.op("act", lambda e, s_ps=s_ps, p_sb=p_sb, n=n: e.activation(p_sb[:, 0:n], s_ps[:, 0:n], AF.Exp, scale=scale),
                     reads=[s_ps], writes=[p_sb])
                if kc + 2 < nk:
                    qk(kc + 2)
                P.op("pe", lambda e, p_sb=p_sb, kc=kc, n=n, nk=nk: e.matmul(ps_o[:, 0:n], V[:, kc, :], p_sb[:, 0:n], start=(kc == 0), stop=(kc == nk - 1)),
                     reads=[V, p_sb], writes=[ps_o], signal=False)
                P.op("pe", lambda e, p_sb=p_sb, kc=kc, n=n, nk=nk: e.matmul(ps_l[:, 0:n], ones_b[:], p_sb[:, 0:n], start=(kc == 0), stop=(kc == nk - 1)),
                     reads=[ones_b, p_sb], writes=[ps_l])
            r0 = R_AZ + hh * 128
            P.dma("sp", az[:, 0:n], fm.t[r0:r0 + 128, o:o + n], reads=[fm], writes=[az])
            P.op("act", lambda e, n=n: e.activation(az[:, 0:n], az[:, 0:n], AF.Silu), reads=[az], writes=[az])
            P.op("dve", lambda e, n=n: e.reciprocal(rl[:, 0:n], ps_l[:, 0:n]), reads=[ps_l], writes=[rl])
            P.op("dve", lambda e, n=n: e.tensor_tensor(ob[:, 0:n], ps_o[:, 0:n], rl[:, 0:n], ALU.mult), reads=[ps_o, rl], writes=[ob])
            ot = oo[qi % 2]
            P.op("pool", lambda e, n=n, ot=ot: e.tensor_tensor(ot[:, 0:n], ob[:, 0:n], az[:, 0:n], ALU.mult), reads=[ob, az], writes=[ot])
            wr_rows(P, atT, hh * 128, 128, o, n, ot, lambda so, pn, ot=ot: ot[:, so:so + pn])


def rope_tables(nlat):
    rows = nlat // 64
    nf = 32
    row = np.repeat(np.arange(rows, dtype=np.float32), 64)
    col = np.tile(np.arange(64, dtype=np.float32), rows)
    inv = (10000.0 ** (-np.arange(nf, dtype=np.float32) / nf)).astype(np.float32)
    ang = np.stack([row[:, None] * inv, col[:, None] * inv], axis=1)
    ang = np.concatenate([ang, ang], axis=-1).reshape(rows * 64, 128).astype(np.float32)
    cosT = np.ascontiguousarray(np.cos(ang).T.astype(np.float32))
    sinT = np.ascontiguousarray(np.sin(ang).T.astype(np.float32))
    rm = np.zeros((128, 128), np.float32)
    for a in range(2):
        for f in range(32):
            rm[a * 64 + 32 + f, a * 64 + f] = -1.0
            rm[a * 64 + f, a * 64 + 32 + f] = 1.0
    return cosT, sinT, rm


def build_B(tall=TALL, nctx=NCTX, ctx_out=True, parts=("b1", "attn")):
    nc = _nc()
    P = Prog(nc)
    nlat = tall - nctx
    hTb = P.dram("hTb", [NDC, 128, tall], BF16, "ExternalInput")
    wq = P.dram("wq", [D, TM_W + FM_W], F32, "ExternalInput")
    cosT = P.dram("cosT", [128, nlat], F32, "ExternalInput")
    sinT = P.dram("sinT", [128, nlat], F32, "ExternalInput")
    rmT = P.dram("rmT", [128, 128], F32, "ExternalInput")
    gqk = P.dram("gqk", [128, 2], F32, "ExternalInput")
    dbg = "dbg" in parts
    tm = P.dram("tm", [tall, TM_W], F32, "ExternalOutput" if dbg else "Internal")
    fm = P.dram("fm", [FM_W, tall], F32, "ExternalOutput" if dbg else "Internal")
    atT = P.dram("atT", [512, tall], BF16, "ExternalOutput")
    s5p = P.dram("s5p", [128, 32, 4], F32, "ExternalInput")
    s5b = P.dram("s5b", [128, 32, 2, 16], F32, "ExternalInput")
    s5c = P.dram("s5c", [128, 32, 16], F32, "ExternalInput")
    s5d = P.dram("s5d", [16, 16], F32, "ExternalInput")
    s5k = P.dram("s5k", [128, 260], F32, "ExternalInput")
    gT = P.dram("gT", [256, tall], BF16, "ExternalOutput")
    gsT = P.dram("gsT", [256, tall], BF16, "ExternalOutput")
    glw = P.dram("glw", [16, 4, 64], F32, "ExternalInput")
    glb = P.dram("glb", [64, 4], F32, "ExternalInput")
    glg = P.dram("glg", [128, 1], F32, "ExternalInput")
    glk = P.dram("glk", [128, 384], F32, "ExternalInput")
    cmk = P.dram("cmk", [64, tall], F32, "ExternalInput")
    glT = P.dram("glT", [256, tall], BF16, "ExternalOutput")
    outs = [atT, gT, gsT, glT]
    if dbg:
        outs += [tm, fm]
    P.open_scope()
    emit_B1(P, hTb, wq, tm, fm, tall)
    P.close_scope()
    if "attn" in parts:
        P.open_scope()
        emit_attn(P, fm, tm, cosT, sinT, rmT, gqk, atT, ctx_out, tall, nctx)
        P.close_scope()
    if "gla" in parts:
        P.open_scope()
        emit_gla(P, fm, tm, glw, glb, glg, glk, cmk, glT, tall, nctx)
        P.close_scope()
    if "s5" in parts:
        P.open_scope()
        emit_s5(P, fm, s5p, s5b, s5c, s5d, s5k, gT, gsT, tall, nctx)
        P.close_scope()
    P.finish(outs)
    P.emit()
    return nc


def emit_s5(P, fm, s5p, s5b, s5c, s5d, cst, gT, gsT, tall=TALL, nctx=NCTX):
    nlat = tall - nctx
    NP = 32
    c = P.sbuf("s5_cst", [128, 260], F32)
    P.dma("sp", c[:], cst[:], reads=[cst], writes=[c])
    ident, swap = c[:, 0:128], c[:, 128:256]
    sgn, m0, m1, hpi = c[:, 256:257], c[:, 257:258], c[:, 258:259], c[:, 259:260]
    prm = P.sbuf("s5_prm", [128, NP, 4], F32)
    Bd = P.sbuf("s5_B", [128, NP, 2, 16], F32)
    Cw = P.sbuf("s5_C", [128, NP, 16], F32)
    dsk = P.sbuf("s5_d", [16, 16], F32)
    P.dma("sp", prm[:], s5p[:], reads=[s5p], writes=[prm])
    P.dma("sp", Bd[:], s5b[:], reads=[s5b], writes=[Bd])
    P.dma("sp", Cw[:], s5c[:], reads=[s5c], writes=[Cw])
    P.dma("sp", dsk[:], s5d[:], reads=[s5d], writes=[dsk])
    names = ["lr", "dt", "a", "th", "mag", "s", "c", "cc", "ss", "Lr", "Li", "den", "L1", "nr", "ni", "c1r", "c1i", "t"]
    T_ = {k: P.sbuf("s5_" + k, [128, NP], F32) for k in names}
    PR = P.sbuf("s5_PR", [128, 13, NP], F32)
    PI = P.sbuf("s5_PI", [128, 13, NP], F32)
    PIs = P.sbuf("s5_PIs", [128, 13, NP], F32)

    def tt(o, a, b, op, eng="dve"):
        P.op(eng, lambda e: e.tensor_tensor(o[:], a[:], b[:], op), reads=[a, b], writes=[o])

    lre, lim, ldt = prm[:, :, 0], prm[:, :, 1], prm[:, :, 2]
    P.op("dve", lambda e: e.tensor_scalar(T_["lr"][:], lre, -1e-4, None, ALU.min), reads=[prm], writes=[T_["lr"]])
    P.op("act", lambda e: e.activation(T_["dt"][:], ldt, AF.Exp), reads=[prm], writes=[T_["dt"]])
    tt(T_["a"], T_["lr"], T_["dt"], ALU.mult)
    P.op("dve", lambda e: e.tensor_tensor(T_["th"][:], lim, T_["dt"][:], ALU.mult), reads=[prm, T_["dt"]], writes=[T_["th"]])
    P.op("act", lambda e: e.activation(T_["mag"][:], T_["a"][:], AF.Exp), reads=[T_["a"]], writes=[T_["mag"]])
    P.op("act", lambda e: e.activation(T_["s"][:], T_["th"][:], AF.Sin, scale=1.0 / 16), reads=[T_["th"]], writes=[T_["s"]])
    P.op("act", lambda e: e.activation(T_["c"][:], T_["th"][:], AF.Sin, bias=hpi, scale=1.0 / 16), reads=[T_["th"], c], writes=[T_["c"]])
    for _ in range(4):
        tt(T_["cc"], T_["c"], T_["c"], ALU.mult)
        tt(T_["ss"], T_["s"], T_["s"], ALU.mult)
        P.op("dve", lambda e: e.scalar_tensor_tensor(T_["s"][:], T_["c"][:], 2.0, T_["s"][:], ALU.mult, ALU.mult),
             reads=[T_["c"], T_["s"]], writes=[T_["s"]])
        tt(T_["c"], T_["cc"], T_["ss"], ALU.subtract)
    tt(T_["Lr"], T_["mag"], T_["c"], ALU.mult)
    tt(T_["Li"], T_["mag"], T_["s"], ALU.mult)
    tt(T_["den"], T_["lr"], T_["lr"], ALU.mult)
    P.op("dve", lambda e: e.tensor_tensor(T_["t"][:], lim, lim, ALU.mult), reads=[prm], writes=[T_["t"]])
    tt(T_["den"], T_["den"], T_["t"], ALU.add)
    P.op("dve", lambda e: e.reciprocal(T_["den"][:], T_["den"][:]), reads=[T_["den"]], writes=[T_["den"]])
    P.op("dve", lambda e: e.tensor_scalar(T_["L1"][:], T_["Lr"][:], -1.0, None, ALU.add), reads=[T_["Lr"]], writes=[T_["L1"]])
    tt(T_["nr"], T_["L1"], T_["lr"], ALU.mult)
    P.op("dve", lambda e: e.tensor_tensor(T_["t"][:], T_["Li"][:], lim, ALU.mult), reads=[prm, T_["Li"]], writes=[T_["t"]])
    tt(T_["nr"], T_["nr"], T_["t"], ALU.add)
    tt(T_["ni"], T_["Li"], T_["lr"], ALU.mult)
    P.op("dve", lambda e: e.tensor_tensor(T_["t"][:], T_["L1"][:], lim, ALU.mult), reads=[prm, T_["L1"]], writes=[T_["t"]])
    tt(T_["ni"], T_["ni"], T_["t"], ALU.subtract)
    tt(T_["c1r"], T_["nr"], T_["den"], ALU.mult)
    tt(T_["c1i"], T_["ni"], T_["den"], ALU.mult)
    P.op("dve", lambda e: e.tensor_copy(PR[:, 0, :], T_["Lr"][:]), reads=[T_["Lr"]], writes=[PR])
    P.op("dve", lambda e: e.tensor_copy(PI[:, 0, :], T_["Li"][:]), reads=[T_["Li"]], writes=[PI])
    for m in range(12):
        P.op("dve", lambda e, m=m: e.tensor_tensor(T_["cc"][:], PR[:, m, :], PR[:, m, :], ALU.mult), reads=[PR], writes=[T_["cc"]])
        P.op("dve", lambda e, m=m: e.tensor_tensor(T_["ss"][:], PI[:, m, :], PI[:, m, :], ALU.mult), reads=[PI], writes=[T_["ss"]])
        P.op("dve", lambda e, m=m: e.scalar_tensor_tensor(PI[:, m + 1, :], PR[:, m, :], 2.0, PI[:, m, :], ALU.mult, ALU.mult),
             reads=[PR, PI], writes=[PI])
        P.op("dve", lambda e, m=m: e.tensor_tensor(PR[:, m + 1, :], T_["cc"][:], T_["ss"][:], ALU.subtract),
             reads=[T_["cc"], T_["ss"], PR], writes=[PR])
    PRx = P.sbuf("s5_PRx", [128, 4, NP], F32)
    PIx = P.sbuf("s5_PIx", [128, 4, NP], F32)
    PIxs = P.sbuf("s5_PIxs", [128, 4, NP], F32)

    def cmul(dr, di, ar, ai, br, bi):
        P.op("dve", lambda e: e.tensor_tensor(T_["cc"][:], ar, br, ALU.mult), reads=[PR, PI, PRx, PIx], writes=[T_["cc"]])
        P.op("dve", lambda e: e.tensor_tensor(T_["ss"][:], ai, bi, ALU.mult), reads=[PR, PI, PRx, PIx], writes=[T_["ss"]])
        P.op("dve", lambda e: e.tensor_tensor(T_["t"][:], ar, bi, ALU.mult), reads=[PR, PI, PRx, PIx], writes=[T_["t"]])
        P.op("dve", lambda e: e.tensor_tensor(T_["den"][:], ai, br, ALU.mult), reads=[PR, PI, PRx, PIx], writes=[T_["den"]])
        P.op("dve", lambda e: e.tensor_tensor(dr, T_["cc"][:], T_["ss"][:], ALU.subtract), reads=[T_["cc"], T_["ss"]], writes=[PRx])
        P.op("dve", lambda e: e.tensor_tensor(di, T_["t"][:], T_["den"][:], ALU.add), reads=[T_["t"], T_["den"]], writes=[PIx])

    cmul(PRx[:, 0, :], PIx[:, 0, :], PR[:, 0, :], PI[:, 0, :], PR[:, 1, :], PI[:, 1, :])
    cmul(PRx[:, 1, :], PIx[:, 1, :], PR[:, 0, :], PI[:, 0, :], PR[:, 2, :], PI[:, 2, :])
    cmul(PRx[:, 2, :], PIx[:, 2, :], PR[:, 1, :], PI[:, 1, :], PR[:, 2, :], PI[:, 2, :])
    cmul(PRx[:, 3, :], PIx[:, 3, :], PRx[:, 0, :], PIx[:, 0, :], PR[:, 2, :], PI[:, 2, :])
    P.op("dve", lambda e: e.tensor_scalar(PIxs[:], PIx[:], sgn, None, ALU.mult), reads=[PIx, c], writes=[PIxs])
    P.op("dve", lambda e: e.tensor_scalar(PIs[:], PI[:], sgn, None, ALU.mult), reads=[PI, c], writes=[PIs])
    P.op("dve", lambda e: e.tensor_scalar(Cw[:], Cw[:], sgn, None, ALU.mult), reads=[Cw, c], writes=[Cw])
    bT = P.sbuf("s5_bT", [16, NP, 128], F32)
    tb = [P.sbuf(f"s5_tb{i}", [128, 16], F32) for i in range(4)]
    for pi in range(NP):
        c1r, c1i = T_["c1r"][:, pi:pi + 1], T_["c1i"][:, pi:pi + 1]
        Bre, Bim = Bd[:, pi, 0, :], Bd[:, pi, 1, :]
        rd = [T_["c1r"], T_["c1i"], Bd]
        P.op("dve", lambda e, Bim=Bim, c1i=c1i: e.tensor_scalar(tb[0][:], Bim, c1i, None, ALU.mult), reads=rd, writes=[tb[0]])
        P.op("dve", lambda e, Bre=Bre, c1r=c1r: e.scalar_tensor_tensor(tb[1][:], Bre, c1r, tb[0][:], ALU.mult, ALU.subtract), reads=rd + [tb[0]], writes=[tb[1]])
        P.op("dve", lambda e, Bre=Bre, c1i=c1i: e.tensor_scalar(tb[2][:], Bre, c1i, None, ALU.mult), reads=rd, writes=[tb[2]])
        P.op("dve", lambda e, Bim=Bim, c1r=c1r: e.scalar_tensor_tensor(tb[3][:], Bim, c1r, tb[2][:], ALU.mult, ALU.add), reads=rd + [tb[2]], writes=[tb[3]])
        P.op("dve", lambda e: e.tensor_scalar(tb[3][:], tb[3][:], m1, None, ALU.mult), reads=[tb[3], c], writes=[tb[3]])
        P.op("dve", lambda e: e.scalar_tensor_tensor(tb[1][:], tb[1][:], m0, tb[3][:], ALU.mult, ALU.add), reads=[tb[1], tb[3], c], writes=[tb[1]])
        pb = P.bank(pi % 2)
        P.op("pe", lambda e, pb=pb: e.transpose(pb[0:16, 0:128], tb[1][:], ident), reads=[tb[1], c], writes=[pb])
        P.op("act", lambda e, pb=pb, pi=pi: e.activation(bT[:, pi, :], pb[0:16, 0:128], AF.Copy), reads=[pb], writes=[bT])
    NSL = 2
    identb = P.sbuf("s5_identb", [128, 128], BF16)
    Cwb = P.sbuf("s5_Cwb", [128, NP, 16], BF16)
    P.op("dve", lambda e: e.tensor_copy(identb[:], ident), reads=[c], writes=[identb])
    P.op("dve", lambda e: e.tensor_copy(Cwb[:], Cw[:]), reads=[Cw], writes=[Cwb])
    uTs = [P.sbuf(f"s5_u{k}", [16, tall], F32) for k in range(NSL)]
    zTs = [P.sbuf(f"s5_z{k}", [16, tall], F32) for k in range(NSL)]
    Lms = [[P.sbuf(f"s5_Lm{k}{d}", [128, 13, 128], BF16) for d in range(2)] for k in range(NSL)]
    Xhs = [[[P.sbuf(f"s5_Xh{k}{d}{q}", [128, tall], BF16) for q in range(2)] for d in range(2)] for k in range(NSL)]
    NB = tall // 8
    Lxs = [[P.sbuf(f"s5_Lx{k}{d}", [128, 4, 128], BF16) for d in range(2)] for k in range(NSL)]
    Ehs = [[[P.sbuf(f"s5_E{k}{d}{q}", [128, NB], BF16) for q in range(2)] for d in range(2)] for k in range(NSL)]
    ys = [P.sbuf(f"s5_y{i}", [16, 512], F32) for i in range(2)]
    gb = [P.sbuf(f"s5_g{i}", [16, 512], BF16) for i in range(2)]
    gsb = [P.sbuf(f"s5_gs{i}", [16, 512], BF16) for i in range(2)]
    nsteps = 0
    while (1 << nsteps) < tall:
        nsteps += 1
    bk = [0]

    def nb():
        bk[0] += 1
        return P.bank(bk[0] % 8)

    blocks = [(0, nctx)]
    o = nctx
    while o < tall:
        blocks.append((o, min(512, tall - o)))
        o += 512

    def a1(o):
        return o - nctx if o >= nctx else nlat + o

    evc = [0]

    def st_load(gi, k):
        uT, zT, Lm, Xh = uTs[k], zTs[k], Lms[k], Xhs[k]
        P.dma("sp", uT[:], fm.t[R_SU + gi * 16:R_SU + (gi + 1) * 16, :], reads=[fm], writes=[uT])
        P.dma("sp", zT[:], fm.t[R_SZ + gi * 16:R_SZ + (gi + 1) * 16, :], reads=[fm], writes=[zT])
        for d in range(2):
            pi = d * 16 + gi
            for m in range(nsteps):
                P.op("dve", lambda e, d=d, m=m, pi=pi: e.tensor_scalar(Lm[d][:, m, :], ident, PR[:, m, pi:pi + 1], None, ALU.mult),
                     reads=[PR, c], writes=[Lm[d]])
                P.op("dve", lambda e, d=d, m=m, pi=pi: e.scalar_tensor_tensor(Lm[d][:, m, :], swap, PIs[:, m, pi:pi + 1], Lm[d][:, m, :], ALU.mult, ALU.add),
                     reads=[PIs, c, Lm[d]], writes=[Lm[d]])
            Lx = Lxs[k][d]
            for x in range(4):
                P.op("dve", lambda e, Lx=Lx, x=x, pi=pi: e.tensor_scalar(Lx[:, x, :], ident, PRx[:, x, pi:pi + 1], None, ALU.mult),
                     reads=[PRx, c], writes=[Lx])
                P.op("dve", lambda e, Lx=Lx, x=x, pi=pi: e.scalar_tensor_tensor(Lx[:, x, :], swap, PIxs[:, x, pi:pi + 1], Lx[:, x, :], ALU.mult, ALU.add),
                     reads=[PIxs, c, Lx], writes=[Lx])
            for (o, n) in blocks:
                pb = nb()
                P.op("pe", lambda e, pb=pb, pi=pi, o=o, n=n: e.matmul(pb[:, 0:n], bT[:, pi, :], uT[:, o:o + n], start=True, stop=True),
                     reads=[bT, uT], writes=[pb])
                od = o if d == 0 else a1(o)
                P.op("act", lambda e, pb=pb, d=d, od=od, n=n: e.activation(Xh[d][0][:, od:od + n], pb[:, 0:n], AF.Copy), reads=[pb], writes=[Xh[d][0]])

    def evac(pb_ap, out_ap, pb, outbuf):
        if evc[0] % 2 == 0:
            P.op("act", lambda e: e.activation(out_ap, pb_ap, AF.Copy), reads=[pb], writes=[outbuf])
        else:
            P.op("dve", lambda e: e.tensor_copy(out_ap, pb_ap), reads=[pb], writes=[outbuf])
        evc[0] += 1

    def Lpow(k, d, pw):
        if pw in (1, 2, 4):
            return Lms[k][d], Lms[k][d][:, {1: 0, 2: 1, 4: 2}[pw], :]
        return Lxs[k][d], Lxs[k][d][:, {3: 0, 5: 1, 6: 2, 7: 3}[pw], :]

    def v3(buf):
        return buf[:, :].rearrange("p (b i) -> p b i", i=8)

    def st_local(k, m, cur):
        s = 1 << m
        w = 8 - s
        for d in range(2):
            Xa, Xb = Xhs[k][d][cur], Xhs[k][d][1 - cur]
            Xa3, Xb3 = v3(Xa), v3(Xb)
            (dlo, slo) = (s, 0) if d == 0 else (0, s)
            b0 = 0
            while b0 < NB:
                nbb = min(64, NB - b0)
                pb = nb()
                ps3 = pb[:, 0:nbb * w].rearrange("p (b i) -> p b i", i=w)
                P.op("pe", lambda e, ps3=ps3, Xa3=Xa3, b0=b0, nbb=nbb, dlo=dlo, w=w: e.matmul(ps3, identb[:], Xa3[:, b0:b0 + nbb, dlo:dlo + w], start=True, stop=False),
                     reads=[identb, Xa], writes=[pb], signal=False)
                P.op("pe", lambda e, ps3=ps3, Xa3=Xa3, b0=b0, nbb=nbb, slo=slo, w=w, d=d, m=m: e.matmul(ps3, Lms[k][d][:, m, :], Xa3[:, b0:b0 + nbb, slo:slo + w], start=False, stop=True),
                     reads=[Lms[k][d], Xa], writes=[pb])
                evac(ps3, Xb3[:, b0:b0 + nbb, dlo:dlo + w], pb, Xb)
                b0 += nbb
            ulo = 0 if d == 0 else w
            P.op("pool", lambda e, Xa3=Xa3, Xb3=Xb3, ulo=ulo, s=s: e.tensor_copy(Xb3[:, :, ulo:ulo + s], Xa3[:, :, ulo:ulo + s]), reads=[Xa], writes=[Xb])

    def st_einit(k, cur):
        for d in range(2):
            Xa3 = v3(Xhs[k][d][cur])
            col = 7 if d == 0 else 0
            P.op("pool", lambda e, Xa3=Xa3, col=col, d=d: e.tensor_copy(Ehs[k][d][0][:], Xa3[:, :, col]), reads=[Xhs[k][d][cur]], writes=[Ehs[k][d][0]])

    def st_bscan(k, m, ecur):
        s = 1 << (m - 3)
        for d in range(2):
            Ea, Eb = Ehs[k][d][ecur], Ehs[k][d][1 - ecur]
            w = NB - s
            o = 0
            while o < w:
                n = min(512, w - o)
                pb = nb()
                src = o if d == 0 else o + s
                dst = o + s if d == 0 else o
                P.op("pe", lambda e, pb=pb, Ea=Ea, dst=dst, n=n: e.matmul(pb[:, 0:n], identb[:], Ea[:, dst:dst + n], start=True, stop=False),
                     reads=[identb, Ea], writes=[pb], signal=False)
                P.op("pe", lambda e, pb=pb, Ea=Ea, src=src, n=n, d=d, m=m: e.matmul(pb[:, 0:n], Lms[k][d][:, m, :], Ea[:, src:src + n], start=False, stop=True),
                     reads=[Lms[k][d], Ea], writes=[pb])
                evac(pb[:, 0:n], Eb[:, dst:dst + n], pb, Eb)
                o += n
            c0 = 0 if d == 0 else w
            P.op("pool", lambda e, Ea=Ea, Eb=Eb, c0=c0, s=s: e.tensor_copy(Eb[:, c0:c0 + s], Ea[:, c0:c0 + s]), reads=[Ea], writes=[Eb])

    def st_fix(k, cur, ecur):
        for d in range(2):
            Xa, Xb = Xhs[k][d][cur], Xhs[k][d][1 - cur]
            Xa3, Xb3 = v3(Xa), v3(Xb)
            H = Ehs[k][d][ecur]
            P.op("pool", lambda e, Xa=Xa, Xb=Xb: e.tensor_copy(Xb[:], Xa[:]), reads=[Xa], writes=[Xb])
            hcol = 7 if d == 0 else 0
            P.op("pool", lambda e, Xb3=Xb3, H=H, hcol=hcol: e.tensor_copy(Xb3[:, :, hcol], H[:]), reads=[H, Xb], writes=[Xb])
            for i in (range(0, 7) if d == 0 else range(1, 8)):
                pw = i + 1 if d == 0 else 8 - i
                Lbuf, Lap = Lpow(k, d, pw)
                o = 0
                while o < NB - 1:
                    n = min(512, NB - 1 - o)
                    pb = nb()
                    bd = o + 1 if d == 0 else o
                    bs = o if d == 0 else o + 1
                    P.op("pe", lambda e, pb=pb, Xa3=Xa3, bd=bd, n=n, i=i: e.matmul(pb[:, 0:n], identb[:], Xa3[:, bd:bd + n, i], start=True, stop=False),
                         reads=[identb, Xa], writes=[pb], signal=False)
                    P.op("pe", lambda e, pb=pb, Lap=Lap, H=H, bs=bs, n=n: e.matmul(pb[:, 0:n], Lap, H[:, bs:bs + n], start=False, stop=True),
                         reads=[Lbuf, H], writes=[pb])
                    evac(pb[:, 0:n], Xb3[:, bd:bd + n, i], pb, Xb)
                    o += n

    def st_step(k, m, cur):
        Lm, Xh = Lms[k], Xhs[k]
        s = 1 << m
        for d in range(2):
            Xa, Xb = Xh[d][cur], Xh[d][1 - cur]
            w = tall - s
            o = 0
            while o < w:
                n = min(512, w - o)
                pb = nb()
                src = o if d == 0 else o + s
                dst = o + s if d == 0 else o
                P.op("pe", lambda e, pb=pb, Xa=Xa, dst=dst, n=n: e.matmul(pb[:, 0:n], identb[:], Xa[:, dst:dst + n], start=True, stop=False),
                     reads=[identb, Xa], writes=[pb], signal=False)
                P.op("pe", lambda e, pb=pb, d=d, m=m, Xa=Xa, src=src, n=n: e.matmul(pb[:, 0:n], Lm[d][:, m, :], Xa[:, src:src + n], start=False, stop=True),
                     reads=[Lm[d], Xa], writes=[pb])
                if evc[0] % 2 == 0:
                    P.op("act", lambda e, pb=pb, Xb=Xb, dst=dst, n=n: e.activation(Xb[:, dst:dst + n], pb[:, 0:n], AF.Copy), reads=[pb], writes=[Xb])
                else:
                    P.op("dve", lambda e, pb=pb, Xb=Xb, dst=dst, n=n: e.tensor_copy(Xb[:, dst:dst + n], pb[:, 0:n]), reads=[pb], writes=[Xb])
                evc[0] += 1
                o += n
            c0 = 0 if d == 0 else w
            P.op("pool", lambda e, Xa=Xa, Xb=Xb, c0=c0, s=s: e.tensor_copy(Xb[:, c0:c0 + s], Xa[:, c0:c0 + s]), reads=[Xa], writes=[Xb])

    def st_out(gi, k, cur):
        uT, zT, Xh = uTs[k], zTs[k], Xhs[k]
        for bi, (o, n) in enumerate(blocks):
            pb = nb()
            P.op("pe", lambda e, pb=pb, o=o, n=n: e.matmul(pb[0:16, 0:n], Cwb[:, gi, :], Xh[0][cur][:, o:o + n], start=True, stop=False),
                 reads=[Cwb, Xh[0][cur]], writes=[pb], signal=False)
            oa = a1(o)
            P.op("pe", lambda e, pb=pb, oa=oa, n=n: e.matmul(pb[0:16, 0:n], Cwb[:, 16 + gi, :], Xh[1][cur][:, oa:oa + n], start=False, stop=True),
                 reads=[Cwb, Xh[1][cur]], writes=[pb])
            y, g, gs = ys[bi % 2], gb[bi % 2], gsb[bi % 2]
            P.op("dve", lambda e, pb=pb, y=y, o=o, n=n: e.scalar_tensor_tensor(y[:, 0:n], uT[:, o:o + n], dsk[:, gi:gi + 1], pb[0:16, 0:n], ALU.mult, ALU.add),
                 reads=[uT, dsk, pb], writes=[y])
            P.op("act", lambda e, y=y, g=g, n=n: e.activation(g[:, 0:n], y[:, 0:n], AF.Gelu_apprx_tanh), reads=[y], writes=[g])
            P.op("act", lambda e, o=o, n=n: e.activation(zT[:, o:o + n], zT[:, o:o + n], AF.Silu), reads=[zT], writes=[zT])
            P.op("dve", lambda e, g=g, gs=gs, o=o, n=n: e.tensor_tensor(gs[:, 0:n], g[:, 0:n], zT[:, o:o + n], ALU.mult), reads=[g, zT], writes=[gs])
            wr_rows(P, gT, gi * 16, 16, o, n, g, lambda so, pn, g=g: g[:, so:so + pn])
            wr_rows(P, gsT, gi * 16, 16, o, n, gs, lambda so, pn, gs=gs: gs[:, so:so + pn])

    for g0 in range(0, 16, NSL):
        for k in range(NSL):
            st_load(g0 + k, k)
        cur = 0
        for m in range(3):
            for k in range(NSL):
                st_local(k, m, cur)
            cur = 1 - cur
        for k in range(NSL):
            st_einit(k, cur)
        ecur = 0
        m = 3
        while (1 << (m - 3)) < NB:
            for k in range(NSL):
                st_bscan(k, m, ecur)
            ecur = 1 - ecur
            m += 1
        for k in range(NSL):
            st_fix(k, cur, ecur)
        cur = 1 - cur
        for k in range(NSL):
            st_out(g0 + k, k, cur)


def s5_consts():
    c = np.zeros((128, 260), np.float32)
    c[:, 0:128] = np.eye(128)
    for p in range(128):
        c[p, 128 + (p + 64) % 128] = 1.0
    c[:64, 256] = 1.0
    c[64:, 256] = -1.0
    c[:64, 257] = 1.0
    c[64:, 258] = 1.0
    c[:, 259] = np.pi / 2
    return c


def s5_host_layout(lam_re, lam_im, log_dt, b_re, b_im, c_re, c_im, d_skip, j):
    gs = slice(j * 16, (j + 1) * 16)
    def dup(x):
        return np.concatenate([x, x], 0)
    lre = lam_re[:, gs].reshape(32, 64).T
    lim = lam_im[:, gs].reshape(32, 64).T
    ldt = np.broadcast_to(log_dt[:, gs].reshape(1, 32), (64, 32))
    prm = np.stack([lre, lim, ldt, np.zeros_like(lre)], -1)
    prm = np.ascontiguousarray(dup(prm)).astype(np.float32)
    bre = b_re[:, gs].reshape(32, 64, 16).transpose(1, 0, 2)
    bim = b_im[:, gs].reshape(32, 64, 16).transpose(1, 0, 2)
    bb = np.ascontiguousarray(dup(np.stack([bre, bim], 2))).astype(np.float32)
    cre = c_re[:, gs].reshape(32, 16, 64).transpose(2, 0, 1)
    cim = c_im[:, gs].reshape(32, 16, 64).transpose(2, 0, 1)
    cc = np.ascontiguousarray(np.concatenate([cre, cim], 0)).astype(np.float32)
    dd = np.ascontiguousarray(d_skip.reshape(64, 16)[gs].T).astype(np.float32)
    return prm, bb, cc, dd
    P.finish(outs)
    P.emit()
    return nc


def emit_gla(P, fm, tm, glw, glb, glg, glk, cmk, glT, tall=TALL, nctx=NCTX):
    nch = tall // 128
    ncc = nctx // 128
    kk = P.sbuf("gl_k", [128, 384], F32)
    P.dma("sp", kk[:], glk[:], reads=[glk], writes=[kk])
    cm = P.sbuf("gl_cm", [64, tall], F32)
    P.dma("sp", cm[:], cmk[:], reads=[cmk], writes=[cm])
    wa = P.sbuf("gl_wa", [16, 4, 64], F32)
    ba = P.sbuf("gl_ba", [64, 4], F32)
    go = P.sbuf("gl_go", [128, 1], F32)
    P.dma("sp", wa[:], glw[:], reads=[glw], writes=[wa])
    P.dma("sp", ba[:], glb[:], reads=[glb], writes=[ba])
    P.dma("sp", go[:], glg[:], reads=[glg], writes=[go])
    P.op("dve", lambda e: e.tensor_scalar(ba[:], ba[:], -1.0, None, ALU.mult), reads=[ba], writes=[ba])
    one1 = P.sbuf("gl_one", [128, 1], F32)
    epsb = P.sbuf("gl_eps", [128, 1], F32)
    ones_f = P.sbuf("gl_1f", [128, 128], F32)
    P.op("dve", lambda e: e.memset(one1[:], 1.0), writes=[one1])
    P.op("dve", lambda e: e.memset(epsb[:], EPS), writes=[epsb])
    P.op("dve", lambda e: e.memset(ones_f[:], 1.0 / 128), writes=[ones_f])
    qh = P.sbuf("gl_q", [64, tall], F32)
    kh = P.sbuf("gl_kh", [64, tall], F32)
    la = P.sbuf("gl_la", [64, tall], F32)
    cs = P.sbuf("gl_cs", [64, tall], F32)
    cB = P.sbuf("gl_cB", [64, tall], F32)
    ex = P.sbuf("gl_ex", [64, tall], F32)
    qe = P.sbuf("gl_qe", [64, tall], BF16)
    ke = P.sbuf("gl_ke", [64, tall], BF16)
    tot = P.sbuf("gl_tot", [64, nch], F32)
    Et = P.sbuf("gl_Et", [64, nch], F32)
    Vt = P.sbuf("gl_V", [128, nch, 128], BF16)
    oacc = P.sbuf("gl_o", [128, tall], F32)
    S = P.sbuf("gl_S", [64, 128], F32)
    Sb = P.sbuf("gl_Sb", [64, 128], BF16)
    scm = [P.sbuf(f"gl_sc{i}", [128, 128], BF16) for i in range(2)]
    kT = [P.sbuf(f"gl_kT{i}", [128, 64], BF16) for i in range(2)]
    sq = P.sbuf("gl_sq", [128, 512], F32)
    rs = P.sbuf("gl_rs", [128, 512], F32)
    gz = P.sbuf("gl_gz", [128, 512], F32)
    yo = P.sbuf("gl_yo", [128, 512], F32)
    ob = [P.sbuf(f"gl_ob{i}", [128, 512], BF16) for i in range(2)]
    bk = [0]

    def nb():
        bk[0] += 1
        return P.bank(bk[0] % 8)

    for hh in range(2):
        P.dma("sp", qh[:], fm.t[R_GQ + hh * 64:R_GQ + (hh + 1) * 64, :], reads=[fm], writes=[qh])
        P.dma("sp", kh[:], fm.t[R_GK + hh * 64:R_GK + (hh + 1) * 64, :], reads=[fm], writes=[kh])
        P.dma("pool", Vt[:], tm.t[:, C_GV + hh * 128:C_GV + (hh + 1) * 128].rearrange("(t p) c -> p t c", p=128), reads=[tm], writes=[Vt])
        for d in range(2):
            pr = d * 2 + hh
            P.dma("sp", ex[0:16, :], fm.t[R_LR + d * 16:R_LR + (d + 1) * 16, :], reads=[fm], writes=[ex])
            o = 0
            while o < tall:
                n = min(512, tall - o)
                pb = nb()
                P.op("pe", lambda e, pb=pb, pr=pr, o=o, n=n: e.matmul(pb[0:64, 0:n], wa[:, pr, :], ex[0:16, o:o + n], start=True, stop=True),
                     reads=[wa, ex], writes=[pb])
                P.op("act", lambda e, pb=pb, pr=pr, o=o, n=n: e.activation(la[:, o:o + n], pb[0:64, 0:n], AF.Exp, bias=ba[:, pr:pr + 1], scale=-1.0),
                     reads=[pb, ba], writes=[la])
                o += n
            P.op("act", lambda e: e.activation(la[:], la[:], AF.Ln, bias=one1[0:64, :], scale=1.0), reads=[la, one1], writes=[la])
            P.op("dve", lambda e: e.tensor_tensor_scan(cs[:], cm[:], la[:], 0.0, ALU.mult, ALU.add), reads=[cm, la], writes=[cs])
            P.op("dve", lambda e: e.tensor_copy(tot[:], cs[:].rearrange("p (c t) -> p c t", t=128)[:, :, 127]), reads=[cs], writes=[tot])
            if d == 0:
                P.op("dve", lambda e: e.tensor_copy(cB[:], cs[:]), reads=[cs], writes=[cB])
            else:
                for c in range(nch):
                    P.op("dve", lambda e, c=c: e.tensor_scalar(cB[:, c * 128:(c + 1) * 128], cs[:, c * 128:(c + 1) * 128], -1.0, tot[:, c:c + 1], ALU.mult, ALU.add),
                         reads=[cs, tot], writes=[cB])
                P.op("dve", lambda e: e.tensor_tensor(cB[:], cB[:], la[:], ALU.add), reads=[cB, la], writes=[cB])
            P.op("act", lambda e: e.activation(ex[:], cB[:], AF.Exp, scale=-1.0 / 16), reads=[cB], writes=[ex])
            P.op("dve", lambda e: e.scalar_tensor_tensor(qe[:], qh[:], 0.125, ex[:], ALU.mult, ALU.mult), reads=[qh, ex], writes=[qe])
            P.op("act", lambda e: e.activation(ex[:], cB[:], AF.Exp, scale=1.0 / 16), reads=[cB], writes=[ex])
            P.op("dve", lambda e: e.tensor_tensor(ke[:], kh[:], ex[:], ALU.mult), reads=[kh, ex], writes=[ke])
            for c in range(nch):
                P.op("dve", lambda e, c=c: e.tensor_scalar(cs[:, c * 128:(c + 1) * 128], cB[:, c * 128:(c + 1) * 128], -1.0, tot[:, c:c + 1], ALU.mult, ALU.add),
                     reads=[cB, tot], writes=[cs])
            P.op("act", lambda e: e.activation(ex[:], cs[:], AF.Exp, scale=-1.0 / 16), reads=[cs], writes=[ex])
            P.op("dve", lambda e: e.tensor_tensor(la[:], kh[:], ex[:], ALU.mult), reads=[kh, ex], writes=[la])
            P.op("act", lambda e: e.activation(Et[:], tot[:], AF.Exp, scale=-1.0 / 16), reads=[tot], writes=[Et])
            P.op("dve", lambda e: e.memset(S[:], 0.0), writes=[S])
            P.op("dve", lambda e: e.memset(Sb[:], 0.0), writes=[Sb])
            if d == 0:
                order = list(range(nch))
            else:
                order = list(range(ncc - 1, -1, -1)) + list(range(nch - 1, ncc - 1, -1))
            mk = kk[:, 0:128] if d == 0 else kk[:, 128:256]
            for i, c in enumerate(order):
                sl = slice(c * 128, (c + 1) * 128)
                p1, p2, p3, p4 = nb(), nb(), nb(), nb()
                sc = scm[i % 2]
                kt = kT[i % 2]
                P.op("pe", lambda e, p1=p1, sl=sl: e.matmul(p1[:, 0:128], ke[:, sl], qe[:, sl], start=True, stop=True), reads=[ke, qe], writes=[p1])
                P.op("dve", lambda e, p1=p1, sc=sc, mk=mk: e.tensor_tensor(sc[:], p1[:, 0:128], mk, ALU.mult), reads=[p1, kk], writes=[sc])
                P.op("pe", lambda e, p2=p2, sc=sc, c=c: e.matmul(p2[:, 0:128], Vt[:, c, :], sc[:], start=True, stop=False), reads=[Vt, sc], writes=[p2], signal=False)
                P.op("pe", lambda e, p2=p2, sl=sl: e.matmul(p2[:, 0:128], Sb[:], qe[:, sl], start=False, stop=True), reads=[Sb, qe], writes=[p2])
                if d == 0:
                    P.op("act", lambda e, p2=p2, sl=sl: e.activation(oacc[:, sl], p2[:, 0:128], AF.Copy), reads=[p2], writes=[oacc])
                else:
                    P.op("dve", lambda e, p2=p2, sl=sl: e.tensor_tensor(oacc[:, sl], oacc[:, sl], p2[:, 0:128], ALU.add), reads=[p2, oacc], writes=[oacc])
                P.op("pe", lambda e, p3=p3, sl=sl: e.transpose(p3[:, 0:64], la[:, sl], kk[0:64, 256:320]), reads=[la, kk], writes=[p3])
                P.op("act", lambda e, p3=p3, kt=kt: e.activation(kt[:], p3[:, 0:64], AF.Copy), reads=[p3], writes=[kt])
                P.op("pe", lambda e, p4=p4, kt=kt, c=c: e.matmul(p4[0:64, 0:128], kt[:], Vt[:, c, :], start=True, stop=True), reads=[kt, Vt], writes=[p4])
                P.op("dve", lambda e, p4=p4, c=c: e.scalar_tensor_tensor(S[:], S[:], Et[:, c:c + 1], p4[0:64, 0:128], ALU.mult, ALU.add), reads=[S, Et, p4], writes=[S])
                P.op("act", lambda e: e.activation(Sb[:], S[:], AF.Copy), reads=[S], writes=[Sb])
        o = 0
        bi = 0
        while o < tall:
            n = min(512, tall - o)
            pb = nb()
            P.op("act", lambda e, o=o, n=n: e.activation(sq[:, 0:n], oacc[:, o:o + n], AF.Square), reads=[oacc], writes=[sq])
            P.op("pe", lambda e, pb=pb, n=n: e.matmul(pb[:, 0:n], ones_f[:], sq[:, 0:n], start=True, stop=True), reads=[ones_f, sq], writes=[pb])
            P.op("act", lambda e, pb=pb, n=n: e.activation(rs[:, 0:n], pb[:, 0:n], AF.Sqrt, bias=epsb[:], scale=1.0), reads=[pb, epsb], writes=[rs])
            P.op("dve", lambda e, n=n: e.reciprocal(rs[:, 0:n], rs[:, 0:n]), reads=[rs], writes=[rs])
            P.op("dve", lambda e, o=o, n=n: e.scalar_tensor_tensor(yo[:, 0:n], oacc[:, o:o + n], go[:, 0:1], rs[:, 0:n], ALU.mult, ALU.mult), reads=[oacc, go, rs], writes=[yo])
            P.dma("sp", gz[:, 0:n], fm.t[R_GZ + hh * 128:R_GZ + (hh + 1) * 128, o:o + n], reads=[fm], writes=[gz])
            P.op("act", lambda e, n=n: e.activation(gz[:, 0:n], gz[:, 0:n], AF.Silu), reads=[gz], writes=[gz])
            ot = ob[bi % 2]
            bi += 1
            P.op("pool", lambda e, ot=ot, n=n: e.tensor_tensor(ot[:, 0:n], yo[:, 0:n], gz[:, 0:n], ALU.mult), reads=[yo, gz], writes=[ot])
            wr_rows(P, glT, hh * 128, 128, o, n, ot, lambda so, pn, ot=ot: ot[:, so:so + pn])
            o += n


def gla_consts(tall):
    k = np.zeros((128, 384), np.float32)
    s = np.arange(128)[:, None]
    t = np.arange(128)[None, :]
    k[:, 0:128] = (s <= t)
    k[:, 128:256] = (s >= t)
    k[:, 256:384] = np.eye(128)
    cm = np.ones((64, tall), np.float32)
    cm[:, ::128] = 0.0
    return k, cm


def build_C(nlat=1024, nctx=64):
    nc = _nc()
    P = Prog(nc)
    NT = nlat + nctx
    gT = P.dram("gT", [1024, NT], BF16, "ExternalInput")
    gsT = P.dram("gsT", [1024, NT], BF16, "ExternalInput")
    glT = P.dram("glT", [1024, NT], BF16, "ExternalInput")
    atT = P.dram("atT", [2048, NT], BF16, "ExternalInput")
    hT = P.dram("hT", [NDC, 128, NT], BF16, "ExternalInput")
    xT = P.dram("xT", [NDC, 128, NT], F32, "ExternalInput")
    wglu = P.dram("wglu", [1024, 1024], F32, "ExternalInput")
    wps = P.dram("wps", [1024, D], F32, "ExternalInput")
    wpg = P.dram("wpg", [1024, D], F32, "ExternalInput")
    wpa = P.dram("wpa", [2048, D], F32, "ExternalInput")
    wmg = P.dram("wmg", [D, 3 * D], F32, "ExternalInput")
    wo = P.dram("wo", [D, D], F32, "ExternalInput")
    gsel = P.dram("gsel", [128, NDC, 2], F32, "ExternalInput")
    xo = P.dram("xo", [NDC, 128, NT], F32, "ExternalOutput")
    gates = P.dram("gates", [96, 128, NT], BF16, "Internal")
    blocks = []
    o = 0
    while o < nlat:
        n = min(512, nlat - o)
        blocks.append((o, n, 0))
        o += n
    if nctx:
        blocks.append((nlat, nctx, 1))
    bk = [0]

    def nb():
        bk[0] += 1
        return P.bank(bk[0] % 8)

    mT = P.sbuf("c_m", [128, NDC, NT], BF16)
    P.open_scope()
    hs = P.sbuf("c_h", [128, NDC, NT], BF16)
    P.dma("sp", hs[:], hT.t.rearrange("c p n -> p c n"), reads=[hT], writes=[hs])
    Wg = [P.sbuf(f"c_wg{i}", [128, NDC, 128], BF16) for i in range(3)]
    gst = [P.sbuf(f"c_gst{i}", [128, NT], BF16) for i in range(2)]
    for mc in range(96):
        W = Wg[mc % 3]
        P.dma("pool", W[:], wmg.t[:, mc * 128:(mc + 1) * 128].rearrange("(k p) n -> p k n", p=128), reads=[wmg], writes=[W])
        st = gst[mc % 2]
        for (o, n, t) in blocks:
            pb = nb()
            for kc in range(NDC):
                P.op("pe", lambda e, pb=pb, W=W, kc=kc, o=o, n=n: e.matmul(pb[:, 0:n], W[:, kc, :], hs[:, kc, o:o + n], start=(kc == 0), stop=(kc == NDC - 1)),
                     reads=[W, hs], writes=[pb], signal=(kc == NDC - 1))
            P.op("act", lambda e, pb=pb, st=st, o=o, n=n: e.activation(st[:, o:o + n], pb[:, 0:n], AF.Sigmoid), reads=[pb], writes=[st])
        P.dma("sp", gates.t[mc], st[:], reads=[st], writes=[gates])
    P.close_scope()
    P.open_scope()
    gs_ = P.sbuf("c_g", [128, 8, NT], BF16)
    ss_ = P.sbuf("c_s", [128, 8, NT], BF16)
    gl_ = P.sbuf("c_gl", [128, 8, NT], BF16)
    at_ = P.sbuf("c_at", [128, 16, NT], BF16)
    P.dma("sp", gs_[:], gT.t.rearrange("(c p) n -> p c n", p=128), reads=[gT], writes=[gs_])
    P.dma("sp", ss_[:], gsT.t.rearrange("(c p) n -> p c n", p=128), reads=[gsT], writes=[ss_])
    P.dma("sp", gl_[:], glT.t.rearrange("(c p) n -> p c n", p=128), reads=[glT], writes=[gl_])
    P.dma("sp", at_[:], atT.t.rearrange("(c p) n -> p c n", p=128), reads=[atT], writes=[at_])
    wgl = P.sbuf("c_wglu", [128, 8, 1024], BF16)
    P.dma("pool", wgl[:], wglu.t.rearrange("(k p) n -> p k n", p=128), reads=[wglu], writes=[wgl])
    sg = [P.sbuf(f"c_sg{i}", [128, 512], BF16) for i in range(2)]
    i = 0
    for oc in range(8):
        for (o, n, t) in blocks:
            pb = nb()
            for kc in range(8):
                P.op("pe", lambda e, pb=pb, kc=kc, oc=oc, o=o, n=n: e.matmul(pb[:, 0:n], wgl[:, kc, oc * 128:(oc + 1) * 128], gs_[:, kc, o:o + n], start=(kc == 0), stop=(kc == 7)),
                     reads=[wgl, gs_], writes=[pb], signal=(kc == 7))
            s_ = sg[i % 2]
            i += 1
            P.op("act", lambda e, pb=pb, s_=s_, n=n: e.activation(s_[:, 0:n], pb[:, 0:n], AF.Sigmoid), reads=[pb], writes=[s_])
            P.op("dve", lambda e, s_=s_, oc=oc, o=o, n=n: e.tensor_tensor(ss_[:, oc, o:o + n], ss_[:, oc, o:o + n], s_[:, 0:n], ALU.mult), reads=[ss_, s_], writes=[ss_])
    w1 = [P.sbuf(f"c_w1{i}", [128, 8, 128], BF16) for i in range(2)]
    w2 = [P.sbuf(f"c_w2{i}", [128, 8, 128], BF16) for i in range(2)]
    w3 = [P.sbuf(f"c_w3{i}", [128, 16, 128], BF16) for i in range(2)]
    g3 = [P.sbuf(f"c_g3{i}", [128, 3, NT], BF16) for i in range(2)]
    m1 = P.sbuf("c_m1", [128, 512], F32)
    m2 = P.sbuf("c_m2", [128, 512], F32)
    m3 = P.sbuf("c_m3", [128, 512], F32)
    for dc in range(NDC):
        a, b_, c_, g_ = w1[dc % 2], w2[dc % 2], w3[dc % 2], g3[dc % 2]
        cs_ = slice(dc * 128, (dc + 1) * 128)
        P.dma("pool", a[:], wps.t[:, cs_].rearrange("(k p) n -> p k n", p=128), reads=[wps], writes=[a])
        P.dma("pool", b_[:], wpg.t[:, cs_].rearrange("(k p) n -> p k n", p=128), reads=[wpg], writes=[b_])
        P.dma("pool", c_[:], wpa.t[:, cs_].rearrange("(k p) n -> p k n", p=128), reads=[wpa], writes=[c_])
        for br in range(3):
            P.dma("sp", g_[:, br, :], gates.t[br * 32 + dc], reads=[gates], writes=[g_])
        for (o, n, t) in blocks:
            p1, p2, p3 = nb(), nb(), nb()
            for kc in range(8):
                P.op("pe", lambda e, p1=p1, a=a, kc=kc, o=o, n=n: e.matmul(p1[:, 0:n], a[:, kc, :], ss_[:, kc, o:o + n], start=(kc == 0), stop=(kc == 7)),
                     reads=[a, ss_], writes=[p1], signal=(kc == 7))
            for kc in range(8):
                P.op("pe", lambda e, p2=p2, b_=b_, kc=kc, o=o, n=n: e.matmul(p2[:, 0:n], b_[:, kc, :], gl_[:, kc, o:o + n], start=(kc == 0), stop=(kc == 7)),
                     reads=[b_, gl_], writes=[p2], signal=(kc == 7))
            for kc in range(16):
                P.op("pe", lambda e, p3=p3, c_=c_, kc=kc, o=o, n=n: e.matmul(p3[:, 0:n], c_[:, kc, :], at_[:, kc, o:o + n], start=(kc == 0), stop=(kc == 15)),
                     reads=[c_, at_], writes=[p3], signal=(kc == 15))
            P.op("dve", lambda e, p1=p1, g_=g_, o=o, n=n: e.tensor_tensor(m1[:, 0:n], p1[:, 0:n], g_[:, 0, o:o + n], ALU.mult), reads=[p1, g_], writes=[m1])
            P.op("dve", lambda e, p2=p2, g_=g_, o=o, n=n: e.tensor_tensor(m2[:, 0:n], p2[:, 0:n], g_[:, 1, o:o + n], ALU.mult), reads=[p2, g_], writes=[m2])
            P.op("dve", lambda e, p3=p3, g_=g_, o=o, n=n: e.tensor_tensor(m3[:, 0:n], p3[:, 0:n], g_[:, 2, o:o + n], ALU.mult), reads=[p3, g_], writes=[m3])
            P.op("pool", lambda e, n=n: e.tensor_tensor(m1[:, 0:n], m1[:, 0:n], m2[:, 0:n], ALU.add), reads=[m1, m2], writes=[m1])
            P.op("pool", lambda e, dc=dc, o=o, n=n: e.tensor_tensor(mT[:, dc, o:o + n], m1[:, 0:n], m3[:, 0:n], ALU.add), reads=[m1, m3], writes=[mT])
    P.close_scope()
    P.open_scope()
    gv = P.sbuf("c_gv", [128, NDC, 2], F32)
    P.dma("sp", gv[:], gsel[:], reads=[gsel], writes=[gv])
    wo_ = [P.sbuf(f"c_wo{i}", [128, NDC, 128], BF16) for i in range(2)]
    xin = [P.sbuf(f"c_xi{i}", [128, NT], F32) for i in range(2)]
    xot = [P.sbuf(f"c_xo{i}", [128, NT], F32) for i in range(2)]
    for dc in range(NDC):
        W = wo_[dc % 2]
        xi, xn = xin[dc % 2], xot[dc % 2]
        P.dma("pool", W[:], wo.t[:, dc * 128:(dc + 1) * 128].rearrange("(k p) n -> p k n", p=128), reads=[wo], writes=[W])
        P.dma("sp", xi[:], xT.t[dc], reads=[xT], writes=[xi])
        for (o, n, t) in blocks:
            pb = nb()
            for kc in range(NDC):
                P.op("pe", lambda e, pb=pb, W=W, kc=kc, o=o, n=n: e.matmul(pb[:, 0:n], W[:, kc, :], mT[:, kc, o:o + n], start=(kc == 0), stop=(kc == NDC - 1)),
                     reads=[W, mT], writes=[pb], signal=(kc == NDC - 1))
            P.op("dve", lambda e, pb=pb, xi=xi, xn=xn, dc=dc, o=o, n=n, t=t: e.scalar_tensor_tensor(
                xn[:, o:o + n], pb[:, 0:n], gv[:, dc, t:t + 1], xi[:, o:o + n], ALU.mult, ALU.add), reads=[pb, gv, xi], writes=[xn])
        P.dma("sp", xo.t[dc], xn[:], reads=[xn], writes=[xo])
    P.close_scope()
    P.finish([xo])
    P.emit()
    return nc


_OFF = dict(su=0, sz=1024, gq=2048, gk=2560, gv=3072, gz=4096, glr=5120, aq=5152, ak=7200, av=7712, az=8224, mg=10272)


def _cols(j):
    r = lambda k, w: list(range(_OFF[k] + j * w, _OFF[k] + (j + 1) * w))
    return (r("gv", 256) + r("av", 128) + r("gq", 128) + r("gk", 128) + r("gz", 256) + r("aq", 512) + r("ak", 128)
            + r("az", 512) + r("su", 256) + r("sz", 256) + list(range(_OFF["glr"], _OFF["glr"] + 32)))


def _fm(a):
    return np.ascontiguousarray(a.T.reshape(NDC, 128, a.shape[0]))


def _run(nc, ins):
    return run_bass_kernel_spmd(nc, ins, core_ids=list(range(NCORES))).results


def kernel(x, c, ctx, c_ctx, norm_g, w_mod, b_mod, w_in, ssm_lam_re, ssm_lam_im, ssm_log_dt, ssm_b_re, ssm_b_im,
           ssm_c_re, ssm_c_im, ssm_d, ssm_w_glu, gla_w_a, gla_b_a, gla_norm_g, attn_q_g, attn_k_g,
           w_proj_ssm, w_proj_gla, w_proj_attn, w_out, final_g):
    f = lambda a: np.asarray(a, dtype=np.float32)
    x, c, ctx, c_ctx = f(x), f(c), f(ctx), f(c_ctx)
    cs3 = np.stack([c[0], c[1], c_ctx], 0)
    cT = np.ascontiguousarray(cs3.reshape(3, 32, 128).transpose(2, 1, 0))
    w_mod, b_mod = f(w_mod), f(b_mod)
    ins = []
    for i in range(8):
        sl = slice(i * 1536, (i + 1) * 1536)
        ins.append({"cT": cT, "wm": np.ascontiguousarray(w_mod[:, :, sl]),
                    "bm": np.ascontiguousarray(b_mod[:, sl].reshape(2, 12, 128).transpose(2, 0, 1))})
    res = _run(build_M(), ins)
    modT = np.concatenate([r["modT"] for r in res], axis=2)
    cores = [(i // 4, i % 4) for i in range(8)]
    xT = []
    for (b, j) in cores:
        loc = np.concatenate([x[b, j * 1024:(j + 1) * 1024], ctx[b, j * 64:(j + 1) * 64]], 0)
        xT.append(_fm(loc))
    cosT, sinT, rm = rope_tables(NLAT)
    glk, cmk = gla_consts(TALL)
    s5k = s5_consts()
    ncA = build_A(1024, 64, True)
    ncB = build_B(TALL, NCTX, True, ("b1", "attn", "gla", "s5"))
    ncC = build_C(1024, 64)
    w_in = f(w_in)
    for l in range(2):
        ngl = np.ascontiguousarray(f(norm_g)[l].reshape(32, 128).T)
        ins = [{"xT": xT[i], "ng": ngl, "ms": np.ascontiguousarray(modT[:, l][:, :, [b, 2]])} for i, (b, j) in enumerate(cores)]
        resA = _run(ncA, ins)
        hT = [r["hT"] for r in resA]
        hTb = []
        for b in range(2):
            hTb.append(np.ascontiguousarray(np.concatenate(
                [hT[b * 4 + j][:, :, 1024:1088] for j in range(4)] + [hT[b * 4 + j][:, :, 0:1024] for j in range(4)], axis=2)))
        ins = []
        for i, (b, j) in enumerate(cores):
            prm, bb, cc, dd = s5_host_layout(f(ssm_lam_re)[l], f(ssm_lam_im)[l], f(ssm_log_dt)[l], f(ssm_b_re)[l], f(ssm_b_im)[l],
                                             f(ssm_c_re)[l], f(ssm_c_im)[l], f(ssm_d)[l], j)
            WA = f(gla_w_a)[l]
            BA = f(gla_b_a)[l]
            glw = np.ascontiguousarray(WA[:, :, 128 * j:128 * (j + 1)].reshape(2, 16, 2, 64).transpose(1, 0, 2, 3).reshape(16, 4, 64))
            glb = np.ascontiguousarray(BA[:, 128 * j:128 * (j + 1)].reshape(4, 64).T)
            ins.append({"hTb": hTb[b], "wq": np.ascontiguousarray(w_in[l][:, _cols(j)]), "cosT": cosT, "sinT": sinT, "rmT": rm,
                        "gqk": np.ascontiguousarray(np.stack([f(attn_q_g)[l], f(attn_k_g)[l]], 1)),
                        "s5p": prm, "s5b": bb, "s5c": cc, "s5d": dd, "s5k": s5k,
                        "glw": glw, "glb": glb, "glg": np.ascontiguousarray(f(gla_norm_g)[l].reshape(128, 1)), "glk": glk, "cmk": cmk})
        resB = _run(ncB, ins)
        wmg = np.ascontiguousarray(w_in[l][:, _OFF["mg"]:])
        ins = []
        for i, (b, j) in enumerate(cores):
            tok = list(range(256 + j * 1024, 256 + (j + 1) * 1024)) + list(range(j * 64, (j + 1) * 64))
            cat = lambda k: np.ascontiguousarray(np.concatenate([resB[b * 4 + jj][k] for jj in range(4)], 0)[:, tok])
            ins.append({"gT": cat("gT"), "gsT": cat("gsT"), "glT": cat("glT"), "atT": cat("atT"), "hT": hT[i], "xT": xT[i],
                        "wglu": f(ssm_w_glu)[l], "wps": f(w_proj_ssm)[l], "wpg": f(w_proj_gla)[l], "wpa": f(w_proj_attn)[l],
                        "wmg": wmg, "wo": f(w_out)[l],
                        "gsel": np.ascontiguousarray(modT[:, l, 64:96][:, :, [b, 2]])})
        resC = _run(ncC, ins)
        xT = [r["xo"] for r in resC]
    ncF = build_A(1024, 64, False)
    fgl = np.ascontiguousarray(f(final_g).reshape(32, 128).T)
    zero = np.zeros((128, 96, 2), np.float32)
    resF = _run(ncF, [{"xT": xT[i], "ng": fgl, "ms": zero} for i in range(8)])
    out = np.zeros((2, 4096, 4096), np.float32)
    for i, (b, j) in enumerate(cores):
        o = resF[i]["hT"].reshape(4096, 1088)[:, 0:1024]
        out[b, j * 1024:(j + 1) * 1024] = o.T
    return out


GROUPS = [[0, 1, 2, 3], [4, 5, 6, 7]]
NK = 8


def _blocks(nctx, tall, step=512):
    bl = [(0, nctx, 1)]
    o = nctx
    while o < tall:
        n = min(step, tall - o)
        bl.append((o, n, 0))
        o += n
    return bl


CW = 256


class TT:
    def __init__(self, P, name, rows, tall, dtype, r0=0, bufs=None):
        self.rows, self.tall, self.r0 = rows, tall, r0
        self.bufs = bufs if bufs is not None else [P.dram(f"{name}_{i}", [rows, CW], dtype) for i in range(tall // CW)]

    def sub(self, r0):
        return TT(None, None, self.rows, self.tall, None, self.r0 + r0, self.bufs)

    def pieces(self, o, n):
        out = []
        so = 0
        while n > 0:
            ci, lo = o // CW, o % CW
            pn = min(n, CW - lo)
            out.append((self.bufs[ci], lo, pn, so))
            o += pn
            n -= pn
            so += pn
        return out


def wr_rows(P, dst, r0, nr, o, n, srcbuf, src_fn, eng="sp"):
    if isinstance(dst, TT):
        for (b, lo, pn, so) in dst.pieces(o, n):
            P.dma(eng, b.t[dst.r0 + r0:dst.r0 + r0 + nr, lo:lo + pn], src_fn(so, pn), reads=[srcbuf], writes=[b])
    else:
        P.dma(eng, dst.t[r0:r0 + nr, o:o + n], src_fn(0, n), reads=[srcbuf], writes=[dst])


def wr_cpn(P, dst, c0, ncn, o, n, srcbuf, src_fn, eng="sp"):
    if isinstance(dst, TT):
        for (b, lo, pn, so) in dst.pieces(o, n):
            P.dma(eng, b.t[dst.r0 + c0 * 128:dst.r0 + (c0 + ncn) * 128, lo:lo + pn].rearrange("(c p) n -> p c n", p=128), src_fn(so, pn),
                  reads=[srcbuf], writes=[b])
    else:
        P.dma(eng, dst.t[c0 * 128:(c0 + ncn) * 128, o:o + n].rearrange("(c p) n -> p c n", p=128), src_fn(0, n), reads=[srcbuf], writes=[dst])


def rd_cpn(P, src, o, n, dstbuf, dst_fn, eng="sp", c0=0, ncn=None):
    if isinstance(src, TT):
        ncn_ = ncn if ncn is not None else src.rows // 128
        for (b, lo, pn, so) in src.pieces(o, n):
            P.dma(eng, dst_fn(so, pn), b.t[src.r0 + c0 * 128:src.r0 + (c0 + ncn_) * 128, lo:lo + pn].rearrange("(c p) n -> p c n", p=128),
                  reads=[b], writes=[dstbuf])
    else:
        P.dma(eng, dst_fn(0, n), src.t[:, :, o:o + n].rearrange("c p n -> p c n"), reads=[src], writes=[dstbuf])


def emit_M2(P, cT, wm, bm, modS):
    sc = P.sbuf("m_sc", [128, 32, 2], F32)
    bs = P.sbuf("m_bs", [128, 2, 24], F32)
    wt = [P.sbuf(f"m_wt{i}", [128, 4, 3072], F32) for i in range(2)]
    P.dma("sp", sc[:], cT[:], reads=[cT], writes=[sc])
    P.dma("sp", bs[:], bm[:], reads=[bm], writes=[bs])
    P.op("act", lambda e: e.activation(sc[:], sc[:], AF.Silu), reads=[sc], writes=[sc])
    it = 0
    for l in range(2):
        for g in range(8):
            w = wt[it % 2]
            pp = P.bank(it % 2)
            it += 1
            P.dma("sp", w[:], wm.t[l, g * 512:(g + 1) * 512, :].rearrange("(k p) n -> p k n", p=128), reads=[wm], writes=[w])
            for j in range(24):
                for k in range(4):
                    kc = g * 4 + k
                    P.op("pe", lambda e, pp=pp, w=w, j=j, k=k, kc=kc: e.matmul(
                        pp[:, j * 2:j * 2 + 2], w[:, k, j * 128:(j + 1) * 128], sc[:, kc, :], start=(k == 0), stop=(k == 3)),
                        reads=[w, sc], writes=[pp], signal=(k == 3 and j == 23))
            dst = modS[:, l].rearrange("p a b -> p (a b)")
            if g == 0:
                P.op("dve", lambda e, pp=pp, dst=dst: e.tensor_copy(dst, pp[:, 0:48]), reads=[pp], writes=[modS])
            else:
                P.op("dve", lambda e, pp=pp, dst=dst: e.tensor_tensor(dst, dst, pp[:, 0:48], ALU.add), reads=[pp, modS], writes=[modS])
    for l in range(2):
        for r in range(2):
            P.op("dve", lambda e, l=l, r=r: e.tensor_tensor(modS[:, l, :, r], modS[:, l, :, r], bs[:, l, :], ALU.add),
                 reads=[modS, bs], writes=[modS])


def emit_A2(P, xs, Av, Bv, arin, arout, dst, dst_dt, tall, nctx, lat_only=False, ag_out=None):
    onesm = P.sbuf("a_ones", [128, 128], F32)
    epsb = P.sbuf("a_eps", [128, 1], F32)
    ss = P.sbuf("a_ss", [128, tall], F32)
    xb = [P.sbuf(f"a_xb{i}", [128, NK, 512], F32) for i in range(2)]
    sq = [P.sbuf(f"a_sq{i}", [128, 512], F32) for i in range(2)]
    tmp = [P.sbuf(f"a_tmp{i}", [128, 512], F32) for i in range(2)]
    ho = [P.sbuf(f"a_ho{i}", [128, NK, 512], dst_dt) for i in range(2)]
    P.op("dve", lambda e: e.memset(onesm[:], 1.0 / D), writes=[onesm])
    P.op("dve", lambda e: e.memset(epsb[:], EPS), writes=[epsb])
    bl = _blocks(nctx, tall)
    for bi, (o, n, t) in enumerate(bl):
        x = xb[bi % 2]
        pb = P.bank(bi % 2)
        P.dma("sp", x[:, :, 0:n], xs.t[:, :, o:o + n].rearrange("c p n -> p c n"), reads=[xs], writes=[x])
        for k in range(NK):
            s = sq[k % 2]
            P.op("act", lambda e, s=s, x=x, k=k, n=n: e.activation(s[:, 0:n], x[:, k, 0:n], AF.Square), reads=[x], writes=[s])
            P.op("pe", lambda e, s=s, pb=pb, k=k, n=n: e.matmul(pb[:, 0:n], onesm[:], s[:, 0:n], start=(k == 0), stop=(k == NK - 1)),
                 reads=[onesm, s], writes=[pb])
        P.op("dve", lambda e, pb=pb, o=o, n=n: e.tensor_copy(ss[:, o:o + n], pb[:, 0:n]), reads=[pb], writes=[ss])
    qw = tall // 4
    for q in range(4):
        P.dma("sp", arin[q][:], ss[:, q * qw:(q + 1) * qw], reads=[ss], writes=[arin[q]])
        P.collective("AllReduce", ALU.add, GROUPS, arin[q][:], arout[q][:], reads=[arin[q]], writes=[arout[q]])
    for q in range(4):
        P.dma("sp", ss[:, q * qw:(q + 1) * qw], arout[q][:], reads=[arout[q]], writes=[ss])
    P.op("act", lambda e: e.activation(ss[:], ss[:], AF.Sqrt, bias=epsb[:], scale=1.0), reads=[ss, epsb], writes=[ss])
    P.op("dve", lambda e: e.reciprocal(ss[:], ss[:]), reads=[ss], writes=[ss])
    for bi, (o, n, t) in enumerate(bl):
        if lat_only and t == 1:
            continue
        x = xb[bi % 2]
        h = ho[bi % 2]
        P.dma("sp", x[:, :, 0:n], xs.t[:, :, o:o + n].rearrange("c p n -> p c n"), reads=[xs], writes=[x])
        for k in range(NK):
            tm_ = tmp[k % 2]
            P.op("dve", lambda e, tm_=tm_, x=x, k=k, n=n, t=t, o=o: e.scalar_tensor_tensor(
                tm_[:, 0:n], x[:, k, 0:n], Av[:, t, k:k + 1], ss[:, o:o + n], ALU.mult, ALU.mult), reads=[x, Av, ss], writes=[tm_])
            P.op("act", lambda e, tm_=tm_, h=h, k=k, n=n, t=t: e.activation(
                h[:, k, 0:n], tm_[:, 0:n], AF.Identity, bias=Bv[:, t, k:k + 1], scale=1.0), reads=[tm_, Bv], writes=[h])
        oo = o - nctx if lat_only else o
        wr_cpn(P, dst, 0, NK, oo, n, h, lambda so, pn, h=h: h[:, :, so:so + pn])
        if ag_out is not None:
            for ci in range(oo // CW, (oo + n) // CW):
                P.collective("AllGather", ALU.bypass, GROUPS, dst.bufs[ci][:], ag_out.bufs[ci][:], reads=[dst.bufs[ci]], writes=[ag_out.bufs[ci]])


def emit_C2(P, hTb, agbout, wglu, wmg, wps, wpg, wpa, wo, gate, xs_in, xs_out, agmin, agmout, tall, nctx):
    bk = [0]

    def nb():
        bk[0] += 1
        return P.bank(bk[0] % 8)

    ags_o, agl_o, aga_o = agbout

    def agv_rd(src, qn, r, o, n, dstbuf, dst_fn):
        for (b, lo, pn, so) in src.pieces(o, n):
            P.dma("sp", dst_fn(so, pn), b.t.rearrange("(r q p) n -> p r q n", r=4, q=qn, p=128)[:, r, :, lo:lo + pn], reads=[b], writes=[dstbuf])

    sTd = P.dram(f"sTd{P.n_ins}", [1024, tall], BF16)
    P.open_scope()
    wgl = P.sbuf("c_wglu", [128, 8, 1024], BF16)
    P.dma("pool", wgl[:], wglu.t.rearrange("(k p) n -> p k n", p=128), reads=[wglu], writes=[wgl])
    gb_ = [P.sbuf(f"c_gb{i}", [128, 4, 4, 512], BF16) for i in range(2)]
    so_ = [P.sbuf(f"c_so{i}", [128, 8, 512], BF16) for i in range(2)]
    sg = [P.sbuf(f"c_sg{i}", [128, 512], BF16) for i in range(2)]
    o = 0
    it = 0
    while o < tall:
        n = min(512, tall - o)
        a = gb_[it % 2]
        so = so_[it % 2]
        it += 1
        for r in range(4):
            agv_rd(ags_o, 4, r, o, n, a, lambda so, pn, a=a, r=r: a[:, r, :, so:so + pn])
        for oc in range(8):
            pb = nb()
            for kc in range(8):
                P.op("pe", lambda e, pb=pb, a=a, kc=kc, oc=oc, n=n: e.matmul(pb[:, 0:n], wgl[:, kc, oc * 128:(oc + 1) * 128], a[:, kc // 2, kc % 2, 0:n],
                                                                           start=(kc == 0), stop=(kc == 7)),
                     reads=[wgl, a], writes=[pb], signal=(kc == 7))
            s_ = sg[oc % 2]
            P.op("act", lambda e, pb=pb, s_=s_, n=n: e.activation(s_[:, 0:n], pb[:, 0:n], AF.Sigmoid), reads=[pb], writes=[s_])
            P.op("dve", lambda e, s_=s_, a=a, so=so, oc=oc, n=n: e.tensor_tensor(so[:, oc, 0:n], a[:, oc // 2, 2 + oc % 2, 0:n], s_[:, 0:n], ALU.mult),
                 reads=[a, s_], writes=[so])
        P.dma("sp", sTd.t[:, o:o + n].rearrange("(c p) n -> p c n", p=128), so[:, :, 0:n], reads=[so], writes=[sTd])
        o += n
    P.close_scope()
    P.open_scope()
    wmgS = P.sbuf("c_wmg", [128, NDC, 3, 384], BF16)
    wprS = P.sbuf("c_wpr", [128, NDC, 384], BF16)
    hb = [P.sbuf(f"c_hb{i}", [128, NDC, 256], BF16) for i in range(2)]
    sbk = [P.sbuf(f"c_sb{i}", [128, 8, 256], BF16) for i in range(2)]
    ab = [P.sbuf(f"c_ab{i}", [128, 4, 6, 256], BF16) for i in range(2)]
    gs3 = [P.sbuf(f"c_g3{i}", [128, 3, 256], F32) for i in range(2)]
    m1 = P.sbuf("c_m1", [128, 256], F32)
    m2 = P.sbuf("c_m2", [128, 256], F32)
    m3 = P.sbuf("c_m3", [128, 256], F32)
    mo = [P.sbuf(f"c_mo{i}", [128, 3, 256], BF16) for i in range(2)]
    it = 0
    for (k0, nk) in ((0, 3), (3, 3), (6, 2)):
        c0, c1 = k0 * 128, (k0 + nk) * 128
        for br in range(3):
            P.dma("pool", wmgS[:, :, br, 0:nk * 128], wmg.t[:, br * 1024 + c0:br * 1024 + c1].rearrange("(k p) n -> p k n", p=128),
                  reads=[wmg], writes=[wmgS])
        P.dma("pool", wprS[:, 0:8, 0:nk * 128], wps.t[:, c0:c1].rearrange("(k p) n -> p k n", p=128), reads=[wps], writes=[wprS])
        P.dma("pool", wprS[:, 8:16, 0:nk * 128], wpg.t[:, c0:c1].rearrange("(k p) n -> p k n", p=128), reads=[wpg], writes=[wprS])
        P.dma("pool", wprS[:, 16:32, 0:nk * 128], wpa.t[:, c0:c1].rearrange("(k p) n -> p k n", p=128), reads=[wpa], writes=[wprS])
        o = 0
        while o < tall:
            n = min(256, tall - o)
            h = hb[it % 2]
            a = ab[it % 2]
            sb = sbk[it % 2]
            mout = mo[it % 2]
            it += 1
            rd_cpn(P, hTb, o, n, h, lambda so, pn, h=h: h[:, :, so:so + pn])
            P.dma("sp", sb[:, :, 0:n], sTd.t[:, o:o + n].rearrange("(c p) n -> p c n", p=128), reads=[sTd], writes=[sb])
            for r in range(4):
                agv_rd(agl_o, 2, r, o, n, a, lambda so, pn, a=a, r=r: a[:, r, 0:2, so:so + pn])
                agv_rd(aga_o, 4, r, o, n, a, lambda so, pn, a=a, r=r: a[:, r, 2:6, so:so + pn])
            for kk in range(nk):
                g3 = gs3[kk % 2]
                cw = slice(kk * 128, (kk + 1) * 128)
                for br in range(3):
                    pg = nb()
                    for kc in range(NDC):
                        P.op("pe", lambda e, pg=pg, h=h, kc=kc, br=br, cw=cw, n=n: e.matmul(pg[:, 0:n], wmgS[:, kc, br, cw], h[:, kc, 0:n],
                                                                                          start=(kc == 0), stop=(kc == NDC - 1)),
                             reads=[wmgS, h], writes=[pg], signal=(kc == NDC - 1))
                    P.op("act", lambda e, pg=pg, g3=g3, br=br, n=n: e.activation(g3[:, br, 0:n], pg[:, 0:n], AF.Sigmoid), reads=[pg], writes=[g3])
                p1, p2, p3 = nb(), nb(), nb()
                for kc in range(8):
                    P.op("pe", lambda e, p1=p1, sb=sb, kc=kc, cw=cw, n=n: e.matmul(p1[:, 0:n], wprS[:, kc, cw], sb[:, kc, 0:n], start=(kc == 0), stop=(kc == 7)),
                         reads=[wprS, sb], writes=[p1], signal=(kc == 7))
                for kc in range(8):
                    P.op("pe", lambda e, p2=p2, a=a, kc=kc, cw=cw, n=n: e.matmul(p2[:, 0:n], wprS[:, 8 + kc, cw], a[:, kc // 2, kc % 2, 0:n], start=(kc == 0), stop=(kc == 7)),
                         reads=[wprS, a], writes=[p2], signal=(kc == 7))
                for kc in range(16):
                    P.op("pe", lambda e, p3=p3, a=a, kc=kc, cw=cw, n=n: e.matmul(p3[:, 0:n], wprS[:, 16 + kc, cw], a[:, kc // 4, 2 + kc % 4, 0:n], start=(kc == 0), stop=(kc == 15)),
                         reads=[wprS, a], writes=[p3], signal=(kc == 15))
                P.op("dve", lambda e, p1=p1, g3=g3, n=n: e.tensor_tensor(m1[:, 0:n], p1[:, 0:n], g3[:, 0, 0:n], ALU.mult), reads=[p1, g3], writes=[m1])
                P.op("dve", lambda e, p2=p2, g3=g3, n=n: e.tensor_tensor(m2[:, 0:n], p2[:, 0:n], g3[:, 1, 0:n], ALU.mult), reads=[p2, g3], writes=[m2])
                P.op("dve", lambda e, p3=p3, g3=g3, n=n: e.tensor_tensor(m3[:, 0:n], p3[:, 0:n], g3[:, 2, 0:n], ALU.mult), reads=[p3, g3], writes=[m3])
                P.op("pool", lambda e, n=n: e.tensor_tensor(m1[:, 0:n], m1[:, 0:n], m2[:, 0:n], ALU.add), reads=[m1, m2], writes=[m1])
                P.op("pool", lambda e, mout=mout, kk=kk, n=n: e.tensor_tensor(mout[:, kk, 0:n], m1[:, 0:n], m3[:, 0:n], ALU.add), reads=[m1, m3], writes=[mout])
            wr_cpn(P, agmin, k0, nk, o, n, mout, lambda so, pn, mout=mout, nk=nk: mout[:, 0:nk, so:so + pn])
            o += n
    P.close_scope()
    for ci in range(len(agmin.bufs)):
        P.collective("AllGather", ALU.bypass, GROUPS, agmin.bufs[ci][:], agmout.bufs[ci][:], reads=[agmin.bufs[ci]], writes=[agmout.bufs[ci]])
    P.open_scope()
    woS = P.sbuf("c_wo", [128, NDC, 1024], BF16)
    for hf in range(2):
        P.dma("pool", woS[:, hf * 16:(hf + 1) * 16, :], wo.t[hf * 2048:(hf + 1) * 2048, :].rearrange("(k p) n -> p k n", p=128), reads=[wo], writes=[woS])
    mb = [P.sbuf(f"c_mb{i}", [128, NDC, 512], BF16) for i in range(2)]
    xi = [P.sbuf(f"c_xi{i}", [128, NK, 512], F32) for i in range(2)]
    xo = [P.sbuf(f"c_xo{i}", [128, NK, 512], F32) for i in range(2)]
    for bi, (o, n, t) in enumerate(_blocks(nctx, tall)):
        m_, xin, xout = mb[bi % 2], xi[bi % 2], xo[bi % 2]
        rd_cpn(P, agmout, o, n, m_, lambda so, pn, m_=m_: m_[:, :, so:so + pn])
        P.dma("sp", xin[:, :, 0:n], xs_in.t[:, :, o:o + n].rearrange("c p n -> p c n"), reads=[xs_in], writes=[xin])
        for k in range(NK):
            pb = nb()
            for kc in range(NDC):
                P.op("pe", lambda e, pb=pb, m_=m_, kc=kc, k=k, n=n: e.matmul(pb[:, 0:n], woS[:, kc, k * 128:(k + 1) * 128], m_[:, kc, 0:n],
                                                                          start=(kc == 0), stop=(kc == NDC - 1)),
                     reads=[woS, m_], writes=[pb], signal=(kc == NDC - 1))
            P.op("dve", lambda e, pb=pb, xin=xin, xout=xout, k=k, n=n, t=t: e.scalar_tensor_tensor(
                xout[:, k, 0:n], pb[:, 0:n], gate[:, t, k:k + 1], xin[:, k, 0:n], ALU.mult, ALU.add), reads=[pb, gate, xin], writes=[xout])
        P.dma("sp", xs_out.t[:, :, o:o + n].rearrange("c p n -> p c n"), xout[:, :, 0:n], reads=[xout], writes=[xs_out])
    P.close_scope()


def build_fused(nlat=NLAT, nctx=NCTX, nlayers=2, stop=99):
    nc = _nc()
    P = Prog(nc)
    tall = nlat + nctx
    I = "ExternalInput"
    xs0 = P.dram("xs", [NK, 128, tall], F32, I)
    cT = P.dram("cT", [128, 32, 2], F32, I)
    wm = P.dram("wm", [2, D, 3072], F32, I)
    bm = P.dram("bm", [128, 2, 24], F32, I)
    ngd = P.dram("ng", [128, 3, NK], F32, I)
    cosT = P.dram("cosT", [128, nlat], F32, I)
    sinT = P.dram("sinT", [128, nlat], F32, I)
    rmT = P.dram("rmT", [128, 128], F32, I)
    s5k = P.dram("s5k", [128, 260], F32, I)
    glk = P.dram("glk", [128, 384], F32, I)
    cmk = P.dram("cmk", [64, tall], F32, I)
    L = []
    for l in range(nlayers):
        d = {}
        for (nm, shp) in (("wq", [D, TM_W + FM_W]), ("wmg", [D, 3072]), ("wglu", [1024, 1024]), ("wps", [1024, 1024]),
                          ("wpg", [1024, 1024]), ("wpa", [2048, 1024]), ("wo", [D, 1024]), ("gqk", [128, 2]),
                          ("s5p", [128, 32, 4]), ("s5b", [128, 32, 2, 16]), ("s5c", [128, 32, 16]), ("s5d", [16, 16]),
                          ("glw", [16, 4, 64]), ("glb", [64, 4]), ("glg", [128, 1])):
            d[nm] = P.dram(f"{nm}{l}", shp, F32, I)
        L.append(d)
    out = P.dram("out", [NK * 128, nlat], F32, "ExternalOutput")
    modS = P.sbuf("modS", [128, 2, 24, 2], F32)
    ngs = P.sbuf("ngs", [128, 3, NK], F32)
    Av = P.sbuf("Av", [128, 2, NK], F32)
    Bv = P.sbuf("Bv", [128, 2, NK], F32)
    Gv = P.sbuf("Gv", [128, 2, NK], F32)
    P.dma("sp", ngs[:], ngd[:], reads=[ngd], writes=[ngs])
    P.open_scope()
    emit_M2(P, cT, wm, bm, modS)
    P.close_scope()
    xs = xs0
    for l in range(nlayers):
        W = L[l]
        for t in range(2):
            P.op("dve", lambda e, t=t, l=l: e.scalar_tensor_tensor(Av[:, t, :], modS[:, l, 8:16, t], 1.0, ngs[:, l, :], ALU.add, ALU.mult),
                 reads=[modS, ngs], writes=[Av])
            P.op("dve", lambda e, t=t, l=l: e.tensor_copy(Bv[:, t, :], modS[:, l, 0:8, t]), reads=[modS], writes=[Bv])
            P.op("dve", lambda e, t=t, l=l: e.tensor_copy(Gv[:, t, :], modS[:, l, 16:24, t]), reads=[modS], writes=[Gv])
        arin = [P.dram(f"arin{l}_{q}", [128, tall // 4], F32) for q in range(4)]
        arout = [P.dram(f"arout{l}_{q}", [128, tall // 4], F32) for q in range(4)]
        aghin = TT(P, f"aghin{l}", NK * 128, tall, BF16)
        aghout = TT(P, f"aghout{l}", NDC * 128, tall, BF16)
        P.open_scope()
        emit_A2(P, xs, Av, Bv, arin, arout, aghin, BF16, tall, nctx, ag_out=aghout)
        P.close_scope()
        if stop == 2:
            break
        hTb = aghout
        tm = P.dram(f"tm{l}", [tall, TM_W], F32)
        fm = P.dram(f"fm{l}", [FM_W, tall], F32)
        ags_i, ags_o = TT(P, f"agsi{l}", 512, tall, BF16), TT(P, f"agso{l}", 4 * 512, tall, BF16)
        agl_i, agl_o = TT(P, f"agli{l}", 256, tall, BF16), TT(P, f"aglo{l}", 4 * 256, tall, BF16)
        aga_i, aga_o = TT(P, f"agai{l}", 512, tall, BF16), TT(P, f"agao{l}", 4 * 512, tall, BF16)
        gT, gsT, glT, atT = ags_i.sub(0), ags_i.sub(256), agl_i, aga_i

        def ag_all(ti, to):
            for ci in range(len(ti.bufs)):
                P.collective("AllGather", ALU.bypass, GROUPS, ti.bufs[ci][:], to.bufs[ci][:], reads=[ti.bufs[ci]], writes=[to.bufs[ci]])
        P.open_scope()
        emit_B1(P, hTb, W["wq"], tm, fm, tall)
        P.close_scope()
        P.open_scope()
        emit_s5(P, fm, W["s5p"], W["s5b"], W["s5c"], W["s5d"], s5k, gT, gsT, tall, nctx)
        P.close_scope()
        ag_all(ags_i, ags_o)
        P.open_scope()
        emit_gla(P, fm, tm, W["glw"], W["glb"], W["glg"], glk, cmk, glT, tall, nctx)
        P.close_scope()
        ag_all(agl_i, agl_o)
        P.open_scope()
        emit_attn(P, fm, tm, cosT, sinT, rmT, W["gqk"], atT, l < nlayers - 1, tall, nctx)
        P.close_scope()
        ag_all(aga_i, aga_o)
        agmin = TT(P, f"agmin{l}", NK * 128, tall, BF16)
        agmout = TT(P, f"agmout{l}", NDC * 128, tall, BF16)
        xs_new = P.dram(f"xs{l + 1}", [NK, 128, tall], F32)
        emit_C2(P, hTb, (ags_o, agl_o, aga_o), W["wglu"], W["wmg"], W["wps"], W["wpg"], W["wpa"], W["wo"], Gv, xs, xs_new, agmin, agmout, tall, nctx)
        xs = xs_new
    if stop < 99:
        P.finish([])
        P.emit()
        return nc
    P.op("dve", lambda e: e.memset(Bv[:], 0.0), writes=[Bv])
    for t in range(2):
        P.op("dve", lambda e, t=t: e.tensor_copy(Av[:, t, :], ngs[:, 2, :]), reads=[ngs], writes=[Av])
    arin = [P.dram(f"arinF_{q}", [128, tall // 4], F32) for q in range(4)]
    arout = [P.dram(f"aroutF_{q}", [128, tall // 4], F32) for q in range(4)]
    P.open_scope()
    emit_A2(P, xs, Av, Bv, arin, arout, out, F32, tall, nctx, lat_only=True)
    P.close_scope()
    P.finish([out])
    P.emit()
    return nc


def _fused_inputs(x, c, ctx, c_ctx, norm_g, w_mod, b_mod, w_in, ssm_lam_re, ssm_lam_im, ssm_log_dt, ssm_b_re, ssm_b_im,
                  ssm_c_re, ssm_c_im, ssm_d, ssm_w_glu, gla_w_a, gla_b_a, gla_norm_g, attn_q_g, attn_k_g,
                  w_proj_ssm, w_proj_gla, w_proj_attn, w_out, final_g):
    f = lambda a: np.asarray(a, dtype=np.float32)
    x, c, ctx, c_ctx = f(x), f(c), f(ctx), f(c_ctx)
    nlat, nctx = x.shape[1], ctx.shape[1]
    tall = nlat + nctx
    w_mod, b_mod, w_in, norm_g, final_g = f(w_mod), f(b_mod), f(w_in), f(norm_g), f(final_g)
    cosT, sinT, rm = rope_tables(nlat)
    glk, cmk = gla_consts(tall)
    s5k = s5_consts()
    ins = []
    for i in range(NCORES):
        b, j = i // 4, i % 4
        dsl = slice(j * 1024, (j + 1) * 1024)
        xa = np.concatenate([ctx[b], x[b]], 0)
        d = {"xs": np.ascontiguousarray(xa[:, dsl].T.reshape(NK, 128, tall)),
             "cT": np.ascontiguousarray(np.stack([c[b], c_ctx], 0).reshape(2, 32, 128).transpose(2, 1, 0)),
             "cosT": cosT, "sinT": sinT, "rmT": rm, "s5k": s5k, "glk": glk, "cmk": cmk}
        mcols = np.concatenate([np.arange(p * 4096 + j * 1024, p * 4096 + (j + 1) * 1024) for p in range(3)])
        d["wm"] = np.ascontiguousarray(w_mod[:, :, mcols])
        d["bm"] = np.ascontiguousarray(b_mod[:, mcols].reshape(2, 24, 128).transpose(2, 0, 1))
        d["ng"] = np.ascontiguousarray(np.stack([norm_g[0][dsl], norm_g[1][dsl], final_g[dsl]], 0).reshape(3, NK, 128).transpose(2, 0, 1))
        for l in range(2):
            gcols = np.concatenate([np.arange(_OFF["mg"] + br * 4096 + j * 1024, _OFF["mg"] + br * 4096 + (j + 1) * 1024) for br in range(3)])
            prm, bb, cc, dd = s5_host_layout(f(ssm_lam_re)[l], f(ssm_lam_im)[l], f(ssm_log_dt)[l], f(ssm_b_re)[l], f(ssm_b_im)[l],
                                             f(ssm_c_re)[l], f(ssm_c_im)[l], f(ssm_d)[l], j)
            WA, BA = f(gla_w_a)[l], f(gla_b_a)[l]
            d[f"wq{l}"] = np.ascontiguousarray(w_in[l][:, _cols(j)])
            d[f"wmg{l}"] = np.ascontiguousarray(w_in[l][:, gcols])
            d[f"wglu{l}"] = np.ascontiguousarray(f(ssm_w_glu)[l])
            d[f"wps{l}"] = np.ascontiguousarray(f(w_proj_ssm)[l][:, dsl])
            d[f"wpg{l}"] = np.ascontiguousarray(f(w_proj_gla)[l][:, dsl])
            d[f"wpa{l}"] = np.ascontiguousarray(f(w_proj_attn)[l][:, dsl])
            d[f"wo{l}"] = np.ascontiguousarray(f(w_out)[l][:, dsl])
            d[f"gqk{l}"] = np.ascontiguousarray(np.stack([f(attn_q_g)[l], f(attn_k_g)[l]], 1))
            d[f"s5p{l}"], d[f"s5b{l}"], d[f"s5c{l}"], d[f"s5d{l}"] = prm, bb, cc, dd
            d[f"glw{l}"] = np.ascontiguousarray(WA[:, :, 128 * j:128 * (j + 1)].reshape(2, 16, 2, 64).transpose(1, 0, 2, 3).reshape(16, 4, 64))
            d[f"glb{l}"] = np.ascontiguousarray(BA[:, 128 * j:128 * (j + 1)].reshape(4, 64).T)
            d[f"glg{l}"] = np.ascontiguousarray(f(gla_norm_g)[l].reshape(128, 1))
        ins.append(d)
    return ins, nlat, nctx


def kernel_fused(**inputs):
    ins, nlat, nctx = _fused_inputs(**inputs)
    import os
    nc = build_fused(nlat, nctx, stop=int(os.environ.get("FSTOP", "99")))
    res = _run(nc, ins)
    out = np.zeros((2, nlat, D), np.float32)
    for i in range(NCORES):
        b, j = i // 4, i % 4
        out[b][:, j * 1024:(j + 1) * 1024] = res[i]["out"].T
    return out


kernel_unfused = kernel


def kernel(**inputs):
    return kernel_fused(**inputs)
```
